# Optimizing a Trainium2 kernel written in Bass

```python
import jax, jax.numpy as jnp
from jax import lax
import numpy as np

D_MODEL = 1024
BATCH = 4
SEQ = 4096
DEPTH = 2
DEC_BATCH = 128
DEC_SEQ = 8
PAST_LEN = 16384
PAGE_SIZE = 128

N_META = 16
WINDOW = 128
BLOCK = 128
PAD_LEN = BLOCK - N_META
N_HEADS = 8
N_KV_HEADS = 2
Q_PER_KV = N_HEADS // N_KV_HEADS
HEAD_DIM = 64
ATT_SCALE = HEAD_DIM ** -0.5
ATT_Q_W = N_HEADS * HEAD_DIM
ATT_KV_W = N_KV_HEADS * HEAD_DIM
GLA_HEADS = 4
GLA_DK = 64
GLA_DV = 128
GLA_K_W = GLA_HEADS * GLA_DK
GLA_V_W = GLA_HEADS * GLA_DV
GATE_RANK = 16
GATE_NORMALIZER = 16.0
GLA_CHUNK = 128
D_MIX = ATT_Q_W + GLA_V_W
IN_W = ATT_Q_W + 2 * ATT_KV_W + 2 * GLA_K_W + GLA_V_W + GATE_RANK + GLA_V_W
D_FFN = ((8 * D_MODEL + 3 * 256 - 1) // (3 * 256)) * 256
RMS_EPS = 1e-6
MASK_VALUE = -1e30

kernel_name = 'hymba_swa_sink_gla_decode_step'


def rms_norm(x, g, eps=RMS_EPS):
    xf = x.astype(jnp.float32)
    y = xf * lax.rsqrt(jnp.mean(xf * xf, axis=-1, keepdims=True) + eps)
    return (y * g.astype(jnp.float32)).astype(x.dtype)


def project(h, w_in, q_norm, k_norm, w_g2, b_g):
    b, t, _ = h.shape
    z = jnp.einsum('btd,de->bte', h, w_in)
    sizes = (ATT_Q_W, ATT_KV_W, ATT_KV_W, GLA_K_W, GLA_K_W, GLA_V_W, GATE_RANK, GLA_V_W)
    offs = []
    acc = 0
    for s in sizes[:-1]:
        acc += s
        offs.append(acc)
    q, k, v, gq, gk, gv, g_low, og = jnp.split(z, offs, axis=-1)
    q = rms_norm(q.reshape(b, t, N_HEADS, HEAD_DIM), q_norm)
    k = rms_norm(k.reshape(b, t, N_KV_HEADS, HEAD_DIM), k_norm)
    v = v.reshape(b, t, N_KV_HEADS, HEAD_DIM)
    gq = gq.reshape(b, t, GLA_HEADS, GLA_DK) * (GLA_DK ** -0.5)
    gk = gk.reshape(b, t, GLA_HEADS, GLA_DK)
    gv = gv.reshape(b, t, GLA_HEADS, GLA_DV)
    logit = jnp.einsum('btr,re->bte', g_low, w_g2) + b_g
    log_decay = (jax.nn.log_sigmoid(logit.astype(jnp.float32)) / GATE_NORMALIZER).reshape(b, t, GLA_HEADS, GLA_DK)
    return q, k, v, gq, gk, gv, log_decay, og


def sink_softmax(scores, mask, sinks):
    sk = sinks.astype(jnp.float32).reshape(N_KV_HEADS, Q_PER_KV)
    s = jnp.where(mask, scores, MASK_VALUE)
    sink_col = jnp.broadcast_to(sk[:, :, None, None], s.shape[:-1] + (1,))
    return jax.nn.softmax(jnp.concatenate([s, sink_col], axis=-1), axis=-1)[..., :-1]


def window_attention_prompt(q, k, v, pos, sinks):
    b, L = q.shape[:2]
    nb = L // BLOCK
    qb = q.reshape(b, nb, BLOCK, N_KV_HEADS, Q_PER_KV, HEAD_DIM)
    kb = k.reshape(b, nb, BLOCK, N_KV_HEADS, HEAD_DIM)
    vb = v.reshape(b, nb, BLOCK, N_KV_HEADS, HEAD_DIM)
    shift = lambda a: jnp.concatenate([jnp.zeros_like(a[:, :1]), a[:, :-1]], axis=1)
    kk = jnp.concatenate([shift(kb), kb], axis=2)
    vv = jnp.concatenate([shift(vb), vb], axis=2)
    pb = pos.reshape(nb, BLOCK)
    pk = jnp.concatenate([pb - BLOCK, pb], axis=1)
    qp = pb[:, :, None]
    kp = pk[:, None, :]
    mask = (kp <= qp) & (qp - kp < WINDOW) & (kp >= 0)
    scores = jnp.einsum('bnqhgd,bnkhd->bnhgqk', qb, kk, preferred_element_type=jnp.float32) * ATT_SCALE
    p = sink_softmax(scores, mask[None, :, None, None], sinks)
    o = jnp.einsum('bnhgqk,bnkhd->bnqhgd', p.astype(v.dtype), vv)
    return o.reshape(b, L, ATT_Q_W)


def window_attention_sample(q, k, v, buf_k, buf_v, sinks):
    b, t = q.shape[:2]
    nbuf = buf_k.shape[1]
    kk = jnp.concatenate([buf_k.astype(k.dtype), k], axis=1)
    vv = jnp.concatenate([buf_v.astype(v.dtype), v], axis=1)
    qp = PAST_LEN + jnp.arange(t)
    kp = PAST_LEN - nbuf + jnp.arange(nbuf + t)
    mask = (kp[None, :] <= qp[:, None]) & (qp[:, None] - kp[None, :] < WINDOW)
    qg = q.reshape(b, t, N_KV_HEADS, Q_PER_KV, HEAD_DIM)
    scores = jnp.einsum('bqhgd,bkhd->bhgqk', qg, kk, preferred_element_type=jnp.float32) * ATT_SCALE
    p = sink_softmax(scores, mask, sinks)
    o = jnp.einsum('bhgqk,bkhd->bqhgd', p.astype(v.dtype), vv).reshape(b, t, ATT_Q_W)
    return o, kk[:, -nbuf:], vv[:, -nbuf:]


def gla_chunked(q, k, v, log_decay, s0, chunk):
    b, t, hg, dk = q.shape
    dv = v.shape[-1]
    nc = t // chunk
    to_chunks = lambda a: a.reshape((b, nc, chunk) + a.shape[2:]).swapaxes(0, 1).astype(jnp.float32)
    tri = jnp.tril(jnp.ones((chunk, chunk), dtype=bool))

    def step(S, xs):
        qc, kc, vc, gc = xs
        G = jnp.cumsum(gc, axis=1)
        o_inter = jnp.einsum('bthk,bhkv->bthv', qc * jnp.exp(G), S)
        diff = jnp.where(tri[None, :, :, None, None], G[:, :, None] - G[:, None, :], -jnp.inf)
        A = jnp.einsum('bthk,bshk,btshk->bths', qc, kc, jnp.exp(diff))
        o_intra = jnp.einsum('bths,bshv->bthv', A, vc)
        g_last = G[:, -1]
        S = jnp.exp(g_last)[..., None] * S + jnp.einsum('bshk,bshv->bhkv', kc * jnp.exp(g_last[:, None] - G), vc)
        return S, o_inter + o_intra

    S, o = lax.scan(step, s0.astype(jnp.float32), (to_chunks(q), to_chunks(k), to_chunks(v), to_chunks(log_decay)))
    return o.swapaxes(0, 1).reshape(b, t, hg, dv), S


def merge_heads(a, o_gla, og, gla_norm, w_o):
    b, t = a.shape[:2]
    g = rms_norm(o_gla.astype(a.dtype), gla_norm) * jax.nn.silu(og.reshape(b, t, GLA_HEADS, GLA_DV))
    merged = jnp.concatenate([a, g.reshape(b, t, GLA_V_W)], axis=-1)
    return jnp.einsum('bte,ed->btd', merged, w_o)


def swiglu(h, w_gate, w_up, w_down):
    u = jax.nn.silu(jnp.einsum('btd,df->btf', h, w_gate)) * jnp.einsum('btd,df->btf', h, w_up)
    return jnp.einsum('btf,fd->btd', u, w_down)


def setup_inputs(seed: int = 0) -> dict:
    key = jax.random.key(seed)
    ks = jax.random.split(key, 20)
    f32 = jnp.float32
    nrm = lambda k, shape, scale: jax.random.normal(k, shape, f32) * scale
    return {
        'x_prompt': nrm(ks[0], (BATCH, SEQ, D_MODEL), 1.0),
        'x_sample': nrm(ks[1], (DEC_BATCH, DEC_SEQ, D_MODEL), 1.0),
        'cache_k': nrm(ks[2], (DEPTH, DEC_BATCH, WINDOW, N_KV_HEADS, HEAD_DIM), 1.0),
        'cache_v': nrm(ks[3], (DEPTH, DEC_BATCH, WINDOW, N_KV_HEADS, HEAD_DIM), 1.0),
        'state_gla': nrm(ks[4], (DEPTH, DEC_BATCH, GLA_HEADS, GLA_DK, GLA_DV), 0.5),
        'meta': nrm(ks[5], (N_META, D_MODEL), 1.0),
        'norm1': 1.0 + nrm(ks[6], (DEPTH, D_MODEL), 0.02),
        'w_in': nrm(ks[7], (DEPTH, D_MODEL, IN_W), D_MODEL ** -0.5),
        'q_norm': 1.0 + nrm(ks[8], (DEPTH, HEAD_DIM), 0.02),
        'k_norm': 1.0 + nrm(ks[9], (DEPTH, HEAD_DIM), 0.02),
        'sinks': nrm(ks[10], (DEPTH, N_HEADS), 0.5),
        'w_g2': nrm(ks[11], (DEPTH, GATE_RANK, GLA_K_W), GATE_RANK ** -0.5),
        'b_g': nrm(ks[12], (DEPTH, GLA_K_W), 0.1),
        'gla_norm': 1.0 + nrm(ks[13], (DEPTH, GLA_DV), 0.02),
        'w_o': nrm(ks[14], (DEPTH, D_MIX, D_MODEL), D_MIX ** -0.5),
        'norm2': 1.0 + nrm(ks[15], (DEPTH, D_MODEL), 0.02),
        'w_gate': nrm(ks[16], (DEPTH, D_MODEL, D_FFN), D_MODEL ** -0.5),
        'w_up': nrm(ks[17], (DEPTH, D_MODEL, D_FFN), D_MODEL ** -0.5),
        'w_down': nrm(ks[18], (DEPTH, D_FFN, D_MODEL), D_FFN ** -0.5),
    }


def reference(x_prompt, x_sample, cache_k, cache_v, state_gla, meta, norm1, w_in, q_norm, k_norm, sinks,
              w_g2, b_g, gla_norm, w_o, norm2, w_gate, w_up, w_down):
    b = x_prompt.shape[0]
    dt = x_prompt.dtype
    x = jnp.concatenate([jnp.zeros((b, PAD_LEN, D_MODEL), dt),
                         jnp.broadcast_to(meta.astype(dt)[None], (b, N_META, D_MODEL)),
                         x_prompt], axis=1)
    pos = jnp.arange(x.shape[1]) - PAD_LEN
    valid = (pos >= 0).astype(dt)
    xs = x_sample
    pk, pv, ps, sk, sv, ss = [], [], [], [], [], []
    for l in range(DEPTH):
        h = rms_norm(x, norm1[l])
        q, k, v, gq, gk, gv, ld, og = project(h, w_in[l], q_norm[l], k_norm[l], w_g2[l], b_g[l])
        a = window_attention_prompt(q, k, v, pos, sinks[l])
        gk = gk * valid[None, :, None, None]
        o, s_fin = gla_chunked(gq, gk, gv, ld, jnp.zeros((b, GLA_HEADS, GLA_DK, GLA_DV), jnp.float32), GLA_CHUNK)
        x = x + merge_heads(a, o, og, gla_norm[l], w_o[l])
        x = x + swiglu(rms_norm(x, norm2[l]), w_gate[l], w_up[l], w_down[l])
        pk.append(k[:, -WINDOW:])
        pv.append(v[:, -WINDOW:])
        ps.append(s_fin)
        h = rms_norm(xs, norm1[l])
        q, k, v, gq, gk, gv, ld, og = project(h, w_in[l], q_norm[l], k_norm[l], w_g2[l], b_g[l])
        a, kbuf, vbuf = window_attention_sample(q, k, v, cache_k[l], cache_v[l], sinks[l])
        o, s_fin = gla_chunked(gq, gk, gv, ld, state_gla[l], xs.shape[1])
        xs = xs + merge_heads(a, o, og, gla_norm[l], w_o[l])
        xs = xs + swiglu(rms_norm(xs, norm2[l]), w_gate[l], w_up[l], w_down[l])
        sk.append(kbuf)
        sv.append(vbuf)
        ss.append(s_fin)
    y_prompt = x[:, PAD_LEN + N_META:]
    return (y_prompt, xs, jnp.stack(pk), jnp.stack(pv), jnp.stack(ps), jnp.stack(sk), jnp.stack(sv), jnp.stack(ss))
```

```python
import contextlib
import numpy as np
import ml_dtypes
import concourse.bass as bass
import concourse.mybir as mybir
from concourse.bass_utils import run_bass_kernel_spmd

F32 = mybir.dt.float32
BF16 = mybir.dt.bfloat16
AF = mybir.ActivationFunctionType
ALU = mybir.AluOpType
AX = mybir.AxisListType


class Buf:
    def __init__(self, name, t):
        self.name = name
        self.t = t
        self.lw = {}
        self.rd = {}
        self.dsem = None
        self.dcount = 0
        self.excl = False


class Prog:
    ENGS = ("pe", "act", "dve", "pool", "sp")

    def __init__(self, nc):
        self.nc = nc
        self.stack = contextlib.ExitStack()
        self.sems = {}
        self.count = {e: 0 for e in self.ENGS}
        self.seen = {e: {} for e in self.ENGS}
        self.ops = {e: [] for e in self.ENGS}
        for e in self.ENGS:
            self.sems["E_" + e] = self.stack.enter_context(nc.semaphore("sem_" + e))
        self.out_tokens = {}
        self.nbufs = 0

    def sb(self, name, shape, dtype):
        t = self.stack.enter_context(self.nc.sbuf_tensor("s_" + name, list(shape), dtype))
        return Buf(name, t)

    def ps(self, name, shape, dtype):
        t = self.stack.enter_context(self.nc.psum_tensor(name, list(shape), dtype))
        b = Buf(name, t)
        b.excl = True
        return b

    def view(self, name, t):
        return Buf(name, t)

    def _dsem(self, buf, queue):
        if buf.dsem is None:
            buf.dsem = {}
            buf.dcount = {}
        if queue not in buf.dsem:
            key = "D_%d_%s_%s" % (self.nbufs, buf.name, queue)
            self.nbufs += 1
            self.sems[key] = self.stack.enter_context(self.nc.semaphore("dsem_%d" % self.nbufs))
            buf.dsem[queue] = key
            buf.dcount[queue] = 0
        return buf.dsem[queue]

    def _waits(self, eng, reads, writes, ignore_waw=False):
        need = {}
        for b in reads:
            for k, v in b.lw.items():
                if need.get(k, 0) < v:
                    need[k] = v
            if b.excl:
                for k, v in b.rd.items():
                    if k != "E_" + eng and need.get(k, 0) < v:
                        need[k] = v
        for b in writes:
            if not ignore_waw:
                for k, v in b.lw.items():
                    if need.get(k, 0) < v:
                        need[k] = v
            for k, v in b.rd.items():
                if need.get(k, 0) < v:
                    need[k] = v
        out = []
        seen = self.seen[eng]
        for k, v in need.items():
            if eng == "pe" and k == "E_pe":
                continue
            if seen.get(k, 0) < v:
                seen[k] = v
                out.append((k, v))
        return out

    def _commit(self, tok, reads, writes, ignore_waw=False):
        k, v = tok
        for b in reads:
            if b.rd.get(k, 0) < v:
                b.rd[k] = v
        for b in writes:
            if ignore_waw:
                b.lw[k] = v
            else:
                b.lw = {k: v}
            b.rd = {}

    def op(self, eng, fn, reads=(), writes=()):
        waits = self._waits(eng, reads, writes)
        self.count[eng] += 1
        tok = ("E_" + eng, self.count[eng])
        self._commit(tok, reads, writes)
        self.ops[eng].append((waits, fn, tok[0], 1))
        return tok

    def dma(self, queue, sem_buf, out_ap, in_ap, reads=(), writes=(), out=False, ignore_waw=False):
        waits = self._waits(queue, reads, writes, ignore_waw=ignore_waw)
        key = self._dsem(sem_buf, queue)
        sem_buf.dcount[queue] += 16
        tok = (key, sem_buf.dcount[queue])
        self._commit(tok, reads, writes, ignore_waw=ignore_waw)
        self.ops[queue].append((waits, (lambda e: e.dma_start(out=out_ap, in_=in_ap)), key, 16))
        if out:
            self.out_tokens[key] = sem_buf.dcount[queue]
        return tok

    def load(self, queue, buf, dst_ap, src_ap, ignore_waw=False, extra_reads=()):
        return self.dma(queue, buf, dst_ap, src_ap, reads=list(extra_reads), writes=[buf], ignore_waw=ignore_waw)

    def store(self, queue, buf, dst_ap, src_ap, out=True, extra_writes=()):
        return self.dma(queue, buf, dst_ap, src_ap, reads=[buf], writes=list(extra_writes), out=out)

    def finish(self):
        nc = self.nc
        handles = {}
        final_waits = list(self.out_tokens.items())
        with nc.Block() as block:
            def emit(e, name):
                for waits, fn, semkey, inc in self.ops[name]:
                    for k, v in waits:
                        e.wait_ge(self.sems[k], v)
                    fn(e).then_inc(self.sems[semkey], inc)
                if name == "sp":
                    for k, v in final_waits:
                        e.wait_ge(self.sems[k], v)

            @block.tensor
            def _(e):
                emit(e, "pe")

            @block.scalar
            def _(e):
                emit(e, "act")

            @block.vector
            def _(e):
                emit(e, "dve")

            @block.gpsimd
            def _(e):
                emit(e, "pool")

            @block.sync
            def _(e):
                emit(e, "sp")
        self.stack.close()


D = 1024
DF = 2816
NFT = 22
NBP = 33
NB = 34
SB = 33
NCH = 25
CH = 4096
EPS = 1e-6
NL = 2
CQ, CK, CV, CGQ, CGK, CGV, CGL, COG = 0, 512, 640, 768, 1024, 1280, 1792, 1808
import os
STOP = int(os.environ.get("MK_STOP", "99"))
SUB = int(os.environ.get("MK_SUB", "0"))
STOPAT = tuple(int(v) for v in os.environ.get("MK_STOPAT", "0,0").split(","))


class _Stop(Exception):
    pass


BLKS = [int(v) for v in os.environ["MK_BLKS"].split(",")] if os.environ.get("MK_BLKS") else list(range(NB))


def build():
    nc = bass.Bass("TRN2", target_bir_lowering=False)
    P = Prog(nc)

    def din(name, shape, dtype=F32):
        return nc.dram_tensor(name, list(shape), dtype, kind="ExternalInput").ap()

    def dout(name, shape, dtype=F32):
        return nc.dram_tensor(name, list(shape), dtype, kind="ExternalOutput").ap()

    xin = din("xin", [NB * 128, D])
    w_in = din("w_in", [NL, D, 2320])
    w_o = din("w_o", [NL, D, D])
    w_gate = din("w_gate", [NL, D, DF])
    w_up = din("w_up", [NL, D, DF])
    w_down = din("w_down", [NL, DF, D])
    g1_d = din("g1", [128, NL * 8])
    g2_d = din("g2", [128, NL * 8])
    gg_d = din("gg", [128, NL])
    qkg_d = din("qkg", [64, NL * 2])
    snk_d = din("snk", [128, NL * 8])
    wg2_d = din("wg2", [32, NL * 256])
    bg_d = din("bg", [32, NL * 256])
    ck_d = din("ck", [NL, 16, 128, 128])
    cv_d = din("cv", [NL, 16, 128, 128])
    st_d = din("st", [NL, 16, 4, 64, 128])
    identb_d = din("identb", [128, 128], BF16)
    identf_d = din("identf", [128, 128])
    masks_d = din("masks", [128, 5 * 128 + 8], BF16)
    umat_d = din("umat", [128, 256])
    valid_d = din("valid", [128, NB])
    eq_d = din("eq", [64, 16 * 128], BF16)
    e2_d = din("e2", [128, 16], BF16)

    yp = dout("yp", [(NBP - 1) * 128, D])
    ys = dout("ys", [128, D])
    pk = dout("pk", [NL, 128, 128])
    pv = dout("pv", [NL, 128, 128])
    pst = dout("pst", [NL, 4, 64, 128])
    sk = dout("sk", [NL, 16, 128, 128])
    sv = dout("sv", [NL, 16, 128, 128])
    sst = dout("sst", [NL, 16, 4, 64, 128])

    wsc_t = nc.dram_tensor("wsc", [NL, NCH, 128, CH], BF16, kind="ExternalOutput").ap()
    wsc = Buf("wsc", wsc_t)

    sb = P.sb
    identb = sb("identb", [128, 128], BF16)
    identf = sb("identf", [128, 128], F32)
    masks = sb("masks", [128, 5 * 128 + 8], BF16)
    umat = sb("umat", [128, 256], F32)
    valid = sb("valid", [128, NB], F32)
    eq = sb("eq", [64, 16, 128], BF16)
    e2 = sb("e2", [128, 16], BF16)
    g1 = sb("g1", [128, NL * 8], F32)
    g2 = sb("g2", [128, NL * 8], F32)
    gg = sb("gg", [128, NL], F32)
    qkg = sb("qkg", [64, NL * 2], F32)
    esink = sb("esink", [128, NL * 8], F32)
    wg2 = sb("wg2", [32, NL * 256], F32)
    bg = sb("bg", [32, NL * 256], F32)
    ones64 = sb("ones64", [64, 64], BF16)
    ones1 = sb("ones1", [32, 128], F32)

    x = sb("x", [128, D], F32)
    junk = sb("junk", [128, D], BF16)
    st4 = sb("st4", [128, 8], F32)
    hb = sb("hb", [128, D], BF16)
    hT = sb("hT", [128, 8, 128], BF16)
    qkraw = sb("qkraw", [64, 10, 128], F32)
    qksq = sb("qksq", [64, 10, 128], BF16)
    rqk = sb("rqk", [64, 10, 128], F32)
    qn = sb("qn", [64, 8, 128], BF16)
    kn32 = sb("kn32", [64, 2, 128], F32)
    kbuf = [sb("kbuf%d" % l, [64, 2, 2, 128], BF16) for l in range(NL)]
    vaug = [sb("vaug%d" % l, [128, 2, 2, 65], BF16) for l in range(NL)]
    gqraw = sb("gqraw", [64, 4, 128], F32)
    gkraw = sb("gkraw", [64, 4, 128], F32)
    glowT = sb("glowT", [32, 128], F32)
    v32 = sb("v32", [128, 128], F32)
    gvb = sb("gvb", [128, 512], BF16)
    sog = sb("sog", [128, 512], F32)
    e1 = sb("e1", [128, 256], F32)
    lnt = sb("lnt", [128, 256], F32)
    eG = sb("eG", [64, 4, 128], F32)
    enG = sb("enG", [64, 4, 128], F32)
    qtil = sb("qtil", [64, 4, 128], BF16)
    kt32 = sb("kt32", [64, 4, 128], F32)
    ktil = sb("ktil", [64, 4, 128], BF16)
    khatT = sb("khatT", [64, 4, 128], BF16)
    khat = sb("khat", [128, 256], BF16)
    atm = sb("atm", [128, 4, 128], BF16)
    o32 = sb("o32", [128, 4, 128], F32)
    osq = sb("osq", [128, 4, 128], F32)
    gst = sb("gst", [128, 16], F32)
    S = [sb("S%d" % l, [64, 4, 128], F32) for l in range(NL)]
    Sb = [sb("Sb%d" % l, [64, 4, 128], BF16) for l in range(NL)]
    merged = sb("merged", [128, D], BF16)
    mT = sb("mT", [128, 8, 128], BF16)
    uT = sb("uT", [128, NFT, 128], BF16)
    sg = [sb("sg%d" % i, [128, 128], F32) for i in range(2)]
    PT = [sb("PT%d" % i, [128, 2, 512], BF16) for i in range(2)]
    den = sb("den", [128, 8], F32)
    kt_out = sb("kt_out", [128, 128], F32)
    ring = [sb("ring%d" % i, [128, CH], BF16) for i in range(4)]
    stg = [sb("stg%d" % i, [128, 8, 128], F32) for i in range(3)]
    cht = [sb("cht%d" % i, [128, CH], BF16) for i in range(2)]
    cst = sb("cst", [128, 16, 128], F32)
    ckb = sb("ckb", [128, 16, 128], BF16)
    kcT = sb("kcT", [64, 16, 2, 128], BF16)
    vc = sb("vc", [128, 16, 2, 65], BF16)
    ptc = sb("ptc", [128, 1024], BF16)
    otc = sb("otc", [65, 2, 4, 128], F32)
    s0q = sb("s0q", [64, 4, 4, 128], F32)
    s0b = sb("s0b", [64, 4, 4, 128], BF16)
    qx = sb("qx", [64, 4, 4, 128], BF16)
    qns = sb("qns", [64, 16, 2, 32], BF16)
    khx = sb("khx", [128, 4, 256], BF16)

    pbF = [P.ps("pbF%d" % i, [128, 512], F32) for i in range(3)]
    pbT = [P.ps("pbT%d" % i, [128, 512], F32) for i in range(2)]
    pbA = [P.ps("pbA%d" % i, [128, 512], F32) for i in range(2)]
    pb16 = P.ps("pb16", [128, 1024], BF16)
    rr = {"F": 0, "T": 0, "A": 0}

    def bankF():
        rr["F"] += 1
        return pbF[rr["F"] % 3]

    def bankT():
        rr["T"] += 1
        return pbT[rr["T"] % 2]

    def bankA():
        rr["A"] += 1
        return pbA[rr["A"] % 2]

    def mm(out, lhsT, rhs, start, stop, r, w):
        P.op("pe", lambda e: e.matmul(out, lhsT, rhs, start=start, stop=stop), reads=r, writes=w)

    def tr(out, in_, ident, r, w):
        P.op("pe", lambda e: e.transpose(out, in_, ident), reads=r, writes=w)

    def act(out, in_, func, r, w, **kw):
        P.op("act", lambda e: e.activation(out, in_, func, **kw), reads=r, writes=w)

    def cp(eng, out, in_, r, w):
        if eng == "act":
            act(out, in_, AF.Copy, r, w)
        else:
            P.op(eng, lambda e: e.tensor_copy(out, in_), reads=r, writes=w)

    def tt(eng, out, a, b, op, r, w):
        P.op(eng, lambda e: e.tensor_tensor(out, a, b, op), reads=r, writes=w)

    def tsc(eng, out, a, s1, s2, op0, op1, r, w):
        if s2 is None:
            P.op(eng, lambda e: e.tensor_scalar(out, a, s1, None, op0), reads=r, writes=w)
        else:
            P.op(eng, lambda e: e.tensor_scalar(out, a, s1, s2, op0, op1), reads=r, writes=w)

    def stt(eng, out, a, s, b, op0, op1, r, w):
        P.op(eng, lambda e: e.scalar_tensor_tensor(out, a, s, b, op0, op1), reads=r, writes=w)

    def mset(eng, buf, ap, val):
        P.op(eng, lambda e: e.memset(ap, val), writes=[buf])

    cld = Buf("cld", None)
    cbufs = []
    for b_, d_ in ((identb, identb_d), (identf, identf_d), (masks, masks_d), (umat, umat_d), (valid, valid_d),
                   (e2, e2_d), (g1, g1_d), (g2, g2_d), (gg, gg_d), (qkg, qkg_d), (esink, snk_d), (wg2, wg2_d), (bg, bg_d)):
        P.dma("pool", cld, b_.t[:, :], d_, writes=[b_])
        cbufs.append(b_)
    P.dma("pool", cld, eq.t[:, :, :], eq_d.rearrange("p (s t) -> p s t", s=16), writes=[eq])
    cbufs.append(eq)
    for b_ in cbufs:
        b_.lw = {cld.dsem["pool"]: cld.dcount["pool"]}
    act(esink.t[:, :], esink.t[:, :], AF.Exp, [esink], [esink])
    mset("dve", ones64, ones64.t[:, :], 1.0)
    mset("dve", ones1, ones1.t[:, :], 1.0)
    for l in range(NL):
        mset("dve", S[l], S[l].t[:, :, :], 0.0)
        mset("dve", Sb[l], Sb[l].t[:, :, :], 0.0)
        mset("pool", kbuf[l], kbuf[l].t[:, :, :, :], 0.0)
        mset("pool", vaug[l], vaug[l].t[:, :, :, :], 0.0)
        mset("pool", vaug[l], vaug[l].t[:, :, :, 64:65], 1.0)
    mset("pool", vc, vc.t[:, :, :, 64:65], 1.0)
    m_own = masks.t[:, 0:128]
    m_prev = masks.t[:, 128:256]
    m_own0 = masks.t[:, 256:384]
    m_samp = masks.t[:, 384:512]
    m_prev1 = masks.t[:, 512:640]
    m_cache = masks.t[:, 640:648]

    prep_state = {"stg": 0, "cht": 0, "eng": 0}
    for ct_ in cht:
        mset("pool", ct_, ct_.t[:, :], 0.0)

    def prep_piece(ct, dst, src, kc, wdt, gain):
        s_ = stg[prep_state["stg"] % 3]
        prep_state["stg"] += 1
        q_ = "pool" if prep_state["stg"] % 2 else "sp"
        P.load(q_, s_, s_.t[:, 0:kc, 0:wdt], src)
        eng = ("dve", "pool")[prep_state["eng"] % 2]
        prep_state["eng"] += 1
        if gain is None:
            if prep_state["eng"] % 3 == 0:
                eng = "act"
            cp(eng, dst, s_.t[:, 0:kc, 0:wdt], [s_], [ct])
        else:
            gbuf, gap = gain
            tt(eng, dst, s_.t[:, 0:kc, 0:wdt], gap.unsqueeze(2).broadcast_to([128, kc, wdt]), ALU.mult, [s_, gbuf], [ct])

    def prep_chunk(l, c, pieces, zero=None):
        ct = cht[prep_state["cht"] % 2]
        prep_state["cht"] += 1
        if zero is not None:
            mset("dve", ct, kview(ct, zero[0])[:, :, zero[1]:zero[2]], 0.0)
        for (dst_fn, src, kc, wdt, gain) in pieces:
            prep_piece(ct, dst_fn(ct), src, kc, wdt, gain)
        for q4 in range(4):
            P.dma("sp", ct, wsc_t[l, c][:, q4 * 1024:(q4 + 1) * 1024], ct.t[:, q4 * 1024:(q4 + 1) * 1024], reads=[ct], writes=[wsc], ignore_waw=True)

    def kview(ct, W):
        return ct.t[:, 0:8 * W].rearrange("p (k w) -> p k w", k=8)

    for l in range(NL):
        wi = w_in[l].rearrange("(k p) c -> p k c", p=128)
        wo_ = w_o[l].rearrange("(k p) c -> p k c", p=128)
        wg_ = w_gate[l].rearrange("(k p) c -> p k c", p=128)
        wu_ = w_up[l].rearrange("(k p) c -> p k c", p=128)
        wd_ = w_down[l].rearrange("(k p) c -> p k c", p=128)
        G1 = (g1, g1.t[:, l * 8:(l + 1) * 8])
        G2 = (g2, g2.t[:, l * 8:(l + 1) * 8])

        def cols(W, d0, s0, n, src, gain):
            out = []
            for o in range(0, n, 128):
                wdt = min(128, n - o)
                out.append(((lambda ct, W=W, a=d0 + o, wdt=wdt: kview(ct, W)[:, :, a:a + wdt]), src[:, :, s0 + o:s0 + o + wdt], 8, wdt, gain))
            return out

        prep_chunk(l, 0, cols(512, 0, CQ, 512, wi, G1))
        prep_chunk(l, 1, cols(384, 0, CK, 128, wi, G1) + cols(384, 128, CGQ, 256, wi, G1))
        prep_chunk(l, 2, cols(288, 0, CGK, 256, wi, G1) + cols(288, 256, CGL, 16, wi, G1), zero=(288, 272, 288))
        prep_chunk(l, 3, cols(128, 0, CV, 128, wi, G1))
        prep_chunk(l, 4, cols(512, 0, CGV, 512, wi, G1))
        prep_chunk(l, 5, cols(512, 0, COG, 512, wi, G1))
        for hf in range(2):
            pcs = []
            for o in range(0, 512, 128):
                a = hf * 512 + o
                pcs.append(((lambda ct, o=o: kview(ct, 512)[:, 0:4, o:o + 128]), wo_[:, 0:4, a:a + 128], 4, 128, None))
                pcs.append(((lambda ct, o=o: kview(ct, 512)[:, 4:8, o:o + 128]), wo_[:, 4:8, a:a + 128], 4, 128,
                            (gg, gg.t[:, l:l + 1].broadcast_to([128, 4]))))
            prep_chunk(l, 6 + hf, pcs)
        for i in range(11):
            prep_chunk(l, 8 + i, cols(512, 0, 256 * i, 256, wg_, G2) + cols(512, 256, 256 * i, 256, wu_, G2))
        for j in range(6):
            nft = min(4, NFT - 4 * j)
            pcs = []
            for o in range(0, 1024, 128):
                pcs.append(((lambda ct, o=o, nft=nft: ct.t[:, :].rearrange("p (k w) -> p k w", k=4)[:, 0:nft, o:o + 128]),
                            wd_[:, 4 * j:4 * j + nft, o:o + 128], nft, 128, None))
            prep_chunk(l, 19 + j, pcs)

    wseq = [(l, c) for _b in BLKS for l in range(NL) for c in range(NCH)]
    wst = {"issued": 0}

    def need(i):
        while wst["issued"] < min(len(wseq), i + 4):
            j = wst["issued"]
            l_, c_ = wseq[j]
            rb = ring[j % 4]
            for q4 in range(4):
                P.dma("sp", rb, rb.t[:, q4 * 1024:(q4 + 1) * 1024], wsc_t[l_, c_][:, q4 * 1024:(q4 + 1) * 1024], reads=[wsc], writes=[rb], ignore_waw=(q4 > 0))
            wst["issued"] += 1
        return ring[i % 4]

    def rmsnorm_T(gslot):
        mset("dve", st4, st4.t[:, 0:1], 0.0)
        act(junk.t[:, :], x.t[:, :], AF.Square, [x], [junk, st4], accum_out=st4.t[:, 0:1])
        act(st4.t[:, 2:3], st4.t[:, 0:1], AF.Ln, [st4], [st4], scale=1.0 / D, bias=epsb.t[:, 0:1])
        act(st4.t[:, 3:4], st4.t[:, 2:3], AF.Exp, [st4], [st4], scale=-0.5)
        tsc("dve", hb.t[:, :], x.t[:, :], st4.t[:, 3:4], None, ALU.mult, None, [x, st4], [hb])
        for kc in range(8):
            tr(pb16.t[:, kc * 128:(kc + 1) * 128], hb.t[:, kc * 128:(kc + 1) * 128], identb.t[:, :], [hb, identb], [pb16])
        cp("act", hT.t[:, :, :], pb16.t[:, :].rearrange("p (k t) -> p k t", k=8), [pb16], [hT])

    epsb = sb("epsb", [128, 1], F32)
    mset("dve", epsb, epsb.t[:, :], EPS)
    epsb1 = sb("epsb1", [128, 1], F32)
    mset("dve", epsb1, epsb1.t[:, :], 1.0)
    pth = Buf("pth", None)

    try:
      for bi, blk in enumerate(BLKS):
          samp = (blk == SB)
          slot = blk % 2
          P.load("pool", x, x.t[:, :], xin[blk * 128:(blk + 1) * 128, :])
          for l in range(NL):
              wbase = (bi * NL + l) * NCH
              if STOP == 1 and (bi, l) == STOPAT:
                  raise _Stop()
              rmsnorm_T(0)
              if STOP == 2 and (bi, l) == STOPAT:
                  raise _Stop()
              W0 = need(wbase + 0)
              if SUB == 1:
                  raise _Stop()
              W0v = W0.t[:, :].rearrange("p (k w) -> p k w", k=8)
              for half in range(2):
                  bk = bankF()
                  for hh in range(4):
                      h = half * 4 + hh
                      for kc in range(8):
                          mm(bk.t[0:64, hh * 128:(hh + 1) * 128], W0v[:, kc, h * 64:(h + 1) * 64], hT.t[:, kc, :], kc == 0, kc == 7, [W0, hT], [bk])
                      if SUB == 2:
                          raise _Stop()
                  bv = bk.t[0:64, :].rearrange("p (h t) -> p h t", h=4)
                  if SUB == 3:
                      raise _Stop()
                  cp("dve", qkraw.t[:, half * 4:half * 4 + 4, :], bv, [bk], [qkraw])
                  if SUB == 4:
                      raise _Stop()
                  act(qksq.t[:, half * 4:half * 4 + 4, :], bv, AF.Square, [bk], [qksq])
                  if SUB == 5:
                      raise _Stop()
              W1 = need(wbase + 1)
              W1v = W1.t[:, 0:8 * 384].rearrange("p (k w) -> p k w", k=8)
              bk = bankF()
              for g in range(2):
                  for kc in range(8):
                      mm(bk.t[0:64, g * 128:(g + 1) * 128], W1v[:, kc, g * 64:(g + 1) * 64], hT.t[:, kc, :], kc == 0, kc == 7, [W1, hT], [bk])
              bv = bk.t[0:64, 0:256].rearrange("p (h t) -> p h t", h=2)
              cp("dve", qkraw.t[:, 8:10, :], bv, [bk], [qkraw])
              act(qksq.t[:, 8:10, :], bv, AF.Square, [bk], [qksq])
              bk = bankF()
              for h in range(4):
                  for kc in range(8):
                      mm(bk.t[0:64, h * 128:(h + 1) * 128], W1v[:, kc, 128 + h * 64:128 + (h + 1) * 64], hT.t[:, kc, :], kc == 0, kc == 7, [W1, hT], [bk])
              cp("act", gqraw.t[:, :, :], bk.t[0:64, :].rearrange("p (h t) -> p h t", h=4), [bk], [gqraw])
              W2 = need(wbase + 2)
              W2v = W2.t[:, 0:8 * 288].rearrange("p (k w) -> p k w", k=8)
              bk = bankF()
              for h in range(4):
                  for kc in range(8):
                      mm(bk.t[0:64, h * 128:(h + 1) * 128], W2v[:, kc, h * 64:(h + 1) * 64], hT.t[:, kc, :], kc == 0, kc == 7, [W2, hT], [bk])
              cp("dve", gkraw.t[:, :, :], bk.t[0:64, :].rearrange("p (h t) -> p h t", h=4), [bk], [gkraw])
              bk = bankF()
              for kc in range(8):
                  mm(bk.t[0:32, 0:128], W2v[:, kc, 256:288], hT.t[:, kc, :], kc == 0, kc == 7, [W2, hT], [bk])
              cp("act", glowT.t[:, :], bk.t[0:32, 0:128], [bk], [glowT])
              if STOP == 3 and (bi, l) == STOPAT:
                  raise _Stop()
              for i, (c0, c1) in enumerate(((0, 4), (4, 8), (8, 10))):
                  bk = bankF()
                  n = (c1 - c0) * 128
                  mm(bk.t[0:64, 0:n], ones64.t[:, :], qksq.t[:, :, :].rearrange("p h t -> p (h t)")[:, c0 * 128:c1 * 128], True, True, [ones64, qksq], [bk])
                  bv = bk.t[0:64, 0:n].rearrange("p (h t) -> p h t", h=c1 - c0)
                  act(rqk.t[:, c0:c1, :], bv, AF.Ln, [bk], [rqk], scale=1.0 / 64, bias=epsb.t[0:64, 0:1])
              act(rqk.t[:, :, :], rqk.t[:, :, :], AF.Exp, [rqk], [rqk], scale=-0.5)
              stt("dve", qn.t[:, :, :], qkraw.t[:, 0:8, :], qkg.t[:, 2 * l:2 * l + 1], rqk.t[:, 0:8, :], ALU.mult, ALU.mult, [qkraw, qkg, rqk], [qn])
              stt("dve", kn32.t[:, :, :], qkraw.t[:, 8:10, :], qkg.t[:, 2 * l + 1:2 * l + 2], rqk.t[:, 8:10, :], ALU.mult, ALU.mult, [qkraw, qkg, rqk], [kn32])
              cp("pool", kbuf[l].t[:, slot, :, :], kn32.t[:, :, :], [kn32], [kbuf[l]])
              if STOP == 4 and (bi, l) == STOPAT:
                  raise _Stop()
              W3 = need(wbase + 3)
              W3v = W3.t[:, 0:8 * 128].rearrange("p (k w) -> p k w", k=8)
              bk = bankT()
              for kc in range(8):
                  mm(bk.t[:, 0:128], hT.t[:, kc, :], W3v[:, kc, :], kc == 0, kc == 7, [W3, hT], [bk])
              cp("act", v32.t[:, :], bk.t[:, 0:128], [bk], [v32])
              cp("dve", vaug[l].t[:, slot, :, 0:64], bk.t[:, 0:128].rearrange("p (g d) -> p g d", g=2), [bk], [vaug[l]])
              W4 = need(wbase + 4)
              W4v = W4.t[:, :].rearrange("p (k w) -> p k w", k=8)
              bk = bankT()
              for kc in range(8):
                  mm(bk.t[:, :], hT.t[:, kc, :], W4v[:, kc, :], kc == 0, kc == 7, [W4, hT], [bk])
              cp("act", gvb.t[:, :], bk.t[:, :], [bk], [gvb])
              W5 = need(wbase + 5)
              W5v = W5.t[:, :].rearrange("p (k w) -> p k w", k=8)
              bk = bankT()
              for kc in range(8):
                  mm(bk.t[:, :], hT.t[:, kc, :], W5v[:, kc, :], kc == 0, kc == 7, [W5, hT], [bk])
              act(sog.t[:, :], bk.t[:, :], AF.Silu, [bk], [sog])

              if samp:
                  P.load("pool", cst, cst.t[:, :, :], ck_d[l].rearrange("s k c -> k s c"))
                  cp("pool", ckb.t[:, :, :], cst.t[:, :, :], [cst], [ckb])
                  for rnd in range(4):
                      for i in range(8):
                          s_, g_ = (rnd * 8 + i) // 2, (rnd * 8 + i) % 2
                          tr(pb16.t[0:64, i * 128:(i + 1) * 128], ckb.t[:, s_, g_ * 64:(g_ + 1) * 64], identb.t[:, :], [ckb, identb], [pb16])
                      cp("dve", kcT.t[:, rnd * 4:rnd * 4 + 4, :, :], pb16.t[0:64, :].rearrange("p (s g t) -> p s g t", s=4, g=2), [pb16], [kcT])
                  P.dma("pool", pth, sk[l, :, 0:120, :], ck_d[l, :, 8:128, :], reads=[], writes=[], out=True)
                  P.load("pool", cst, cst.t[:, :, :], cv_d[l].rearrange("s k c -> k s c"))
                  cp("pool", vc.t[:, :, :, 0:64], cst.t[:, :, :].rearrange("p s (g d) -> p s g d", g=2), [cst], [vc])
                  P.dma("pool", pth, sv[l, :, 0:120, :], cv_d[l, :, 8:128, :], reads=[], writes=[], out=True)
                  cp("pool", qns.t[:, :, :, :].rearrange("p s g (h t) -> p g h s t", h=4)[:, 0], qn.t[:, 0:4, :].rearrange("p h (s t) -> p h s t", t=8), [qn], [qns])
                  cp("pool", qns.t[:, :, :, :].rearrange("p s g (h t) -> p g h s t", h=4)[:, 1], qn.t[:, 4:8, :].rearrange("p h (s t) -> p h s t", t=8), [qn], [qns])
                  stc = [bankA(), bankA()]
                  for s_ in range(16):
                      for g_ in range(2):
                          bk = stc[s_ // 8]
                          o_ = ((s_ % 8) * 2 + g_) * 32
                          mm(bk.t[:, o_:o_ + 32], kcT.t[:, s_, g_, :], qns.t[:, s_, g_, :], True, True, [kcT, qns], [bk])
                  for hf in range(2):
                      act(ptc.t[:, hf * 512:(hf + 1) * 512], stc[hf].t[:, :], AF.Exp, [stc[hf]], [ptc], scale=0.125)
                  tt("pool", ptc.t[:, :].rearrange("p (a q) -> p a q", q=8), ptc.t[:, :].rearrange("p (a q) -> p a q", q=8),
                     m_cache.unsqueeze(1).broadcast_to([128, 128, 8]), ALU.mult, [ptc, masks], [ptc])
                  otb = [bankA(), bankA()]
                  for s_ in range(16):
                      for g_ in range(2):
                          bk = otb[s_ // 8]
                          o_ = ((s_ % 8) * 2 + g_) * 32
                          mm(bk.t[0:65, o_:o_ + 32], vc.t[:, s_, g_, :], ptc.t[:, (s_ * 2 + g_) * 32:(s_ * 2 + g_ + 1) * 32], True, True, [vc, ptc], [bk])
                  for hf in range(2):
                      for g_ in range(2):
                        cp("act" if g_ else "dve", otc.t[:, g_, :, hf * 64:(hf + 1) * 64].rearrange("p h (s t) -> p s h t", t=8),
                           otb[hf].t[0:65, :].rearrange("p (s g h t) -> p s g h t", s=8, g=2, h=4)[:, :, g_, :, :], [otb[hf]], [otc])

              if STOP == 5 and (bi, l) == STOPAT:
                  raise _Stop()
              pt = PT[(blk * NL + l) % 2]
              for g in range(2):
                  bo = bankA()
                  rq_ = qn.t[:, :, :].rearrange("p h t -> p (h t)")[:, 4 * g * 128:(4 * g + 4) * 128]
                  mm(bo.t[:, :], kbuf[l].t[:, slot, g, :], rq_, True, True, [kbuf[l], qn], [bo])
                  act(pt.t[:, 0, :], bo.t[:, :], AF.Exp, [bo], [pt], scale=0.125)
                  mk = m_samp if samp else (m_own0 if blk == 0 else m_own)
                  tt("pool", pt.t[:, 0, :].rearrange("p (h t) -> p h t", h=4), pt.t[:, 0, :].rearrange("p (h t) -> p h t", h=4),
                     mk.unsqueeze(1).broadcast_to([128, 4, 128]), ALU.mult, [pt, masks], [pt])
                  has_prev = (not samp) and blk > 0
                  if has_prev:
                      bp = bankA()
                      mm(bp.t[:, :], kbuf[l].t[:, 1 - slot, g, :], rq_, True, True, [kbuf[l], qn], [bp])
                      act(pt.t[:, 1, :], bp.t[:, :], AF.Exp, [bp], [pt], scale=0.125)
                      tt("pool", pt.t[:, 1, :].rearrange("p (h t) -> p h t", h=4), pt.t[:, 1, :].rearrange("p (h t) -> p h t", h=4),
                         (m_prev1 if blk == 1 else m_prev).unsqueeze(1).broadcast_to([128, 4, 128]), ALU.mult, [pt, masks], [pt])
                  bv_ = bankT()
                  for h in range(4):
                      oc = bv_.t[:, h * 65:(h + 1) * 65]
                      last_own = not (has_prev or samp)
                      mm(oc, pt.t[:, 0, h * 128:(h + 1) * 128], vaug[l].t[:, slot, g, :], True, last_own, [pt, vaug[l]], [bv_])
                      if has_prev:
                          mm(oc, pt.t[:, 1, h * 128:(h + 1) * 128], vaug[l].t[:, 1 - slot, g, :], False, True, [pt, vaug[l]], [bv_])
                      if samp:
                          mm(oc, otc.t[:, g, h, :], identf.t[0:65, 0:65], False, True, [otc, identf], [bv_])
                  pv4 = bv_.t[:, 0:260].rearrange("p (h e) -> p h e", h=4)
                  tt("dve", den.t[:, 0:4], pv4[:, :, 64], esink.t[:, l * 8 + 4 * g:l * 8 + 4 * g + 4], ALU.add, [bv_, esink], [den])
                  P.op("dve", lambda e: e.reciprocal(den.t[:, 4:8], den.t[:, 0:4]), reads=[den], writes=[den])
                  tt("dve", merged.t[:, g * 256:(g + 1) * 256].rearrange("p (h d) -> p h d", h=4), pv4[:, :, 0:64],
                     den.t[:, 4:8].unsqueeze(2).broadcast_to([128, 4, 64]), ALU.mult, [bv_, den], [merged])

              if STOP == 6 and (bi, l) == STOPAT:
                  raise _Stop()
              bl = bankA()
              mm(bl.t[:, 0:256], glowT.t[:, :], wg2.t[:, l * 256:(l + 1) * 256], True, False, [glowT, wg2], [bl])
              mm(bl.t[:, 0:256], ones1.t[:, :], bg.t[:, l * 256:(l + 1) * 256], False, True, [ones1, bg], [bl])
              act(e1.t[:, :], bl.t[:, 0:256], AF.Exp, [bl], [e1], scale=-1.0)
              act(lnt.t[:, :], e1.t[:, :], AF.Ln, [e1], [lnt], bias=epsb1.t[:, 0:1])
              U = umat.t[:, 128:256] if samp else umat.t[:, 0:128]
              gmask = m_samp if samp else m_own
              bgT = bankA()
              for h in range(4):
                  mm(bgT.t[0:64, h * 128:(h + 1) * 128], lnt.t[:, h * 64:(h + 1) * 64], U, True, True, [lnt, umat], [bgT])
              g4 = bgT.t[0:64, :].rearrange("p (h t) -> p h t", h=4)
              act(eG.t[:, :, :], g4, AF.Exp, [bgT], [eG])
              act(enG.t[:, :, :], g4, AF.Exp, [bgT], [enG], scale=-1.0)
              stt("dve", qtil.t[:, :, :], gqraw.t[:, :, :], 0.125, eG.t[:, :, :], ALU.mult, ALU.mult, [gqraw, eG], [qtil])
              tt("dve", kt32.t[:, :, :], gkraw.t[:, :, :], enG.t[:, :, :], ALU.mult, [gkraw, enG], [kt32])
              cp("pool", ktil.t[:, :, :], kt32.t[:, :, :], [kt32], [ktil])
              if samp:
                  egl = eG.t[:, :, :].rearrange("p h (s t) -> p h s t", t=8)[:, :, :, 7:8].broadcast_to([64, 4, 16, 8])
                  tt("pool", khatT.t[:, :, :].rearrange("p h (s t) -> p h s t", t=8), kt32.t[:, :, :].rearrange("p h (s t) -> p h s t", t=8),
                     egl, ALU.mult, [kt32, eG], [khatT])
              else:
                  egl = eG.t[:, :, 127:128].broadcast_to([64, 4, 128])
                  tt("pool", khatT.t[:, :, :], kt32.t[:, :, :], egl, ALU.mult, [kt32, eG], [khatT])
              for h in range(4):
                  tr(pb16.t[:, h * 64:(h + 1) * 64], khatT.t[:, h, :], identb.t[0:64, 0:64], [khatT, identb], [pb16])
              cp("dve", khat.t[:, :], pb16.t[:, 0:256], [pb16], [khat])
              ba = bankA()
              for h in range(4):
                  mm(ba.t[:, h * 128:(h + 1) * 128], ktil.t[:, h, :], qtil.t[:, h, :], True, True, [ktil, qtil], [ba])
              tt("dve", atm.t[:, :, :], ba.t[:, :].rearrange("p (h t) -> p h t", h=4), gmask.unsqueeze(1).broadcast_to([128, 4, 128]), ALU.mult, [ba, masks], [atm])
              bo = bankT()
              if not samp:
                  for h in range(4):
                      oc = bo.t[:, h * 128:(h + 1) * 128]
                      mm(oc, atm.t[:, h, :], gvb.t[:, h * 128:(h + 1) * 128], True, False, [atm, gvb], [bo])
                      mm(oc, qtil.t[:, h, :], Sb[l].t[:, h, :], False, True, [qtil, Sb[l]], [bo])
                  bs = bankA()
                  for h in range(4):
                      mm(bs.t[0:64, h * 128:(h + 1) * 128], khat.t[:, h * 64:(h + 1) * 64], gvb.t[:, h * 128:(h + 1) * 128], True, True, [khat, gvb], [bs])
                  tt("dve", S[l].t[:, :, :], S[l].t[:, :, :], eG.t[:, :, 127:128].broadcast_to([64, 4, 128]), ALU.mult, [S[l], eG], [S[l]])
                  tt("dve", S[l].t[:, :, :], S[l].t[:, :, :], bs.t[0:64, :].rearrange("p (h v) -> p h v", h=4), ALU.add, [S[l], bs], [S[l]])
                  cp("pool", Sb[l].t[:, :, :], S[l].t[:, :, :], [S[l]], [Sb[l]])
              else:
                  obk = [pbF[0], pbF[1], pbF[2], bo]
                  for h in range(4):
                      mm(obk[h].t[:, 0:128], atm.t[:, h, :], gvb.t[:, h * 128:(h + 1) * 128], True, False, [atm, gvb], [obk[h]])
                  eg4 = eG.t[:, :, :].rearrange("p h (s t) -> p h s t", t=8)
                  for qd in range(4):
                      P.load("pool", s0q, s0q.t[:, :, :, :], st_d[l, 4 * qd:4 * qd + 4].rearrange("s h k v -> k s h v"))
                      cp("pool", s0b.t[:, :, :, :], s0q.t[:, :, :, :], [s0q], [s0b])
                      tt("dve", qx.t[:, :, :, :], qtil.t[:, :, :].unsqueeze(2).broadcast_to([64, 4, 4, 128]),
                         eq.t[:, 4 * qd:4 * qd + 4, :].unsqueeze(1).broadcast_to([64, 4, 4, 128]), ALU.mult, [qtil, eq], [qx])
                      tt("pool", khx.t[:, :, :], khat.t[:, :].unsqueeze(1).broadcast_to([128, 4, 256]),
                         e2.t[:, 4 * qd:4 * qd + 4].unsqueeze(2).broadcast_to([128, 4, 256]), ALU.mult, [khat, e2], [khx])
                      for h in range(4):
                          for s_ in range(4):
                              last = (qd == 3 and s_ == 3)
                              mm(obk[h].t[:, 0:128], qx.t[:, h, s_, :], s0b.t[:, s_, h, :], False, last, [qx, s0b], [obk[h]])
                      for s_ in range(4):
                          bs = bankA()
                          for h in range(4):
                              mm(bs.t[0:64, h * 128:(h + 1) * 128], khx.t[:, s_, h * 64:(h + 1) * 64], gvb.t[:, h * 128:(h + 1) * 128], True, True, [khx, gvb], [bs])
                          sa = 4 * qd + s_
                          tt("dve", s0q.t[:, s_, :, :], s0q.t[:, s_, :, :], eg4[:, :, sa, 7:8].broadcast_to([64, 4, 128]), ALU.mult, [s0q, eG], [s0q])
                          tt("dve", s0q.t[:, s_, :, :], s0q.t[:, s_, :, :], bs.t[0:64, :].rearrange("p (h v) -> p h v", h=4), ALU.add, [s0q, bs], [s0q])
                      P.store("pool", s0q, sst[l, 4 * qd:4 * qd + 4].rearrange("s h k v -> k s h v"), s0q.t[:, :, :, :])
              if samp:
                  for h in range(4):
                      cp("act" if h % 2 else "dve", o32.t[:, h, :], obk[h].t[:, 0:128], [obk[h]], [o32])
              else:
                  cp("act", o32.t[:, :, :], bo.t[:, :].rearrange("p (h v) -> p h v", h=4), [bo], [o32])
              tt("pool", osq.t[:, :, :], o32.t[:, :, :], o32.t[:, :, :], ALU.mult, [o32], [osq])
              P.op("dve", lambda e: e.tensor_reduce(gst.t[:, 0:4], osq.t[:, :, :], AX.X, ALU.add), reads=[osq], writes=[gst])
              act(gst.t[:, 4:8], gst.t[:, 0:4], AF.Ln, [gst], [gst], scale=1.0 / 128, bias=epsb.t[:, 0:1])
              act(gst.t[:, 8:12], gst.t[:, 4:8], AF.Exp, [gst], [gst], scale=-0.5)
              tt("dve", o32.t[:, :, :], o32.t[:, :, :], gst.t[:, 8:12].unsqueeze(2).broadcast_to([128, 4, 128]), ALU.mult, [o32, gst], [o32])
              tt("pool", merged.t[:, 512:1024], o32.t[:, :, :].rearrange("p h v -> p (h v)"), sog.t[:, :], ALU.mult, [o32, sog], [merged])

              if blk == NBP - 1 or samp:
                  for g in range(2):
                      tr(pbT[0].t[:, g * 64:(g + 1) * 64], kn32.t[:, g, :], identf.t[0:64, 0:64], [kn32, identf], [pbT[0]])
                  cp("dve", kt_out.t[:, :], pbT[0].t[:, 0:128], [pbT[0]], [kt_out])
                  if samp:
                      for s_ in range(16):
                          P.store("pool", kt_out, sk[l, s_, 120:128, :], kt_out.t[8 * s_:8 * s_ + 8, :])
                          P.store("pool", v32, sv[l, s_, 120:128, :], v32.t[8 * s_:8 * s_ + 8, :])
                  else:
                      P.store("pool", kt_out, pk[l], kt_out.t[:, :])
                      P.store("pool", v32, pv[l], v32.t[:, :])
                      P.store("pool", S[l], pst[l].rearrange("h k v -> k h v"), S[l].t[:, :, :])

              if STOP == 7 and (bi, l) == STOPAT:
                  raise _Stop()
              for kc in range(8):
                  tr(pb16.t[:, kc * 128:(kc + 1) * 128], merged.t[:, kc * 128:(kc + 1) * 128], identb.t[:, :], [merged, identb], [pb16])
              cp("act", mT.t[:, :, :], pb16.t[:, :].rearrange("p (k t) -> p k t", k=8), [pb16], [mT])
              for hf in range(2):
                  Wc = need(wbase + 6 + hf)
                  Wv = Wc.t[:, :].rearrange("p (k w) -> p k w", k=8)
                  bk = bankT()
                  for kc in range(8):
                      mm(bk.t[:, :], mT.t[:, kc, :], Wv[:, kc, :], kc == 0, kc == 7, [Wc, mT], [bk])
                  stt("dve", x.t[:, hf * 512:(hf + 1) * 512], bk.t[:, :], valid.t[:, blk:blk + 1], x.t[:, hf * 512:(hf + 1) * 512], ALU.mult, ALU.add, [bk, valid, x], [x])

              if STOP == 8 and (bi, l) == STOPAT:
                  raise _Stop()
              rmsnorm_T(1)
              for i in range(11):
                  Wc = need(wbase + 8 + i)
                  Wv = Wc.t[:, :].rearrange("p (k a w) -> p k a w", k=8, a=2)
                  for j in range(2):
                      ft = 2 * i + j
                      bk = bankF()
                      for a in range(2):
                          for kc in range(8):
                              mm(bk.t[:, a * 128:(a + 1) * 128], Wv[:, kc, a, j * 128:(j + 1) * 128], hT.t[:, kc, :], kc == 0, kc == 7, [Wc, hT], [bk])
                      sg_ = sg[ft % 2]
                      act(sg_.t[:, :], bk.t[:, 0:128], AF.Silu, [bk], [sg_])
                      tt("dve", uT.t[:, ft, :], sg_.t[:, :], bk.t[:, 128:256], ALU.mult, [sg_, bk], [uT])
              bd = [bankT(), bankT()]
              for ft in range(NFT):
                  Wc = need(wbase + 19 + ft // 4)
                  Wv = Wc.t[:, :].rearrange("p (k w) -> p k w", k=4)
                  for hf in range(2):
                      mm(bd[hf].t[:, :], uT.t[:, ft, :], Wv[:, ft % 4, hf * 512:(hf + 1) * 512], ft == 0, ft == NFT - 1, [Wc, uT], [bd[hf]])
              for hf in range(2):
                  stt("dve", x.t[:, hf * 512:(hf + 1) * 512], bd[hf].t[:, :], valid.t[:, blk:blk + 1], x.t[:, hf * 512:(hf + 1) * 512], ALU.mult, ALU.add, [bd[hf], valid, x], [x])
          if samp:
              P.store("pool", x, ys, x.t[:, :])
          elif blk >= 1:
              P.store("pool", x, yp[(blk - 1) * 128:blk * 128, :], x.t[:, :])
    except _Stop:
        pass
    P.finish()
    return nc


def _consts():
    bf = ml_dtypes.bfloat16
    j = np.arange(128)[:, None]
    i = np.arange(128)[None, :]
    own = (j <= i)
    prev = (j > i)
    own0 = own & (j >= 112)
    samp = (j // 8 == i // 8) & (j % 8 <= i % 8)
    cache = (np.arange(128)[:, None] > np.arange(8)[None, :])
    prev1 = prev & (j >= 112)
    masks = np.concatenate([own, prev, own0, samp, prev1, cache], axis=1).astype(np.float32).astype(bf)
    umat = np.concatenate([own.astype(np.float32), samp.astype(np.float32)], axis=1) * np.float32(-1.0 / 16.0)
    valid = np.ones((128, NB), np.float32)
    valid[0:112, 0] = 0.0
    t = np.arange(128)
    e2 = (t[:, None] // 8 == np.arange(16)[None, :]).astype(np.float32)
    eq = np.broadcast_to(e2.T[None, :, :], (64, 16, 128)).reshape(64, 16 * 128)
    return dict(identb=np.eye(128, dtype=np.float32).astype(bf), identf=np.eye(128, dtype=np.float32),
                masks=masks, umat=np.ascontiguousarray(umat.astype(np.float32)), valid=valid,
                eq=np.ascontiguousarray(eq).astype(bf), e2=e2.astype(bf))


_NC_CACHE = {}


def kernel(**inp):
    f = lambda a: np.ascontiguousarray(np.asarray(a, dtype=np.float32))
    x_prompt, x_sample = f(inp["x_prompt"]), f(inp["x_sample"])
    cache_k, cache_v, state_gla = f(inp["cache_k"]), f(inp["cache_v"]), f(inp["state_gla"])
    meta = f(inp["meta"])
    norm1, norm2, gla_norm = f(inp["norm1"]), f(inp["norm2"]), f(inp["gla_norm"])
    q_norm, k_norm, sinks = f(inp["q_norm"]), f(inp["k_norm"]), f(inp["sinks"])
    w_g2, b_g = f(inp["w_g2"]), f(inp["b_g"])
    common = dict(
        w_in=f(inp["w_in"]), w_o=f(inp["w_o"]), w_gate=f(inp["w_gate"]), w_up=f(inp["w_up"]), w_down=f(inp["w_down"]),
        g1=np.ascontiguousarray(norm1.reshape(NL, 8, 128).transpose(2, 0, 1).reshape(128, NL * 8)),
        g2=np.ascontiguousarray(norm2.reshape(NL, 8, 128).transpose(2, 0, 1).reshape(128, NL * 8)),
        gg=np.ascontiguousarray(gla_norm.T),
        qkg=np.ascontiguousarray(np.stack([q_norm[0], k_norm[0], q_norm[1], k_norm[1]], axis=1)),
        snk=np.ascontiguousarray(np.broadcast_to(sinks.reshape(1, NL * 8), (128, NL * 8))),
        wg2=np.ascontiguousarray(np.concatenate([w_g2.transpose(1, 0, 2).reshape(16, NL * 256), np.zeros((16, NL * 256), np.float32)], 0)),
        bg=np.ascontiguousarray(np.concatenate([b_g.reshape(1, NL * 256), np.zeros((31, NL * 256), np.float32)], 0)),
    )
    common.update(_consts())
    in_maps = []
    for c in range(8):
        seq = c % 4
        xin = np.zeros((NB * 128, D), np.float32)
        xin[112:128] = meta
        xin[128:NBP * 128] = x_prompt[seq]
        xin[NBP * 128:] = x_sample[16 * c:16 * c + 16].reshape(128, D)
        m = dict(common)
        m["xin"] = xin
        m["ck"] = np.ascontiguousarray(cache_k[:, 16 * c:16 * c + 16].reshape(NL, 16, 128, 128))
        m["cv"] = np.ascontiguousarray(cache_v[:, 16 * c:16 * c + 16].reshape(NL, 16, 128, 128))
        m["st"] = np.ascontiguousarray(state_gla[:, 16 * c:16 * c + 16])
        in_maps.append(m)
    if "nc" not in _NC_CACHE:
        _NC_CACHE["nc"] = build()
    res = run_bass_kernel_spmd(_NC_CACHE["nc"], in_maps, core_ids=list(range(8)))
    R = res.results
    y_prompt = np.stack([R[c]["yp"] for c in range(4)], axis=0).astype(np.float32)
    y_sample = np.concatenate([R[c]["ys"].reshape(16, 8, D) for c in range(8)], axis=0).astype(np.float32)
    pk = np.stack([R[c]["pk"].reshape(NL, 128, 2, 64) for c in range(4)], axis=1).astype(np.float32)
    pv = np.stack([R[c]["pv"].reshape(NL, 128, 2, 64) for c in range(4)], axis=1).astype(np.float32)
    pst = np.stack([R[c]["pst"] for c in range(4)], axis=1).astype(np.float32)
    sk = np.concatenate([R[c]["sk"].reshape(NL, 16, 128, 2, 64) for c in range(8)], axis=1).astype(np.float32)
    sv = np.concatenate([R[c]["sv"].reshape(NL, 16, 128, 2, 64) for c in range(8)], axis=1).astype(np.float32)
    sst = np.concatenate([R[c]["sst"] for c in range(8)], axis=1).astype(np.float32)
    return (y_prompt, y_sample, pk, pv, pst, sk, sv, sst)
```

```python
import contextlib
import numpy as np
import ml_dtypes
import concourse.bass as bass
import concourse.mybir as mybir
from concourse.bass_utils import run_bass_kernel_spmd

F32 = mybir.dt.float32
BF16 = mybir.dt.bfloat16
AF = mybir.ActivationFunctionType
ALU = mybir.AluOpType
AX = mybir.AxisListType


class Buf:
    def __init__(self, name, t):
        self.name = name
        self.t = t
        self.lw = {}
        self.rd = {}
        self.dsem = None
        self.dcount = 0
        self.excl = False
        self.aliases = []


class Prog:
    ENGS = ("pe", "act", "dve", "pool", "sp")

    def __init__(self, nc):
        self.nc = nc
        self.stack = contextlib.ExitStack()
        self.sems = {}
        self.count = {e: 0 for e in self.ENGS}
        self.seen = {e: {} for e in self.ENGS}
        self.ops = {e: [] for e in self.ENGS}
        for e in self.ENGS:
            self.sems["E_" + e] = self.stack.enter_context(nc.semaphore("sem_" + e))
        self.out_tokens = {}
        self.nbufs = 0

    def sb(self, name, shape, dtype):
        t = self.stack.enter_context(self.nc.sbuf_tensor("s_" + name, list(shape), dtype))
        return Buf(name, t)

    def ps(self, name, shape, dtype):
        t = self.stack.enter_context(self.nc.psum_tensor(name, list(shape), dtype))
        b = Buf(name, t)
        b.excl = True
        return b

    def view(self, name, t):
        return Buf(name, t)

    def _dsem(self, buf, queue):
        if buf.dsem is None:
            buf.dsem = {}
            buf.dcount = {}
        if queue not in buf.dsem:
            key = "D_%d_%s_%s" % (self.nbufs, buf.name, queue)
            self.nbufs += 1
            self.sems[key] = self.stack.enter_context(self.nc.semaphore("dsem_%d" % self.nbufs))
            buf.dsem[queue] = key
            buf.dcount[queue] = 0
        return buf.dsem[queue]

    def _waits(self, eng, reads, writes, ignore_waw=False):
        need = {}
        for b in reads:
            for k, v in b.lw.items():
                if need.get(k, 0) < v:
                    need[k] = v
            if b.excl:
                for k, v in b.rd.items():
                    if k != "E_" + eng and need.get(k, 0) < v:
                        need[k] = v
        for b in writes:
            if not ignore_waw:
                for k, v in b.lw.items():
                    if need.get(k, 0) < v:
                        need[k] = v
            for k, v in b.rd.items():
                if need.get(k, 0) < v:
                    need[k] = v
            for al in b.aliases:
                for dd in (al.lw, al.rd):
                    for k, v in dd.items():
                        if need.get(k, 0) < v:
                            need[k] = v
        out = []
        seen = self.seen[eng]
        for k, v in need.items():
            if eng == "pe" and k == "E_pe":
                continue
            if seen.get(k, 0) < v:
                seen[k] = v
                out.append((k, v))
        return out

    def _commit(self, tok, reads, writes, ignore_waw=False):
        k, v = tok
        for b in reads:
            if b.rd.get(k, 0) < v:
                b.rd[k] = v
        for b in writes:
            if ignore_waw:
                b.lw[k] = v
            else:
                b.lw = {k: v}
            b.rd = {}

    def op(self, eng, fn, reads=(), writes=()):
        waits = self._waits(eng, reads, writes)
        self.count[eng] += 1
        tok = ("E_" + eng, self.count[eng])
        self._commit(tok, reads, writes)
        self.ops[eng].append((waits, fn, tok[0], 1))
        return tok

    def dma(self, queue, sem_buf, out_ap, in_ap, reads=(), writes=(), out=False, ignore_waw=False):
        waits = self._waits(queue, reads, writes, ignore_waw=ignore_waw)
        key = self._dsem(sem_buf, queue)
        sem_buf.dcount[queue] += 16
        tok = (key, sem_buf.dcount[queue])
        self._commit(tok, reads, writes, ignore_waw=ignore_waw)
        self.ops[queue].append((waits, (lambda e: e.dma_start(out=out_ap, in_=in_ap)), key, 16))
        if out:
            self.out_tokens[key] = sem_buf.dcount[queue]
        return tok

    def load(self, queue, buf, dst_ap, src_ap, ignore_waw=False, extra_reads=()):
        return self.dma(queue, buf, dst_ap, src_ap, reads=list(extra_reads), writes=[buf], ignore_waw=ignore_waw)

    def store(self, queue, buf, dst_ap, src_ap, out=True, extra_writes=()):
        return self.dma(queue, buf, dst_ap, src_ap, reads=[buf], writes=list(extra_writes), out=out)

    def finish(self):
        nc = self.nc
        handles = {}
        final_waits = list(self.out_tokens.items())
        with nc.Block() as block:
            def emit(e, name):
                for waits, fn, semkey, inc in self.ops[name]:
                    for k, v in waits:
                        e.wait_ge(self.sems[k], v)
                    fn(e).then_inc(self.sems[semkey], inc)
                if name == "sp":
                    for k, v in final_waits:
                        e.wait_ge(self.sems[k], v)

            @block.tensor
            def _(e):
                emit(e, "pe")

            @block.scalar
            def _(e):
                emit(e, "act")

            @block.vector
            def _(e):
                emit(e, "dve")

            @block.gpsimd
            def _(e):
                emit(e, "pool")

            @block.sync
            def _(e):
                emit(e, "sp")
        self.stack.close()


D = 1024
DF = 2816
NFT = 22
NBP = 33
NB = 34
SB = 33
NCH = 25
CH = 4096
EPS = 1e-6
NL = 2
CQ, CK, CV, CGQ, CGK, CGV, CGL, COG = 0, 512, 640, 768, 1024, 1280, 1792, 1808
import os
STOP = int(os.environ.get("MK_STOP", "99"))
SUB = int(os.environ.get("MK_SUB", "0"))
STOPAT = tuple(int(v) for v in os.environ.get("MK_STOPAT", "0,0").split(","))


class _Stop(Exception):
    pass


BLKS = [int(v) for v in os.environ["MK_BLKS"].split(",")] if os.environ.get("MK_BLKS") else list(range(NB))


GMAX = int(os.environ.get("MK_G", "3"))
if os.environ.get("MK_GROUPS"):
    GROUPS = [[int(v) for v in g.split(",")] for g in os.environ["MK_GROUPS"].split(";")]
else:
    GROUPS = [list(range(i, min(i + GMAX, NBP))) for i in range(0, NBP, GMAX)] + [[SB]]
TM = 128 * max(len(g) for g in GROUPS)
GM = TM // 128


def build():
    nc = bass.Bass("TRN2", target_bir_lowering=False)
    P = Prog(nc)

    def din(name, shape, dtype=F32):
        return nc.dram_tensor(name, list(shape), dtype, kind="ExternalInput").ap()

    def dout(name, shape, dtype=F32):
        return nc.dram_tensor(name, list(shape), dtype, kind="ExternalOutput").ap()

    xin = din("xin", [NB * 128, D])
    w_in = din("w_in", [NL, D, 2320])
    w_o = din("w_o", [NL, D, D])
    w_gate = din("w_gate", [NL, D, DF])
    w_up = din("w_up", [NL, D, DF])
    w_down = din("w_down", [NL, DF, D])
    g1_d = din("g1", [128, NL * 8])
    g2_d = din("g2", [128, NL * 8])
    gg_d = din("gg", [128, NL])
    qkg_d = din("qkg", [64, NL * 2])
    snk_d = din("snk", [128, NL * 8])
    wg2_d = din("wg2", [32, NL * 256])
    bg_d = din("bg", [32, NL * 256])
    ck_d = din("ck", [NL, 16, 128, 128])
    cv_d = din("cv", [NL, 16, 128, 128])
    st_d = din("st", [NL, 16, 4, 64, 128])
    identb_d = din("identb", [128, 128], BF16)
    identf_d = din("identf", [128, 128])
    masks_d = din("masks", [128, 5 * 128 + 8], BF16)
    umat_d = din("umat", [128, 256])
    valid_d = din("valid", [128, NB])
    eq_d = din("eq", [64, 16 * 128], BF16)
    e2_d = din("e2", [128, 16], BF16)

    yp = dout("yp", [(NBP - 1) * 128, D])
    ys = dout("ys", [128, D])
    pk = dout("pk", [NL, 128, 128])
    pv = dout("pv", [NL, 128, 128])
    pst = dout("pst", [NL, 4, 64, 128])
    sk = dout("sk", [NL, 16, 128, 128])
    sv = dout("sv", [NL, 16, 128, 128])
    sst = dout("sst", [NL, 16, 4, 64, 128])

    wsc_t = nc.dram_tensor("wsc", [NL, NCH, 128, CH], BF16, kind="ExternalOutput").ap()
    wsc = Buf("wsc", wsc_t)

    sb = P.sb
    identb = sb("identb", [128, 128], BF16)
    identf = sb("identf", [128, 128], F32)
    masks = sb("masks", [128, 5 * 128 + 8], BF16)
    umat = sb("umat", [128, 256], F32)
    valid = sb("valid", [128, NB], F32)
    eq = sb("eq", [64, 16, 128], BF16)
    e2 = sb("e2", [128, 16], BF16)
    g1 = sb("g1", [128, NL * 8], F32)
    g2 = sb("g2", [128, NL * 8], F32)
    gg = sb("gg", [128, NL], F32)
    qkg = sb("qkg", [64, NL * 2], F32)
    esink = sb("esink", [128, NL * 8], F32)
    wg2 = sb("wg2", [32, NL * 256], F32)
    bg = sb("bg", [32, NL * 256], F32)
    ones64 = sb("ones64", [64, 64], BF16)
    ones1 = sb("ones1", [32, 128], F32)
    epsb = sb("epsb", [128, 1], F32)
    epsb1 = sb("epsb1", [128, 1], F32)

    x = sb("x", [128, GM, D], F32)
    st4 = sb("st4", [128, 8], F32)
    hb = sb("hb", [128, D], BF16)
    hT = sb("hT", [128, 8, TM], BF16)
    qkraw = [sb("qkraw%d" % i, [64, TM], F32) for i in range(2)]
    qksq = [sb("qksq%d" % i, [64, TM], BF16) for i in range(2)]
    rqh = [sb("rqh%d" % i, [64, TM], F32) for i in range(2)]
    qn = sb("qn", [64, GM, 8, 128], BF16)
    kn32 = sb("kn32", [64, 2, TM], F32)
    kbuf = [sb("kbuf%d" % l, [64, GM + 1, 2, 128], BF16) for l in range(NL)]
    vaug = [sb("vaug%d" % l, [128, GM + 1, 2, 65], BF16) for l in range(NL)]
    gqraw = sb("gqraw", [64, 4, TM], BF16)
    gkraw = sb("gkraw", [64, 4, TM], BF16)
    glowT = sb("glowT", [32, TM], F32)
    v32 = sb("v32", [128, GM, 128], F32)
    gvb = sb("gvb", [128, GM, 512], BF16)
    sog = sb("sog", [128, GM, 512], BF16)
    e1 = sb("e1", [128, 256], F32)
    lnt = sb("lnt", [128, 256], F32)
    eG = sb("eG", [64, 4, 128], F32)
    enG = sb("enG", [64, 4, 128], F32)
    qtil = sb("qtil", [64, 4, 128], BF16)
    kt32 = sb("kt32", [64, 4, 128], F32)
    ktil = sb("ktil", [64, 4, 128], BF16)
    khatT = sb("khatT", [64, 4, 128], BF16)
    khat = sb("khat", [128, 256], BF16)
    atm = sb("atm", [128, 4, 128], BF16)
    o32 = sb("o32", [128, 4, 128], F32)
    osq = sb("osq", [128, 4, 128], F32)
    gst = sb("gst", [128, 16], F32)
    S = [sb("S%d" % l, [64, 4, 128], F32) for l in range(NL)]
    Sb = [sb("Sb%d" % l, [64, 4, 128], BF16) for l in range(NL)]
    merged = sb("merged", [128, GM, D], BF16)
    uT = sb("uT", [128, NFT, TM], BF16)
    sg = [sb("sg%d" % i, [128, TM], F32) for i in range(2)]
    PT = [sb("PT%d" % i, [128, 2, 512], BF16) for i in range(2)]
    den = sb("den", [128, 8], F32)
    kt_out = sb("kt_out", [128, 128], F32)
    ring = [sb("ring%d" % i, [128, CH], BF16) for i in range(4)]

    AW = 12832
    arena = P.stack.enter_context(nc.sbuf_tensor("s_arena", [128, AW], F32))

    def carve(name, off, words, dtype, pat=None, parts=128, **kw):
        v = arena[0:parts, off:off + words]
        if dtype is BF16:
            v = v.bitcast(BF16)
        if pat is not None:
            v = v.rearrange(pat, **kw)
        return Buf(name, v)

    cst = carve("cst", 0, 2048, F32, "p (s c) -> p s c", s=16)
    ckb = carve("ckb", 2048, 1024, BF16, "p (s c) -> p s c", s=16)
    vc = carve("vc", 3072, 1040, BF16, "p (s g e) -> p s g e", s=16, g=2)
    ptc = carve("ptc", 4112, 512, BF16)
    khx = carve("khx", 4624, 512, BF16, "p (s c) -> p s c", s=4)
    kcT = carve("kcT", 5136, 2048, BF16, "p (s g t) -> p s g t", s=16, g=2, parts=64)
    otc = carve("otc", 7184, 1024, F32, "p (g h t) -> p g h t", g=2, h=4, parts=65)
    s0q = carve("s0q", 8208, 2048, F32, "p (s h v) -> p s h v", s=4, h=4, parts=64)
    s0b = carve("s0b", 10256, 1024, BF16, "p (s h v) -> p s h v", s=4, h=4, parts=64)
    qx = carve("qx", 11280, 1024, BF16, "p (h s t) -> p h s t", h=4, s=4, parts=64)
    qns = carve("qns", 12304, 512, BF16, "p (s g c) -> p s g c", s=16, g=2, parts=64)
    samp_bufs = [cst, ckb, vc, ptc, khx, kcT, otc, s0q, s0b, qx, qns]
    stg = [carve("stg%d" % i, 1024 * i, 1024, F32, "p (k w) -> p k w", k=8) for i in range(3)]
    cht = [carve("cht%d" % i, 3072 + 2048 * i, 2048, BF16) for i in range(2)]
    for a_ in stg + cht:
        for b_ in samp_bufs:
            a_.aliases.append(b_)
            b_.aliases.append(a_)

    pbF = [P.ps("pbF%d" % i, [128, 512], F32) for i in range(3)]
    pbT = [P.ps("pbT%d" % i, [128, 512], F32) for i in range(2)]
    pbA = [P.ps("pbA%d" % i, [128, 512], F32) for i in range(2)]
    pb16 = P.ps("pb16", [128, 1024], BF16)
    rr = {"F": 0, "T": 0, "A": 0}

    def bankF():
        rr["F"] += 1
        return pbF[rr["F"] % 3]

    def bankT():
        rr["T"] += 1
        return pbT[rr["T"] % 2]

    def bankA():
        rr["A"] += 1
        return pbA[rr["A"] % 2]

    def mm(out, lhsT, rhs, start, stop, r, w):
        P.op("pe", lambda e: e.matmul(out, lhsT, rhs, start=start, stop=stop), reads=r, writes=w)

    def tr(out, in_, ident, r, w):
        P.op("pe", lambda e: e.transpose(out, in_, ident), reads=r, writes=w)

    def act(out, in_, func, r, w, **kw):
        P.op("act", lambda e: e.activation(out, in_, func, **kw), reads=r, writes=w)

    def cp(eng, out, in_, r, w):
        if eng == "act":
            act(out, in_, AF.Copy, r, w)
        else:
            P.op(eng, lambda e: e.tensor_copy(out, in_), reads=r, writes=w)

    def tt(eng, out, a, b, op, r, w):
        P.op(eng, lambda e: e.tensor_tensor(out, a, b, op), reads=r, writes=w)

    def tsc(eng, out, a, s1, s2, op0, op1, r, w):
        if s2 is None:
            P.op(eng, lambda e: e.tensor_scalar(out, a, s1, None, op0), reads=r, writes=w)
        else:
            P.op(eng, lambda e: e.tensor_scalar(out, a, s1, s2, op0, op1), reads=r, writes=w)

    def stt(eng, out, a, s, b, op0, op1, r, w):
        P.op(eng, lambda e: e.scalar_tensor_tensor(out, a, s, b, op0, op1), reads=r, writes=w)

    def mset(eng, buf, ap, val):
        P.op(eng, lambda e: e.memset(ap, val), writes=[buf])

    cld = Buf("cld", None)
    cbufs = []
    for b_, d_ in ((identb, identb_d), (identf, identf_d), (masks, masks_d), (umat, umat_d), (valid, valid_d),
                   (e2, e2_d), (g1, g1_d), (g2, g2_d), (gg, gg_d), (qkg, qkg_d), (esink, snk_d), (wg2, wg2_d), (bg, bg_d)):
        P.dma("pool", cld, b_.t[:, :], d_, writes=[b_])
        cbufs.append(b_)
    P.dma("pool", cld, eq.t[:, :, :], eq_d.rearrange("p (s t) -> p s t", s=16), writes=[eq])
    cbufs.append(eq)
    for b_ in cbufs:
        b_.lw = {cld.dsem["pool"]: cld.dcount["pool"]}
    act(esink.t[:, :], esink.t[:, :], AF.Exp, [esink], [esink])
    mset("dve", ones64, ones64.t[:, :], 1.0)
    mset("dve", ones1, ones1.t[:, :], 1.0)
    mset("dve", epsb, epsb.t[:, :], EPS)
    mset("dve", epsb1, epsb1.t[:, :], 1.0)
    for l in range(NL):
        mset("dve", S[l], S[l].t[:, :, :], 0.0)
        mset("dve", Sb[l], Sb[l].t[:, :, :], 0.0)
        mset("pool", kbuf[l], kbuf[l].t[:, :, :, :], 0.0)
        mset("pool", vaug[l], vaug[l].t[:, :, :, :], 0.0)
        mset("pool", vaug[l], vaug[l].t[:, :, :, 64:65], 1.0)
    m_own = masks.t[:, 0:128]
    m_prev = masks.t[:, 128:256]
    m_own0 = masks.t[:, 256:384]
    m_samp = masks.t[:, 384:512]
    m_prev1 = masks.t[:, 512:640]
    m_cache = masks.t[:, 640:648]
    pth = Buf("pth", None)

    prep_state = {"stg": 0, "cht": 0, "eng": 0}
    for ct_ in cht:
        mset("pool", ct_, ct_.t[:, :], 0.0)

    def kview(ct, W):
        return ct.t[:, 0:8 * W].rearrange("p (k w) -> p k w", k=8)

    def prep_piece(ct, dst, src, kc, wdt, gain):
        s_ = stg[prep_state["stg"] % 3]
        prep_state["stg"] += 1
        q_ = "pool" if prep_state["stg"] % 2 else "sp"
        P.load(q_, s_, s_.t[:, 0:kc, 0:wdt], src)
        eng = ("dve", "pool")[prep_state["eng"] % 2]
        prep_state["eng"] += 1
        if gain is None:
            if prep_state["eng"] % 3 == 0:
                eng = "act"
            cp(eng, dst, s_.t[:, 0:kc, 0:wdt], [s_], [ct])
        else:
            gbuf, gap = gain
            tt(eng, dst, s_.t[:, 0:kc, 0:wdt], gap.unsqueeze(2).broadcast_to([128, kc, wdt]), ALU.mult, [s_, gbuf], [ct])

    def prep_chunk(l, c, pieces, zero=None):
        ct = cht[prep_state["cht"] % 2]
        prep_state["cht"] += 1
        if zero is not None:
            mset("dve", ct, kview(ct, zero[0])[:, :, zero[1]:zero[2]], 0.0)
        for (dst_fn, src, kc, wdt, gain) in pieces:
            prep_piece(ct, dst_fn(ct), src, kc, wdt, gain)
        for q4 in range(4):
            P.dma("sp", ct, wsc_t[l, c][:, q4 * 1024:(q4 + 1) * 1024], ct.t[:, q4 * 1024:(q4 + 1) * 1024], reads=[ct], writes=[wsc], ignore_waw=True)

    for l in range(NL):
        wi = w_in[l].rearrange("(k p) c -> p k c", p=128)
        wo_ = w_o[l].rearrange("(k p) c -> p k c", p=128)
        wg_ = w_gate[l].rearrange("(k p) c -> p k c", p=128)
        wu_ = w_up[l].rearrange("(k p) c -> p k c", p=128)
        wd_ = w_down[l].rearrange("(k p) c -> p k c", p=128)
        G1 = (g1, g1.t[:, l * 8:(l + 1) * 8])
        G2 = (g2, g2.t[:, l * 8:(l + 1) * 8])

        def cols(W, d0, s0, n, src, gain):
            out = []
            for o in range(0, n, 128):
                wdt = min(128, n - o)
                out.append(((lambda ct, W=W, a=d0 + o, wdt=wdt: kview(ct, W)[:, :, a:a + wdt]), src[:, :, s0 + o:s0 + o + wdt], 8, wdt, gain))
            return out

        prep_chunk(l, 0, cols(512, 0, CQ, 512, wi, G1))
        prep_chunk(l, 1, cols(384, 0, CK, 128, wi, G1) + cols(384, 128, CGQ, 256, wi, G1))
        prep_chunk(l, 2, cols(288, 0, CGK, 256, wi, G1) + cols(288, 256, CGL, 16, wi, G1), zero=(288, 272, 288))
        prep_chunk(l, 3, cols(128, 0, CV, 128, wi, G1))
        prep_chunk(l, 4, cols(512, 0, CGV, 512, wi, G1))
        prep_chunk(l, 5, cols(512, 0, COG, 512, wi, G1))
        for hf in range(2):
            pcs = []
            for o in range(0, 512, 128):
                a = hf * 512 + o
                pcs.append(((lambda ct, o=o: kview(ct, 512)[:, 0:4, o:o + 128]), wo_[:, 0:4, a:a + 128], 4, 128, None))
                pcs.append(((lambda ct, o=o: kview(ct, 512)[:, 4:8, o:o + 128]), wo_[:, 4:8, a:a + 128], 4, 128,
                            (gg, gg.t[:, l:l + 1].broadcast_to([128, 4]))))
            prep_chunk(l, 6 + hf, pcs)
        for i in range(11):
            prep_chunk(l, 8 + i, cols(512, 0, 256 * i, 256, wg_, G2) + cols(512, 256, 256 * i, 256, wu_, G2))
        for j in range(6):
            nft = min(4, NFT - 4 * j)
            pcs = []
            for o in range(0, 1024, 128):
                pcs.append(((lambda ct, o=o, nft=nft: ct.t[:, :].rearrange("p (k w) -> p k w", k=4)[:, 0:nft, o:o + 128]),
                            wd_[:, 4 * j:4 * j + nft, o:o + 128], nft, 128, None))
            prep_chunk(l, 19 + j, pcs)

    wseq = [(l, c) for _g in GROUPS for l in range(NL) for c in range(NCH)]
    wst = {"issued": 0}

    def need(i):
        while wst["issued"] < min(len(wseq), i + 3):
            j = wst["issued"]
            l_, c_ = wseq[j]
            rb = ring[j % 4]
            for q4 in range(4):
                P.dma("sp", rb, rb.t[:, q4 * 1024:(q4 + 1) * 1024], wsc_t[l_, c_][:, q4 * 1024:(q4 + 1) * 1024], reads=[wsc], writes=[rb], ignore_waw=(q4 > 0))
            wst["issued"] += 1
        return ring[i % 4]

    def rmsnorm_T(j):
        mset("dve", st4, st4.t[:, 0:1], 0.0)
        act(hb.t[:, :], x.t[:, j, :], AF.Square, [x], [hb, st4], accum_out=st4.t[:, 0:1])
        act(st4.t[:, 2:3], st4.t[:, 0:1], AF.Ln, [st4], [st4], scale=1.0 / D, bias=epsb.t[:, 0:1])
        act(st4.t[:, 3:4], st4.t[:, 2:3], AF.Exp, [st4], [st4], scale=-0.5)
        tsc("dve", hb.t[:, :], x.t[:, j, :], st4.t[:, 3:4], None, ALU.mult, None, [x, st4], [hb])
        for kc in range(8):
            tr(pb16.t[:, kc * 128:(kc + 1) * 128], hb.t[:, kc * 128:(kc + 1) * 128], identb.t[:, :], [hb, identb], [pb16])
        cp("act", hT.t[:, :, j * 128:(j + 1) * 128], pb16.t[:, :].rearrange("p (k t) -> p k t", k=8), [pb16], [hT])

    try:
        for gi, blks in enumerate(GROUPS):
            nb = len(blks)
            T = nb * 128
            samp = (blks[0] == SB)
            for j, blk in enumerate(blks):
                P.load("pool", x, x.t[:, j, :], xin[blk * 128:(blk + 1) * 128, :], ignore_waw=(j > 0))
            for l in range(NL):
                wbase = (gi * NL + l) * NCH
                for j in range(nb):
                    rmsnorm_T(j)
                if STOP == 2 and (gi, l) == STOPAT:
                    raise _Stop()
                W0 = need(wbase + 0)
                W0v = W0.t[:, :].rearrange("p (k w) -> p k w", k=8)
                W1 = need(wbase + 1)
                W1v = W1.t[:, 0:8 * 384].rearrange("p (k w) -> p k w", k=8)
                pend = []

                def qk_tail(h, sl):
                    bk2 = bankF()
                    mm(bk2.t[0:64, 0:T], ones64.t[:, :], qksq[sl].t[:, 0:T], True, True, [ones64, qksq[sl]], [bk2])
                    act(rqh[sl].t[:, 0:T], bk2.t[0:64, 0:T], AF.Ln, [bk2], [rqh[sl]], scale=1.0 / 64, bias=epsb.t[0:64, 0:1])
                    act(rqh[sl].t[:, 0:T], rqh[sl].t[:, 0:T], AF.Exp, [rqh[sl]], [rqh[sl]], scale=-0.5)
                    if h < 8:
                        stt("dve", qn.t[:, 0:nb, h, :], qkraw[sl].t[:, 0:T].rearrange("p (j t) -> p j t", t=128), qkg.t[:, 2 * l:2 * l + 1],
                            rqh[sl].t[:, 0:T].rearrange("p (j t) -> p j t", t=128), ALU.mult, ALU.mult, [qkraw[sl], qkg, rqh[sl]], [qn])
                    else:
                        stt("dve", kn32.t[:, h - 8, 0:T], qkraw[sl].t[:, 0:T], qkg.t[:, 2 * l + 1:2 * l + 2], rqh[sl].t[:, 0:T],
                            ALU.mult, ALU.mult, [qkraw[sl], qkg, rqh[sl]], [kn32])

                for h in range(10):
                    sl = h % 2
                    bk = bankF()
                    for kc in range(8):
                        lw = W0v[:, kc, h * 64:(h + 1) * 64] if h < 8 else W1v[:, kc, (h - 8) * 64:(h - 7) * 64]
                        mm(bk.t[0:64, 0:T], lw, hT.t[:, kc, 0:T], kc == 0, kc == 7, [W0 if h < 8 else W1, hT], [bk])
                    cp("dve", qkraw[sl].t[:, 0:T], bk.t[0:64, 0:T], [bk], [qkraw[sl]])
                    act(qksq[sl].t[:, 0:T], bk.t[0:64, 0:T], AF.Square, [bk], [qksq[sl]])
                    if pend:
                        pend.pop()()
                    pend.append(lambda h=h, sl=sl: qk_tail(h, sl))
                pend.pop()()
                for j, blk in enumerate(blks):
                    cp("pool", kbuf[l].t[:, j + 1, :, :], kn32.t[:, :, j * 128:(j + 1) * 128], [kn32], [kbuf[l]])
                W2 = need(wbase + 2)
                W2v = W2.t[:, 0:8 * 288].rearrange("p (k w) -> p k w", k=8)
                for h in range(4):
                    bk = bankF()
                    for kc in range(8):
                        mm(bk.t[0:64, 0:T], W1v[:, kc, 128 + h * 64:128 + (h + 1) * 64], hT.t[:, kc, 0:T], kc == 0, kc == 7, [W1, hT], [bk])
                    cp("act", gqraw.t[:, h, 0:T], bk.t[0:64, 0:T], [bk], [gqraw])
                for h in range(4):
                    bk = bankF()
                    for kc in range(8):
                        mm(bk.t[0:64, 0:T], W2v[:, kc, h * 64:(h + 1) * 64], hT.t[:, kc, 0:T], kc == 0, kc == 7, [W2, hT], [bk])
                    cp("dve", gkraw.t[:, h, 0:T], bk.t[0:64, 0:T], [bk], [gkraw])
                bk = bankF()
                for kc in range(8):
                    mm(bk.t[0:32, 0:T], W2v[:, kc, 256:288], hT.t[:, kc, 0:T], kc == 0, kc == 7, [W2, hT], [bk])
                cp("act", glowT.t[:, 0:T], bk.t[0:32, 0:T], [bk], [glowT])
                if STOP == 4 and (gi, l) == STOPAT:
                    raise _Stop()
                W3 = need(wbase + 3)
                W3v = W3.t[:, 0:8 * 128].rearrange("p (k w) -> p k w", k=8)
                for j, blk in enumerate(blks):
                    bk = bankT()
                    for kc in range(8):
                        mm(bk.t[:, 0:128], hT.t[:, kc, j * 128:(j + 1) * 128], W3v[:, kc, :], kc == 0, kc == 7, [W3, hT], [bk])
                    cp("act", v32.t[:, j, :], bk.t[:, 0:128], [bk], [v32])
                    cp("dve", vaug[l].t[:, j + 1, :, 0:64], bk.t[:, 0:128].rearrange("p (g d) -> p g d", g=2), [bk], [vaug[l]])
                    if j + 1 < nb:
                        pass
                W4 = need(wbase + 4)
                W4v = W4.t[:, :].rearrange("p (k w) -> p k w", k=8)
                for j in range(nb):
                    bk = bankT()
                    for kc in range(8):
                        mm(bk.t[:, :], hT.t[:, kc, j * 128:(j + 1) * 128], W4v[:, kc, :], kc == 0, kc == 7, [W4, hT], [bk])
                    cp("act", gvb.t[:, j, :], bk.t[:, :], [bk], [gvb])
                W5 = need(wbase + 5)
                W5v = W5.t[:, :].rearrange("p (k w) -> p k w", k=8)
                for j in range(nb):
                    bk = bankT()
                    for kc in range(8):
                        mm(bk.t[:, :], hT.t[:, kc, j * 128:(j + 1) * 128], W5v[:, kc, :], kc == 0, kc == 7, [W5, hT], [bk])
                    act(sog.t[:, j, :], bk.t[:, :], AF.Silu, [bk], [sog])

                if STOP == 5 and (gi, l) == STOPAT:
                    raise _Stop()
                if samp:
                    if l == 0:
                        mset("pool", vc, vc.t[:, :, :, 64:65], 1.0)
                    P.load("pool", cst, cst.t[:, :, :], ck_d[l].rearrange("s k c -> k s c"))
                    cp("pool", ckb.t[:, :, :], cst.t[:, :, :], [cst], [ckb])
                    for rnd in range(4):
                        for i in range(8):
                            s_, g_ = (rnd * 8 + i) // 2, (rnd * 8 + i) % 2
                            tr(pb16.t[0:64, i * 128:(i + 1) * 128], ckb.t[:, s_, g_ * 64:(g_ + 1) * 64], identb.t[:, :], [ckb, identb], [pb16])
                        cp("dve", kcT.t[:, rnd * 4:rnd * 4 + 4, :, :], pb16.t[0:64, :].rearrange("p (s g t) -> p s g t", s=4, g=2), [pb16], [kcT])
                    P.dma("pool", pth, sk[l, :, 0:120, :], ck_d[l, :, 8:128, :], reads=[], writes=[], out=True)
                    P.load("pool", cst, cst.t[:, :, :], cv_d[l].rearrange("s k c -> k s c"))
                    cp("pool", vc.t[:, :, :, 0:64], cst.t[:, :, :].rearrange("p s (g d) -> p s g d", g=2), [cst], [vc])
                    P.dma("pool", pth, sv[l, :, 0:120, :], cv_d[l, :, 8:128, :], reads=[], writes=[], out=True)
                    for g_ in range(2):
                        cp("pool", qns.t[:, :, g_, :].rearrange("p s (h t) -> p h s t", h=4), qn.t[:, 0, 4 * g_:4 * g_ + 4, :].rearrange("p h (s t) -> p h s t", t=8), [qn], [qns])
                    stc = [bankA(), bankA()]
                    for s_ in range(16):
                        for g_ in range(2):
                            bk = stc[s_ // 8]
                            o_ = ((s_ % 8) * 2 + g_) * 32
                            mm(bk.t[:, o_:o_ + 32], kcT.t[:, s_, g_, :], qns.t[:, s_, g_, :], True, True, [kcT, qns], [bk])
                    for hf in range(2):
                        act(ptc.t[:, hf * 512:(hf + 1) * 512], stc[hf].t[:, :], AF.Exp, [stc[hf]], [ptc], scale=0.125)
                    tt("pool", ptc.t[:, :].rearrange("p (a q) -> p a q", q=8), ptc.t[:, :].rearrange("p (a q) -> p a q", q=8),
                       m_cache.unsqueeze(1).broadcast_to([128, 128, 8]), ALU.mult, [ptc, masks], [ptc])
                    otb = [bankA(), bankA()]
                    for s_ in range(16):
                        for g_ in range(2):
                            bk = otb[s_ // 8]
                            o_ = ((s_ % 8) * 2 + g_) * 32
                            mm(bk.t[0:65, o_:o_ + 32], vc.t[:, s_, g_, :], ptc.t[:, (s_ * 2 + g_) * 32:(s_ * 2 + g_ + 1) * 32], True, True, [vc, ptc], [bk])
                    for hf in range(2):
                        for g_ in range(2):
                            cp("act" if g_ else "dve", otc.t[:, g_, :, hf * 64:(hf + 1) * 64].rearrange("p h (s t) -> p s h t", t=8),
                               otb[hf].t[0:65, :].rearrange("p (s g h t) -> p s g h t", s=8, g=2, h=4)[:, :, g_, :, :], [otb[hf]], [otc])

                for j, blk in enumerate(blks):
                    slot = blk % 2
                    if j > 0:
                        pass
                    pt = PT[(blk + l) % 2]
                    for g in range(2):
                        bo = bankA()
                        rq_ = qn.t[:, j, 4 * g:4 * g + 4, :].rearrange("p h t -> p (h t)")
                        mm(bo.t[:, :], kbuf[l].t[:, j + 1, g, :], rq_, True, True, [kbuf[l], qn], [bo])
                        act(pt.t[:, 0, :], bo.t[:, :], AF.Exp, [bo], [pt], scale=0.125)
                        mk = m_samp if samp else (m_own0 if blk == 0 else m_own)
                        tt("pool", pt.t[:, 0, :].rearrange("p (h t) -> p h t", h=4), pt.t[:, 0, :].rearrange("p (h t) -> p h t", h=4),
                           mk.unsqueeze(1).broadcast_to([128, 4, 128]), ALU.mult, [pt, masks], [pt])
                        has_prev = (not samp) and blk > 0
                        if has_prev:
                            bp = bankA()
                            mm(bp.t[:, :], kbuf[l].t[:, j, g, :], rq_, True, True, [kbuf[l], qn], [bp])
                            act(pt.t[:, 1, :], bp.t[:, :], AF.Exp, [bp], [pt], scale=0.125)
                            tt("pool", pt.t[:, 1, :].rearrange("p (h t) -> p h t", h=4), pt.t[:, 1, :].rearrange("p (h t) -> p h t", h=4),
                               (m_prev1 if blk == 1 else m_prev).unsqueeze(1).broadcast_to([128, 4, 128]), ALU.mult, [pt, masks], [pt])
                        bv_ = bankT()
                        for h in range(4):
                            oc = bv_.t[:, h * 65:(h + 1) * 65]
                            last_own = not (has_prev or samp)
                            mm(oc, pt.t[:, 0, h * 128:(h + 1) * 128], vaug[l].t[:, j + 1, g, :], True, last_own, [pt, vaug[l]], [bv_])
                            if has_prev:
                                mm(oc, pt.t[:, 1, h * 128:(h + 1) * 128], vaug[l].t[:, j, g, :], False, True, [pt, vaug[l]], [bv_])
                            if samp:
                                mm(oc, otc.t[:, g, h, :], identf.t[0:65, 0:65], False, True, [otc, identf], [bv_])
                        pv4 = bv_.t[:, 0:260].rearrange("p (h e) -> p h e", h=4)
                        tt("dve", den.t[:, 0:4], pv4[:, :, 64], esink.t[:, l * 8 + 4 * g:l * 8 + 4 * g + 4], ALU.add, [bv_, esink], [den])
                        P.op("dve", lambda e: e.reciprocal(den.t[:, 4:8], den.t[:, 0:4]), reads=[den], writes=[den])
                        tt("dve", merged.t[:, j, g * 256:(g + 1) * 256].rearrange("p (h d) -> p h d", h=4), pv4[:, :, 0:64],
                           den.t[:, 4:8].unsqueeze(2).broadcast_to([128, 4, 64]), ALU.mult, [bv_, den], [merged])

                    c0, c1 = j * 128, (j + 1) * 128
                    bl = bankA()
                    mm(bl.t[:, 0:256], glowT.t[:, c0:c1], wg2.t[:, l * 256:(l + 1) * 256], True, False, [glowT, wg2], [bl])
                    mm(bl.t[:, 0:256], ones1.t[:, :], bg.t[:, l * 256:(l + 1) * 256], False, True, [ones1, bg], [bl])
                    act(e1.t[:, :], bl.t[:, 0:256], AF.Exp, [bl], [e1], scale=-1.0)
                    act(lnt.t[:, :], e1.t[:, :], AF.Ln, [e1], [lnt], bias=epsb1.t[:, 0:1])
                    U = umat.t[:, 128:256] if samp else umat.t[:, 0:128]
                    gmask = m_samp if samp else m_own
                    bgT = bankA()
                    for h in range(4):
                        mm(bgT.t[0:64, h * 128:(h + 1) * 128], lnt.t[:, h * 64:(h + 1) * 64], U, True, True, [lnt, umat], [bgT])
                    g4 = bgT.t[0:64, :].rearrange("p (h t) -> p h t", h=4)
                    act(eG.t[:, :, :], g4, AF.Exp, [bgT], [eG])
                    act(enG.t[:, :, :], g4, AF.Exp, [bgT], [enG], scale=-1.0)
                    stt("dve", qtil.t[:, :, :], gqraw.t[:, :, c0:c1], 0.125, eG.t[:, :, :], ALU.mult, ALU.mult, [gqraw, eG], [qtil])
                    tt("dve", kt32.t[:, :, :], gkraw.t[:, :, c0:c1], enG.t[:, :, :], ALU.mult, [gkraw, enG], [kt32])
                    cp("pool", ktil.t[:, :, :], kt32.t[:, :, :], [kt32], [ktil])
                    if samp:
                        egl = eG.t[:, :, :].rearrange("p h (s t) -> p h s t", t=8)[:, :, :, 7:8].broadcast_to([64, 4, 16, 8])
                        tt("pool", khatT.t[:, :, :].rearrange("p h (s t) -> p h s t", t=8), kt32.t[:, :, :].rearrange("p h (s t) -> p h s t", t=8),
                           egl, ALU.mult, [kt32, eG], [khatT])
                    else:
                        egl = eG.t[:, :, 127:128].broadcast_to([64, 4, 128])
                        tt("pool", khatT.t[:, :, :], kt32.t[:, :, :], egl, ALU.mult, [kt32, eG], [khatT])
                    for h in range(4):
                        tr(pb16.t[:, h * 64:(h + 1) * 64], khatT.t[:, h, :], identb.t[0:64, 0:64], [khatT, identb], [pb16])
                    cp("dve", khat.t[:, :], pb16.t[:, 0:256], [pb16], [khat])
                    ba = bankA()
                    for h in range(4):
                        mm(ba.t[:, h * 128:(h + 1) * 128], ktil.t[:, h, :], qtil.t[:, h, :], True, True, [ktil, qtil], [ba])
                    tt("dve", atm.t[:, :, :], ba.t[:, :].rearrange("p (h t) -> p h t", h=4), gmask.unsqueeze(1).broadcast_to([128, 4, 128]), ALU.mult, [ba, masks], [atm])
                    bo = bankT()
                    if not samp:
                        for h in range(4):
                            oc = bo.t[:, h * 128:(h + 1) * 128]
                            mm(oc, atm.t[:, h, :], gvb.t[:, j, h * 128:(h + 1) * 128], True, False, [atm, gvb], [bo])
                            mm(oc, qtil.t[:, h, :], Sb[l].t[:, h, :], False, True, [qtil, Sb[l]], [bo])
                        bs = bankA()
                        for h in range(4):
                            mm(bs.t[0:64, h * 128:(h + 1) * 128], khat.t[:, h * 64:(h + 1) * 64], gvb.t[:, j, h * 128:(h + 1) * 128], True, True, [khat, gvb], [bs])
                        tt("dve", S[l].t[:, :, :], S[l].t[:, :, :], eG.t[:, :, 127:128].broadcast_to([64, 4, 128]), ALU.mult, [S[l], eG], [S[l]])
                        tt("dve", S[l].t[:, :, :], S[l].t[:, :, :], bs.t[0:64, :].rearrange("p (h v) -> p h v", h=4), ALU.add, [S[l], bs], [S[l]])
                        cp("pool", Sb[l].t[:, :, :], S[l].t[:, :, :], [S[l]], [Sb[l]])
                    else:
                        obk = [pbF[0], pbF[1], pbF[2], bo]
                        for h in range(4):
                            mm(obk[h].t[:, 0:128], atm.t[:, h, :], gvb.t[:, j, h * 128:(h + 1) * 128], True, False, [atm, gvb], [obk[h]])
                        eg4 = eG.t[:, :, :].rearrange("p h (s t) -> p h s t", t=8)
                        for qd in range(4):
                            P.load("pool", s0q, s0q.t[:, :, :, :], st_d[l, 4 * qd:4 * qd + 4].rearrange("s h k v -> k s h v"))
                            cp("pool", s0b.t[:, :, :, :], s0q.t[:, :, :, :], [s0q], [s0b])
                            tt("dve", qx.t[:, :, :, :], qtil.t[:, :, :].unsqueeze(2).broadcast_to([64, 4, 4, 128]),
                               eq.t[:, 4 * qd:4 * qd + 4, :].unsqueeze(1).broadcast_to([64, 4, 4, 128]), ALU.mult, [qtil, eq], [qx])
                            tt("pool", khx.t[:, :, :], khat.t[:, :].unsqueeze(1).broadcast_to([128, 4, 256]),
                               e2.t[:, 4 * qd:4 * qd + 4].unsqueeze(2).broadcast_to([128, 4, 256]), ALU.mult, [khat, e2], [khx])
                            for h in range(4):
                                for s_ in range(4):
                                    last = (qd == 3 and s_ == 3)
                                    mm(obk[h].t[:, 0:128], qx.t[:, h, s_, :], s0b.t[:, s_, h, :], False, last, [qx, s0b], [obk[h]])
                            for s_ in range(4):
                                bs = bankA()
                                for h in range(4):
                                    mm(bs.t[0:64, h * 128:(h + 1) * 128], khx.t[:, s_, h * 64:(h + 1) * 64], gvb.t[:, j, h * 128:(h + 1) * 128], True, True, [khx, gvb], [bs])
                                sa = 4 * qd + s_
                                tt("dve", s0q.t[:, s_, :, :], s0q.t[:, s_, :, :], eg4[:, :, sa, 7:8].broadcast_to([64, 4, 128]), ALU.mult, [s0q, eG], [s0q])
                                tt("dve", s0q.t[:, s_, :, :], s0q.t[:, s_, :, :], bs.t[0:64, :].rearrange("p (h v) -> p h v", h=4), ALU.add, [s0q, bs], [s0q])
                            P.store("pool", s0q, sst[l, 4 * qd:4 * qd + 4].rearrange("s h k v -> k s h v"), s0q.t[:, :, :, :])
                    if samp:
                        for h in range(4):
                            cp("act" if h % 2 else "dve", o32.t[:, h, :], obk[h].t[:, 0:128], [obk[h]], [o32])
                    else:
                        cp("act", o32.t[:, :, :], bo.t[:, :].rearrange("p (h v) -> p h v", h=4), [bo], [o32])
                    tt("pool", osq.t[:, :, :], o32.t[:, :, :], o32.t[:, :, :], ALU.mult, [o32], [osq])
                    P.op("dve", lambda e: e.tensor_reduce(gst.t[:, 0:4], osq.t[:, :, :], AX.X, ALU.add), reads=[osq], writes=[gst])
                    act(gst.t[:, 4:8], gst.t[:, 0:4], AF.Ln, [gst], [gst], scale=1.0 / 128, bias=epsb.t[:, 0:1])
                    act(gst.t[:, 8:12], gst.t[:, 4:8], AF.Exp, [gst], [gst], scale=-0.5)
                    tt("dve", o32.t[:, :, :], o32.t[:, :, :], gst.t[:, 8:12].unsqueeze(2).broadcast_to([128, 4, 128]), ALU.mult, [o32, gst], [o32])
                    tt("pool", merged.t[:, j, 512:1024], o32.t[:, :, :].rearrange("p h v -> p (h v)"), sog.t[:, j, :], ALU.mult, [o32, sog], [merged])

                    if blk == NBP - 1 or samp:
                        for g in range(2):
                            tr(pbT[0].t[:, g * 64:(g + 1) * 64], kn32.t[:, g, c0:c1], identf.t[0:64, 0:64], [kn32, identf], [pbT[0]])
                        cp("dve", kt_out.t[:, :], pbT[0].t[:, 0:128], [pbT[0]], [kt_out])
                        if samp:
                            for s_ in range(16):
                                P.store("pool", kt_out, sk[l, s_, 120:128, :], kt_out.t[8 * s_:8 * s_ + 8, :])
                                P.store("pool", v32, sv[l, s_, 120:128, :], v32.t[8 * s_:8 * s_ + 8, j, :])
                        else:
                            P.store("pool", kt_out, pk[l], kt_out.t[:, :])
                            P.store("pool", v32, pv[l], v32.t[:, j, :])
                            P.store("pool", S[l], pst[l].rearrange("h k v -> k h v"), S[l].t[:, :, :])

                if STOP == 7 and (gi, l) == STOPAT:
                    raise _Stop()
                if not samp:
                    cp("pool", kbuf[l].t[:, 0, :, :], kbuf[l].t[:, nb, :, :], [kbuf[l]], [kbuf[l]])
                    cp("pool", vaug[l].t[:, 0, :, 0:64], vaug[l].t[:, nb, :, 0:64], [vaug[l]], [vaug[l]])
                for j in range(nb):
                    for kc in range(8):
                        tr(pb16.t[:, kc * 128:(kc + 1) * 128], merged.t[:, j, kc * 128:(kc + 1) * 128], identb.t[:, :], [merged, identb], [pb16])
                    cp("act", hT.t[:, :, j * 128:(j + 1) * 128], pb16.t[:, :].rearrange("p (k t) -> p k t", k=8), [pb16], [hT])
                for hf in range(2):
                    Wc = need(wbase + 6 + hf)
                    Wv = Wc.t[:, :].rearrange("p (k w) -> p k w", k=8)
                    for j, blk in enumerate(blks):
                        bk = bankT()
                        for kc in range(8):
                            mm(bk.t[:, :], hT.t[:, kc, j * 128:(j + 1) * 128], Wv[:, kc, :], kc == 0, kc == 7, [Wc, hT], [bk])
                        stt("dve", x.t[:, j, hf * 512:(hf + 1) * 512], bk.t[:, :], valid.t[:, blk:blk + 1], x.t[:, j, hf * 512:(hf + 1) * 512], ALU.mult, ALU.add, [bk, valid, x], [x])

                if STOP == 8 and (gi, l) == STOPAT:
                    raise _Stop()
                for j in range(nb):
                    rmsnorm_T(j)
                for i in range(11):
                    Wc = need(wbase + 8 + i)
                    Wv = Wc.t[:, :].rearrange("p (k a w) -> p k a w", k=8, a=2)
                    for jj in range(2):
                        ft = 2 * i + jj
                        bg_ = bankF()
                        bu_ = bankA()
                        for kc in range(8):
                            mm(bg_.t[:, 0:T], Wv[:, kc, 0, jj * 128:(jj + 1) * 128], hT.t[:, kc, 0:T], kc == 0, kc == 7, [Wc, hT], [bg_])
                        for kc in range(8):
                            mm(bu_.t[:, 0:T], Wv[:, kc, 1, jj * 128:(jj + 1) * 128], hT.t[:, kc, 0:T], kc == 0, kc == 7, [Wc, hT], [bu_])
                        sg_ = sg[ft % 2]
                        act(sg_.t[:, 0:T], bg_.t[:, 0:T], AF.Silu, [bg_], [sg_])
                        tt("dve", uT.t[:, ft, 0:T], sg_.t[:, 0:T], bu_.t[:, 0:T], ALU.mult, [sg_, bu_], [uT])
                dbk = [pbT[0], pbT[1], pbA[0], pbA[1], pbF[0], pbF[1]]
                for ft in range(NFT):
                    Wc = need(wbase + 19 + ft // 4)
                    Wv = Wc.t[:, :].rearrange("p (k w) -> p k w", k=4)
                    for j in range(nb):
                        for hf in range(2):
                            bd = dbk[j * 2 + hf]
                            mm(bd.t[:, :], uT.t[:, ft, j * 128:(j + 1) * 128], Wv[:, ft % 4, hf * 512:(hf + 1) * 512], ft == 0, ft == NFT - 1, [Wc, uT], [bd])
                for j, blk in enumerate(blks):
                    for hf in range(2):
                        bd = dbk[j * 2 + hf]
                        stt("dve", x.t[:, j, hf * 512:(hf + 1) * 512], bd.t[:, :], valid.t[:, blk:blk + 1], x.t[:, j, hf * 512:(hf + 1) * 512], ALU.mult, ALU.add, [bd, valid, x], [x])
            for j, blk in enumerate(blks):
                if samp:
                    P.dma("pool", x, ys, x.t[:, j, :], reads=[x], out=True)
                elif blk >= 1:
                    P.dma("pool", x, yp[(blk - 1) * 128:blk * 128, :], x.t[:, j, :], reads=[x], out=True)
    except _Stop:
        pass
    P.finish()
    return nc


def _consts():
    bf = ml_dtypes.bfloat16
    j = np.arange(128)[:, None]
    i = np.arange(128)[None, :]
    own = (j <= i)
    prev = (j > i)
    own0 = own & (j >= 112)
    samp = (j // 8 == i // 8) & (j % 8 <= i % 8)
    cache = (np.arange(128)[:, None] > np.arange(8)[None, :])
    prev1 = prev & (j >= 112)
    masks = np.concatenate([own, prev, own0, samp, prev1, cache], axis=1).astype(np.float32).astype(bf)
    umat = np.concatenate([own.astype(np.float32), samp.astype(np.float32)], axis=1) * np.float32(-1.0 / 16.0)
    valid = np.ones((128, NB), np.float32)
    valid[0:112, 0] = 0.0
    t = np.arange(128)
    e2 = (t[:, None] // 8 == np.arange(16)[None, :]).astype(np.float32)
    eq = np.broadcast_to(e2.T[None, :, :], (64, 16, 128)).reshape(64, 16 * 128)
    return dict(identb=np.eye(128, dtype=np.float32).astype(bf), identf=np.eye(128, dtype=np.float32),
                masks=masks, umat=np.ascontiguousarray(umat.astype(np.float32)), valid=valid,
                eq=np.ascontiguousarray(eq).astype(bf), e2=e2.astype(bf))


_NC_CACHE = {}


def kernel(**inp):
    f = lambda a: np.ascontiguousarray(np.asarray(a, dtype=np.float32))
    x_prompt, x_sample = f(inp["x_prompt"]), f(inp["x_sample"])
    cache_k, cache_v, state_gla = f(inp["cache_k"]), f(inp["cache_v"]), f(inp["state_gla"])
    meta = f(inp["meta"])
    norm1, norm2, gla_norm = f(inp["norm1"]), f(inp["norm2"]), f(inp["gla_norm"])
    q_norm, k_norm, sinks = f(inp["q_norm"]), f(inp["k_norm"]), f(inp["sinks"])
    w_g2, b_g = f(inp["w_g2"]), f(inp["b_g"])
    common = dict(
        w_in=f(inp["w_in"]), w_o=f(inp["w_o"]), w_gate=f(inp["w_gate"]), w_up=f(inp["w_up"]), w_down=f(inp["w_down"]),
        g1=np.ascontiguousarray(norm1.reshape(NL, 8, 128).transpose(2, 0, 1).reshape(128, NL * 8)),
        g2=np.ascontiguousarray(norm2.reshape(NL, 8, 128).transpose(2, 0, 1).reshape(128, NL * 8)),
        gg=np.ascontiguousarray(gla_norm.T),
        qkg=np.ascontiguousarray(np.stack([q_norm[0], k_norm[0], q_norm[1], k_norm[1]], axis=1)),
        snk=np.ascontiguousarray(np.broadcast_to(sinks.reshape(1, NL * 8), (128, NL * 8))),
        wg2=np.ascontiguousarray(np.concatenate([w_g2.transpose(1, 0, 2).reshape(16, NL * 256), np.zeros((16, NL * 256), np.float32)], 0)),
        bg=np.ascontiguousarray(np.concatenate([b_g.reshape(1, NL * 256), np.zeros((31, NL * 256), np.float32)], 0)),
    )
    common.update(_consts())
    in_maps = []
    for c in range(8):
        seq = c % 4
        xin = np.zeros((NB * 128, D), np.float32)
        xin[112:128] = meta
        xin[128:NBP * 128] = x_prompt[seq]
        xin[NBP * 128:] = x_sample[16 * c:16 * c + 16].reshape(128, D)
        m = dict(common)
        m["xin"] = xin
        m["ck"] = np.ascontiguousarray(cache_k[:, 16 * c:16 * c + 16].reshape(NL, 16, 128, 128))
        m["cv"] = np.ascontiguousarray(cache_v[:, 16 * c:16 * c + 16].reshape(NL, 16, 128, 128))
        m["st"] = np.ascontiguousarray(state_gla[:, 16 * c:16 * c + 16])
        in_maps.append(m)
    if "nc" not in _NC_CACHE:
        _NC_CACHE["nc"] = build()
    res = run_bass_kernel_spmd(_NC_CACHE["nc"], in_maps, core_ids=list(range(8)))
    R = res.results
    y_prompt = np.stack([R[c]["yp"] for c in range(4)], axis=0).astype(np.float32)
    y_sample = np.concatenate([R[c]["ys"].reshape(16, 8, D) for c in range(8)], axis=0).astype(np.float32)
    pk = np.stack([R[c]["pk"].reshape(NL, 128, 2, 64) for c in range(4)], axis=1).astype(np.float32)
    pv = np.stack([R[c]["pv"].reshape(NL, 128, 2, 64) for c in range(4)], axis=1).astype(np.float32)
    pst = np.stack([R[c]["pst"] for c in range(4)], axis=1).astype(np.float32)
    sk = np.concatenate([R[c]["sk"].reshape(NL, 16, 128, 2, 64) for c in range(8)], axis=1).astype(np.float32)
    sv = np.concatenate([R[c]["sv"].reshape(NL, 16, 128, 2, 64) for c in range(8)], axis=1).astype(np.float32)
    sst = np.concatenate([R[c]["sst"] for c in range(8)], axis=1).astype(np.float32)
    return (y_prompt, y_sample, pk, pv, pst, sk, sv, sst)
```

```python
import contextlib
import numpy as np
import ml_dtypes
import concourse.bass as bass
import concourse.mybir as mybir
from concourse.bass_utils import run_bass_kernel_spmd

F32 = mybir.dt.float32
BF16 = mybir.dt.bfloat16
AF = mybir.ActivationFunctionType
ALU = mybir.AluOpType
AX = mybir.AxisListType


class Buf:
    def __init__(self, name, t):
        self.name = name
        self.t = t
        self.lw = {}
        self.rd = {}
        self.dsem = None
        self.dcount = 0
        self.excl = False
        self.aliases = []


class Prog:
    ENGS = ("pe", "act", "dve", "pool", "sp")

    def __init__(self, nc):
        self.nc = nc
        self.stack = contextlib.ExitStack()
        self.sems = {}
        self.count = {e: 0 for e in self.ENGS}
        self.seen = {e: {} for e in self.ENGS}
        self.ops = {e: [] for e in self.ENGS}
        for e in self.ENGS:
            self.sems["E_" + e] = self.stack.enter_context(nc.semaphore("sem_" + e))
        self.out_tokens = {}
        self.nbufs = 0

    def sb(self, name, shape, dtype):
        t = self.stack.enter_context(self.nc.sbuf_tensor("s_" + name, list(shape), dtype))
        return Buf(name, t)

    def ps(self, name, shape, dtype):
        t = self.stack.enter_context(self.nc.psum_tensor(name, list(shape), dtype))
        b = Buf(name, t)
        b.excl = True
        return b

    def view(self, name, t):
        return Buf(name, t)

    def _dsem(self, buf, queue):
        if buf.dsem is None:
            buf.dsem = {}
            buf.dcount = {}
        if queue not in buf.dsem:
            key = "D_%d_%s_%s" % (self.nbufs, buf.name, queue)
            self.nbufs += 1
            self.sems[key] = self.stack.enter_context(self.nc.semaphore("dsem_%d" % self.nbufs))
            buf.dsem[queue] = key
            buf.dcount[queue] = 0
        return buf.dsem[queue]

    def _waits(self, eng, reads, writes, ignore_waw=False):
        need = {}
        for b in reads:
            for k, v in b.lw.items():
                if need.get(k, 0) < v:
                    need[k] = v
            if b.excl:
                for k, v in b.rd.items():
                    if k != "E_" + eng and need.get(k, 0) < v:
                        need[k] = v
        for b in writes:
            if not ignore_waw:
                for k, v in b.lw.items():
                    if need.get(k, 0) < v:
                        need[k] = v
            for k, v in b.rd.items():
                if need.get(k, 0) < v:
                    need[k] = v
            for al in b.aliases:
                for dd in (al.lw, al.rd):
                    for k, v in dd.items():
                        if need.get(k, 0) < v:
                            need[k] = v
        out = []
        seen = self.seen[eng]
        for k, v in need.items():
            if eng == "pe" and k == "E_pe":
                continue
            if seen.get(k, 0) < v:
                seen[k] = v
                out.append((k, v))
        return out

    def _commit(self, tok, reads, writes, ignore_waw=False):
        k, v = tok
        for b in reads:
            if b.rd.get(k, 0) < v:
                b.rd[k] = v
        for b in writes:
            if ignore_waw:
                b.lw[k] = v
            else:
                b.lw = {k: v}
            b.rd = {}

    def op(self, eng, fn, reads=(), writes=()):
        waits = self._waits(eng, reads, writes)
        self.count[eng] += 1
        tok = ("E_" + eng, self.count[eng])
        self._commit(tok, reads, writes)
        self.ops[eng].append((waits, fn, tok[0], 1))
        return tok

    def dma(self, queue, sem_buf, out_ap, in_ap, reads=(), writes=(), out=False, ignore_waw=False):
        waits = self._waits(queue, reads, writes, ignore_waw=ignore_waw)
        key = self._dsem(sem_buf, queue)
        sem_buf.dcount[queue] += 16
        tok = (key, sem_buf.dcount[queue])
        self._commit(tok, reads, writes, ignore_waw=ignore_waw)
        self.ops[queue].append((waits, (lambda e: e.dma_start(out=out_ap, in_=in_ap)), key, 16))
        if out:
            self.out_tokens[key] = sem_buf.dcount[queue]
        return tok

    def load(self, queue, buf, dst_ap, src_ap, ignore_waw=False, extra_reads=()):
        return self.dma(queue, buf, dst_ap, src_ap, reads=list(extra_reads), writes=[buf], ignore_waw=ignore_waw)

    def store(self, queue, buf, dst_ap, src_ap, out=True, extra_writes=()):
        return self.dma(queue, buf, dst_ap, src_ap, reads=[buf], writes=list(extra_writes), out=out)

    def finish(self):
        nc = self.nc
        handles = {}
        final_waits = list(self.out_tokens.items())
        with nc.Block() as block:
            def emit(e, name):
                for waits, fn, semkey, inc in self.ops[name]:
                    for k, v in waits:
                        e.wait_ge(self.sems[k], v)
                    fn(e).then_inc(self.sems[semkey], inc)
                if name == "sp":
                    for k, v in final_waits:
                        e.wait_ge(self.sems[k], v)

            @block.tensor
            def _(e):
                emit(e, "pe")

            @block.scalar
            def _(e):
                emit(e, "act")

            @block.vector
            def _(e):
                emit(e, "dve")

            @block.gpsimd
            def _(e):
                emit(e, "pool")

            @block.sync
            def _(e):
                emit(e, "sp")
        self.stack.close()


D = 1024
DF = 2816
NFT = 22
NBP = 33
NB = 34
SB = 33
NCH = 25
CH = 4096
EPS = 1e-6
NL = 2
CQ, CK, CV, CGQ, CGK, CGV, CGL, COG = 0, 512, 640, 768, 1024, 1280, 1792, 1808
import os
STOP = int(os.environ.get("MK_STOP", "99"))
SUB = int(os.environ.get("MK_SUB", "0"))
STOPAT = tuple(int(v) for v in os.environ.get("MK_STOPAT", "0,0").split(","))


class _Stop(Exception):
    pass


BLKS = [int(v) for v in os.environ["MK_BLKS"].split(",")] if os.environ.get("MK_BLKS") else list(range(NB))


GMAX = int(os.environ.get("MK_G", "3"))
NLANE = int(os.environ.get("MK_LANES", "2"))
if os.environ.get("MK_GROUPS"):
    GROUPS = [[int(v) for v in g.split(",")] for g in os.environ["MK_GROUPS"].split(";")]
else:
    GROUPS = [list(range(i, min(i + GMAX, NBP))) for i in range(0, NBP, GMAX)] + [[SB]]
TM = 128 * max(len(g) for g in GROUPS)
GM = TM // 128


def build():
    nc = bass.Bass("TRN2", target_bir_lowering=False)
    P = Prog(nc)

    def din(name, shape, dtype=F32):
        return nc.dram_tensor(name, list(shape), dtype, kind="ExternalInput").ap()

    def dout(name, shape, dtype=F32):
        return nc.dram_tensor(name, list(shape), dtype, kind="ExternalOutput").ap()

    xin = din("xin", [NB * 128, D])
    w_in = din("w_in", [NL, D, 2320])
    w_o = din("w_o", [NL, D, D])
    w_gate = din("w_gate", [NL, D, DF])
    w_up = din("w_up", [NL, D, DF])
    w_down = din("w_down", [NL, DF, D])
    g1_d = din("g1", [128, NL * 8])
    g2_d = din("g2", [128, NL * 8])
    gg_d = din("gg", [128, NL])
    qkg_d = din("qkg", [64, NL * 2])
    snk_d = din("snk", [128, NL * 8])
    wg2_d = din("wg2", [32, NL * 256])
    bg_d = din("bg", [32, NL * 256])
    ck_d = din("ck", [NL, 16, 128, 128])
    cv_d = din("cv", [NL, 16, 128, 128])
    st_d = din("st", [NL, 16, 4, 64, 128])
    identb_d = din("identb", [128, 128], BF16)
    identf_d = din("identf", [128, 128])
    masks_d = din("masks", [128, 5 * 128 + 8], BF16)
    umat_d = din("umat", [128, 256])
    valid_d = din("valid", [128, NB])
    eq_d = din("eq", [64, 16 * 128], BF16)
    e2_d = din("e2", [128, 16], BF16)

    yp = dout("yp", [(NBP - 1) * 128, D])
    ys = dout("ys", [128, D])
    pk = dout("pk", [NL, 128, 128])
    pv = dout("pv", [NL, 128, 128])
    pst = dout("pst", [NL, 4, 64, 128])
    sk = dout("sk", [NL, 16, 128, 128])
    sv = dout("sv", [NL, 16, 128, 128])
    sst = dout("sst", [NL, 16, 4, 64, 128])

    wsc_t = nc.dram_tensor("wsc", [NL, NCH, 128, CH], BF16, kind="ExternalOutput").ap()
    wsc = Buf("wsc", wsc_t)

    sb = P.sb
    identb = sb("identb", [128, 128], BF16)
    identf = sb("identf", [128, 128], F32)
    masks = sb("masks", [128, 5 * 128 + 8], BF16)
    umat = sb("umat", [128, 256], F32)
    valid = sb("valid", [128, NB], F32)
    eq = sb("eq", [64, 16, 128], BF16)
    e2 = sb("e2", [128, 16], BF16)
    g1 = sb("g1", [128, NL * 8], F32)
    g2 = sb("g2", [128, NL * 8], F32)
    gg = sb("gg", [128, NL], F32)
    qkg = sb("qkg", [64, NL * 2], F32)
    esink = sb("esink", [128, NL * 8], F32)
    wg2 = sb("wg2", [32, NL * 256], F32)
    bg = sb("bg", [32, NL * 256], F32)
    ones64 = sb("ones64", [64, 64], BF16)
    ones1 = sb("ones1", [32, 128], F32)
    epsb = sb("epsb", [128, 1], F32)
    epsb1 = sb("epsb1", [128, 1], F32)

    x = sb("x", [128, GM, D], F32)
    st4 = sb("st4", [128, 8], F32)
    hb = sb("hb", [128, D], BF16)
    hT = sb("hT", [128, 8, TM], BF16)
    qkraw = [sb("qkraw%d" % i, [64, TM], F32) for i in range(2)]
    qksq = [sb("qksq%d" % i, [64, TM], BF16) for i in range(2)]
    rqh = [sb("rqh%d" % i, [64, TM], F32) for i in range(2)]
    qn = sb("qn", [64, GM, 8, 128], BF16)
    kn32 = sb("kn32", [64, 2, TM], F32)
    kbuf = [sb("kbuf%d" % l, [64, GM + 1, 2, 128], BF16) for l in range(NL)]
    vaug = [sb("vaug%d" % l, [128, GM + 1, 2, 65], BF16) for l in range(NL)]
    gqraw = sb("gqraw", [64, 4, TM], BF16)
    gkraw = sb("gkraw", [64, 4, TM], BF16)
    glowT = sb("glowT", [32, TM], F32)
    v32 = sb("v32", [128, GM, 128], F32)
    gvb = sb("gvb", [128, GM, 512], BF16)
    sog = sb("sog", [128, GM, 512], BF16)
    UW = NFT * TM // 2
    A2W = UW + 2 * TM
    arena2 = P.stack.enter_context(nc.sbuf_tensor("s_arena2", [128, A2W], F32))

    def carve2(name, off, words, dtype, pat=None, parts=128, **kw):
        v = arena2[0:parts, off:off + words]
        if dtype is BF16:
            v = v.bitcast(BF16)
        if pat is not None:
            v = v.rearrange(pat, **kw)
        return Buf(name, v)

    uT = carve2("uT", 0, UW, BF16, "p (f t) -> p f t", f=NFT)
    sg = [carve2("sg%d" % i, UW + i * TM, TM, F32) for i in range(2)]
    tsets = []
    for i_ in range(NLANE):
        n_ = lambda s_: "%s_%d" % (s_, i_)
        if i_ == 0:
            tsets.append(dict(
                lnt=sb(n_("lnt"), [128, 256], F32), eG=sb(n_("eG"), [64, 4, 128], F32), enG=sb(n_("enG"), [64, 4, 128], F32),
                qtil=sb(n_("qtil"), [64, 4, 128], BF16), kt32=sb(n_("kt32"), [64, 4, 128], F32), ktil=sb(n_("ktil"), [64, 4, 128], BF16),
                khatT=sb(n_("khatT"), [64, 4, 128], BF16), khat=sb(n_("khat"), [128, 256], BF16), atm=sb(n_("atm"), [128, 4, 128], BF16),
                o32=sb(n_("o32"), [128, 4, 128], F32), gst=sb(n_("gst"), [128, 16], F32),
                PT=[sb(n_("PTa"), [128, 2, 512], BF16), sb(n_("PTb"), [128, 2, 512], BF16)],
                den=[sb(n_("dena"), [128, 8], F32), sb(n_("denb"), [128, 8], F32)]))
        else:
            assert i_ == 1
            o_ = [0]

            def c2(nm, words, dtype, pat=None, parts=128, **kw):
                b = carve2(n_(nm), o_[0], words, dtype, pat, parts, **kw)
                o_[0] += words
                return b
            ts1 = dict(
                lnt=c2("lnt", 256, F32), eG=c2("eG", 512, F32, "p (h t) -> p h t", parts=64, h=4), enG=c2("enG", 512, F32, "p (h t) -> p h t", parts=64, h=4),
                qtil=c2("qtil", 256, BF16, "p (h t) -> p h t", parts=64, h=4), kt32=c2("kt32", 512, F32, "p (h t) -> p h t", parts=64, h=4),
                ktil=c2("ktil", 256, BF16, "p (h t) -> p h t", parts=64, h=4), khatT=c2("khatT", 256, BF16, "p (h t) -> p h t", parts=64, h=4),
                khat=c2("khat", 128, BF16), atm=c2("atm", 256, BF16, "p (h t) -> p h t", h=4), o32=c2("o32", 512, F32, "p (h t) -> p h t", h=4),
                gst=c2("gst", 16, F32),
                PT=[c2("PTa", 512, BF16, "p (a t) -> p a t", a=2), c2("PTb", 512, BF16, "p (a t) -> p a t", a=2)],
                den=[c2("dena", 8, F32), c2("denb", 8, F32)])
            assert o_[0] <= A2W, (o_[0], A2W)
            tsets.append(ts1)
            flat = [v for v in ts1.values() if isinstance(v, Buf)] + ts1["PT"] + ts1["den"]
            for a_ in flat:
                for b_ in [uT] + sg:
                    a_.aliases.append(b_)
                    b_.aliases.append(a_)
    S = [sb("S%d" % l, [64, 4, 128], F32) for l in range(NL)]
    Sb = [sb("Sb%d" % l, [64, 4, 128], BF16) for l in range(NL)]
    merged = sb("merged", [128, GM, D], BF16)
    kt_out = sb("kt_out", [128, 128], F32)
    ring = [sb("ring%d" % i, [128, CH], BF16) for i in range(4)]

    AW = 12832
    arena = P.stack.enter_context(nc.sbuf_tensor("s_arena", [128, AW], F32))

    def carve(name, off, words, dtype, pat=None, parts=128, **kw):
        v = arena[0:parts, off:off + words]
        if dtype is BF16:
            v = v.bitcast(BF16)
        if pat is not None:
            v = v.rearrange(pat, **kw)
        return Buf(name, v)

    cst = carve("cst", 0, 2048, F32, "p (s c) -> p s c", s=16)
    ckb = carve("ckb", 2048, 1024, BF16, "p (s c) -> p s c", s=16)
    vc = carve("vc", 3072, 1040, BF16, "p (s g e) -> p s g e", s=16, g=2)
    ptc = carve("ptc", 4112, 512, BF16)
    khx = carve("khx", 4624, 512, BF16, "p (s c) -> p s c", s=4)
    kcT = carve("kcT", 5136, 2048, BF16, "p (s g t) -> p s g t", s=16, g=2, parts=64)
    otc = carve("otc", 7184, 1024, F32, "p (g h t) -> p g h t", g=2, h=4, parts=65)
    s0q = carve("s0q", 8208, 2048, F32, "p (s h v) -> p s h v", s=4, h=4, parts=64)
    s0b = carve("s0b", 10256, 1024, BF16, "p (s h v) -> p s h v", s=4, h=4, parts=64)
    qx = carve("qx", 11280, 1024, BF16, "p (h s t) -> p h s t", h=4, s=4, parts=64)
    qns = carve("qns", 12304, 512, BF16, "p (s g c) -> p s g c", s=16, g=2, parts=64)
    samp_bufs = [cst, ckb, vc, ptc, khx, kcT, otc, s0q, s0b, qx, qns]
    stg = [carve("stg%d" % i, 2048 * i, 2048, F32, "p (k w) -> p k w", k=8) for i in range(3)]
    cht = [carve("cht%d" % i, 6144 + 2048 * i, 2048, BF16) for i in range(2)]
    for a_ in stg + cht:
        for b_ in samp_bufs:
            a_.aliases.append(b_)
            b_.aliases.append(a_)

    pbF = [P.ps("pbF%d" % i, [128, 512], F32) for i in range(3)]
    pbT = [P.ps("pbT%d" % i, [128, 512], F32) for i in range(2)]
    pbA = [P.ps("pbA%d" % i, [128, 512], F32) for i in range(2)]
    pb16 = P.ps("pb16", [128, 1024], BF16)
    rr = {"F": 0, "T": 0, "A": 0}

    def bankF():
        rr["F"] += 1
        return pbF[rr["F"] % 3]

    def bankT():
        rr["T"] += 1
        return pbT[rr["T"] % 2]

    def bankA():
        rr["A"] += 1
        return pbA[rr["A"] % 2]

    def mm(out, lhsT, rhs, start, stop, r, w):
        P.op("pe", lambda e: e.matmul(out, lhsT, rhs, start=start, stop=stop), reads=r, writes=w)

    def tr(out, in_, ident, r, w):
        P.op("pe", lambda e: e.transpose(out, in_, ident), reads=r, writes=w)

    def act(out, in_, func, r, w, **kw):
        P.op("act", lambda e: e.activation(out, in_, func, **kw), reads=r, writes=w)

    def cp(eng, out, in_, r, w):
        if eng == "act":
            act(out, in_, AF.Copy, r, w)
        else:
            P.op(eng, lambda e: e.tensor_copy(out, in_), reads=r, writes=w)

    def tt(eng, out, a, b, op, r, w):
        P.op(eng, lambda e: e.tensor_tensor(out, a, b, op), reads=r, writes=w)

    def tsc(eng, out, a, s1, s2, op0, op1, r, w):
        if s2 is None:
            P.op(eng, lambda e: e.tensor_scalar(out, a, s1, None, op0), reads=r, writes=w)
        else:
            P.op(eng, lambda e: e.tensor_scalar(out, a, s1, s2, op0, op1), reads=r, writes=w)

    def stt(eng, out, a, s, b, op0, op1, r, w):
        P.op(eng, lambda e: e.scalar_tensor_tensor(out, a, s, b, op0, op1), reads=r, writes=w)

    def mset(eng, buf, ap, val):
        P.op(eng, lambda e: e.memset(ap, val), writes=[buf])

    cld = Buf("cld", None)
    cbufs = []
    for b_, d_ in ((identb, identb_d), (identf, identf_d), (masks, masks_d), (umat, umat_d), (valid, valid_d),
                   (e2, e2_d), (g1, g1_d), (g2, g2_d), (gg, gg_d), (qkg, qkg_d), (esink, snk_d), (wg2, wg2_d), (bg, bg_d)):
        P.dma("pool", cld, b_.t[:, :], d_, writes=[b_])
        cbufs.append(b_)
    P.dma("pool", cld, eq.t[:, :, :], eq_d.rearrange("p (s t) -> p s t", s=16), writes=[eq])
    cbufs.append(eq)
    for b_ in cbufs:
        b_.lw = {cld.dsem["pool"]: cld.dcount["pool"]}
    act(esink.t[:, :], esink.t[:, :], AF.Exp, [esink], [esink])
    mset("dve", ones64, ones64.t[:, :], 1.0)
    mset("dve", ones1, ones1.t[:, :], 1.0)
    mset("dve", epsb, epsb.t[:, :], EPS)
    mset("dve", epsb1, epsb1.t[:, :], 1.0)
    for l in range(NL):
        mset("dve", S[l], S[l].t[:, :, :], 0.0)
        mset("dve", Sb[l], Sb[l].t[:, :, :], 0.0)
        mset("pool", kbuf[l], kbuf[l].t[:, :, :, :], 0.0)
        mset("pool", vaug[l], vaug[l].t[:, :, :, :], 0.0)
        mset("pool", vaug[l], vaug[l].t[:, :, :, 64:65], 1.0)
    m_own = masks.t[:, 0:128]
    m_prev = masks.t[:, 128:256]
    m_own0 = masks.t[:, 256:384]
    m_samp = masks.t[:, 384:512]
    m_prev1 = masks.t[:, 512:640]
    m_cache = masks.t[:, 640:648]
    pth = Buf("pth", None)

    prep_state = {"stg": 0, "cht": 0, "eng": 0}
    for ct_ in cht:
        mset("pool", ct_, ct_.t[:, :], 0.0)

    def kview(ct, W):
        return ct.t[:, 0:8 * W].rearrange("p (k w) -> p k w", k=8)

    def prep_piece(ct, dst, src, kc, wdt, gain):
        s_ = stg[prep_state["stg"] % 3]
        prep_state["stg"] += 1
        q_ = "act" if prep_state["stg"] % 2 else "sp"
        P.load(q_, s_, s_.t[:, 0:kc, 0:wdt], src)
        eng = ("dve", "pool")[prep_state["eng"] % 2]
        prep_state["eng"] += 1
        if gain is None:
            if prep_state["eng"] % 3 == 0:
                eng = "act"
            cp(eng, dst, s_.t[:, 0:kc, 0:wdt], [s_], [ct])
        else:
            gbuf, gap = gain
            tt(eng, dst, s_.t[:, 0:kc, 0:wdt], gap.unsqueeze(2).broadcast_to([128, kc, wdt]), ALU.mult, [s_, gbuf], [ct])

    def prep_chunk(l, c, pieces, zero=None):
        ct = cht[prep_state["cht"] % 2]
        prep_state["cht"] += 1
        if zero is not None:
            mset("dve", ct, kview(ct, zero[0])[:, :, zero[1]:zero[2]], 0.0)
        for (dst_fn, src, kc, wdt, gain) in pieces:
            prep_piece(ct, dst_fn(ct), src, kc, wdt, gain)
        for q4 in range(4):
            P.dma("sp", ct, wsc_t[l, c][:, q4 * 1024:(q4 + 1) * 1024], ct.t[:, q4 * 1024:(q4 + 1) * 1024], reads=[ct], writes=[wsc], ignore_waw=True)

    for l in range(NL):
        wi = w_in[l].rearrange("(k p) c -> p k c", p=128)
        wo_ = w_o[l].rearrange("(k p) c -> p k c", p=128)
        wg_ = w_gate[l].rearrange("(k p) c -> p k c", p=128)
        wu_ = w_up[l].rearrange("(k p) c -> p k c", p=128)
        wd_ = w_down[l].rearrange("(k p) c -> p k c", p=128)
        G1 = (g1, g1.t[:, l * 8:(l + 1) * 8])
        G2 = (g2, g2.t[:, l * 8:(l + 1) * 8])

        def cols(W, d0, s0, n, src, gain):
            out = []
            for o in range(0, n, 256):
                wdt = min(256, n - o)
                out.append(((lambda ct, W=W, a=d0 + o, wdt=wdt: kview(ct, W)[:, :, a:a + wdt]), src[:, :, s0 + o:s0 + o + wdt], 8, wdt, gain))
            return out

        prep_chunk(l, 0, cols(512, 0, CQ, 512, wi, G1))
        prep_chunk(l, 1, cols(384, 0, CK, 128, wi, G1) + cols(384, 128, CGQ, 256, wi, G1))
        prep_chunk(l, 2, cols(288, 0, CGK, 256, wi, G1) + cols(288, 256, CGL, 16, wi, G1), zero=(288, 272, 288))
        prep_chunk(l, 3, cols(128, 0, CV, 128, wi, G1))
        prep_chunk(l, 4, cols(512, 0, CGV, 512, wi, G1))
        prep_chunk(l, 5, cols(512, 0, COG, 512, wi, G1))
        for hf in range(2):
            pcs = []
            for o in range(0, 512, 256):
                a = hf * 512 + o
                pcs.append(((lambda ct, o=o: kview(ct, 512)[:, 0:4, o:o + 256]), wo_[:, 0:4, a:a + 256], 4, 256, None))
                pcs.append(((lambda ct, o=o: kview(ct, 512)[:, 4:8, o:o + 256]), wo_[:, 4:8, a:a + 256], 4, 256,
                            (gg, gg.t[:, l:l + 1].broadcast_to([128, 4]))))
            prep_chunk(l, 6 + hf, pcs)
        for i in range(11):
            prep_chunk(l, 8 + i, cols(512, 0, 256 * i, 256, wg_, G2) + cols(512, 256, 256 * i, 256, wu_, G2))
        for j in range(6):
            nft = min(4, NFT - 4 * j)
            pcs = []
            for o in range(0, 1024, 256):
                pcs.append(((lambda ct, o=o, nft=nft: ct.t[:, :].rearrange("p (k w) -> p k w", k=4)[:, 0:nft, o:o + 256]),
                            wd_[:, 4 * j:4 * j + nft, o:o + 256], nft, 256, None))
            prep_chunk(l, 19 + j, pcs)

    wseq = [(l, c) for _g in GROUPS for l in range(NL) for c in range(NCH)]
    wst = {"issued": 0}

    def need(i):
        while wst["issued"] < min(len(wseq), i + 3):
            j = wst["issued"]
            l_, c_ = wseq[j]
            rb = ring[j % 4]
            for q4 in range(4):
                P.dma("sp", rb, rb.t[:, q4 * 1024:(q4 + 1) * 1024], wsc_t[l_, c_][:, q4 * 1024:(q4 + 1) * 1024], reads=[wsc], writes=[rb], ignore_waw=(q4 > 0))
            wst["issued"] += 1
        return ring[i % 4]

    def rmsnorm_T(j):
        mset("dve", st4, st4.t[:, 0:1], 0.0)
        act(hb.t[:, :], x.t[:, j, :], AF.Square, [x], [hb, st4], accum_out=st4.t[:, 0:1])
        act(st4.t[:, 2:3], st4.t[:, 0:1], AF.Ln, [st4], [st4], scale=1.0 / D, bias=epsb.t[:, 0:1])
        act(st4.t[:, 3:4], st4.t[:, 2:3], AF.Exp, [st4], [st4], scale=-0.5)
        tsc("dve", hb.t[:, :], x.t[:, j, :], st4.t[:, 3:4], None, ALU.mult, None, [x, st4], [hb])
        for kc in range(8):
            tr(pb16.t[:, kc * 128:(kc + 1) * 128], hb.t[:, kc * 128:(kc + 1) * 128], identb.t[:, :], [hb, identb], [pb16])
        cp("act", hT.t[:, :, j * 128:(j + 1) * 128], pb16.t[:, :].rearrange("p (k t) -> p k t", k=8), [pb16], [hT])

    try:
        for gi, blks in enumerate(GROUPS):
            nb = len(blks)
            T = nb * 128
            samp = (blks[0] == SB)
            for j, blk in enumerate(blks):
                P.load("pool", x, x.t[:, j, :], xin[blk * 128:(blk + 1) * 128, :], ignore_waw=(j > 0))
            for l in range(NL):
                wbase = (gi * NL + l) * NCH
                for j in range(nb):
                    rmsnorm_T(j)
                if STOP == 2 and (gi, l) == STOPAT:
                    raise _Stop()
                W0 = need(wbase + 0)
                W0v = W0.t[:, :].rearrange("p (k w) -> p k w", k=8)
                W1 = need(wbase + 1)
                W1v = W1.t[:, 0:8 * 384].rearrange("p (k w) -> p k w", k=8)
                pend = []

                def qk_tail(h, sl):
                    bk2 = bankF()
                    mm(bk2.t[0:64, 0:T], ones64.t[:, :], qksq[sl].t[:, 0:T], True, True, [ones64, qksq[sl]], [bk2])
                    act(rqh[sl].t[:, 0:T], bk2.t[0:64, 0:T], AF.Ln, [bk2], [rqh[sl]], scale=1.0 / 64, bias=epsb.t[0:64, 0:1])
                    act(rqh[sl].t[:, 0:T], rqh[sl].t[:, 0:T], AF.Exp, [rqh[sl]], [rqh[sl]], scale=-0.5)
                    if h < 8:
                        stt("dve", qn.t[:, 0:nb, h, :], qkraw[sl].t[:, 0:T].rearrange("p (j t) -> p j t", t=128), qkg.t[:, 2 * l:2 * l + 1],
                            rqh[sl].t[:, 0:T].rearrange("p (j t) -> p j t", t=128), ALU.mult, ALU.mult, [qkraw[sl], qkg, rqh[sl]], [qn])
                    else:
                        stt("dve", kn32.t[:, h - 8, 0:T], qkraw[sl].t[:, 0:T], qkg.t[:, 2 * l + 1:2 * l + 2], rqh[sl].t[:, 0:T],
                            ALU.mult, ALU.mult, [qkraw[sl], qkg, rqh[sl]], [kn32])

                for h in range(10):
                    sl = h % 2
                    bk = bankF()
                    for kc in range(8):
                        lw = W0v[:, kc, h * 64:(h + 1) * 64] if h < 8 else W1v[:, kc, (h - 8) * 64:(h - 7) * 64]
                        mm(bk.t[0:64, 0:T], lw, hT.t[:, kc, 0:T], kc == 0, kc == 7, [W0 if h < 8 else W1, hT], [bk])
                    cp("dve", qkraw[sl].t[:, 0:T], bk.t[0:64, 0:T], [bk], [qkraw[sl]])
                    act(qksq[sl].t[:, 0:T], bk.t[0:64, 0:T], AF.Square, [bk], [qksq[sl]])
                    if pend:
                        pend.pop()()
                    pend.append(lambda h=h, sl=sl: qk_tail(h, sl))
                pend.pop()()
                for j, blk in enumerate(blks):
                    cp("pool", kbuf[l].t[:, j + 1, :, :], kn32.t[:, :, j * 128:(j + 1) * 128], [kn32], [kbuf[l]])
                W2 = need(wbase + 2)
                W2v = W2.t[:, 0:8 * 288].rearrange("p (k w) -> p k w", k=8)
                for h in range(4):
                    bk = bankF()
                    for kc in range(8):
                        mm(bk.t[0:64, 0:T], W1v[:, kc, 128 + h * 64:128 + (h + 1) * 64], hT.t[:, kc, 0:T], kc == 0, kc == 7, [W1, hT], [bk])
                    cp("act", gqraw.t[:, h, 0:T], bk.t[0:64, 0:T], [bk], [gqraw])
                for h in range(4):
                    bk = bankF()
                    for kc in range(8):
                        mm(bk.t[0:64, 0:T], W2v[:, kc, h * 64:(h + 1) * 64], hT.t[:, kc, 0:T], kc == 0, kc == 7, [W2, hT], [bk])
                    cp("dve", gkraw.t[:, h, 0:T], bk.t[0:64, 0:T], [bk], [gkraw])
                bk = bankF()
                for kc in range(8):
                    mm(bk.t[0:32, 0:T], W2v[:, kc, 256:288], hT.t[:, kc, 0:T], kc == 0, kc == 7, [W2, hT], [bk])
                cp("act", glowT.t[:, 0:T], bk.t[0:32, 0:T], [bk], [glowT])
                if STOP == 4 and (gi, l) == STOPAT:
                    raise _Stop()
                W3 = need(wbase + 3)
                W3v = W3.t[:, 0:8 * 128].rearrange("p (k w) -> p k w", k=8)
                for j, blk in enumerate(blks):
                    bk = bankT()
                    for kc in range(8):
                        mm(bk.t[:, 0:128], hT.t[:, kc, j * 128:(j + 1) * 128], W3v[:, kc, :], kc == 0, kc == 7, [W3, hT], [bk])
                    cp("act", v32.t[:, j, :], bk.t[:, 0:128], [bk], [v32])
                    cp("dve", vaug[l].t[:, j + 1, :, 0:64], bk.t[:, 0:128].rearrange("p (g d) -> p g d", g=2), [bk], [vaug[l]])
                    if j + 1 < nb:
                        pass
                W4 = need(wbase + 4)
                W4v = W4.t[:, :].rearrange("p (k w) -> p k w", k=8)
                for j in range(nb):
                    bk = bankT()
                    for kc in range(8):
                        mm(bk.t[:, :], hT.t[:, kc, j * 128:(j + 1) * 128], W4v[:, kc, :], kc == 0, kc == 7, [W4, hT], [bk])
                    cp("act", gvb.t[:, j, :], bk.t[:, :], [bk], [gvb])
                W5 = need(wbase + 5)
                W5v = W5.t[:, :].rearrange("p (k w) -> p k w", k=8)
                for j in range(nb):
                    bk = bankT()
                    for kc in range(8):
                        mm(bk.t[:, :], hT.t[:, kc, j * 128:(j + 1) * 128], W5v[:, kc, :], kc == 0, kc == 7, [W5, hT], [bk])
                    act(sog.t[:, j, :], bk.t[:, :], AF.Silu, [bk], [sog])

                if STOP == 5 and (gi, l) == STOPAT:
                    raise _Stop()
                if samp:
                    if l == 0:
                        mset("pool", vc, vc.t[:, :, :, 64:65], 1.0)
                    P.load("pool", cst, cst.t[:, :, :], ck_d[l].rearrange("s k c -> k s c"))
                    cp("pool", ckb.t[:, :, :], cst.t[:, :, :], [cst], [ckb])
                    for rnd in range(4):
                        for i in range(8):
                            s_, g_ = (rnd * 8 + i) // 2, (rnd * 8 + i) % 2
                            tr(pb16.t[0:64, i * 128:(i + 1) * 128], ckb.t[:, s_, g_ * 64:(g_ + 1) * 64], identb.t[:, :], [ckb, identb], [pb16])
                        cp("dve", kcT.t[:, rnd * 4:rnd * 4 + 4, :, :], pb16.t[0:64, :].rearrange("p (s g t) -> p s g t", s=4, g=2), [pb16], [kcT])
                    P.dma("pool", pth, sk[l, :, 0:120, :], ck_d[l, :, 8:128, :], reads=[], writes=[], out=True)
                    P.load("pool", cst, cst.t[:, :, :], cv_d[l].rearrange("s k c -> k s c"))
                    cp("pool", vc.t[:, :, :, 0:64], cst.t[:, :, :].rearrange("p s (g d) -> p s g d", g=2), [cst], [vc])
                    P.dma("pool", pth, sv[l, :, 0:120, :], cv_d[l, :, 8:128, :], reads=[], writes=[], out=True)
                    for g_ in range(2):
                        cp("pool", qns.t[:, :, g_, :].rearrange("p s (h t) -> p h s t", h=4), qn.t[:, 0, 4 * g_:4 * g_ + 4, :].rearrange("p h (s t) -> p h s t", t=8), [qn], [qns])
                    stc = [bankA(), bankA()]
                    for s_ in range(16):
                        for g_ in range(2):
                            bk = stc[s_ // 8]
                            o_ = ((s_ % 8) * 2 + g_) * 32
                            mm(bk.t[:, o_:o_ + 32], kcT.t[:, s_, g_, :], qns.t[:, s_, g_, :], True, True, [kcT, qns], [bk])
                    for hf in range(2):
                        act(ptc.t[:, hf * 512:(hf + 1) * 512], stc[hf].t[:, :], AF.Exp, [stc[hf]], [ptc], scale=0.125)
                    tt("pool", ptc.t[:, :].rearrange("p (a q) -> p a q", q=8), ptc.t[:, :].rearrange("p (a q) -> p a q", q=8),
                       m_cache.unsqueeze(1).broadcast_to([128, 128, 8]), ALU.mult, [ptc, masks], [ptc])
                    otb = [bankA(), bankA()]
                    for s_ in range(16):
                        for g_ in range(2):
                            bk = otb[s_ // 8]
                            o_ = ((s_ % 8) * 2 + g_) * 32
                            mm(bk.t[0:65, o_:o_ + 32], vc.t[:, s_, g_, :], ptc.t[:, (s_ * 2 + g_) * 32:(s_ * 2 + g_ + 1) * 32], True, True, [vc, ptc], [bk])
                    for hf in range(2):
                        for g_ in range(2):
                            cp("act" if g_ else "dve", otc.t[:, g_, :, hf * 64:(hf + 1) * 64].rearrange("p h (s t) -> p s h t", t=8),
                               otb[hf].t[0:65, :].rearrange("p (s g h t) -> p s g h t", s=8, g=2, h=4)[:, :, g_, :, :], [otb[hf]], [otc])

                def attn_chain(j, blk, g, ts):
                    pt = ts["PT"][g]
                    den = ts["den"][g]
                    rq_ = qn.t[:, j, 4 * g:4 * g + 4, :].rearrange("p h t -> p (h t)")
                    has_prev = (not samp) and blk > 0
                    sbk = yield from take(2 if has_prev else 1)
                    bo = sbk[0]
                    mm(bo.t[:, :], kbuf[l].t[:, j + 1, g, :], rq_, True, True, [kbuf[l], qn], [bo])
                    if has_prev:
                        bp = sbk[1]
                        mm(bp.t[:, :], kbuf[l].t[:, j, g, :], rq_, True, True, [kbuf[l], qn], [bp])
                    yield
                    act(pt.t[:, 0, :], bo.t[:, :], AF.Exp, [bo], [pt], scale=0.125)
                    if has_prev:
                        act(pt.t[:, 1, :], bp.t[:, :], AF.Exp, [bp], [pt], scale=0.125)
                    give(*sbk)
                    yield
                    mk = m_samp if samp else (m_own0 if blk == 0 else m_own)
                    tt("pool", pt.t[:, 0, :].rearrange("p (h t) -> p h t", h=4), pt.t[:, 0, :].rearrange("p (h t) -> p h t", h=4),
                       mk.unsqueeze(1).broadcast_to([128, 4, 128]), ALU.mult, [pt, masks], [pt])
                    if has_prev:
                        tt("pool", pt.t[:, 1, :].rearrange("p (h t) -> p h t", h=4), pt.t[:, 1, :].rearrange("p (h t) -> p h t", h=4),
                           (m_prev1 if blk == 1 else m_prev).unsqueeze(1).broadcast_to([128, 4, 128]), ALU.mult, [pt, masks], [pt])
                    yield
                    bv_ = (yield from take(1))[0]
                    for h in range(4):
                        oc = bv_.t[:, h * 65:(h + 1) * 65]
                        last_own = not (has_prev or samp)
                        mm(oc, pt.t[:, 0, h * 128:(h + 1) * 128], vaug[l].t[:, j + 1, g, :], True, last_own, [pt, vaug[l]], [bv_])
                        if has_prev:
                            mm(oc, pt.t[:, 1, h * 128:(h + 1) * 128], vaug[l].t[:, j, g, :], False, True, [pt, vaug[l]], [bv_])
                        if samp:
                            mm(oc, otc.t[:, g, h, :], identf.t[0:65, 0:65], False, True, [otc, identf], [bv_])
                    yield
                    pv4 = bv_.t[:, 0:260].rearrange("p (h e) -> p h e", h=4)
                    tt("dve", den.t[:, 0:4], pv4[:, :, 64], esink.t[:, l * 8 + 4 * g:l * 8 + 4 * g + 4], ALU.add, [bv_, esink], [den])
                    P.op("dve", lambda e: e.reciprocal(den.t[:, 4:8], den.t[:, 0:4]), reads=[den], writes=[den])
                    tt("dve", merged.t[:, j, g * 256:(g + 1) * 256].rearrange("p (h d) -> p h d", h=4), pv4[:, :, 0:64],
                       den.t[:, 4:8].unsqueeze(2).broadcast_to([128, 4, 64]), ALU.mult, [bv_, den], [merged])
                    give(bv_)

                def gla_chain(j, blk, ts):
                    lnt, eG, enG, qtil, kt32, ktil = ts["lnt"], ts["eG"], ts["enG"], ts["qtil"], ts["kt32"], ts["ktil"]
                    khatT, khat, atm, o32, gst = ts["khatT"], ts["khat"], ts["atm"], ts["o32"], ts["gst"]
                    c0, c1 = j * 128, (j + 1) * 128
                    bl = (yield from take(1))[0]
                    mm(bl.t[:, 0:256], glowT.t[:, c0:c1], wg2.t[:, l * 256:(l + 1) * 256], True, False, [glowT, wg2], [bl])
                    mm(bl.t[:, 0:256], ones1.t[:, :], bg.t[:, l * 256:(l + 1) * 256], False, True, [ones1, bg], [bl])
                    yield
                    act(lnt.t[:, :], bl.t[:, 0:256], AF.Exp, [bl], [lnt], scale=-1.0)
                    give(bl)
                    act(lnt.t[:, :], lnt.t[:, :], AF.Ln, [lnt], [lnt], bias=epsb1.t[:, 0:1])
                    yield
                    U = umat.t[:, 128:256] if samp else umat.t[:, 0:128]
                    gmask = m_samp if samp else m_own
                    bgT = (yield from take(1))[0]
                    for h in range(4):
                        mm(bgT.t[0:64, h * 128:(h + 1) * 128], lnt.t[:, h * 64:(h + 1) * 64], U, True, True, [lnt, umat], [bgT])
                    yield
                    g4 = bgT.t[0:64, :].rearrange("p (h t) -> p h t", h=4)
                    act(eG.t[:, :, :], g4, AF.Exp, [bgT], [eG])
                    act(enG.t[:, :, :], g4, AF.Exp, [bgT], [enG], scale=-1.0)
                    give(bgT)
                    yield
                    stt("dve", qtil.t[:, :, :], gqraw.t[:, :, c0:c1], 0.125, eG.t[:, :, :], ALU.mult, ALU.mult, [gqraw, eG], [qtil])
                    tt("dve", kt32.t[:, :, :], gkraw.t[:, :, c0:c1], enG.t[:, :, :], ALU.mult, [gkraw, enG], [kt32])
                    yield
                    cp("pool", ktil.t[:, :, :], kt32.t[:, :, :], [kt32], [ktil])
                    if samp:
                        egl = eG.t[:, :, :].rearrange("p h (s t) -> p h s t", t=8)[:, :, :, 7:8].broadcast_to([64, 4, 16, 8])
                        tt("pool", khatT.t[:, :, :].rearrange("p h (s t) -> p h s t", t=8), kt32.t[:, :, :].rearrange("p h (s t) -> p h s t", t=8),
                           egl, ALU.mult, [kt32, eG], [khatT])
                    else:
                        egl = eG.t[:, :, 127:128].broadcast_to([64, 4, 128])
                        tt("pool", khatT.t[:, :, :], kt32.t[:, :, :], egl, ALU.mult, [kt32, eG], [khatT])
                    yield
                    for h in range(4):
                        tr(pb16.t[:, h * 64:(h + 1) * 64], khatT.t[:, h, :], identb.t[0:64, 0:64], [khatT, identb], [pb16])
                    cp("dve", khat.t[:, :], pb16.t[:, 0:256], [pb16], [khat])
                    ba = (yield from take(1))[0]
                    for h in range(4):
                        mm(ba.t[:, h * 128:(h + 1) * 128], ktil.t[:, h, :], qtil.t[:, h, :], True, True, [ktil, qtil], [ba])
                    yield
                    tt("dve", atm.t[:, :, :], ba.t[:, :].rearrange("p (h t) -> p h t", h=4), gmask.unsqueeze(1).broadcast_to([128, 4, 128]), ALU.mult, [ba, masks], [atm])
                    give(ba)
                    yield
                    if not samp:
                        bo, bs = yield from take(2)
                        for h in range(4):
                            oc = bo.t[:, h * 128:(h + 1) * 128]
                            mm(oc, atm.t[:, h, :], gvb.t[:, j, h * 128:(h + 1) * 128], True, False, [atm, gvb], [bo])
                            mm(oc, qtil.t[:, h, :], Sb[l].t[:, h, :], False, True, [qtil, Sb[l]], [bo])
                        for h in range(4):
                            mm(bs.t[0:64, h * 128:(h + 1) * 128], khat.t[:, h * 64:(h + 1) * 64], gvb.t[:, j, h * 128:(h + 1) * 128], True, True, [khat, gvb], [bs])
                        tt("dve", S[l].t[:, :, :], S[l].t[:, :, :], eG.t[:, :, 127:128].broadcast_to([64, 4, 128]), ALU.mult, [S[l], eG], [S[l]])
                        tt("dve", S[l].t[:, :, :], S[l].t[:, :, :], bs.t[0:64, :].rearrange("p (h v) -> p h v", h=4), ALU.add, [S[l], bs], [S[l]])
                        cp("pool", Sb[l].t[:, :, :], S[l].t[:, :, :], [S[l]], [Sb[l]])
                        give(bs)
                    else:
                        obk = yield from take(4)
                        for h in range(4):
                            mm(obk[h].t[:, 0:128], atm.t[:, h, :], gvb.t[:, j, h * 128:(h + 1) * 128], True, False, [atm, gvb], [obk[h]])
                        eg4 = eG.t[:, :, :].rearrange("p h (s t) -> p h s t", t=8)
                        for qd in range(4):
                            P.load("pool", s0q, s0q.t[:, :, :, :], st_d[l, 4 * qd:4 * qd + 4].rearrange("s h k v -> k s h v"))
                            cp("pool", s0b.t[:, :, :, :], s0q.t[:, :, :, :], [s0q], [s0b])
                            tt("dve", qx.t[:, :, :, :], qtil.t[:, :, :].unsqueeze(2).broadcast_to([64, 4, 4, 128]),
                               eq.t[:, 4 * qd:4 * qd + 4, :].unsqueeze(1).broadcast_to([64, 4, 4, 128]), ALU.mult, [qtil, eq], [qx])
                            tt("pool", khx.t[:, :, :], khat.t[:, :].unsqueeze(1).broadcast_to([128, 4, 256]),
                               e2.t[:, 4 * qd:4 * qd + 4].unsqueeze(2).broadcast_to([128, 4, 256]), ALU.mult, [khat, e2], [khx])
                            for h in range(4):
                                for s_ in range(4):
                                    last = (qd == 3 and s_ == 3)
                                    mm(obk[h].t[:, 0:128], qx.t[:, h, s_, :], s0b.t[:, s_, h, :], False, last, [qx, s0b], [obk[h]])
                            for s_ in range(4):
                                bs = (yield from take(1))[0]
                                for h in range(4):
                                    mm(bs.t[0:64, h * 128:(h + 1) * 128], khx.t[:, s_, h * 64:(h + 1) * 64], gvb.t[:, j, h * 128:(h + 1) * 128], True, True, [khx, gvb], [bs])
                                sa = 4 * qd + s_
                                tt("dve", s0q.t[:, s_, :, :], s0q.t[:, s_, :, :], eg4[:, :, sa, 7:8].broadcast_to([64, 4, 128]), ALU.mult, [s0q, eG], [s0q])
                                tt("dve", s0q.t[:, s_, :, :], s0q.t[:, s_, :, :], bs.t[0:64, :].rearrange("p (h v) -> p h v", h=4), ALU.add, [s0q, bs], [s0q])
                                give(bs)
                            P.store("pool", s0q, sst[l, 4 * qd:4 * qd + 4].rearrange("s h k v -> k s h v"), s0q.t[:, :, :, :])
                    yield
                    if samp:
                        for h in range(4):
                            cp("act" if h % 2 else "dve", o32.t[:, h, :], obk[h].t[:, 0:128], [obk[h]], [o32])
                        give(*obk)
                    else:
                        cp("act", o32.t[:, :, :], bo.t[:, :].rearrange("p (h v) -> p h v", h=4), [bo], [o32])
                        give(bo)
                    mset("dve", gst, gst.t[:, 0:4], 0.0)
                    yield
                    for h in range(4):
                        act(atm.t[:, h, :], o32.t[:, h, :], AF.Square, [o32], [atm, gst], accum_out=gst.t[:, h:h + 1])
                    act(gst.t[:, 4:8], gst.t[:, 0:4], AF.Ln, [gst], [gst], scale=1.0 / 128, bias=epsb.t[:, 0:1])
                    act(gst.t[:, 8:12], gst.t[:, 4:8], AF.Exp, [gst], [gst], scale=-0.5)
                    yield
                    tt("dve", o32.t[:, :, :], o32.t[:, :, :], gst.t[:, 8:12].unsqueeze(2).broadcast_to([128, 4, 128]), ALU.mult, [o32, gst], [o32])
                    yield
                    tt("pool", merged.t[:, j, 512:1024], o32.t[:, :, :].rearrange("p h v -> p (h v)"), sog.t[:, j, :], ALU.mult, [o32, sog], [merged])
                    if blk == NBP - 1 or samp:
                        tb = (yield from take(1))[0]
                        for g in range(2):
                            tr(tb.t[:, g * 64:(g + 1) * 64], kn32.t[:, g, c0:c1], identf.t[0:64, 0:64], [kn32, identf], [tb])
                        cp("dve", kt_out.t[:, :], tb.t[:, 0:128], [tb], [kt_out])
                        give(tb)
                        if samp:
                            for s_ in range(16):
                                P.store("pool", kt_out, sk[l, s_, 120:128, :], kt_out.t[8 * s_:8 * s_ + 8, :])
                                P.store("pool", v32, sv[l, s_, 120:128, :], v32.t[8 * s_:8 * s_ + 8, j, :])
                        else:
                            P.store("pool", kt_out, pk[l], kt_out.t[:, :])
                            P.store("pool", v32, pv[l], v32.t[:, j, :])
                            P.store("pool", S[l], pst[l].rearrange("h k v -> k h v"), S[l].t[:, :, :])

                freeb = list(pbF) + list(pbT) + list(pbA)

                def take(n):
                    spins = 0
                    while len(freeb) < n:
                        spins += 1
                        assert spins < 10000, "psum bank deadlock"
                        yield
                    return [freeb.pop(0) for _ in range(n)]

                def give(*bs_):
                    freeb.extend(bs_)

                lanes = [[(j, blk) for j, blk in enumerate(blks) if j % NLANE == ln] for ln in range(NLANE)]
                active = [[] for _ in range(NLANE)]
                while any(lanes) or any(active):
                    for ln in range(NLANE):
                        if not active[ln] and lanes[ln]:
                            j, blk = lanes[ln].pop(0)
                            active[ln] = [attn_chain(j, blk, 0, tsets[ln]), attn_chain(j, blk, 1, tsets[ln]), gla_chain(j, blk, tsets[ln])]
                        for gen in list(active[ln]):
                            try:
                                next(gen)
                            except StopIteration:
                                active[ln].remove(gen)
                if STOP == 7 and (gi, l) == STOPAT:
                    raise _Stop()
                if not samp:
                    cp("pool", kbuf[l].t[:, 0, :, :], kbuf[l].t[:, nb, :, :], [kbuf[l]], [kbuf[l]])
                    cp("pool", vaug[l].t[:, 0, :, 0:64], vaug[l].t[:, nb, :, 0:64], [vaug[l]], [vaug[l]])
                for j in range(nb):
                    for kc in range(8):
                        tr(pb16.t[:, kc * 128:(kc + 1) * 128], merged.t[:, j, kc * 128:(kc + 1) * 128], identb.t[:, :], [merged, identb], [pb16])
                    cp("act", hT.t[:, :, j * 128:(j + 1) * 128], pb16.t[:, :].rearrange("p (k t) -> p k t", k=8), [pb16], [hT])
                for hf in range(2):
                    Wc = need(wbase + 6 + hf)
                    Wv = Wc.t[:, :].rearrange("p (k w) -> p k w", k=8)
                    for j, blk in enumerate(blks):
                        bk = bankT()
                        for kc in range(8):
                            mm(bk.t[:, :], hT.t[:, kc, j * 128:(j + 1) * 128], Wv[:, kc, :], kc == 0, kc == 7, [Wc, hT], [bk])
                        stt("dve", x.t[:, j, hf * 512:(hf + 1) * 512], bk.t[:, :], valid.t[:, blk:blk + 1], x.t[:, j, hf * 512:(hf + 1) * 512], ALU.mult, ALU.add, [bk, valid, x], [x])

                if STOP == 8 and (gi, l) == STOPAT:
                    raise _Stop()
                for j in range(nb):
                    rmsnorm_T(j)
                for i in range(11):
                    Wc = need(wbase + 8 + i)
                    Wv = Wc.t[:, :].rearrange("p (k a w) -> p k a w", k=8, a=2)
                    for jj in range(2):
                        ft = 2 * i + jj
                        bg_ = bankF()
                        bu_ = bankA()
                        for kc in range(8):
                            mm(bg_.t[:, 0:T], Wv[:, kc, 0, jj * 128:(jj + 1) * 128], hT.t[:, kc, 0:T], kc == 0, kc == 7, [Wc, hT], [bg_])
                        for kc in range(8):
                            mm(bu_.t[:, 0:T], Wv[:, kc, 1, jj * 128:(jj + 1) * 128], hT.t[:, kc, 0:T], kc == 0, kc == 7, [Wc, hT], [bu_])
                        sg_ = sg[ft % 2]
                        act(sg_.t[:, 0:T], bg_.t[:, 0:T], AF.Silu, [bg_], [sg_])
                        tt("dve", uT.t[:, ft, 0:T], sg_.t[:, 0:T], bu_.t[:, 0:T], ALU.mult, [sg_, bu_], [uT])
                dbk = [pbT[0], pbT[1], pbA[0], pbA[1], pbF[0], pbF[1]]
                for ft in range(NFT):
                    Wc = need(wbase + 19 + ft // 4)
                    Wv = Wc.t[:, :].rearrange("p (k w) -> p k w", k=4)
                    for j in range(nb):
                        for hf in range(2):
                            bd = dbk[j * 2 + hf]
                            mm(bd.t[:, :], uT.t[:, ft, j * 128:(j + 1) * 128], Wv[:, ft % 4, hf * 512:(hf + 1) * 512], ft == 0, ft == NFT - 1, [Wc, uT], [bd])
                for j, blk in enumerate(blks):
                    for hf in range(2):
                        bd = dbk[j * 2 + hf]
                        stt("dve", x.t[:, j, hf * 512:(hf + 1) * 512], bd.t[:, :], valid.t[:, blk:blk + 1], x.t[:, j, hf * 512:(hf + 1) * 512], ALU.mult, ALU.add, [bd, valid, x], [x])
            for j, blk in enumerate(blks):
                if samp:
                    P.dma("pool", x, ys, x.t[:, j, :], reads=[x], out=True)
                elif blk >= 1:
                    P.dma("pool", x, yp[(blk - 1) * 128:blk * 128, :], x.t[:, j, :], reads=[x], out=True)
    except _Stop:
        pass
    P.finish()
    return nc


def _consts():
    bf = ml_dtypes.bfloat16
    j = np.arange(128)[:, None]
    i = np.arange(128)[None, :]
    own = (j <= i)
    prev = (j > i)
    own0 = own & (j >= 112)
    samp = (j // 8 == i // 8) & (j % 8 <= i % 8)
    cache = (np.arange(128)[:, None] > np.arange(8)[None, :])
    prev1 = prev & (j >= 112)
    masks = np.concatenate([own, prev, own0, samp, prev1, cache], axis=1).astype(np.float32).astype(bf)
    umat = np.concatenate([own.astype(np.float32), samp.astype(np.float32)], axis=1) * np.float32(-1.0 / 16.0)
    valid = np.ones((128, NB), np.float32)
    valid[0:112, 0] = 0.0
    t = np.arange(128)
    e2 = (t[:, None] // 8 == np.arange(16)[None, :]).astype(np.float32)
    eq = np.broadcast_to(e2.T[None, :, :], (64, 16, 128)).reshape(64, 16 * 128)
    return dict(identb=np.eye(128, dtype=np.float32).astype(bf), identf=np.eye(128, dtype=np.float32),
                masks=masks, umat=np.ascontiguousarray(umat.astype(np.float32)), valid=valid,
                eq=np.ascontiguousarray(eq).astype(bf), e2=e2.astype(bf))


_NC_CACHE = {}


def kernel(**inp):
    f = lambda a: np.ascontiguousarray(np.asarray(a, dtype=np.float32))
    x_prompt, x_sample = f(inp["x_prompt"]), f(inp["x_sample"])
    cache_k, cache_v, state_gla = f(inp["cache_k"]), f(inp["cache_v"]), f(inp["state_gla"])
    meta = f(inp["meta"])
    norm1, norm2, gla_norm = f(inp["norm1"]), f(inp["norm2"]), f(inp["gla_norm"])
    q_norm, k_norm, sinks = f(inp["q_norm"]), f(inp["k_norm"]), f(inp["sinks"])
    w_g2, b_g = f(inp["w_g2"]), f(inp["b_g"])
    common = dict(
        w_in=f(inp["w_in"]), w_o=f(inp["w_o"]), w_gate=f(inp["w_gate"]), w_up=f(inp["w_up"]), w_down=f(inp["w_down"]),
        g1=np.ascontiguousarray(norm1.reshape(NL, 8, 128).transpose(2, 0, 1).reshape(128, NL * 8)),
        g2=np.ascontiguousarray(norm2.reshape(NL, 8, 128).transpose(2, 0, 1).reshape(128, NL * 8)),
        gg=np.ascontiguousarray(gla_norm.T),
        qkg=np.ascontiguousarray(np.stack([q_norm[0], k_norm[0], q_norm[1], k_norm[1]], axis=1)),
        snk=np.ascontiguousarray(np.broadcast_to(sinks.reshape(1, NL * 8), (128, NL * 8))),
        wg2=np.ascontiguousarray(np.concatenate([w_g2.transpose(1, 0, 2).reshape(16, NL * 256), np.zeros((16, NL * 256), np.float32)], 0)),
        bg=np.ascontiguousarray(np.concatenate([b_g.reshape(1, NL * 256), np.zeros((31, NL * 256), np.float32)], 0)),
    )
    common.update(_consts())
    in_maps = []
    for c in range(8):
        seq = c % 4
        xin = np.zeros((NB * 128, D), np.float32)
        xin[112:128] = meta
        xin[128:NBP * 128] = x_prompt[seq]
        xin[NBP * 128:] = x_sample[16 * c:16 * c + 16].reshape(128, D)
        m = dict(common)
        m["xin"] = xin
        m["ck"] = np.ascontiguousarray(cache_k[:, 16 * c:16 * c + 16].reshape(NL, 16, 128, 128))
        m["cv"] = np.ascontiguousarray(cache_v[:, 16 * c:16 * c + 16].reshape(NL, 16, 128, 128))
        m["st"] = np.ascontiguousarray(state_gla[:, 16 * c:16 * c + 16])
        in_maps.append(m)
    if "nc" not in _NC_CACHE:
        _NC_CACHE["nc"] = build()
    res = run_bass_kernel_spmd(_NC_CACHE["nc"], in_maps, core_ids=list(range(8)))
    R = res.results
    y_prompt = np.stack([R[c]["yp"] for c in range(4)], axis=0).astype(np.float32)
    y_sample = np.concatenate([R[c]["ys"].reshape(16, 8, D) for c in range(8)], axis=0).astype(np.float32)
    pk = np.stack([R[c]["pk"].reshape(NL, 128, 2, 64) for c in range(4)], axis=1).astype(np.float32)
    pv = np.stack([R[c]["pv"].reshape(NL, 128, 2, 64) for c in range(4)], axis=1).astype(np.float32)
    pst = np.stack([R[c]["pst"] for c in range(4)], axis=1).astype(np.float32)
    sk = np.concatenate([R[c]["sk"].reshape(NL, 16, 128, 2, 64) for c in range(8)], axis=1).astype(np.float32)
    sv = np.concatenate([R[c]["sv"].reshape(NL, 16, 128, 2, 64) for c in range(8)], axis=1).astype(np.float32)
    sst = np.concatenate([R[c]["sst"] for c in range(8)], axis=1).astype(np.float32)
    return (y_prompt, y_sample, pk, pv, pst, sk, sv, sst)
```

```python
import contextlib
import numpy as np
import ml_dtypes
import concourse.bass as bass
import concourse.mybir as mybir
from concourse.bass_utils import run_bass_kernel_spmd

F32 = mybir.dt.float32
BF16 = mybir.dt.bfloat16
AF = mybir.ActivationFunctionType
ALU = mybir.AluOpType
AX = mybir.AxisListType


class Buf:
    def __init__(self, name, t):
        self.name = name
        self.t = t
        self.lw = {}
        self.rd = {}
        self.dsem = None
        self.dcount = 0
        self.excl = False
        self.aliases = []


class Prog:
    ENGS = ("pe", "act", "dve", "pool", "sp")

    def __init__(self, nc):
        self.nc = nc
        self.stack = contextlib.ExitStack()
        self.sems = {}
        self.count = {e: 0 for e in self.ENGS}
        self.seen = {e: {} for e in self.ENGS}
        self.ops = {e: [] for e in self.ENGS}
        for e in self.ENGS:
            self.sems["E_" + e] = self.stack.enter_context(nc.semaphore("sem_" + e))
        self.out_tokens = {}
        self.nbufs = 0

    def sb(self, name, shape, dtype):
        t = self.stack.enter_context(self.nc.sbuf_tensor("s_" + name, list(shape), dtype))
        return Buf(name, t)

    def ps(self, name, shape, dtype):
        t = self.stack.enter_context(self.nc.psum_tensor(name, list(shape), dtype))
        b = Buf(name, t)
        b.excl = True
        return b

    def view(self, name, t):
        return Buf(name, t)

    def _dsem(self, buf, queue):
        if buf.dsem is None:
            buf.dsem = {}
            buf.dcount = {}
        if queue not in buf.dsem:
            key = "D_%d_%s_%s" % (self.nbufs, buf.name, queue)
            self.nbufs += 1
            self.sems[key] = self.stack.enter_context(self.nc.semaphore("dsem_%d" % self.nbufs))
            buf.dsem[queue] = key
            buf.dcount[queue] = 0
        return buf.dsem[queue]

    def _waits(self, eng, reads, writes, ignore_waw=False):
        need = {}
        for b in reads:
            for k, v in b.lw.items():
                if need.get(k, 0) < v:
                    need[k] = v
            if b.excl:
                for k, v in b.rd.items():
                    if k != "E_" + eng and need.get(k, 0) < v:
                        need[k] = v
        for b in writes:
            if not ignore_waw:
                for k, v in b.lw.items():
                    if need.get(k, 0) < v:
                        need[k] = v
            for k, v in b.rd.items():
                if need.get(k, 0) < v:
                    need[k] = v
            for al in b.aliases:
                for dd in (al.lw, al.rd):
                    for k, v in dd.items():
                        if need.get(k, 0) < v:
                            need[k] = v
        out = []
        seen = self.seen[eng]
        for k, v in need.items():
            if eng == "pe" and k == "E_pe":
                continue
            if seen.get(k, 0) < v:
                seen[k] = v
                out.append((k, v))
        return out

    def _commit(self, tok, reads, writes, ignore_waw=False):
        k, v = tok
        for b in reads:
            if b.rd.get(k, 0) < v:
                b.rd[k] = v
        for b in writes:
            if ignore_waw:
                b.lw[k] = v
            else:
                b.lw = {k: v}
            b.rd = {}

    def op(self, eng, fn, reads=(), writes=()):
        waits = self._waits(eng, reads, writes)
        self.count[eng] += 1
        tok = ("E_" + eng, self.count[eng])
        self._commit(tok, reads, writes)
        self.ops[eng].append((waits, fn, tok[0], 1))
        return tok

    def dma(self, queue, sem_buf, out_ap, in_ap, reads=(), writes=(), out=False, ignore_waw=False):
        waits = self._waits(queue, reads, writes, ignore_waw=ignore_waw)
        key = self._dsem(sem_buf, queue)
        sem_buf.dcount[queue] += 16
        tok = (key, sem_buf.dcount[queue])
        self._commit(tok, reads, writes, ignore_waw=ignore_waw)
        self.ops[queue].append((waits, (lambda e: e.dma_start(out=out_ap, in_=in_ap)), key, 16))
        if out:
            self.out_tokens[key] = sem_buf.dcount[queue]
        return tok

    def load(self, queue, buf, dst_ap, src_ap, ignore_waw=False, extra_reads=()):
        return self.dma(queue, buf, dst_ap, src_ap, reads=list(extra_reads), writes=[buf], ignore_waw=ignore_waw)

    def store(self, queue, buf, dst_ap, src_ap, out=True, extra_writes=()):
        return self.dma(queue, buf, dst_ap, src_ap, reads=[buf], writes=list(extra_writes), out=out)

    def finish(self):
        nc = self.nc
        handles = {}
        final_waits = list(self.out_tokens.items())
        with nc.Block() as block:
            def emit(e, name):
                for waits, fn, semkey, inc in self.ops[name]:
                    for k, v in waits:
                        e.wait_ge(self.sems[k], v)
                    fn(e).then_inc(self.sems[semkey], inc)
                if name == "sp":
                    for k, v in final_waits:
                        e.wait_ge(self.sems[k], v)

            @block.tensor
            def _(e):
                emit(e, "pe")

            @block.scalar
            def _(e):
                emit(e, "act")

            @block.vector
            def _(e):
                emit(e, "dve")

            @block.gpsimd
            def _(e):
                emit(e, "pool")

            @block.sync
            def _(e):
                emit(e, "sp")
        self.stack.close()


D = 1024
DF = 2816
NFT = 22
NBP = 33
NB = 34
SB = 33
NCH = 25
CH = 4096
EPS = 1e-6
NL = 2
CQ, CK, CV, CGQ, CGK, CGV, CGL, COG = 0, 512, 640, 768, 1024, 1280, 1792, 1808
import os
STOP = int(os.environ.get("MK_STOP", "99"))
SUB = int(os.environ.get("MK_SUB", "0"))
STOPAT = tuple(int(v) for v in os.environ.get("MK_STOPAT", "0,0").split(","))


class _Stop(Exception):
    pass


BLKS = [int(v) for v in os.environ["MK_BLKS"].split(",")] if os.environ.get("MK_BLKS") else list(range(NB))


GMAX = int(os.environ.get("MK_G", "3"))
NLANE = int(os.environ.get("MK_LANES", "2"))
if os.environ.get("MK_GROUPS"):
    GROUPS = [[int(v) for v in g.split(",")] for g in os.environ["MK_GROUPS"].split(";")]
else:
    GROUPS = [list(range(i, min(i + GMAX, NBP))) for i in range(0, NBP, GMAX)] + [[SB]]
TM = 128 * max(len(g) for g in GROUPS)
GM = TM // 128


PHASES = []


def build():
    nc = bass.Bass("TRN2", target_bir_lowering=False)
    P = Prog(nc)
    PHASES.clear()

    def phase(name):
        PHASES.append((name, P.count["pe"]))

    def din(name, shape, dtype=F32):
        return nc.dram_tensor(name, list(shape), dtype, kind="ExternalInput").ap()

    def dout(name, shape, dtype=F32):
        return nc.dram_tensor(name, list(shape), dtype, kind="ExternalOutput").ap()

    xin = din("xin", [NB * 128, D])
    w_in = din("w_in", [NL, D, 2320])
    w_o = din("w_o", [NL, D, D])
    w_gate = din("w_gate", [NL, D, DF])
    w_up = din("w_up", [NL, D, DF])
    w_down = din("w_down", [NL, DF, D])
    g1_d = din("g1", [128, NL * 8])
    g2_d = din("g2", [128, NL * 8])
    gg_d = din("gg", [128, NL])
    qkg_d = din("qkg", [64, NL * 2])
    snk_d = din("snk", [128, NL * 8])
    wg2_d = din("wg2", [32, NL * 256])
    bg_d = din("bg", [32, NL * 256])
    ck_d = din("ck", [NL, 16, 128, 128])
    cv_d = din("cv", [NL, 16, 128, 128])
    st_d = din("st", [NL, 16, 4, 64, 128])
    identb_d = din("identb", [128, 128], BF16)
    identf_d = din("identf", [128, 128])
    masks_d = din("masks", [128, 5 * 128 + 8], BF16)
    umat_d = din("umat", [128, 256])
    valid_d = din("valid", [128, NB])
    eq_d = din("eq", [64, 16 * 128], BF16)
    e2_d = din("e2", [128, 16], BF16)

    yp = dout("yp", [(NBP - 1) * 128, D])
    ys = dout("ys", [128, D])
    pk = dout("pk", [NL, 128, 128])
    pv = dout("pv", [NL, 128, 128])
    pst = dout("pst", [NL, 4, 64, 128])
    sk = dout("sk", [NL, 16, 128, 128])
    sv = dout("sv", [NL, 16, 128, 128])
    sst = dout("sst", [NL, 16, 4, 64, 128])

    wsc_t = nc.dram_tensor("wsc", [NL, NCH, 128, CH], BF16, kind="ExternalOutput").ap()
    wsc = Buf("wsc", wsc_t)

    sb = P.sb
    identb = sb("identb", [128, 128], BF16)
    identf = sb("identf", [128, 128], F32)
    masks = sb("masks", [128, 5 * 128 + 8], BF16)
    umat = sb("umat", [128, 256], F32)
    valid = sb("valid", [128, NB], F32)
    eq = sb("eq", [64, 16, 128], BF16)
    e2 = sb("e2", [128, 16], BF16)
    g1 = sb("g1", [128, NL * 8], F32)
    g2 = sb("g2", [128, NL * 8], F32)
    gg = sb("gg", [128, NL], F32)
    qkg = sb("qkg", [64, NL * 2], F32)
    esink = sb("esink", [128, NL * 8], F32)
    wg2 = sb("wg2", [32, NL * 256], F32)
    bg = sb("bg", [32, NL * 256], F32)
    ones64 = sb("ones64", [64, 64], BF16)
    ones1 = sb("ones1", [32, 128], F32)
    epsb = sb("epsb", [128, 1], F32)
    epsb1 = sb("epsb1", [128, 1], F32)

    x = sb("x", [128, GM, D], F32)
    st4 = sb("st4", [128, 8], F32)
    hb = sb("hb", [128, D], BF16)
    hT = sb("hT", [128, 8, TM], BF16)
    qkraw = [sb("qkraw%d" % i, [64, TM], F32) for i in range(2)]
    qksq = [sb("qksq%d" % i, [64, TM], BF16) for i in range(2)]
    rqh = [sb("rqh%d" % i, [64, TM], F32) for i in range(2)]
    qn = sb("qn", [64, GM, 8, 128], BF16)
    kn32 = sb("kn32", [64, 2, TM], F32)
    kbuf = [sb("kbuf%d" % l, [64, GM + 1, 2, 128], BF16) for l in range(NL)]
    vaug = [sb("vaug%d" % l, [128, GM + 1, 2, 65], BF16) for l in range(NL)]
    gqraw = sb("gqraw", [64, 4, TM], BF16)
    gkraw = sb("gkraw", [64, 4, TM], BF16)
    glowT = sb("glowT", [32, TM], F32)
    v32 = sb("v32", [128, GM, 128], F32)
    gvb = sb("gvb", [128, GM, 512], BF16)
    sog = sb("sog", [128, GM, 512], BF16)
    UW = NFT * TM // 2
    A2W = UW + 2 * TM
    arena2 = P.stack.enter_context(nc.sbuf_tensor("s_arena2", [128, A2W], F32))

    def carve2(name, off, words, dtype, pat=None, parts=128, **kw):
        v = arena2[0:parts, off:off + words]
        if dtype is BF16:
            v = v.bitcast(BF16)
        if pat is not None:
            v = v.rearrange(pat, **kw)
        return Buf(name, v)

    uT = carve2("uT", 0, UW, BF16, "p (f t) -> p f t", f=NFT)
    sg = [carve2("sg%d" % i, UW + i * TM, TM, F32) for i in range(2)]
    tsets = []
    for i_ in range(NLANE):
        n_ = lambda s_: "%s_%d" % (s_, i_)
        if i_ == 0:
            tsets.append(dict(
                lnt=sb(n_("lnt"), [128, 256], F32), eG=sb(n_("eG"), [64, 4, 128], F32), enG=sb(n_("enG"), [64, 4, 128], F32),
                qtil=sb(n_("qtil"), [64, 4, 128], BF16), kt32=sb(n_("kt32"), [64, 4, 128], F32), ktil=sb(n_("ktil"), [64, 4, 128], BF16),
                khatT=sb(n_("khatT"), [64, 4, 128], BF16), khat=sb(n_("khat"), [128, 256], BF16), atm=sb(n_("atm"), [128, 4, 128], BF16),
                o32=sb(n_("o32"), [128, 4, 128], F32), gst=sb(n_("gst"), [128, 16], F32),
                PT=[sb(n_("PTa"), [128, 2, 512], BF16), sb(n_("PTb"), [128, 2, 512], BF16)],
                den=[sb(n_("dena"), [128, 8], F32), sb(n_("denb"), [128, 8], F32)]))
        else:
            assert i_ == 1
            o_ = [0]

            def c2(nm, words, dtype, pat=None, parts=128, **kw):
                b = carve2(n_(nm), o_[0], words, dtype, pat, parts, **kw)
                o_[0] += words
                return b
            ts1 = dict(
                lnt=c2("lnt", 256, F32), eG=c2("eG", 512, F32, "p (h t) -> p h t", parts=64, h=4), enG=c2("enG", 512, F32, "p (h t) -> p h t", parts=64, h=4),
                qtil=c2("qtil", 256, BF16, "p (h t) -> p h t", parts=64, h=4), kt32=c2("kt32", 512, F32, "p (h t) -> p h t", parts=64, h=4),
                ktil=c2("ktil", 256, BF16, "p (h t) -> p h t", parts=64, h=4), khatT=c2("khatT", 256, BF16, "p (h t) -> p h t", parts=64, h=4),
                khat=c2("khat", 128, BF16), atm=c2("atm", 256, BF16, "p (h t) -> p h t", h=4), o32=c2("o32", 512, F32, "p (h t) -> p h t", h=4),
                gst=c2("gst", 16, F32),
                PT=[c2("PTa", 512, BF16, "p (a t) -> p a t", a=2), c2("PTb", 512, BF16, "p (a t) -> p a t", a=2)],
                den=[c2("dena", 8, F32), c2("denb", 8, F32)])
            assert o_[0] <= A2W, (o_[0], A2W)
            tsets.append(ts1)
            flat = [v for v in ts1.values() if isinstance(v, Buf)] + ts1["PT"] + ts1["den"]
            for a_ in flat:
                for b_ in [uT] + sg:
                    a_.aliases.append(b_)
                    b_.aliases.append(a_)
    S = [sb("S%d" % l, [64, 4, 128], F32) for l in range(NL)]
    Sb = [sb("Sb%d" % l, [64, 4, 128], BF16) for l in range(NL)]
    merged = sb("merged", [128, GM, D], BF16)
    kt_out = sb("kt_out", [128, 128], F32)
    ring = [sb("ring%d" % i, [128, CH], BF16) for i in range(4)]

    AW = 12832
    arena = P.stack.enter_context(nc.sbuf_tensor("s_arena", [128, AW], F32))

    def carve(name, off, words, dtype, pat=None, parts=128, **kw):
        v = arena[0:parts, off:off + words]
        if dtype is BF16:
            v = v.bitcast(BF16)
        if pat is not None:
            v = v.rearrange(pat, **kw)
        return Buf(name, v)

    cst = carve("cst", 0, 2048, F32, "p (s c) -> p s c", s=16)
    ckb = carve("ckb", 2048, 1024, BF16, "p (s c) -> p s c", s=16)
    vc = carve("vc", 3072, 1040, BF16, "p (s g e) -> p s g e", s=16, g=2)
    ptc = carve("ptc", 4112, 512, BF16)
    khx = carve("khx", 4624, 512, BF16, "p (s c) -> p s c", s=4)
    kcT = carve("kcT", 5136, 2048, BF16, "p (s g t) -> p s g t", s=16, g=2, parts=64)
    otc = carve("otc", 7184, 1024, F32, "p (g h t) -> p g h t", g=2, h=4, parts=65)
    s0q = carve("s0q", 8208, 2048, F32, "p (s h v) -> p s h v", s=4, h=4, parts=64)
    s0b = carve("s0b", 10256, 1024, BF16, "p (s h v) -> p s h v", s=4, h=4, parts=64)
    qx = carve("qx", 11280, 1024, BF16, "p (h s t) -> p h s t", h=4, s=4, parts=64)
    qns = carve("qns", 12304, 512, BF16, "p (s g c) -> p s g c", s=16, g=2, parts=64)
    samp_bufs = [cst, ckb, vc, ptc, khx, kcT, otc, s0q, s0b, qx, qns]
    stg = [carve("stg%d" % i, 2048 * i, 2048, F32, "p (k w) -> p k w", k=8) for i in range(3)]
    cht = [carve("cht%d" % i, 6144 + 2048 * i, 2048, BF16) for i in range(3)]
    for a_ in stg + cht:
        for b_ in samp_bufs:
            a_.aliases.append(b_)
            b_.aliases.append(a_)

    pbF = [P.ps("pbF%d" % i, [128, 512], F32) for i in range(3)]
    pbT = [P.ps("pbT%d" % i, [128, 512], F32) for i in range(2)]
    pbA = [P.ps("pbA%d" % i, [128, 512], F32) for i in range(2)]
    pb16 = P.ps("pb16", [128, 1024], BF16)
    rr = {"F": 0, "T": 0, "A": 0}

    def bankF():
        rr["F"] += 1
        return pbF[rr["F"] % 3]

    def bankT():
        rr["T"] += 1
        return pbT[rr["T"] % 2]

    def bankA():
        rr["A"] += 1
        return pbA[rr["A"] % 2]

    def mm(out, lhsT, rhs, start, stop, r, w):
        P.op("pe", lambda e: e.matmul(out, lhsT, rhs, start=start, stop=stop), reads=r, writes=w)

    def tr(out, in_, ident, r, w):
        P.op("pe", lambda e: e.transpose(out, in_, ident), reads=r, writes=w)

    def act(out, in_, func, r, w, **kw):
        P.op("act", lambda e: e.activation(out, in_, func, **kw), reads=r, writes=w)

    def cp(eng, out, in_, r, w):
        if eng == "act":
            act(out, in_, AF.Copy, r, w)
        else:
            P.op(eng, lambda e: e.tensor_copy(out, in_), reads=r, writes=w)

    def tt(eng, out, a, b, op, r, w):
        P.op(eng, lambda e: e.tensor_tensor(out, a, b, op), reads=r, writes=w)

    def tsc(eng, out, a, s1, s2, op0, op1, r, w):
        if s2 is None:
            P.op(eng, lambda e: e.tensor_scalar(out, a, s1, None, op0), reads=r, writes=w)
        else:
            P.op(eng, lambda e: e.tensor_scalar(out, a, s1, s2, op0, op1), reads=r, writes=w)

    def stt(eng, out, a, s, b, op0, op1, r, w):
        P.op(eng, lambda e: e.scalar_tensor_tensor(out, a, s, b, op0, op1), reads=r, writes=w)

    def mset(eng, buf, ap, val):
        P.op(eng, lambda e: e.memset(ap, val), writes=[buf])

    cld = Buf("cld", None)
    cbufs = []
    for b_, d_ in ((identb, identb_d), (identf, identf_d), (masks, masks_d), (umat, umat_d), (valid, valid_d),
                   (e2, e2_d), (g1, g1_d), (g2, g2_d), (gg, gg_d), (qkg, qkg_d), (esink, snk_d), (wg2, wg2_d), (bg, bg_d)):
        P.dma("pool", cld, b_.t[:, :], d_, writes=[b_])
        cbufs.append(b_)
    P.dma("pool", cld, eq.t[:, :, :], eq_d.rearrange("p (s t) -> p s t", s=16), writes=[eq])
    cbufs.append(eq)
    for b_ in cbufs:
        b_.lw = {cld.dsem["pool"]: cld.dcount["pool"]}
    act(esink.t[:, :], esink.t[:, :], AF.Exp, [esink], [esink])
    mset("dve", ones64, ones64.t[:, :], 1.0)
    mset("dve", ones1, ones1.t[:, :], 1.0)
    mset("dve", epsb, epsb.t[:, :], EPS)
    mset("dve", epsb1, epsb1.t[:, :], 1.0)
    for l in range(NL):
        mset("dve", S[l], S[l].t[:, :, :], 0.0)
        mset("dve", Sb[l], Sb[l].t[:, :, :], 0.0)
        mset("pool", kbuf[l], kbuf[l].t[:, :, :, :], 0.0)
        mset("pool", vaug[l], vaug[l].t[:, :, :, :], 0.0)
        mset("pool", vaug[l], vaug[l].t[:, :, :, 64:65], 1.0)
    m_own = masks.t[:, 0:128]
    m_prev = masks.t[:, 128:256]
    m_own0 = masks.t[:, 256:384]
    m_samp = masks.t[:, 384:512]
    m_prev1 = masks.t[:, 512:640]
    m_cache = masks.t[:, 640:648]
    pth = Buf("pth", None)

    prep_state = {"stg": 0, "cht": 0, "eng": 0}
    for ct_ in cht:
        mset("pool", ct_, ct_.t[:, :], 0.0)

    def kview(ct, W):
        return ct.t[:, 0:8 * W].rearrange("p (k w) -> p k w", k=8)

    def prep_piece(ct, dst, src, kc, wdt, gain):
        s_ = stg[prep_state["stg"] % 3]
        prep_state["stg"] += 1
        q_ = "act" if prep_state["stg"] % 2 else "sp"
        P.load(q_, s_, s_.t[:, 0:kc, 0:wdt], src)
        eng = ("dve", "pool")[prep_state["eng"] % 2]
        prep_state["eng"] += 1
        if gain is None:
            if prep_state["eng"] % 3 == 0:
                eng = "act"
            cp(eng, dst, s_.t[:, 0:kc, 0:wdt], [s_], [ct])
        else:
            gbuf, gap = gain
            tt(eng, dst, s_.t[:, 0:kc, 0:wdt], gap.unsqueeze(2).broadcast_to([128, kc, wdt]), ALU.mult, [s_, gbuf], [ct])

    prep_tasks = {}
    wsc_c = {(l_, c_): Buf("wsc_%d_%d" % (l_, c_), None) for l_ in range(NL) for c_ in range(NCH)}

    def prep_chunk(l, c, pieces, zero=None):
        def task(ct):
            if zero is not None:
                mset("dve", ct, kview(ct, zero[0])[:, :, zero[1]:zero[2]], 0.0)
            for (dst_fn, src, kc, wdt, gain) in pieces:
                prep_piece(ct, dst_fn(ct), src, kc, wdt, gain)
            for q4 in range(4):
                P.dma("sp", ct, wsc_t[l, c][:, q4 * 1024:(q4 + 1) * 1024], ct.t[:, q4 * 1024:(q4 + 1) * 1024], reads=[ct], writes=[wsc_c[(l, c)]], ignore_waw=True)
        prep_tasks[(l, c)] = task

    for l in range(NL):
        wi = w_in[l].rearrange("(k p) c -> p k c", p=128)
        wo_ = w_o[l].rearrange("(k p) c -> p k c", p=128)
        wg_ = w_gate[l].rearrange("(k p) c -> p k c", p=128)
        wu_ = w_up[l].rearrange("(k p) c -> p k c", p=128)
        wd_ = w_down[l].rearrange("(k p) c -> p k c", p=128)
        G1 = (g1, g1.t[:, l * 8:(l + 1) * 8])
        G2 = (g2, g2.t[:, l * 8:(l + 1) * 8])

        def cols(W, d0, s0, n, src, gain):
            out = []
            for o in range(0, n, 256):
                wdt = min(256, n - o)
                out.append(((lambda ct, W=W, a=d0 + o, wdt=wdt: kview(ct, W)[:, :, a:a + wdt]), src[:, :, s0 + o:s0 + o + wdt], 8, wdt, gain))
            return out

        prep_chunk(l, 0, cols(512, 0, CQ, 512, wi, G1))
        prep_chunk(l, 1, cols(384, 0, CK, 128, wi, G1) + cols(384, 128, CGQ, 256, wi, G1))
        prep_chunk(l, 2, cols(288, 0, CGK, 256, wi, G1) + cols(288, 256, CGL, 16, wi, G1), zero=(288, 272, 288))
        prep_chunk(l, 3, cols(128, 0, CV, 128, wi, G1))
        prep_chunk(l, 4, cols(512, 0, CGV, 512, wi, G1))
        prep_chunk(l, 5, cols(512, 0, COG, 512, wi, G1))
        for hf in range(2):
            pcs = []
            for o in range(0, 512, 256):
                a = hf * 512 + o
                pcs.append(((lambda ct, o=o: kview(ct, 512)[:, 0:4, o:o + 256]), wo_[:, 0:4, a:a + 256], 4, 256, None))
                pcs.append(((lambda ct, o=o: kview(ct, 512)[:, 4:8, o:o + 256]), wo_[:, 4:8, a:a + 256], 4, 256,
                            (gg, gg.t[:, l:l + 1].broadcast_to([128, 4]))))
            prep_chunk(l, 6 + hf, pcs)
        for i in range(11):
            prep_chunk(l, 8 + i, cols(512, 0, 256 * i, 256, wg_, G2) + cols(512, 256, 256 * i, 256, wu_, G2))
        for j in range(6):
            nft = min(4, NFT - 4 * j)
            pcs = []
            for o in range(0, 1024, 256):
                pcs.append(((lambda ct, o=o, nft=nft: ct.t[:, :].rearrange("p (k w) -> p k w", k=4)[:, 0:nft, o:o + 256]),
                            wd_[:, 4 * j:4 * j + nft, o:o + 256], nft, 256, None))
            prep_chunk(l, 19 + j, pcs)

    wseq = [(l, c) for _g in GROUPS for l in range(NL) for c in range(NCH)]
    wst = {"issued": 0}

    NPREP = NL * NCH

    def need(i):
        while wst["issued"] < min(len(wseq), i + 2 if i < NPREP else i + 3):
            j = wst["issued"]
            l_, c_ = wseq[j]
            if j < NPREP:
                prep_tasks[(l_, c_)](cht[j % 3])
            else:
                rb = ring[j % 4]
                for q4 in range(4):
                    P.dma("sp", rb, rb.t[:, q4 * 1024:(q4 + 1) * 1024], wsc_t[l_, c_][:, q4 * 1024:(q4 + 1) * 1024], reads=[wsc_c[(l_, c_)]], writes=[rb], ignore_waw=(q4 > 0))
            wst["issued"] += 1
        return cht[i % 3] if i < NPREP else ring[i % 4]

    def rmsnorm_T(j):
        mset("dve", st4, st4.t[:, 0:1], 0.0)
        act(hb.t[:, :], x.t[:, j, :], AF.Square, [x], [hb, st4], accum_out=st4.t[:, 0:1])
        act(st4.t[:, 2:3], st4.t[:, 0:1], AF.Ln, [st4], [st4], scale=1.0 / D, bias=epsb.t[:, 0:1])
        act(st4.t[:, 3:4], st4.t[:, 2:3], AF.Exp, [st4], [st4], scale=-0.5)
        tsc("dve", hb.t[:, :], x.t[:, j, :], st4.t[:, 3:4], None, ALU.mult, None, [x, st4], [hb])
        for kc in range(8):
            tr(pb16.t[:, kc * 128:(kc + 1) * 128], hb.t[:, kc * 128:(kc + 1) * 128], identb.t[:, :], [hb, identb], [pb16])
        cp("act", hT.t[:, :, j * 128:(j + 1) * 128], pb16.t[:, :].rearrange("p (k t) -> p k t", k=8), [pb16], [hT])

    try:
        for gi, blks in enumerate(GROUPS):
            nb = len(blks)
            T = nb * 128
            samp = (blks[0] == SB)
            for j, blk in enumerate(blks):
                P.load("pool", x, x.t[:, j, :], xin[blk * 128:(blk + 1) * 128, :], ignore_waw=(j > 0))
            for l in range(NL):
                wbase = (gi * NL + l) * NCH
                phase("norm1")
                for j in range(nb):
                    rmsnorm_T(j)
                if STOP == 2 and (gi, l) == STOPAT:
                    raise _Stop()
                phase("qk")
                W0 = need(wbase + 0)
                W0v = W0.t[:, :].rearrange("p (k w) -> p k w", k=8)
                W1 = need(wbase + 1)
                W1v = W1.t[:, 0:8 * 384].rearrange("p (k w) -> p k w", k=8)
                pend = []

                def qk_tail(h, sl):
                    bk2 = bankF()
                    mm(bk2.t[0:64, 0:T], ones64.t[:, :], qksq[sl].t[:, 0:T], True, True, [ones64, qksq[sl]], [bk2])
                    act(rqh[sl].t[:, 0:T], bk2.t[0:64, 0:T], AF.Ln, [bk2], [rqh[sl]], scale=1.0 / 64, bias=epsb.t[0:64, 0:1])
                    act(rqh[sl].t[:, 0:T], rqh[sl].t[:, 0:T], AF.Exp, [rqh[sl]], [rqh[sl]], scale=-0.5)
                    if h < 8:
                        stt("dve", qn.t[:, 0:nb, h, :], qkraw[sl].t[:, 0:T].rearrange("p (j t) -> p j t", t=128), qkg.t[:, 2 * l:2 * l + 1],
                            rqh[sl].t[:, 0:T].rearrange("p (j t) -> p j t", t=128), ALU.mult, ALU.mult, [qkraw[sl], qkg, rqh[sl]], [qn])
                    else:
                        stt("dve", kn32.t[:, h - 8, 0:T], qkraw[sl].t[:, 0:T], qkg.t[:, 2 * l + 1:2 * l + 2], rqh[sl].t[:, 0:T],
                            ALU.mult, ALU.mult, [qkraw[sl], qkg, rqh[sl]], [kn32])

                for h in range(10):
                    sl = h % 2
                    bk = bankF()
                    for kc in range(8):
                        lw = W0v[:, kc, h * 64:(h + 1) * 64] if h < 8 else W1v[:, kc, (h - 8) * 64:(h - 7) * 64]
                        mm(bk.t[0:64, 0:T], lw, hT.t[:, kc, 0:T], kc == 0, kc == 7, [W0 if h < 8 else W1, hT], [bk])
                    cp("dve", qkraw[sl].t[:, 0:T], bk.t[0:64, 0:T], [bk], [qkraw[sl]])
                    act(qksq[sl].t[:, 0:T], bk.t[0:64, 0:T], AF.Square, [bk], [qksq[sl]])
                    if pend:
                        pend.pop()()
                    pend.append(lambda h=h, sl=sl: qk_tail(h, sl))
                pend.pop()()
                for j, blk in enumerate(blks):
                    cp("pool", kbuf[l].t[:, j + 1, :, :], kn32.t[:, :, j * 128:(j + 1) * 128], [kn32], [kbuf[l]])
                phase("gqgk")
                W2 = need(wbase + 2)
                W2v = W2.t[:, 0:8 * 288].rearrange("p (k w) -> p k w", k=8)
                for h in range(4):
                    bk = bankF()
                    for kc in range(8):
                        mm(bk.t[0:64, 0:T], W1v[:, kc, 128 + h * 64:128 + (h + 1) * 64], hT.t[:, kc, 0:T], kc == 0, kc == 7, [W1, hT], [bk])
                    cp("act", gqraw.t[:, h, 0:T], bk.t[0:64, 0:T], [bk], [gqraw])
                for h in range(4):
                    bk = bankF()
                    for kc in range(8):
                        mm(bk.t[0:64, 0:T], W2v[:, kc, h * 64:(h + 1) * 64], hT.t[:, kc, 0:T], kc == 0, kc == 7, [W2, hT], [bk])
                    cp("dve", gkraw.t[:, h, 0:T], bk.t[0:64, 0:T], [bk], [gkraw])
                bk = bankF()
                for kc in range(8):
                    mm(bk.t[0:32, 0:T], W2v[:, kc, 256:288], hT.t[:, kc, 0:T], kc == 0, kc == 7, [W2, hT], [bk])
                cp("act", glowT.t[:, 0:T], bk.t[0:32, 0:T], [bk], [glowT])
                if STOP == 4 and (gi, l) == STOPAT:
                    raise _Stop()
                phase("tokmaj")
                W3 = need(wbase + 3)
                W3v = W3.t[:, 0:8 * 128].rearrange("p (k w) -> p k w", k=8)
                for j, blk in enumerate(blks):
                    bk = bankT()
                    for kc in range(8):
                        mm(bk.t[:, 0:128], hT.t[:, kc, j * 128:(j + 1) * 128], W3v[:, kc, :], kc == 0, kc == 7, [W3, hT], [bk])
                    cp("act", v32.t[:, j, :], bk.t[:, 0:128], [bk], [v32])
                    cp("dve", vaug[l].t[:, j + 1, :, 0:64], bk.t[:, 0:128].rearrange("p (g d) -> p g d", g=2), [bk], [vaug[l]])
                    if j + 1 < nb:
                        pass
                W4 = need(wbase + 4)
                W4v = W4.t[:, :].rearrange("p (k w) -> p k w", k=8)
                for j in range(nb):
                    bk = bankT()
                    for kc in range(8):
                        mm(bk.t[:, :], hT.t[:, kc, j * 128:(j + 1) * 128], W4v[:, kc, :], kc == 0, kc == 7, [W4, hT], [bk])
                    cp("act", gvb.t[:, j, :], bk.t[:, :], [bk], [gvb])
                W5 = need(wbase + 5)
                W5v = W5.t[:, :].rearrange("p (k w) -> p k w", k=8)
                for j in range(nb):
                    bk = bankT()
                    for kc in range(8):
                        mm(bk.t[:, :], hT.t[:, kc, j * 128:(j + 1) * 128], W5v[:, kc, :], kc == 0, kc == 7, [W5, hT], [bk])
                    act(sog.t[:, j, :], bk.t[:, :], AF.Silu, [bk], [sog])

                if STOP == 5 and (gi, l) == STOPAT:
                    raise _Stop()
                phase("chains")
                if samp:
                    if l == 0:
                        mset("pool", vc, vc.t[:, :, :, 64:65], 1.0)
                    P.load("pool", cst, cst.t[:, :, :], ck_d[l].rearrange("s k c -> k s c"))
                    cp("pool", ckb.t[:, :, :], cst.t[:, :, :], [cst], [ckb])
                    for rnd in range(4):
                        for i in range(8):
                            s_, g_ = (rnd * 8 + i) // 2, (rnd * 8 + i) % 2
                            tr(pb16.t[0:64, i * 128:(i + 1) * 128], ckb.t[:, s_, g_ * 64:(g_ + 1) * 64], identb.t[:, :], [ckb, identb], [pb16])
                        cp("dve", kcT.t[:, rnd * 4:rnd * 4 + 4, :, :], pb16.t[0:64, :].rearrange("p (s g t) -> p s g t", s=4, g=2), [pb16], [kcT])
                    P.dma("pool", pth, sk[l, :, 0:120, :], ck_d[l, :, 8:128, :], reads=[], writes=[], out=True)
                    P.load("pool", cst, cst.t[:, :, :], cv_d[l].rearrange("s k c -> k s c"))
                    cp("pool", vc.t[:, :, :, 0:64], cst.t[:, :, :].rearrange("p s (g d) -> p s g d", g=2), [cst], [vc])
                    P.dma("pool", pth, sv[l, :, 0:120, :], cv_d[l, :, 8:128, :], reads=[], writes=[], out=True)
                    for g_ in range(2):
                        cp("pool", qns.t[:, :, g_, :].rearrange("p s (h t) -> p h s t", h=4), qn.t[:, 0, 4 * g_:4 * g_ + 4, :].rearrange("p h (s t) -> p h s t", t=8), [qn], [qns])
                    stc = [bankA(), bankA()]
                    for s_ in range(16):
                        for g_ in range(2):
                            bk = stc[s_ // 8]
                            o_ = ((s_ % 8) * 2 + g_) * 32
                            mm(bk.t[:, o_:o_ + 32], kcT.t[:, s_, g_, :], qns.t[:, s_, g_, :], True, True, [kcT, qns], [bk])
                    for hf in range(2):
                        act(ptc.t[:, hf * 512:(hf + 1) * 512], stc[hf].t[:, :], AF.Exp, [stc[hf]], [ptc], scale=0.125)
                    tt("pool", ptc.t[:, :].rearrange("p (a q) -> p a q", q=8), ptc.t[:, :].rearrange("p (a q) -> p a q", q=8),
                       m_cache.unsqueeze(1).broadcast_to([128, 128, 8]), ALU.mult, [ptc, masks], [ptc])
                    otb = [bankA(), bankA()]
                    for s_ in range(16):
                        for g_ in range(2):
                            bk = otb[s_ // 8]
                            o_ = ((s_ % 8) * 2 + g_) * 32
                            mm(bk.t[0:65, o_:o_ + 32], vc.t[:, s_, g_, :], ptc.t[:, (s_ * 2 + g_) * 32:(s_ * 2 + g_ + 1) * 32], True, True, [vc, ptc], [bk])
                    for hf in range(2):
                        for g_ in range(2):
                            cp("act" if g_ else "dve", otc.t[:, g_, :, hf * 64:(hf + 1) * 64].rearrange("p h (s t) -> p s h t", t=8),
                               otb[hf].t[0:65, :].rearrange("p (s g h t) -> p s g h t", s=8, g=2, h=4)[:, :, g_, :, :], [otb[hf]], [otc])

                def attn_chain(j, blk, g, ts):
                    pt = ts["PT"][g]
                    den = ts["den"][g]
                    rq_ = qn.t[:, j, 4 * g:4 * g + 4, :].rearrange("p h t -> p (h t)")
                    has_prev = (not samp) and blk > 0
                    sbk = yield from take(2 if has_prev else 1)
                    bo = sbk[0]
                    mm(bo.t[:, :], kbuf[l].t[:, j + 1, g, :], rq_, True, True, [kbuf[l], qn], [bo])
                    if has_prev:
                        bp = sbk[1]
                        mm(bp.t[:, :], kbuf[l].t[:, j, g, :], rq_, True, True, [kbuf[l], qn], [bp])
                    yield
                    act(pt.t[:, 0, :], bo.t[:, :], AF.Exp, [bo], [pt], scale=0.125)
                    if has_prev:
                        act(pt.t[:, 1, :], bp.t[:, :], AF.Exp, [bp], [pt], scale=0.125)
                    give(*sbk)
                    yield
                    mk = m_samp if samp else (m_own0 if blk == 0 else m_own)
                    tt("pool", pt.t[:, 0, :].rearrange("p (h t) -> p h t", h=4), pt.t[:, 0, :].rearrange("p (h t) -> p h t", h=4),
                       mk.unsqueeze(1).broadcast_to([128, 4, 128]), ALU.mult, [pt, masks], [pt])
                    if has_prev:
                        tt("pool", pt.t[:, 1, :].rearrange("p (h t) -> p h t", h=4), pt.t[:, 1, :].rearrange("p (h t) -> p h t", h=4),
                           (m_prev1 if blk == 1 else m_prev).unsqueeze(1).broadcast_to([128, 4, 128]), ALU.mult, [pt, masks], [pt])
                    yield
                    bv_ = (yield from take(1))[0]
                    for h in range(4):
                        oc = bv_.t[:, h * 65:(h + 1) * 65]
                        last_own = not (has_prev or samp)
                        mm(oc, pt.t[:, 0, h * 128:(h + 1) * 128], vaug[l].t[:, j + 1, g, :], True, last_own, [pt, vaug[l]], [bv_])
                        if has_prev:
                            mm(oc, pt.t[:, 1, h * 128:(h + 1) * 128], vaug[l].t[:, j, g, :], False, True, [pt, vaug[l]], [bv_])
                        if samp:
                            mm(oc, otc.t[:, g, h, :], identf.t[0:65, 0:65], False, True, [otc, identf], [bv_])
                    yield
                    pv4 = bv_.t[:, 0:260].rearrange("p (h e) -> p h e", h=4)
                    tt("dve", den.t[:, 0:4], pv4[:, :, 64], esink.t[:, l * 8 + 4 * g:l * 8 + 4 * g + 4], ALU.add, [bv_, esink], [den])
                    P.op("dve", lambda e: e.reciprocal(den.t[:, 4:8], den.t[:, 0:4]), reads=[den], writes=[den])
                    tt("dve", merged.t[:, j, g * 256:(g + 1) * 256].rearrange("p (h d) -> p h d", h=4), pv4[:, :, 0:64],
                       den.t[:, 4:8].unsqueeze(2).broadcast_to([128, 4, 64]), ALU.mult, [bv_, den], [merged])
                    give(bv_)

                def gla_chain(j, blk, ts):
                    lnt, eG, enG, qtil, kt32, ktil = ts["lnt"], ts["eG"], ts["enG"], ts["qtil"], ts["kt32"], ts["ktil"]
                    khatT, khat, atm, o32, gst = ts["khatT"], ts["khat"], ts["atm"], ts["o32"], ts["gst"]
                    c0, c1 = j * 128, (j + 1) * 128
                    bl = (yield from take(1))[0]
                    mm(bl.t[:, 0:256], glowT.t[:, c0:c1], wg2.t[:, l * 256:(l + 1) * 256], True, False, [glowT, wg2], [bl])
                    mm(bl.t[:, 0:256], ones1.t[:, :], bg.t[:, l * 256:(l + 1) * 256], False, True, [ones1, bg], [bl])
                    yield
                    act(lnt.t[:, :], bl.t[:, 0:256], AF.Exp, [bl], [lnt], scale=-1.0)
                    give(bl)
                    act(lnt.t[:, :], lnt.t[:, :], AF.Ln, [lnt], [lnt], bias=epsb1.t[:, 0:1])
                    yield
                    U = umat.t[:, 128:256] if samp else umat.t[:, 0:128]
                    gmask = m_samp if samp else m_own
                    bgT = (yield from take(1))[0]
                    for h in range(4):
                        mm(bgT.t[0:64, h * 128:(h + 1) * 128], lnt.t[:, h * 64:(h + 1) * 64], U, True, True, [lnt, umat], [bgT])
                    yield
                    g4 = bgT.t[0:64, :].rearrange("p (h t) -> p h t", h=4)
                    act(eG.t[:, :, :], g4, AF.Exp, [bgT], [eG])
                    act(enG.t[:, :, :], g4, AF.Exp, [bgT], [enG], scale=-1.0)
                    give(bgT)
                    yield
                    stt("dve", qtil.t[:, :, :], gqraw.t[:, :, c0:c1], 0.125, eG.t[:, :, :], ALU.mult, ALU.mult, [gqraw, eG], [qtil])
                    tt("dve", kt32.t[:, :, :], gkraw.t[:, :, c0:c1], enG.t[:, :, :], ALU.mult, [gkraw, enG], [kt32])
                    yield
                    cp("pool", ktil.t[:, :, :], kt32.t[:, :, :], [kt32], [ktil])
                    if samp:
                        egl = eG.t[:, :, :].rearrange("p h (s t) -> p h s t", t=8)[:, :, :, 7:8].broadcast_to([64, 4, 16, 8])
                        tt("pool", khatT.t[:, :, :].rearrange("p h (s t) -> p h s t", t=8), kt32.t[:, :, :].rearrange("p h (s t) -> p h s t", t=8),
                           egl, ALU.mult, [kt32, eG], [khatT])
                    else:
                        egl = eG.t[:, :, 127:128].broadcast_to([64, 4, 128])
                        tt("pool", khatT.t[:, :, :], kt32.t[:, :, :], egl, ALU.mult, [kt32, eG], [khatT])
                    yield
                    for h in range(4):
                        tr(pb16.t[:, h * 64:(h + 1) * 64], khatT.t[:, h, :], identb.t[0:64, 0:64], [khatT, identb], [pb16])
                    cp("dve", khat.t[:, :], pb16.t[:, 0:256], [pb16], [khat])
                    ba = (yield from take(1))[0]
                    for h in range(4):
                        mm(ba.t[:, h * 128:(h + 1) * 128], ktil.t[:, h, :], qtil.t[:, h, :], True, True, [ktil, qtil], [ba])
                    yield
                    tt("dve", atm.t[:, :, :], ba.t[:, :].rearrange("p (h t) -> p h t", h=4), gmask.unsqueeze(1).broadcast_to([128, 4, 128]), ALU.mult, [ba, masks], [atm])
                    give(ba)
                    yield
                    if not samp:
                        bo, bs = yield from take(2)
                        for h in range(4):
                            oc = bo.t[:, h * 128:(h + 1) * 128]
                            mm(oc, atm.t[:, h, :], gvb.t[:, j, h * 128:(h + 1) * 128], True, False, [atm, gvb], [bo])
                            mm(oc, qtil.t[:, h, :], Sb[l].t[:, h, :], False, True, [qtil, Sb[l]], [bo])
                        for h in range(4):
                            mm(bs.t[0:64, h * 128:(h + 1) * 128], khat.t[:, h * 64:(h + 1) * 64], gvb.t[:, j, h * 128:(h + 1) * 128], True, True, [khat, gvb], [bs])
                        tt("dve", S[l].t[:, :, :], S[l].t[:, :, :], eG.t[:, :, 127:128].broadcast_to([64, 4, 128]), ALU.mult, [S[l], eG], [S[l]])
                        tt("dve", S[l].t[:, :, :], S[l].t[:, :, :], bs.t[0:64, :].rearrange("p (h v) -> p h v", h=4), ALU.add, [S[l], bs], [S[l]])
                        cp("pool", Sb[l].t[:, :, :], S[l].t[:, :, :], [S[l]], [Sb[l]])
                        give(bs)
                    else:
                        obk = yield from take(4)
                        for h in range(4):
                            mm(obk[h].t[:, 0:128], atm.t[:, h, :], gvb.t[:, j, h * 128:(h + 1) * 128], True, False, [atm, gvb], [obk[h]])
                        eg4 = eG.t[:, :, :].rearrange("p h (s t) -> p h s t", t=8)
                        for qd in range(4):
                            P.load("pool", s0q, s0q.t[:, :, :, :], st_d[l, 4 * qd:4 * qd + 4].rearrange("s h k v -> k s h v"))
                            cp("pool", s0b.t[:, :, :, :], s0q.t[:, :, :, :], [s0q], [s0b])
                            tt("dve", qx.t[:, :, :, :], qtil.t[:, :, :].unsqueeze(2).broadcast_to([64, 4, 4, 128]),
                               eq.t[:, 4 * qd:4 * qd + 4, :].unsqueeze(1).broadcast_to([64, 4, 4, 128]), ALU.mult, [qtil, eq], [qx])
                            tt("pool", khx.t[:, :, :], khat.t[:, :].unsqueeze(1).broadcast_to([128, 4, 256]),
                               e2.t[:, 4 * qd:4 * qd + 4].unsqueeze(2).broadcast_to([128, 4, 256]), ALU.mult, [khat, e2], [khx])
                            for h in range(4):
                                for s_ in range(4):
                                    last = (qd == 3 and s_ == 3)
                                    mm(obk[h].t[:, 0:128], qx.t[:, h, s_, :], s0b.t[:, s_, h, :], False, last, [qx, s0b], [obk[h]])
                            for s_ in range(4):
                                bs = (yield from take(1))[0]
                                for h in range(4):
                                    mm(bs.t[0:64, h * 128:(h + 1) * 128], khx.t[:, s_, h * 64:(h + 1) * 64], gvb.t[:, j, h * 128:(h + 1) * 128], True, True, [khx, gvb], [bs])
                                sa = 4 * qd + s_
                                tt("dve", s0q.t[:, s_, :, :], s0q.t[:, s_, :, :], eg4[:, :, sa, 7:8].broadcast_to([64, 4, 128]), ALU.mult, [s0q, eG], [s0q])
                                tt("dve", s0q.t[:, s_, :, :], s0q.t[:, s_, :, :], bs.t[0:64, :].rearrange("p (h v) -> p h v", h=4), ALU.add, [s0q, bs], [s0q])
                                give(bs)
                            P.store("pool", s0q, sst[l, 4 * qd:4 * qd + 4].rearrange("s h k v -> k s h v"), s0q.t[:, :, :, :])
                    yield
                    if samp:
                        for h in range(4):
                            cp("act" if h % 2 else "dve", o32.t[:, h, :], obk[h].t[:, 0:128], [obk[h]], [o32])
                        give(*obk)
                    else:
                        cp("act", o32.t[:, :, :], bo.t[:, :].rearrange("p (h v) -> p h v", h=4), [bo], [o32])
                        give(bo)
                    mset("dve", gst, gst.t[:, 0:4], 0.0)
                    yield
                    for h in range(4):
                        act(atm.t[:, h, :], o32.t[:, h, :], AF.Square, [o32], [atm, gst], accum_out=gst.t[:, h:h + 1])
                    act(gst.t[:, 4:8], gst.t[:, 0:4], AF.Ln, [gst], [gst], scale=1.0 / 128, bias=epsb.t[:, 0:1])
                    act(gst.t[:, 8:12], gst.t[:, 4:8], AF.Exp, [gst], [gst], scale=-0.5)
                    yield
                    tt("dve", o32.t[:, :, :], o32.t[:, :, :], gst.t[:, 8:12].unsqueeze(2).broadcast_to([128, 4, 128]), ALU.mult, [o32, gst], [o32])
                    yield
                    tt("pool", merged.t[:, j, 512:1024], o32.t[:, :, :].rearrange("p h v -> p (h v)"), sog.t[:, j, :], ALU.mult, [o32, sog], [merged])
                    if blk == NBP - 1 or samp:
                        tb = (yield from take(1))[0]
                        for g in range(2):
                            tr(tb.t[:, g * 64:(g + 1) * 64], kn32.t[:, g, c0:c1], identf.t[0:64, 0:64], [kn32, identf], [tb])
                        cp("dve", kt_out.t[:, :], tb.t[:, 0:128], [tb], [kt_out])
                        give(tb)
                        if samp:
                            for s_ in range(16):
                                P.store("pool", kt_out, sk[l, s_, 120:128, :], kt_out.t[8 * s_:8 * s_ + 8, :])
                                P.store("pool", v32, sv[l, s_, 120:128, :], v32.t[8 * s_:8 * s_ + 8, j, :])
                        else:
                            P.store("pool", kt_out, pk[l], kt_out.t[:, :])
                            P.store("pool", v32, pv[l], v32.t[:, j, :])
                            P.store("pool", S[l], pst[l].rearrange("h k v -> k h v"), S[l].t[:, :, :])

                freeb = list(pbF) + list(pbT) + list(pbA)

                def take(n):
                    spins = 0
                    while len(freeb) < n:
                        spins += 1
                        assert spins < 10000, "psum bank deadlock"
                        yield
                    return [freeb.pop(0) for _ in range(n)]

                def give(*bs_):
                    freeb.extend(bs_)

                lanes = [[(j, blk) for j, blk in enumerate(blks) if j % NLANE == ln] for ln in range(NLANE)]
                active = [[] for _ in range(NLANE)]
                while any(lanes) or any(active):
                    for ln in range(NLANE):
                        if not active[ln] and lanes[ln]:
                            j, blk = lanes[ln].pop(0)
                            active[ln] = [attn_chain(j, blk, 0, tsets[ln]), attn_chain(j, blk, 1, tsets[ln]), gla_chain(j, blk, tsets[ln])]
                        for gen in list(active[ln]):
                            try:
                                next(gen)
                            except StopIteration:
                                active[ln].remove(gen)
                if STOP == 7 and (gi, l) == STOPAT:
                    raise _Stop()
                phase("wo")
                if not samp:
                    cp("pool", kbuf[l].t[:, 0, :, :], kbuf[l].t[:, nb, :, :], [kbuf[l]], [kbuf[l]])
                    cp("pool", vaug[l].t[:, 0, :, 0:64], vaug[l].t[:, nb, :, 0:64], [vaug[l]], [vaug[l]])
                for j in range(nb):
                    for kc in range(8):
                        tr(pb16.t[:, kc * 128:(kc + 1) * 128], merged.t[:, j, kc * 128:(kc + 1) * 128], identb.t[:, :], [merged, identb], [pb16])
                    cp("act", hT.t[:, :, j * 128:(j + 1) * 128], pb16.t[:, :].rearrange("p (k t) -> p k t", k=8), [pb16], [hT])
                for hf in range(2):
                    Wc = need(wbase + 6 + hf)
                    Wv = Wc.t[:, :].rearrange("p (k w) -> p k w", k=8)
                    for j, blk in enumerate(blks):
                        bk = bankT()
                        for kc in range(8):
                            mm(bk.t[:, :], hT.t[:, kc, j * 128:(j + 1) * 128], Wv[:, kc, :], kc == 0, kc == 7, [Wc, hT], [bk])
                        stt("dve", x.t[:, j, hf * 512:(hf + 1) * 512], bk.t[:, :], valid.t[:, blk:blk + 1], x.t[:, j, hf * 512:(hf + 1) * 512], ALU.mult, ALU.add, [bk, valid, x], [x])

                if STOP == 8 and (gi, l) == STOPAT:
                    raise _Stop()
                phase("ffn_gu")
                for j in range(nb):
                    rmsnorm_T(j)
                for i in range(11):
                    Wc = need(wbase + 8 + i)
                    Wv = Wc.t[:, :].rearrange("p (k a w) -> p k a w", k=8, a=2)
                    for jj in range(2):
                        ft = 2 * i + jj
                        bg_ = bankF()
                        bu_ = bankA()
                        for kc in range(8):
                            mm(bg_.t[:, 0:T], Wv[:, kc, 0, jj * 128:(jj + 1) * 128], hT.t[:, kc, 0:T], kc == 0, kc == 7, [Wc, hT], [bg_])
                        for kc in range(8):
                            mm(bu_.t[:, 0:T], Wv[:, kc, 1, jj * 128:(jj + 1) * 128], hT.t[:, kc, 0:T], kc == 0, kc == 7, [Wc, hT], [bu_])
                        sg_ = sg[ft % 2]
                        act(sg_.t[:, 0:T], bg_.t[:, 0:T], AF.Silu, [bg_], [sg_])
                        tt("dve", uT.t[:, ft, 0:T], sg_.t[:, 0:T], bu_.t[:, 0:T], ALU.mult, [sg_, bu_], [uT])
                phase("ffn_down")
                dbk = [pbT[0], pbT[1], pbA[0], pbA[1], pbF[0], pbF[1]]
                for ft in range(NFT):
                    Wc = need(wbase + 19 + ft // 4)
                    Wv = Wc.t[:, :].rearrange("p (k w) -> p k w", k=4)
                    for j in range(nb):
                        for hf in range(2):
                            bd = dbk[j * 2 + hf]
                            mm(bd.t[:, :], uT.t[:, ft, j * 128:(j + 1) * 128], Wv[:, ft % 4, hf * 512:(hf + 1) * 512], ft == 0, ft == NFT - 1, [Wc, uT], [bd])
                for j, blk in enumerate(blks):
                    for hf in range(2):
                        bd = dbk[j * 2 + hf]
                        stt("dve", x.t[:, j, hf * 512:(hf + 1) * 512], bd.t[:, :], valid.t[:, blk:blk + 1], x.t[:, j, hf * 512:(hf + 1) * 512], ALU.mult, ALU.add, [bd, valid, x], [x])
            for j, blk in enumerate(blks):
                if samp:
                    P.dma("pool", x, ys, x.t[:, j, :], reads=[x], out=True)
                elif blk >= 1:
                    P.dma("pool", x, yp[(blk - 1) * 128:blk * 128, :], x.t[:, j, :], reads=[x], out=True)
    except _Stop:
        pass
    P.finish()
    return nc


def _consts():
    bf = ml_dtypes.bfloat16
    j = np.arange(128)[:, None]
    i = np.arange(128)[None, :]
    own = (j <= i)
    prev = (j > i)
    own0 = own & (j >= 112)
    samp = (j // 8 == i // 8) & (j % 8 <= i % 8)
    cache = (np.arange(128)[:, None] > np.arange(8)[None, :])
    prev1 = prev & (j >= 112)
    masks = np.concatenate([own, prev, own0, samp, prev1, cache], axis=1).astype(np.float32).astype(bf)
    umat = np.concatenate([own.astype(np.float32), samp.astype(np.float32)], axis=1) * np.float32(-1.0 / 16.0)
    valid = np.ones((128, NB), np.float32)
    valid[0:112, 0] = 0.0
    t = np.arange(128)
    e2 = (t[:, None] // 8 == np.arange(16)[None, :]).astype(np.float32)
    eq = np.broadcast_to(e2.T[None, :, :], (64, 16, 128)).reshape(64, 16 * 128)
    return dict(identb=np.eye(128, dtype=np.float32).astype(bf), identf=np.eye(128, dtype=np.float32),
                masks=masks, umat=np.ascontiguousarray(umat.astype(np.float32)), valid=valid,
                eq=np.ascontiguousarray(eq).astype(bf), e2=e2.astype(bf))


_NC_CACHE = {}


def kernel(**inp):
    f = lambda a: np.ascontiguousarray(np.asarray(a, dtype=np.float32))
    x_prompt, x_sample = f(inp["x_prompt"]), f(inp["x_sample"])
    cache_k, cache_v, state_gla = f(inp["cache_k"]), f(inp["cache_v"]), f(inp["state_gla"])
    meta = f(inp["meta"])
    norm1, norm2, gla_norm = f(inp["norm1"]), f(inp["norm2"]), f(inp["gla_norm"])
    q_norm, k_norm, sinks = f(inp["q_norm"]), f(inp["k_norm"]), f(inp["sinks"])
    w_g2, b_g = f(inp["w_g2"]), f(inp["b_g"])
    common = dict(
        w_in=f(inp["w_in"]), w_o=f(inp["w_o"]), w_gate=f(inp["w_gate"]), w_up=f(inp["w_up"]), w_down=f(inp["w_down"]),
        g1=np.ascontiguousarray(norm1.reshape(NL, 8, 128).transpose(2, 0, 1).reshape(128, NL * 8)),
        g2=np.ascontiguousarray(norm2.reshape(NL, 8, 128).transpose(2, 0, 1).reshape(128, NL * 8)),
        gg=np.ascontiguousarray(gla_norm.T),
        qkg=np.ascontiguousarray(np.stack([q_norm[0], k_norm[0], q_norm[1], k_norm[1]], axis=1)),
        snk=np.ascontiguousarray(np.broadcast_to(sinks.reshape(1, NL * 8), (128, NL * 8))),
        wg2=np.ascontiguousarray(np.concatenate([w_g2.transpose(1, 0, 2).reshape(16, NL * 256), np.zeros((16, NL * 256), np.float32)], 0)),
        bg=np.ascontiguousarray(np.concatenate([b_g.reshape(1, NL * 256), np.zeros((31, NL * 256), np.float32)], 0)),
    )
    common.update(_consts())
    in_maps = []
    for c in range(8):
        seq = c % 4
        xin = np.zeros((NB * 128, D), np.float32)
        xin[112:128] = meta
        xin[128:NBP * 128] = x_prompt[seq]
        xin[NBP * 128:] = x_sample[16 * c:16 * c + 16].reshape(128, D)
        m = dict(common)
        m["xin"] = xin
        m["ck"] = np.ascontiguousarray(cache_k[:, 16 * c:16 * c + 16].reshape(NL, 16, 128, 128))
        m["cv"] = np.ascontiguousarray(cache_v[:, 16 * c:16 * c + 16].reshape(NL, 16, 128, 128))
        m["st"] = np.ascontiguousarray(state_gla[:, 16 * c:16 * c + 16])
        in_maps.append(m)
    if "nc" not in _NC_CACHE:
        _NC_CACHE["nc"] = build()
    res = run_bass_kernel_spmd(_NC_CACHE["nc"], in_maps, core_ids=list(range(8)))
    R = res.results
    y_prompt = np.stack([R[c]["yp"] for c in range(4)], axis=0).astype(np.float32)
    y_sample = np.concatenate([R[c]["ys"].reshape(16, 8, D) for c in range(8)], axis=0).astype(np.float32)
    pk = np.stack([R[c]["pk"].reshape(NL, 128, 2, 64) for c in range(4)], axis=1).astype(np.float32)
    pv = np.stack([R[c]["pv"].reshape(NL, 128, 2, 64) for c in range(4)], axis=1).astype(np.float32)
    pst = np.stack([R[c]["pst"] for c in range(4)], axis=1).astype(np.float32)
    sk = np.concatenate([R[c]["sk"].reshape(NL, 16, 128, 2, 64) for c in range(8)], axis=1).astype(np.float32)
    sv = np.concatenate([R[c]["sv"].reshape(NL, 16, 128, 2, 64) for c in range(8)], axis=1).astype(np.float32)
    sst = np.concatenate([R[c]["sst"] for c in range(8)], axis=1).astype(np.float32)
    return (y_prompt, y_sample, pk, pv, pst, sk, sv, sst)
```

```python
import contextlib
import numpy as np
import ml_dtypes
import concourse.bass as bass
import concourse.mybir as mybir
from concourse.bass_utils import run_bass_kernel_spmd

F32 = mybir.dt.float32
BF16 = mybir.dt.bfloat16
AF = mybir.ActivationFunctionType
ALU = mybir.AluOpType
AX = mybir.AxisListType


class Buf:
    def __init__(self, name, t):
        self.name = name
        self.t = t
        self.lw = {}
        self.rd = {}
        self.dsem = None
        self.dcount = 0
        self.excl = False
        self.aliases = []


class Prog:
    ENGS = ("pe", "act", "dve", "pool", "sp")

    def __init__(self, nc):
        self.nc = nc
        self.stack = contextlib.ExitStack()
        self.sems = {}
        self.count = {e: 0 for e in self.ENGS}
        self.seen = {e: {} for e in self.ENGS}
        self.ops = {e: [] for e in self.ENGS}
        for e in self.ENGS:
            self.sems["E_" + e] = self.stack.enter_context(nc.semaphore("sem_" + e))
        self.out_tokens = {}
        self.nbufs = 0

    def sb(self, name, shape, dtype):
        t = self.stack.enter_context(self.nc.sbuf_tensor("s_" + name, list(shape), dtype))
        return Buf(name, t)

    def ps(self, name, shape, dtype):
        t = self.stack.enter_context(self.nc.psum_tensor(name, list(shape), dtype))
        b = Buf(name, t)
        b.excl = True
        return b

    def view(self, name, t):
        return Buf(name, t)

    def _dsem(self, buf, queue):
        if buf.dsem is None:
            buf.dsem = {}
            buf.dcount = {}
        if queue not in buf.dsem:
            key = "D_%d_%s_%s" % (self.nbufs, buf.name, queue)
            self.nbufs += 1
            self.sems[key] = self.stack.enter_context(self.nc.semaphore("dsem_%d" % self.nbufs))
            buf.dsem[queue] = key
            buf.dcount[queue] = 0
        return buf.dsem[queue]

    def _waits(self, eng, reads, writes, ignore_waw=False):
        need = {}
        for b in reads:
            for k, v in b.lw.items():
                if need.get(k, 0) < v:
                    need[k] = v
            if b.excl:
                for k, v in b.rd.items():
                    if k != "E_" + eng and need.get(k, 0) < v:
                        need[k] = v
        for b in writes:
            if not ignore_waw:
                for k, v in b.lw.items():
                    if need.get(k, 0) < v:
                        need[k] = v
            for k, v in b.rd.items():
                if need.get(k, 0) < v:
                    need[k] = v
            for al in b.aliases:
                for dd in (al.lw, al.rd):
                    for k, v in dd.items():
                        if need.get(k, 0) < v:
                            need[k] = v
        out = []
        seen = self.seen[eng]
        for k, v in need.items():
            if eng == "pe" and k == "E_pe":
                continue
            if seen.get(k, 0) < v:
                seen[k] = v
                out.append((k, v))
        return out

    def _commit(self, tok, reads, writes, ignore_waw=False):
        k, v = tok
        for b in reads:
            if b.rd.get(k, 0) < v:
                b.rd[k] = v
        for b in writes:
            if ignore_waw:
                b.lw[k] = v
            else:
                b.lw = {k: v}
            b.rd = {}

    def op(self, eng, fn, reads=(), writes=()):
        waits = self._waits(eng, reads, writes)
        self.count[eng] += 1
        tok = ("E_" + eng, self.count[eng])
        self._commit(tok, reads, writes)
        self.ops[eng].append((waits, fn, tok[0], 1))
        return tok

    def dma(self, queue, sem_buf, out_ap, in_ap, reads=(), writes=(), out=False, ignore_waw=False):
        waits = self._waits(queue, reads, writes, ignore_waw=ignore_waw)
        key = self._dsem(sem_buf, queue)
        sem_buf.dcount[queue] += 16
        tok = (key, sem_buf.dcount[queue])
        self._commit(tok, reads, writes, ignore_waw=ignore_waw)
        self.ops[queue].append((waits, (lambda e: e.dma_start(out=out_ap, in_=in_ap)), key, 16))
        if out:
            self.out_tokens[key] = sem_buf.dcount[queue]
        return tok

    def load(self, queue, buf, dst_ap, src_ap, ignore_waw=False, extra_reads=()):
        return self.dma(queue, buf, dst_ap, src_ap, reads=list(extra_reads), writes=[buf], ignore_waw=ignore_waw)

    def store(self, queue, buf, dst_ap, src_ap, out=True, extra_writes=()):
        return self.dma(queue, buf, dst_ap, src_ap, reads=[buf], writes=list(extra_writes), out=out)

    def finish(self):
        nc = self.nc
        handles = {}
        final_waits = list(self.out_tokens.items())
        with nc.Block() as block:
            def emit(e, name):
                for waits, fn, semkey, inc in self.ops[name]:
                    for k, v in waits:
                        e.wait_ge(self.sems[k], v)
                    fn(e).then_inc(self.sems[semkey], inc)
                if name == "sp":
                    for k, v in final_waits:
                        e.wait_ge(self.sems[k], v)

            @block.tensor
            def _(e):
                emit(e, "pe")

            @block.scalar
            def _(e):
                emit(e, "act")

            @block.vector
            def _(e):
                emit(e, "dve")

            @block.gpsimd
            def _(e):
                emit(e, "pool")

            @block.sync
            def _(e):
                emit(e, "sp")
        self.stack.close()


D = 1024
DF = 2816
NFT = 22
NBP = 33
NB = 34
SB = 33
NCH = 25
CH = 4096
EPS = 1e-6
NL = 2
CQ, CK, CV, CGQ, CGK, CGV, CGL, COG = 0, 512, 640, 768, 1024, 1280, 1792, 1808
import os
STOP = int(os.environ.get("MK_STOP", "99"))
SUB = int(os.environ.get("MK_SUB", "0"))
STOPAT = tuple(int(v) for v in os.environ.get("MK_STOPAT", "0,0").split(","))


class _Stop(Exception):
    pass


BLKS = [int(v) for v in os.environ["MK_BLKS"].split(",")] if os.environ.get("MK_BLKS") else list(range(NB))


GMAX = int(os.environ.get("MK_G", "3"))
NLANE = int(os.environ.get("MK_LANES", "2"))
if os.environ.get("MK_GROUPS"):
    GROUPS = [[int(v) for v in g.split(",")] for g in os.environ["MK_GROUPS"].split(";")]
else:
    GROUPS = [list(range(i, min(i + GMAX, NBP))) for i in range(0, NBP, GMAX)] + [[SB]]
TM = 128 * max(len(g) for g in GROUPS)
GM = TM // 128


PHASES = []


def build():
    nc = bass.Bass("TRN2", target_bir_lowering=False)
    P = Prog(nc)
    PHASES.clear()

    def phase(name):
        PHASES.append((name, P.count["pe"]))

    def din(name, shape, dtype=F32):
        return nc.dram_tensor(name, list(shape), dtype, kind="ExternalInput").ap()

    def dout(name, shape, dtype=F32):
        return nc.dram_tensor(name, list(shape), dtype, kind="ExternalOutput").ap()

    xin = din("xin", [NB * 128, D])
    w_in = din("w_in", [NL, D, 2320])
    w_o = din("w_o", [NL, D, D])
    w_gate = din("w_gate", [NL, D, DF])
    w_up = din("w_up", [NL, D, DF])
    w_down = din("w_down", [NL, DF, D])
    g1_d = din("g1", [128, NL * 8])
    g2_d = din("g2", [128, NL * 8])
    gg_d = din("gg", [128, NL])
    qkg_d = din("qkg", [64, NL * 2])
    snk_d = din("snk", [128, NL * 8])
    wg2_d = din("wg2", [32, NL * 256])
    bg_d = din("bg", [32, NL * 256])
    ck_d = din("ck", [NL, 16, 128, 128])
    cv_d = din("cv", [NL, 16, 128, 128])
    st_d = din("st", [NL, 16, 4, 64, 128])
    identb_d = din("identb", [128, 128], BF16)
    identf_d = din("identf", [128, 128])
    masks_d = din("masks", [128, 5 * 128 + 8], BF16)
    umat_d = din("umat", [128, 256])
    valid_d = din("valid", [128, NB])
    eq_d = din("eq", [64, 16 * 128], BF16)
    e2_d = din("e2", [128, 16], BF16)

    yp = dout("yp", [(NBP - 1) * 128, D])
    ys = dout("ys", [128, D])
    pk = dout("pk", [NL, 128, 128])
    pv = dout("pv", [NL, 128, 128])
    pst = dout("pst", [NL, 4, 64, 128])
    sk = dout("sk", [NL, 16, 128, 128])
    sv = dout("sv", [NL, 16, 128, 128])
    sst = dout("sst", [NL, 16, 4, 64, 128])

    wsc_t = nc.dram_tensor("wsc", [NL, NCH, 128, CH], BF16, kind="ExternalOutput").ap()
    wsc = Buf("wsc", wsc_t)

    sb = P.sb
    identb = sb("identb", [128, 128], BF16)
    identf = sb("identf", [128, 128], F32)
    masks = sb("masks", [128, 5 * 128 + 8], BF16)
    umat = sb("umat", [128, 256], F32)
    valid = sb("valid", [128, NB], F32)
    eq = sb("eq", [64, 16, 128], BF16)
    e2 = sb("e2", [128, 16], BF16)
    g1 = sb("g1", [128, NL * 8], F32)
    g2 = sb("g2", [128, NL * 8], F32)
    gg = sb("gg", [128, NL], F32)
    qkg = sb("qkg", [64, NL * 2], F32)
    esink = sb("esink", [128, NL * 8], F32)
    wg2 = sb("wg2", [32, NL * 256], F32)
    bg = sb("bg", [32, NL * 256], F32)
    ones64 = sb("ones64", [64, 64], BF16)
    ones1 = sb("ones1", [32, 128], F32)
    epsb = sb("epsb", [128, 1], F32)
    epsb1 = sb("epsb1", [128, 1], F32)

    x = sb("x", [128, GM, D], F32)
    st4 = sb("st4", [128, 8], F32)
    hb = sb("hb", [128, D], BF16)
    hT = sb("hT", [128, 8, TM], BF16)
    qkraw = [sb("qkraw%d" % i, [64, TM], F32) for i in range(3)]
    qksq = [sb("qksq%d" % i, [64, TM], BF16) for i in range(3)]
    rqh = [sb("rqh%d" % i, [64, TM], F32) for i in range(3)]
    qn = sb("qn", [64, GM, 8, 128], BF16)
    kn32 = sb("kn32", [64, 2, TM], F32)
    kbuf = [sb("kbuf%d" % l, [64, GM + 1, 2, 128], BF16) for l in range(NL)]
    vaug = [sb("vaug%d" % l, [128, GM + 1, 2, 65], BF16) for l in range(NL)]
    gqraw = sb("gqraw", [64, 4, TM], BF16)
    gkraw = sb("gkraw", [64, 4, TM], BF16)
    glowT = sb("glowT", [32, TM], F32)
    v32 = sb("v32", [128, GM, 128], F32)
    gvb = sb("gvb", [128, GM, 512], BF16)
    sog = sb("sog", [128, GM, 512], BF16)
    UW = NFT * TM // 2
    A2W = UW + 2 * TM
    arena2 = P.stack.enter_context(nc.sbuf_tensor("s_arena2", [128, A2W], F32))

    def carve2(name, off, words, dtype, pat=None, parts=128, **kw):
        v = arena2[0:parts, off:off + words]
        if dtype is BF16:
            v = v.bitcast(BF16)
        if pat is not None:
            v = v.rearrange(pat, **kw)
        return Buf(name, v)

    uT = carve2("uT", 0, UW, BF16, "p (f t) -> p f t", f=NFT)
    sg = [carve2("sg%d" % i, UW + i * TM, TM, F32) for i in range(2)]
    tsets = []
    for i_ in range(NLANE):
        n_ = lambda s_: "%s_%d" % (s_, i_)
        if i_ == 0:
            tsets.append(dict(
                lnt=sb(n_("lnt"), [128, 256], F32), eG=sb(n_("eG"), [64, 4, 128], F32), enG=sb(n_("enG"), [64, 4, 128], F32),
                qtil=sb(n_("qtil"), [64, 4, 128], BF16), kt32=sb(n_("kt32"), [64, 4, 128], F32), ktil=sb(n_("ktil"), [64, 4, 128], BF16),
                khatT=sb(n_("khatT"), [64, 4, 128], BF16), khat=sb(n_("khat"), [128, 256], BF16), atm=sb(n_("atm"), [128, 4, 128], BF16),
                o32=sb(n_("o32"), [128, 4, 128], F32), gst=sb(n_("gst"), [128, 16], F32),
                PT=[sb(n_("PTa"), [128, 2, 512], BF16), sb(n_("PTb"), [128, 2, 512], BF16)],
                den=[sb(n_("dena"), [128, 8], F32), sb(n_("denb"), [128, 8], F32)]))
        else:
            assert i_ == 1
            o_ = [0]

            def c2(nm, words, dtype, pat=None, parts=128, **kw):
                b = carve2(n_(nm), o_[0], words, dtype, pat, parts, **kw)
                o_[0] += words
                return b
            ts1 = dict(
                lnt=c2("lnt", 256, F32), eG=c2("eG", 512, F32, "p (h t) -> p h t", parts=64, h=4), enG=c2("enG", 512, F32, "p (h t) -> p h t", parts=64, h=4),
                qtil=c2("qtil", 256, BF16, "p (h t) -> p h t", parts=64, h=4), kt32=c2("kt32", 512, F32, "p (h t) -> p h t", parts=64, h=4),
                ktil=c2("ktil", 256, BF16, "p (h t) -> p h t", parts=64, h=4), khatT=c2("khatT", 256, BF16, "p (h t) -> p h t", parts=64, h=4),
                khat=c2("khat", 128, BF16), atm=c2("atm", 256, BF16, "p (h t) -> p h t", h=4), o32=c2("o32", 512, F32, "p (h t) -> p h t", h=4),
                gst=c2("gst", 16, F32),
                PT=[c2("PTa", 512, BF16, "p (a t) -> p a t", a=2), c2("PTb", 512, BF16, "p (a t) -> p a t", a=2)],
                den=[c2("dena", 8, F32), c2("denb", 8, F32)])
            assert o_[0] <= A2W, (o_[0], A2W)
            tsets.append(ts1)
            flat = [v for v in ts1.values() if isinstance(v, Buf)] + ts1["PT"] + ts1["den"]
            for a_ in flat:
                for b_ in [uT] + sg:
                    a_.aliases.append(b_)
                    b_.aliases.append(a_)
    S = [sb("S%d" % l, [64, 4, 128], F32) for l in range(NL)]
    Sb = [sb("Sb%d" % l, [64, 4, 128], BF16) for l in range(NL)]
    merged = sb("merged", [128, GM, D], BF16)
    kt_out = sb("kt_out", [128, 128], F32)
    ring = [sb("ring%d" % i, [128, CH], BF16) for i in range(4)]

    AW = 12832
    arena = P.stack.enter_context(nc.sbuf_tensor("s_arena", [128, AW], F32))

    def carve(name, off, words, dtype, pat=None, parts=128, **kw):
        v = arena[0:parts, off:off + words]
        if dtype is BF16:
            v = v.bitcast(BF16)
        if pat is not None:
            v = v.rearrange(pat, **kw)
        return Buf(name, v)

    cst = carve("cst", 0, 2048, F32, "p (s c) -> p s c", s=16)
    ckb = carve("ckb", 2048, 1024, BF16, "p (s c) -> p s c", s=16)
    vc = carve("vc", 3072, 1040, BF16, "p (s g e) -> p s g e", s=16, g=2)
    ptc = carve("ptc", 4112, 512, BF16)
    khx = carve("khx", 4624, 512, BF16, "p (s c) -> p s c", s=4)
    kcT = carve("kcT", 5136, 2048, BF16, "p (s g t) -> p s g t", s=16, g=2, parts=64)
    otc = carve("otc", 7184, 1024, F32, "p (g h t) -> p g h t", g=2, h=4, parts=65)
    s0q = carve("s0q", 8208, 2048, F32, "p (s h v) -> p s h v", s=4, h=4, parts=64)
    s0b = carve("s0b", 10256, 1024, BF16, "p (s h v) -> p s h v", s=4, h=4, parts=64)
    qx = carve("qx", 11280, 1024, BF16, "p (h s t) -> p h s t", h=4, s=4, parts=64)
    qns = carve("qns", 12304, 512, BF16, "p (s g c) -> p s g c", s=16, g=2, parts=64)
    samp_bufs = [cst, ckb, vc, ptc, khx, kcT, otc, s0q, s0b, qx, qns]
    stg = [carve("stg%d" % i, 2048 * i, 2048, F32, "p (k w) -> p k w", k=8) for i in range(3)]
    cht = [carve("cht%d" % i, 6144 + 2048 * i, 2048, BF16) for i in range(3)]
    for a_ in stg + cht:
        for b_ in samp_bufs:
            a_.aliases.append(b_)
            b_.aliases.append(a_)

    pbF = [P.ps("pbF%d" % i, [128, 512], F32) for i in range(3)]
    pbT = [P.ps("pbT%d" % i, [128, 512], F32) for i in range(2)]
    pbA = [P.ps("pbA%d" % i, [128, 512], F32) for i in range(2)]
    pb16 = P.ps("pb16", [128, 1024], BF16)
    rr = {"F": 0, "T": 0, "A": 0}

    def bankF():
        rr["F"] += 1
        return pbF[rr["F"] % 3]

    def bankT():
        rr["T"] += 1
        return pbT[rr["T"] % 2]

    def bankA():
        rr["A"] += 1
        return pbA[rr["A"] % 2]

    def mm(out, lhsT, rhs, start, stop, r, w):
        P.op("pe", lambda e: e.matmul(out, lhsT, rhs, start=start, stop=stop), reads=r, writes=w)

    def tr(out, in_, ident, r, w):
        P.op("pe", lambda e: e.transpose(out, in_, ident), reads=r, writes=w)

    def act(out, in_, func, r, w, **kw):
        P.op("act", lambda e: e.activation(out, in_, func, **kw), reads=r, writes=w)

    def cp(eng, out, in_, r, w):
        if eng == "act":
            act(out, in_, AF.Copy, r, w)
        else:
            P.op(eng, lambda e: e.tensor_copy(out, in_), reads=r, writes=w)

    def tt(eng, out, a, b, op, r, w):
        P.op(eng, lambda e: e.tensor_tensor(out, a, b, op), reads=r, writes=w)

    def tsc(eng, out, a, s1, s2, op0, op1, r, w):
        if s2 is None:
            P.op(eng, lambda e: e.tensor_scalar(out, a, s1, None, op0), reads=r, writes=w)
        else:
            P.op(eng, lambda e: e.tensor_scalar(out, a, s1, s2, op0, op1), reads=r, writes=w)

    def stt(eng, out, a, s, b, op0, op1, r, w):
        P.op(eng, lambda e: e.scalar_tensor_tensor(out, a, s, b, op0, op1), reads=r, writes=w)

    def mset(eng, buf, ap, val):
        P.op(eng, lambda e: e.memset(ap, val), writes=[buf])

    cld = Buf("cld", None)
    cbufs = []
    for b_, d_ in ((identb, identb_d), (identf, identf_d), (masks, masks_d), (umat, umat_d), (valid, valid_d),
                   (e2, e2_d), (g1, g1_d), (g2, g2_d), (gg, gg_d), (qkg, qkg_d), (esink, snk_d), (wg2, wg2_d), (bg, bg_d)):
        P.dma("pool", cld, b_.t[:, :], d_, writes=[b_])
        cbufs.append(b_)
    P.dma("pool", cld, eq.t[:, :, :], eq_d.rearrange("p (s t) -> p s t", s=16), writes=[eq])
    cbufs.append(eq)
    for b_ in cbufs:
        b_.lw = {cld.dsem["pool"]: cld.dcount["pool"]}
    act(esink.t[:, :], esink.t[:, :], AF.Exp, [esink], [esink])
    mset("dve", ones64, ones64.t[:, :], 1.0)
    mset("dve", ones1, ones1.t[:, :], 1.0)
    mset("dve", epsb, epsb.t[:, :], EPS)
    mset("dve", epsb1, epsb1.t[:, :], 1.0)
    for l in range(NL):
        mset("dve", S[l], S[l].t[:, :, :], 0.0)
        mset("dve", Sb[l], Sb[l].t[:, :, :], 0.0)
        mset("pool", kbuf[l], kbuf[l].t[:, :, :, :], 0.0)
        mset("pool", vaug[l], vaug[l].t[:, :, :, :], 0.0)
        mset("pool", vaug[l], vaug[l].t[:, :, :, 64:65], 1.0)
    m_own = masks.t[:, 0:128]
    m_prev = masks.t[:, 128:256]
    m_own0 = masks.t[:, 256:384]
    m_samp = masks.t[:, 384:512]
    m_prev1 = masks.t[:, 512:640]
    m_cache = masks.t[:, 640:648]
    pth = Buf("pth", None)

    prep_state = {"stg": 0, "cht": 0, "eng": 0}
    for ct_ in cht:
        mset("pool", ct_, ct_.t[:, :], 0.0)

    def kview(ct, W):
        return ct.t[:, 0:8 * W].rearrange("p (k w) -> p k w", k=8)

    def prep_piece(ct, dst, src, kc, wdt, gain):
        s_ = stg[prep_state["stg"] % 3]
        prep_state["stg"] += 1
        q_ = "act" if prep_state["stg"] % 2 else "sp"
        P.load(q_, s_, s_.t[:, 0:kc, 0:wdt], src)
        eng = ("dve", "pool")[prep_state["eng"] % 2]
        prep_state["eng"] += 1
        if gain is None:
            if prep_state["eng"] % 3 == 0:
                eng = "act"
            cp(eng, dst, s_.t[:, 0:kc, 0:wdt], [s_], [ct])
        else:
            gbuf, gap = gain
            tt(eng, dst, s_.t[:, 0:kc, 0:wdt], gap.unsqueeze(2).broadcast_to([128, kc, wdt]), ALU.mult, [s_, gbuf], [ct])

    prep_tasks = {}
    wsc_c = {(l_, c_): Buf("wsc_%d_%d" % (l_, c_), None) for l_ in range(NL) for c_ in range(NCH)}

    def prep_chunk(l, c, pieces, zero=None):
        def task(ct):
            if zero is not None:
                mset("dve", ct, kview(ct, zero[0])[:, :, zero[1]:zero[2]], 0.0)
            for (dst_fn, src, kc, wdt, gain) in pieces:
                prep_piece(ct, dst_fn(ct), src, kc, wdt, gain)
            for q4 in range(4):
                P.dma("sp", ct, wsc_t[l, c][:, q4 * 1024:(q4 + 1) * 1024], ct.t[:, q4 * 1024:(q4 + 1) * 1024], reads=[ct], writes=[wsc_c[(l, c)]], ignore_waw=True)
        prep_tasks[(l, c)] = task

    for l in range(NL):
        wi = w_in[l].rearrange("(k p) c -> p k c", p=128)
        wo_ = w_o[l].rearrange("(k p) c -> p k c", p=128)
        wg_ = w_gate[l].rearrange("(k p) c -> p k c", p=128)
        wu_ = w_up[l].rearrange("(k p) c -> p k c", p=128)
        wd_ = w_down[l].rearrange("(k p) c -> p k c", p=128)
        G1 = (g1, g1.t[:, l * 8:(l + 1) * 8])
        G2 = (g2, g2.t[:, l * 8:(l + 1) * 8])

        def cols(W, d0, s0, n, src, gain):
            out = []
            for o in range(0, n, 256):
                wdt = min(256, n - o)
                out.append(((lambda ct, W=W, a=d0 + o, wdt=wdt: kview(ct, W)[:, :, a:a + wdt]), src[:, :, s0 + o:s0 + o + wdt], 8, wdt, gain))
            return out

        prep_chunk(l, 0, cols(512, 0, CQ, 512, wi, G1))
        prep_chunk(l, 1, cols(384, 0, CK, 128, wi, G1) + cols(384, 128, CGQ, 256, wi, G1))
        prep_chunk(l, 2, cols(288, 0, CGK, 256, wi, G1) + cols(288, 256, CGL, 16, wi, G1), zero=(288, 272, 288))
        prep_chunk(l, 3, cols(128, 0, CV, 128, wi, G1))
        prep_chunk(l, 4, cols(512, 0, CGV, 512, wi, G1))
        prep_chunk(l, 5, cols(512, 0, COG, 512, wi, G1))
        for hf in range(2):
            pcs = []
            for o in range(0, 512, 256):
                a = hf * 512 + o
                pcs.append(((lambda ct, o=o: kview(ct, 512)[:, 0:4, o:o + 256]), wo_[:, 0:4, a:a + 256], 4, 256, None))
                pcs.append(((lambda ct, o=o: kview(ct, 512)[:, 4:8, o:o + 256]), wo_[:, 4:8, a:a + 256], 4, 256,
                            (gg, gg.t[:, l:l + 1].broadcast_to([128, 4]))))
            prep_chunk(l, 6 + hf, pcs)
        for i in range(11):
            prep_chunk(l, 8 + i, cols(512, 0, 256 * i, 256, wg_, G2) + cols(512, 256, 256 * i, 256, wu_, G2))
        for j in range(6):
            nft = min(4, NFT - 4 * j)
            pcs = []
            for o in range(0, 1024, 256):
                pcs.append(((lambda ct, o=o, nft=nft: ct.t[:, :].rearrange("p (k w) -> p k w", k=4)[:, 0:nft, o:o + 256]),
                            wd_[:, 4 * j:4 * j + nft, o:o + 256], nft, 256, None))
            prep_chunk(l, 19 + j, pcs)

    wseq = [(l, c) for _g in GROUPS for l in range(NL) for c in range(NCH)]
    wst = {"issued": 0}

    NPREP = NL * NCH

    def need(i):
        while wst["issued"] < min(len(wseq), i + 1 if i < NPREP else i + 2):
            j = wst["issued"]
            l_, c_ = wseq[j]
            if j < NPREP:
                prep_tasks[(l_, c_)](cht[j % 3])
            else:
                rb = ring[j % 4]
                for q4 in range(4):
                    P.dma("sp", rb, rb.t[:, q4 * 1024:(q4 + 1) * 1024], wsc_t[l_, c_][:, q4 * 1024:(q4 + 1) * 1024], reads=[wsc_c[(l_, c_)]], writes=[rb], ignore_waw=(q4 > 0))
            wst["issued"] += 1
        return cht[i % 3] if i < NPREP else ring[i % 4]

    tb_state = {"i": 0}

    def tbank():
        tb_state["i"] += 1
        k = tb_state["i"] % 3
        if k == 0:
            return pb16, pb16.t[:, :]
        bkk = pbT[k - 1]
        return bkk, bkk.t[:, :].bitcast(BF16)

    def rmsnorm_T(j):
        mset("dve", st4, st4.t[:, 0:1], 0.0)
        act(hb.t[:, :], x.t[:, j, :], AF.Square, [x], [hb, st4], accum_out=st4.t[:, 0:1])
        act(st4.t[:, 2:3], st4.t[:, 0:1], AF.Ln, [st4], [st4], scale=1.0 / D, bias=epsb.t[:, 0:1])
        act(st4.t[:, 3:4], st4.t[:, 2:3], AF.Exp, [st4], [st4], scale=-0.5)
        tsc("dve", hb.t[:, :], x.t[:, j, :], st4.t[:, 3:4], None, ALU.mult, None, [x, st4], [hb])
        tbk, tbv = tbank()
        for kc in range(8):
            tr(tbv[:, kc * 128:(kc + 1) * 128], hb.t[:, kc * 128:(kc + 1) * 128], identb.t[:, :], [hb, identb], [tbk])
        cp("act", hT.t[:, :, j * 128:(j + 1) * 128], tbv.rearrange("p (k t) -> p k t", k=8), [tbk], [hT])

    try:
        for gi, blks in enumerate(GROUPS):
            nb = len(blks)
            T = nb * 128
            samp = (blks[0] == SB)
            for j, blk in enumerate(blks):
                P.load("pool", x, x.t[:, j, :], xin[blk * 128:(blk + 1) * 128, :], ignore_waw=(j > 0))
            for l in range(NL):
                wbase = (gi * NL + l) * NCH
                phase("norm1")
                for j in range(nb):
                    rmsnorm_T(j)
                if STOP == 2 and (gi, l) == STOPAT:
                    raise _Stop()
                phase("qk")
                W0 = need(wbase + 0)
                W0v = W0.t[:, :].rearrange("p (k w) -> p k w", k=8)
                W1 = need(wbase + 1)
                W1v = W1.t[:, 0:8 * 384].rearrange("p (k w) -> p k w", k=8)
                pend = []

                def qk_tail(h, sl):
                    bk2 = bankA()
                    mm(bk2.t[0:64, 0:T], ones64.t[:, :], qksq[sl].t[:, 0:T], True, True, [ones64, qksq[sl]], [bk2])
                    act(rqh[sl].t[:, 0:T], bk2.t[0:64, 0:T], AF.Ln, [bk2], [rqh[sl]], scale=1.0 / 64, bias=epsb.t[0:64, 0:1])
                    act(rqh[sl].t[:, 0:T], rqh[sl].t[:, 0:T], AF.Exp, [rqh[sl]], [rqh[sl]], scale=-0.5)
                    if h < 8:
                        stt("dve", qn.t[:, 0:nb, h, :], qkraw[sl].t[:, 0:T].rearrange("p (j t) -> p j t", t=128), qkg.t[:, 2 * l:2 * l + 1],
                            rqh[sl].t[:, 0:T].rearrange("p (j t) -> p j t", t=128), ALU.mult, ALU.mult, [qkraw[sl], qkg, rqh[sl]], [qn])
                    else:
                        stt("dve", kn32.t[:, h - 8, 0:T], qkraw[sl].t[:, 0:T], qkg.t[:, 2 * l + 1:2 * l + 2], rqh[sl].t[:, 0:T],
                            ALU.mult, ALU.mult, [qkraw[sl], qkg, rqh[sl]], [kn32])

                W2 = need(wbase + 2)
                W2v = W2.t[:, 0:8 * 288].rearrange("p (k w) -> p k w", k=8)
                extra = []

                def g_head(kind, h):
                    bk_ = bankT()
                    if kind == "gq":
                        lwf, Wb, dst, eng, rows = (lambda kc: W1v[:, kc, 128 + h * 64:128 + (h + 1) * 64]), W1, gqraw.t[:, h, 0:T], "act", 64
                    elif kind == "gk":
                        lwf, Wb, dst, eng, rows = (lambda kc: W2v[:, kc, h * 64:(h + 1) * 64]), W2, gkraw.t[:, h, 0:T], "dve", 64
                    else:
                        lwf, Wb, dst, eng, rows = (lambda kc: W2v[:, kc, 256:288]), W2, glowT.t[:, 0:T], "act", 32
                    for kc in range(8):
                        mm(bk_.t[0:rows, 0:T], lwf(kc), hT.t[:, kc, 0:T], kc == 0, kc == 7, [Wb, hT], [bk_])
                    dbuf = gqraw if kind == "gq" else (gkraw if kind == "gk" else glowT)
                    cp(eng, dst, bk_.t[0:rows, 0:T], [bk_], [dbuf])

                for h_ in range(4):
                    extra.append(lambda h_=h_: g_head("gq", h_))
                for h_ in range(4):
                    extra.append(lambda h_=h_: g_head("gk", h_))
                extra.append(lambda: g_head("gl", 0))
                for h in range(10):
                    sl = h % 3
                    bk = bankF()
                    for kc in range(8):
                        lw = W0v[:, kc, h * 64:(h + 1) * 64] if h < 8 else W1v[:, kc, (h - 8) * 64:(h - 7) * 64]
                        mm(bk.t[0:64, 0:T], lw, hT.t[:, kc, 0:T], kc == 0, kc == 7, [W0 if h < 8 else W1, hT], [bk])
                    cp("dve", qkraw[sl].t[:, 0:T], bk.t[0:64, 0:T], [bk], [qkraw[sl]])
                    act(qksq[sl].t[:, 0:T], bk.t[0:64, 0:T], AF.Square, [bk], [qksq[sl]])
                    if extra:
                        extra.pop(0)()
                    if pend:
                        pend.pop()()
                    pend.append(lambda h=h, sl=sl: qk_tail(h, sl))
                pend.pop()()
                for j, blk in enumerate(blks):
                    cp("pool", kbuf[l].t[:, j + 1, :, :], kn32.t[:, :, j * 128:(j + 1) * 128], [kn32], [kbuf[l]])
                phase("gqgk")
                while extra:
                    extra.pop(0)()
                if STOP == 4 and (gi, l) == STOPAT:
                    raise _Stop()
                phase("tokmaj")
                W3 = need(wbase + 3)
                W3v = W3.t[:, 0:8 * 128].rearrange("p (k w) -> p k w", k=8)
                for j, blk in enumerate(blks):
                    bk = bankT()
                    for kc in range(8):
                        mm(bk.t[:, 0:128], hT.t[:, kc, j * 128:(j + 1) * 128], W3v[:, kc, :], kc == 0, kc == 7, [W3, hT], [bk])
                    cp("act", v32.t[:, j, :], bk.t[:, 0:128], [bk], [v32])
                    cp("dve", vaug[l].t[:, j + 1, :, 0:64], bk.t[:, 0:128].rearrange("p (g d) -> p g d", g=2), [bk], [vaug[l]])
                    if j + 1 < nb:
                        pass
                W4 = need(wbase + 4)
                W4v = W4.t[:, :].rearrange("p (k w) -> p k w", k=8)
                for j in range(nb):
                    bk = bankT()
                    for kc in range(8):
                        mm(bk.t[:, :], hT.t[:, kc, j * 128:(j + 1) * 128], W4v[:, kc, :], kc == 0, kc == 7, [W4, hT], [bk])
                    cp("act", gvb.t[:, j, :], bk.t[:, :], [bk], [gvb])
                W5 = need(wbase + 5)
                W5v = W5.t[:, :].rearrange("p (k w) -> p k w", k=8)
                for j in range(nb):
                    bk = bankT()
                    for kc in range(8):
                        mm(bk.t[:, :], hT.t[:, kc, j * 128:(j + 1) * 128], W5v[:, kc, :], kc == 0, kc == 7, [W5, hT], [bk])
                    act(sog.t[:, j, :], bk.t[:, :], AF.Silu, [bk], [sog])

                if STOP == 5 and (gi, l) == STOPAT:
                    raise _Stop()
                phase("chains")
                if samp:
                    if l == 0:
                        mset("pool", vc, vc.t[:, :, :, 64:65], 1.0)
                    P.load("pool", cst, cst.t[:, :, :], ck_d[l].rearrange("s k c -> k s c"))
                    cp("pool", ckb.t[:, :, :], cst.t[:, :, :], [cst], [ckb])
                    for rnd in range(4):
                        for i in range(8):
                            s_, g_ = (rnd * 8 + i) // 2, (rnd * 8 + i) % 2
                            tr(pb16.t[0:64, i * 128:(i + 1) * 128], ckb.t[:, s_, g_ * 64:(g_ + 1) * 64], identb.t[:, :], [ckb, identb], [pb16])
                        cp("dve", kcT.t[:, rnd * 4:rnd * 4 + 4, :, :], pb16.t[0:64, :].rearrange("p (s g t) -> p s g t", s=4, g=2), [pb16], [kcT])
                    P.dma("pool", pth, sk[l, :, 0:120, :], ck_d[l, :, 8:128, :], reads=[], writes=[], out=True)
                    P.load("pool", cst, cst.t[:, :, :], cv_d[l].rearrange("s k c -> k s c"))
                    cp("pool", vc.t[:, :, :, 0:64], cst.t[:, :, :].rearrange("p s (g d) -> p s g d", g=2), [cst], [vc])
                    P.dma("pool", pth, sv[l, :, 0:120, :], cv_d[l, :, 8:128, :], reads=[], writes=[], out=True)
                    for g_ in range(2):
                        cp("pool", qns.t[:, :, g_, :].rearrange("p s (h t) -> p h s t", h=4), qn.t[:, 0, 4 * g_:4 * g_ + 4, :].rearrange("p h (s t) -> p h s t", t=8), [qn], [qns])
                    stc = [bankA(), bankA()]
                    for s_ in range(16):
                        for g_ in range(2):
                            bk = stc[s_ // 8]
                            o_ = ((s_ % 8) * 2 + g_) * 32
                            mm(bk.t[:, o_:o_ + 32], kcT.t[:, s_, g_, :], qns.t[:, s_, g_, :], True, True, [kcT, qns], [bk])
                    for hf in range(2):
                        act(ptc.t[:, hf * 512:(hf + 1) * 512], stc[hf].t[:, :], AF.Exp, [stc[hf]], [ptc], scale=0.125)
                    tt("pool", ptc.t[:, :].rearrange("p (a q) -> p a q", q=8), ptc.t[:, :].rearrange("p (a q) -> p a q", q=8),
                       m_cache.unsqueeze(1).broadcast_to([128, 128, 8]), ALU.mult, [ptc, masks], [ptc])
                    otb = [bankA(), bankA()]
                    for s_ in range(16):
                        for g_ in range(2):
                            bk = otb[s_ // 8]
                            o_ = ((s_ % 8) * 2 + g_) * 32
                            mm(bk.t[0:65, o_:o_ + 32], vc.t[:, s_, g_, :], ptc.t[:, (s_ * 2 + g_) * 32:(s_ * 2 + g_ + 1) * 32], True, True, [vc, ptc], [bk])
                    for hf in range(2):
                        for g_ in range(2):
                            cp("act" if g_ else "dve", otc.t[:, g_, :, hf * 64:(hf + 1) * 64].rearrange("p h (s t) -> p s h t", t=8),
                               otb[hf].t[0:65, :].rearrange("p (s g h t) -> p s g h t", s=8, g=2, h=4)[:, :, g_, :, :], [otb[hf]], [otc])

                def attn_chain(j, blk, g, ts):
                    pt = ts["PT"][g]
                    den = ts["den"][g]
                    rq_ = qn.t[:, j, 4 * g:4 * g + 4, :].rearrange("p h t -> p (h t)")
                    has_prev = (not samp) and blk > 0
                    sbk = yield from take(2 if has_prev else 1)
                    bo = sbk[0]
                    mm(bo.t[:, :], kbuf[l].t[:, j + 1, g, :], rq_, True, True, [kbuf[l], qn], [bo])
                    if has_prev:
                        bp = sbk[1]
                        mm(bp.t[:, :], kbuf[l].t[:, j, g, :], rq_, True, True, [kbuf[l], qn], [bp])
                    yield
                    act(pt.t[:, 0, :], bo.t[:, :], AF.Exp, [bo], [pt], scale=0.125)
                    if has_prev:
                        act(pt.t[:, 1, :], bp.t[:, :], AF.Exp, [bp], [pt], scale=0.125)
                    give(*sbk)
                    yield
                    mk = m_samp if samp else (m_own0 if blk == 0 else m_own)
                    tt("pool", pt.t[:, 0, :].rearrange("p (h t) -> p h t", h=4), pt.t[:, 0, :].rearrange("p (h t) -> p h t", h=4),
                       mk.unsqueeze(1).broadcast_to([128, 4, 128]), ALU.mult, [pt, masks], [pt])
                    if has_prev:
                        tt("pool", pt.t[:, 1, :].rearrange("p (h t) -> p h t", h=4), pt.t[:, 1, :].rearrange("p (h t) -> p h t", h=4),
                           (m_prev1 if blk == 1 else m_prev).unsqueeze(1).broadcast_to([128, 4, 128]), ALU.mult, [pt, masks], [pt])
                    yield
                    bv_ = (yield from take(1))[0]
                    for h in range(4):
                        oc = bv_.t[:, h * 65:(h + 1) * 65]
                        last_own = not (has_prev or samp)
                        mm(oc, pt.t[:, 0, h * 128:(h + 1) * 128], vaug[l].t[:, j + 1, g, :], True, last_own, [pt, vaug[l]], [bv_])
                        if has_prev:
                            mm(oc, pt.t[:, 1, h * 128:(h + 1) * 128], vaug[l].t[:, j, g, :], False, True, [pt, vaug[l]], [bv_])
                        if samp:
                            mm(oc, otc.t[:, g, h, :], identf.t[0:65, 0:65], False, True, [otc, identf], [bv_])
                    yield
                    pv4 = bv_.t[:, 0:260].rearrange("p (h e) -> p h e", h=4)
                    tt("dve", den.t[:, 0:4], pv4[:, :, 64], esink.t[:, l * 8 + 4 * g:l * 8 + 4 * g + 4], ALU.add, [bv_, esink], [den])
                    P.op("dve", lambda e: e.reciprocal(den.t[:, 4:8], den.t[:, 0:4]), reads=[den], writes=[den])
                    tt("dve", merged.t[:, j, g * 256:(g + 1) * 256].rearrange("p (h d) -> p h d", h=4), pv4[:, :, 0:64],
                       den.t[:, 4:8].unsqueeze(2).broadcast_to([128, 4, 64]), ALU.mult, [bv_, den], [merged])
                    give(bv_)

                def gla_chain(j, blk, ts):
                    lnt, eG, enG, qtil, kt32, ktil = ts["lnt"], ts["eG"], ts["enG"], ts["qtil"], ts["kt32"], ts["ktil"]
                    khatT, khat, atm, o32, gst = ts["khatT"], ts["khat"], ts["atm"], ts["o32"], ts["gst"]
                    c0, c1 = j * 128, (j + 1) * 128
                    bl = (yield from take(1))[0]
                    mm(bl.t[:, 0:256], glowT.t[:, c0:c1], wg2.t[:, l * 256:(l + 1) * 256], True, False, [glowT, wg2], [bl])
                    mm(bl.t[:, 0:256], ones1.t[:, :], bg.t[:, l * 256:(l + 1) * 256], False, True, [ones1, bg], [bl])
                    yield
                    act(lnt.t[:, :], bl.t[:, 0:256], AF.Exp, [bl], [lnt], scale=-1.0)
                    give(bl)
                    act(lnt.t[:, :], lnt.t[:, :], AF.Ln, [lnt], [lnt], bias=epsb1.t[:, 0:1])
                    yield
                    U = umat.t[:, 128:256] if samp else umat.t[:, 0:128]
                    gmask = m_samp if samp else m_own
                    bgT = (yield from take(1))[0]
                    for h in range(4):
                        mm(bgT.t[0:64, h * 128:(h + 1) * 128], lnt.t[:, h * 64:(h + 1) * 64], U, True, True, [lnt, umat], [bgT])
                    yield
                    g4 = bgT.t[0:64, :].rearrange("p (h t) -> p h t", h=4)
                    act(eG.t[:, :, :], g4, AF.Exp, [bgT], [eG])
                    act(enG.t[:, :, :], g4, AF.Exp, [bgT], [enG], scale=-1.0)
                    give(bgT)
                    yield
                    stt("dve", qtil.t[:, :, :], gqraw.t[:, :, c0:c1], 0.125, eG.t[:, :, :], ALU.mult, ALU.mult, [gqraw, eG], [qtil])
                    tt("dve", kt32.t[:, :, :], gkraw.t[:, :, c0:c1], enG.t[:, :, :], ALU.mult, [gkraw, enG], [kt32])
                    yield
                    cp("pool", ktil.t[:, :, :], kt32.t[:, :, :], [kt32], [ktil])
                    if samp:
                        egl = eG.t[:, :, :].rearrange("p h (s t) -> p h s t", t=8)[:, :, :, 7:8].broadcast_to([64, 4, 16, 8])
                        tt("pool", khatT.t[:, :, :].rearrange("p h (s t) -> p h s t", t=8), kt32.t[:, :, :].rearrange("p h (s t) -> p h s t", t=8),
                           egl, ALU.mult, [kt32, eG], [khatT])
                    else:
                        egl = eG.t[:, :, 127:128].broadcast_to([64, 4, 128])
                        tt("pool", khatT.t[:, :, :], kt32.t[:, :, :], egl, ALU.mult, [kt32, eG], [khatT])
                    yield
                    for h in range(4):
                        tr(pb16.t[:, h * 64:(h + 1) * 64], khatT.t[:, h, :], identb.t[0:64, 0:64], [khatT, identb], [pb16])
                    cp("dve", khat.t[:, :], pb16.t[:, 0:256], [pb16], [khat])
                    ba = (yield from take(1))[0]
                    for h in range(4):
                        mm(ba.t[:, h * 128:(h + 1) * 128], ktil.t[:, h, :], qtil.t[:, h, :], True, True, [ktil, qtil], [ba])
                    yield
                    tt("dve", atm.t[:, :, :], ba.t[:, :].rearrange("p (h t) -> p h t", h=4), gmask.unsqueeze(1).broadcast_to([128, 4, 128]), ALU.mult, [ba, masks], [atm])
                    give(ba)
                    yield
                    if not samp:
                        bo, bs = yield from take(2)
                        for h in range(4):
                            oc = bo.t[:, h * 128:(h + 1) * 128]
                            mm(oc, atm.t[:, h, :], gvb.t[:, j, h * 128:(h + 1) * 128], True, False, [atm, gvb], [bo])
                            mm(oc, qtil.t[:, h, :], Sb[l].t[:, h, :], False, True, [qtil, Sb[l]], [bo])
                        for h in range(4):
                            mm(bs.t[0:64, h * 128:(h + 1) * 128], khat.t[:, h * 64:(h + 1) * 64], gvb.t[:, j, h * 128:(h + 1) * 128], True, True, [khat, gvb], [bs])
                        tt("dve", S[l].t[:, :, :], S[l].t[:, :, :], eG.t[:, :, 127:128].broadcast_to([64, 4, 128]), ALU.mult, [S[l], eG], [S[l]])
                        tt("dve", S[l].t[:, :, :], S[l].t[:, :, :], bs.t[0:64, :].rearrange("p (h v) -> p h v", h=4), ALU.add, [S[l], bs], [S[l]])
                        cp("pool", Sb[l].t[:, :, :], S[l].t[:, :, :], [S[l]], [Sb[l]])
                        give(bs)
                    else:
                        obk = yield from take(4)
                        for h in range(4):
                            mm(obk[h].t[:, 0:128], atm.t[:, h, :], gvb.t[:, j, h * 128:(h + 1) * 128], True, False, [atm, gvb], [obk[h]])
                        eg4 = eG.t[:, :, :].rearrange("p h (s t) -> p h s t", t=8)
                        for qd in range(4):
                            P.load("pool", s0q, s0q.t[:, :, :, :], st_d[l, 4 * qd:4 * qd + 4].rearrange("s h k v -> k s h v"))
                            cp("pool", s0b.t[:, :, :, :], s0q.t[:, :, :, :], [s0q], [s0b])
                            tt("dve", qx.t[:, :, :, :], qtil.t[:, :, :].unsqueeze(2).broadcast_to([64, 4, 4, 128]),
                               eq.t[:, 4 * qd:4 * qd + 4, :].unsqueeze(1).broadcast_to([64, 4, 4, 128]), ALU.mult, [qtil, eq], [qx])
                            tt("pool", khx.t[:, :, :], khat.t[:, :].unsqueeze(1).broadcast_to([128, 4, 256]),
                               e2.t[:, 4 * qd:4 * qd + 4].unsqueeze(2).broadcast_to([128, 4, 256]), ALU.mult, [khat, e2], [khx])
                            for h in range(4):
                                for s_ in range(4):
                                    last = (qd == 3 and s_ == 3)
                                    mm(obk[h].t[:, 0:128], qx.t[:, h, s_, :], s0b.t[:, s_, h, :], False, last, [qx, s0b], [obk[h]])
                            for s_ in range(4):
                                bs = (yield from take(1))[0]
                                for h in range(4):
                                    mm(bs.t[0:64, h * 128:(h + 1) * 128], khx.t[:, s_, h * 64:(h + 1) * 64], gvb.t[:, j, h * 128:(h + 1) * 128], True, True, [khx, gvb], [bs])
                                sa = 4 * qd + s_
                                tt("dve", s0q.t[:, s_, :, :], s0q.t[:, s_, :, :], eg4[:, :, sa, 7:8].broadcast_to([64, 4, 128]), ALU.mult, [s0q, eG], [s0q])
                                tt("dve", s0q.t[:, s_, :, :], s0q.t[:, s_, :, :], bs.t[0:64, :].rearrange("p (h v) -> p h v", h=4), ALU.add, [s0q, bs], [s0q])
                                give(bs)
                            P.store("pool", s0q, sst[l, 4 * qd:4 * qd + 4].rearrange("s h k v -> k s h v"), s0q.t[:, :, :, :])
                    yield
                    if samp:
                        for h in range(4):
                            cp("act" if h % 2 else "dve", o32.t[:, h, :], obk[h].t[:, 0:128], [obk[h]], [o32])
                        give(*obk)
                    else:
                        cp("act", o32.t[:, :, :], bo.t[:, :].rearrange("p (h v) -> p h v", h=4), [bo], [o32])
                        give(bo)
                    mset("dve", gst, gst.t[:, 0:4], 0.0)
                    yield
                    for h in range(4):
                        act(atm.t[:, h, :], o32.t[:, h, :], AF.Square, [o32], [atm, gst], accum_out=gst.t[:, h:h + 1])
                    act(gst.t[:, 4:8], gst.t[:, 0:4], AF.Ln, [gst], [gst], scale=1.0 / 128, bias=epsb.t[:, 0:1])
                    act(gst.t[:, 8:12], gst.t[:, 4:8], AF.Exp, [gst], [gst], scale=-0.5)
                    yield
                    tt("dve", o32.t[:, :, :], o32.t[:, :, :], gst.t[:, 8:12].unsqueeze(2).broadcast_to([128, 4, 128]), ALU.mult, [o32, gst], [o32])
                    yield
                    tt("pool", merged.t[:, j, 512:1024], o32.t[:, :, :].rearrange("p h v -> p (h v)"), sog.t[:, j, :], ALU.mult, [o32, sog], [merged])
                    if blk == NBP - 1 or samp:
                        tb = (yield from take(1))[0]
                        for g in range(2):
                            tr(tb.t[:, g * 64:(g + 1) * 64], kn32.t[:, g, c0:c1], identf.t[0:64, 0:64], [kn32, identf], [tb])
                        cp("dve", kt_out.t[:, :], tb.t[:, 0:128], [tb], [kt_out])
                        give(tb)
                        if samp:
                            for s_ in range(16):
                                P.store("pool", kt_out, sk[l, s_, 120:128, :], kt_out.t[8 * s_:8 * s_ + 8, :])
                                P.store("pool", v32, sv[l, s_, 120:128, :], v32.t[8 * s_:8 * s_ + 8, j, :])
                        else:
                            P.store("pool", kt_out, pk[l], kt_out.t[:, :])
                            P.store("pool", v32, pv[l], v32.t[:, j, :])
                            P.store("pool", S[l], pst[l].rearrange("h k v -> k h v"), S[l].t[:, :, :])

                freeb = list(pbF) + list(pbT) + list(pbA)

                def take(n):
                    spins = 0
                    while len(freeb) < n:
                        spins += 1
                        assert spins < 10000, "psum bank deadlock"
                        yield
                    return [freeb.pop(0) for _ in range(n)]

                def give(*bs_):
                    freeb.extend(bs_)

                lanes = [[(j, blk) for j, blk in enumerate(blks) if j % NLANE == ln] for ln in range(NLANE)]
                active = [[] for _ in range(NLANE)]
                while any(lanes) or any(active):
                    for ln in range(NLANE):
                        if not active[ln] and lanes[ln]:
                            j, blk = lanes[ln].pop(0)
                            active[ln] = [attn_chain(j, blk, 0, tsets[ln]), attn_chain(j, blk, 1, tsets[ln]), gla_chain(j, blk, tsets[ln])]
                        for gen in list(active[ln]):
                            try:
                                next(gen)
                            except StopIteration:
                                active[ln].remove(gen)
                if STOP == 7 and (gi, l) == STOPAT:
                    raise _Stop()
                phase("wo")
                if not samp:
                    cp("pool", kbuf[l].t[:, 0, :, :], kbuf[l].t[:, nb, :, :], [kbuf[l]], [kbuf[l]])
                    cp("pool", vaug[l].t[:, 0, :, 0:64], vaug[l].t[:, nb, :, 0:64], [vaug[l]], [vaug[l]])
                for j in range(nb):
                    tbk, tbv = tbank()
                    for kc in range(8):
                        tr(tbv[:, kc * 128:(kc + 1) * 128], merged.t[:, j, kc * 128:(kc + 1) * 128], identb.t[:, :], [merged, identb], [tbk])
                    cp("dve" if j % 2 else "act", hT.t[:, :, j * 128:(j + 1) * 128], tbv.rearrange("p (k t) -> p k t", k=8), [tbk], [hT])
                for hf in range(2):
                    Wc = need(wbase + 6 + hf)
                    Wv = Wc.t[:, :].rearrange("p (k w) -> p k w", k=8)
                    for j, blk in enumerate(blks):
                        bk = bankT()
                        for kc in range(8):
                            mm(bk.t[:, :], hT.t[:, kc, j * 128:(j + 1) * 128], Wv[:, kc, :], kc == 0, kc == 7, [Wc, hT], [bk])
                        stt("dve", x.t[:, j, hf * 512:(hf + 1) * 512], bk.t[:, :], valid.t[:, blk:blk + 1], x.t[:, j, hf * 512:(hf + 1) * 512], ALU.mult, ALU.add, [bk, valid, x], [x])

                if STOP == 8 and (gi, l) == STOPAT:
                    raise _Stop()
                phase("ffn_gu")
                for j in range(nb):
                    rmsnorm_T(j)
                for i in range(11):
                    Wc = need(wbase + 8 + i)
                    Wv = Wc.t[:, :].rearrange("p (k a w) -> p k a w", k=8, a=2)
                    for jj in range(2):
                        ft = 2 * i + jj
                        bg_ = bankF()
                        bu_ = bankA()
                        for kc in range(8):
                            mm(bg_.t[:, 0:T], Wv[:, kc, 0, jj * 128:(jj + 1) * 128], hT.t[:, kc, 0:T], kc == 0, kc == 7, [Wc, hT], [bg_])
                        for kc in range(8):
                            mm(bu_.t[:, 0:T], Wv[:, kc, 1, jj * 128:(jj + 1) * 128], hT.t[:, kc, 0:T], kc == 0, kc == 7, [Wc, hT], [bu_])
                        sg_ = sg[ft % 2]
                        act(sg_.t[:, 0:T], bg_.t[:, 0:T], AF.Silu, [bg_], [sg_])
                        tt("dve", uT.t[:, ft, 0:T], sg_.t[:, 0:T], bu_.t[:, 0:T], ALU.mult, [sg_, bu_], [uT])
                phase("ffn_down")
                dbk = [pbT[0], pbT[1], pbA[0], pbA[1], pbF[0], pbF[1]]
                for ft in range(NFT):
                    Wc = need(wbase + 19 + ft // 4)
                    Wv = Wc.t[:, :].rearrange("p (k w) -> p k w", k=4)
                    for j in range(nb):
                        for hf in range(2):
                            bd = dbk[j * 2 + hf]
                            mm(bd.t[:, :], uT.t[:, ft, j * 128:(j + 1) * 128], Wv[:, ft % 4, hf * 512:(hf + 1) * 512], ft == 0, ft == NFT - 1, [Wc, uT], [bd])
                for j, blk in enumerate(blks):
                    for hf in range(2):
                        bd = dbk[j * 2 + hf]
                        stt("dve", x.t[:, j, hf * 512:(hf + 1) * 512], bd.t[:, :], valid.t[:, blk:blk + 1], x.t[:, j, hf * 512:(hf + 1) * 512], ALU.mult, ALU.add, [bd, valid, x], [x])
            for j, blk in enumerate(blks):
                if samp:
                    P.dma("pool", x, ys, x.t[:, j, :], reads=[x], out=True)
                elif blk >= 1:
                    P.dma("pool", x, yp[(blk - 1) * 128:blk * 128, :], x.t[:, j, :], reads=[x], out=True)
    except _Stop:
        pass
    P.finish()
    return nc


def _consts():
    bf = ml_dtypes.bfloat16
    j = np.arange(128)[:, None]
    i = np.arange(128)[None, :]
    own = (j <= i)
    prev = (j > i)
    own0 = own & (j >= 112)
    samp = (j // 8 == i // 8) & (j % 8 <= i % 8)
    cache = (np.arange(128)[:, None] > np.arange(8)[None, :])
    prev1 = prev & (j >= 112)
    masks = np.concatenate([own, prev, own0, samp, prev1, cache], axis=1).astype(np.float32).astype(bf)
    umat = np.concatenate([own.astype(np.float32), samp.astype(np.float32)], axis=1) * np.float32(-1.0 / 16.0)
    valid = np.ones((128, NB), np.float32)
    valid[0:112, 0] = 0.0
    t = np.arange(128)
    e2 = (t[:, None] // 8 == np.arange(16)[None, :]).astype(np.float32)
    eq = np.broadcast_to(e2.T[None, :, :], (64, 16, 128)).reshape(64, 16 * 128)
    return dict(identb=np.eye(128, dtype=np.float32).astype(bf), identf=np.eye(128, dtype=np.float32),
                masks=masks, umat=np.ascontiguousarray(umat.astype(np.float32)), valid=valid,
                eq=np.ascontiguousarray(eq).astype(bf), e2=e2.astype(bf))


_NC_CACHE = {}


def kernel(**inp):
    f = lambda a: np.ascontiguousarray(np.asarray(a, dtype=np.float32))
    x_prompt, x_sample = f(inp["x_prompt"]), f(inp["x_sample"])
    cache_k, cache_v, state_gla = f(inp["cache_k"]), f(inp["cache_v"]), f(inp["state_gla"])
    meta = f(inp["meta"])
    norm1, norm2, gla_norm = f(inp["norm1"]), f(inp["norm2"]), f(inp["gla_norm"])
    q_norm, k_norm, sinks = f(inp["q_norm"]), f(inp["k_norm"]), f(inp["sinks"])
    w_g2, b_g = f(inp["w_g2"]), f(inp["b_g"])
    common = dict(
        w_in=f(inp["w_in"]), w_o=f(inp["w_o"]), w_gate=f(inp["w_gate"]), w_up=f(inp["w_up"]), w_down=f(inp["w_down"]),
        g1=np.ascontiguousarray(norm1.reshape(NL, 8, 128).transpose(2, 0, 1).reshape(128, NL * 8)),
        g2=np.ascontiguousarray(norm2.reshape(NL, 8, 128).transpose(2, 0, 1).reshape(128, NL * 8)),
        gg=np.ascontiguousarray(gla_norm.T),
        qkg=np.ascontiguousarray(np.stack([q_norm[0], k_norm[0], q_norm[1], k_norm[1]], axis=1)),
        snk=np.ascontiguousarray(np.broadcast_to(sinks.reshape(1, NL * 8), (128, NL * 8))),
        wg2=np.ascontiguousarray(np.concatenate([w_g2.transpose(1, 0, 2).reshape(16, NL * 256), np.zeros((16, NL * 256), np.float32)], 0)),
        bg=np.ascontiguousarray(np.concatenate([b_g.reshape(1, NL * 256), np.zeros((31, NL * 256), np.float32)], 0)),
    )
    common.update(_consts())
    in_maps = []
    for c in range(8):
        seq = c % 4
        xin = np.zeros((NB * 128, D), np.float32)
        xin[112:128] = meta
        xin[128:NBP * 128] = x_prompt[seq]
        xin[NBP * 128:] = x_sample[16 * c:16 * c + 16].reshape(128, D)
        m = dict(common)
        m["xin"] = xin
        m["ck"] = np.ascontiguousarray(cache_k[:, 16 * c:16 * c + 16].reshape(NL, 16, 128, 128))
        m["cv"] = np.ascontiguousarray(cache_v[:, 16 * c:16 * c + 16].reshape(NL, 16, 128, 128))
        m["st"] = np.ascontiguousarray(state_gla[:, 16 * c:16 * c + 16])
        in_maps.append(m)
    if "nc" not in _NC_CACHE:
        _NC_CACHE["nc"] = build()
    res = run_bass_kernel_spmd(_NC_CACHE["nc"], in_maps, core_ids=list(range(8)))
    R = res.results
    y_prompt = np.stack([R[c]["yp"] for c in range(4)], axis=0).astype(np.float32)
    y_sample = np.concatenate([R[c]["ys"].reshape(16, 8, D) for c in range(8)], axis=0).astype(np.float32)
    pk = np.stack([R[c]["pk"].reshape(NL, 128, 2, 64) for c in range(4)], axis=1).astype(np.float32)
    pv = np.stack([R[c]["pv"].reshape(NL, 128, 2, 64) for c in range(4)], axis=1).astype(np.float32)
    pst = np.stack([R[c]["pst"] for c in range(4)], axis=1).astype(np.float32)
    sk = np.concatenate([R[c]["sk"].reshape(NL, 16, 128, 2, 64) for c in range(8)], axis=1).astype(np.float32)
    sv = np.concatenate([R[c]["sv"].reshape(NL, 16, 128, 2, 64) for c in range(8)], axis=1).astype(np.float32)
    sst = np.concatenate([R[c]["sst"] for c in range(8)], axis=1).astype(np.float32)
    return (y_prompt, y_sample, pk, pv, pst, sk, sv, sst)
```

```python
import contextlib
import numpy as np
import ml_dtypes
import concourse.bass as bass
import concourse.mybir as mybir
from concourse.bass_utils import run_bass_kernel_spmd

F32 = mybir.dt.float32
BF16 = mybir.dt.bfloat16
AF = mybir.ActivationFunctionType
ALU = mybir.AluOpType
AX = mybir.AxisListType


class Buf:
    def __init__(self, name, t):
        self.name = name
        self.t = t
        self.lw = {}
        self.rd = {}
        self.dsem = None
        self.dcount = 0
        self.excl = False
        self.aliases = []


class Prog:
    ENGS = ("pe", "act", "dve", "pool", "sp")

    def __init__(self, nc):
        self.nc = nc
        self.stack = contextlib.ExitStack()
        self.sems = {}
        self.count = {e: 0 for e in self.ENGS}
        self.seen = {e: {} for e in self.ENGS}
        self.ops = {e: [] for e in self.ENGS}
        for e in self.ENGS:
            self.sems["E_" + e] = self.stack.enter_context(nc.semaphore("sem_" + e))
        self.out_tokens = {}
        self.nbufs = 0

    def sb(self, name, shape, dtype):
        t = self.stack.enter_context(self.nc.sbuf_tensor("s_" + name, list(shape), dtype))
        return Buf(name, t)

    def ps(self, name, shape, dtype):
        t = self.stack.enter_context(self.nc.psum_tensor(name, list(shape), dtype))
        b = Buf(name, t)
        b.excl = True
        return b

    def view(self, name, t):
        return Buf(name, t)

    def _dsem(self, buf, queue):
        if buf.dsem is None:
            buf.dsem = {}
            buf.dcount = {}
        if queue not in buf.dsem:
            key = "D_%d_%s_%s" % (self.nbufs, buf.name, queue)
            self.nbufs += 1
            self.sems[key] = self.stack.enter_context(self.nc.semaphore("dsem_%d" % self.nbufs))
            buf.dsem[queue] = key
            buf.dcount[queue] = 0
        return buf.dsem[queue]

    def _waits(self, eng, reads, writes, ignore_waw=False):
        need = {}
        for b in reads:
            for k, v in b.lw.items():
                if need.get(k, 0) < v:
                    need[k] = v
            if b.excl:
                for k, v in b.rd.items():
                    if k != "E_" + eng and need.get(k, 0) < v:
                        need[k] = v
        for b in writes:
            if not ignore_waw:
                for k, v in b.lw.items():
                    if need.get(k, 0) < v:
                        need[k] = v
            for k, v in b.rd.items():
                if need.get(k, 0) < v:
                    need[k] = v
            for al in b.aliases:
                for dd in (al.lw, al.rd):
                    for k, v in dd.items():
                        if need.get(k, 0) < v:
                            need[k] = v
        out = []
        seen = self.seen[eng]
        for k, v in need.items():
            if eng == "pe" and k == "E_pe":
                continue
            if seen.get(k, 0) < v:
                seen[k] = v
                out.append((k, v))
        return out

    def _commit(self, tok, reads, writes, ignore_waw=False):
        k, v = tok
        for b in reads:
            if b.rd.get(k, 0) < v:
                b.rd[k] = v
        for b in writes:
            if ignore_waw:
                b.lw[k] = v
            else:
                b.lw = {k: v}
            b.rd = {}

    def op(self, eng, fn, reads=(), writes=()):
        waits = self._waits(eng, reads, writes)
        self.count[eng] += 1
        tok = ("E_" + eng, self.count[eng])
        self._commit(tok, reads, writes)
        self.ops[eng].append((waits, fn, tok[0], 1))
        return tok

    def dma(self, queue, sem_buf, out_ap, in_ap, reads=(), writes=(), out=False, ignore_waw=False):
        waits = self._waits(queue, reads, writes, ignore_waw=ignore_waw)
        key = self._dsem(sem_buf, queue)
        sem_buf.dcount[queue] += 16
        tok = (key, sem_buf.dcount[queue])
        self._commit(tok, reads, writes, ignore_waw=ignore_waw)
        self.ops[queue].append((waits, (lambda e: e.dma_start(out=out_ap, in_=in_ap)), key, 16))
        if out:
            self.out_tokens[key] = sem_buf.dcount[queue]
        return tok

    def load(self, queue, buf, dst_ap, src_ap, ignore_waw=False, extra_reads=()):
        return self.dma(queue, buf, dst_ap, src_ap, reads=list(extra_reads), writes=[buf], ignore_waw=ignore_waw)

    def store(self, queue, buf, dst_ap, src_ap, out=True, extra_writes=()):
        return self.dma(queue, buf, dst_ap, src_ap, reads=[buf], writes=list(extra_writes), out=out)

    def finish(self):
        nc = self.nc
        handles = {}
        final_waits = list(self.out_tokens.items())
        with nc.Block() as block:
            def emit(e, name):
                for waits, fn, semkey, inc in self.ops[name]:
                    for k, v in waits:
                        e.wait_ge(self.sems[k], v)
                    fn(e).then_inc(self.sems[semkey], inc)
                if name == "sp":
                    for k, v in final_waits:
                        e.wait_ge(self.sems[k], v)

            @block.tensor
            def _(e):
                emit(e, "pe")

            @block.scalar
            def _(e):
                emit(e, "act")

            @block.vector
            def _(e):
                emit(e, "dve")

            @block.gpsimd
            def _(e):
                emit(e, "pool")

            @block.sync
            def _(e):
                emit(e, "sp")
        self.stack.close()


D = 1024
DF = 2816
NFT = 22
NBP = 33
NB = 34
SB = 33
NCH = 25
CH = 4096
EPS = 1e-6
NL = 2
CQ, CK, CV, CGQ, CGK, CGV, CGL, COG = 0, 512, 640, 768, 1024, 1280, 1792, 1808
import os
STOP = int(os.environ.get("MK_STOP", "99"))
SUB = int(os.environ.get("MK_SUB", "0"))
STOPAT = tuple(int(v) for v in os.environ.get("MK_STOPAT", "0,0").split(","))


class _Stop(Exception):
    pass


BLKS = [int(v) for v in os.environ["MK_BLKS"].split(",")] if os.environ.get("MK_BLKS") else list(range(NB))


GMAX = int(os.environ.get("MK_G", "3"))
NLANE = int(os.environ.get("MK_LANES", "2"))
if os.environ.get("MK_GROUPS"):
    GROUPS = [[int(v) for v in g.split(",")] for g in os.environ["MK_GROUPS"].split(";")]
else:
    GROUPS = [list(range(i, min(i + GMAX, NBP))) for i in range(0, NBP, GMAX)] + [[SB]]
TM = 128 * max(len(g) for g in GROUPS)
GM = TM // 128


PHASES = []


def build():
    nc = bass.Bass("TRN2", target_bir_lowering=False)
    P = Prog(nc)
    PHASES.clear()

    def phase(name):
        PHASES.append((name, P.count["pe"]))

    def din(name, shape, dtype=F32):
        return nc.dram_tensor(name, list(shape), dtype, kind="ExternalInput").ap()

    def dout(name, shape, dtype=F32):
        return nc.dram_tensor(name, list(shape), dtype, kind="ExternalOutput").ap()

    xin = din("xin", [NB * 128, D])
    w_in = din("w_in", [NL, D, 2320])
    w_o = din("w_o", [NL, D, D])
    w_gate = din("w_gate", [NL, D, DF])
    w_up = din("w_up", [NL, D, DF])
    w_down = din("w_down", [NL, DF, D])
    g1_d = din("g1", [128, NL * 8])
    g2_d = din("g2", [128, NL * 8])
    gg_d = din("gg", [128, NL])
    qkg_d = din("qkg", [64, NL * 2])
    snk_d = din("snk", [128, NL * 8])
    wg2_d = din("wg2", [32, NL * 256])
    bg_d = din("bg", [32, NL * 256])
    ck_d = din("ck", [NL, 16, 128, 128])
    cv_d = din("cv", [NL, 16, 128, 128])
    st_d = din("st", [NL, 16, 4, 64, 128])
    identb_d = din("identb", [128, 128], BF16)
    identf_d = din("identf", [128, 128])
    masks_d = din("masks", [128, 5 * 128 + 8], BF16)
    umat_d = din("umat", [128, 256])
    valid_d = din("valid", [128, NB])
    eq_d = din("eq", [64, 16 * 128], BF16)
    e2_d = din("e2", [128, 16], BF16)

    yp = dout("yp", [(NBP - 1) * 128, D])
    ys = dout("ys", [128, D])
    pk = dout("pk", [NL, 128, 128])
    pv = dout("pv", [NL, 128, 128])
    pst = dout("pst", [NL, 4, 64, 128])
    sk = dout("sk", [NL, 16, 128, 128])
    sv = dout("sv", [NL, 16, 128, 128])
    sst = dout("sst", [NL, 16, 4, 64, 128])

    wsc_t = nc.dram_tensor("wsc", [NL, NCH, 128, CH], BF16, kind="ExternalOutput").ap()
    wsc = Buf("wsc", wsc_t)

    sb = P.sb
    identb = sb("identb", [128, 128], BF16)
    identf = sb("identf", [128, 128], F32)
    masks = sb("masks", [128, 5 * 128 + 8], BF16)
    umat = sb("umat", [128, 256], F32)
    valid = sb("valid", [128, NB], F32)
    eq = sb("eq", [64, 16, 128], BF16)
    e2 = sb("e2", [128, 16], BF16)
    g1 = sb("g1", [128, NL * 8], F32)
    g2 = sb("g2", [128, NL * 8], F32)
    gg = sb("gg", [128, NL], F32)
    qkg = sb("qkg", [64, NL * 2], F32)
    esink = sb("esink", [128, NL * 8], F32)
    wg2 = sb("wg2", [32, NL * 256], F32)
    bg = sb("bg", [32, NL * 256], F32)
    ones64 = sb("ones64", [64, 64], BF16)
    ones1 = sb("ones1", [32, 128], F32)
    epsb = sb("epsb", [128, 1], F32)
    epsb1 = sb("epsb1", [128, 1], F32)

    x = sb("x", [128, GM, D], F32)
    st4 = sb("st4", [128, 8], F32)
    hb = sb("hb", [128, D], BF16)
    hT = sb("hT", [128, 8, TM], BF16)
    qkraw = [sb("qkraw%d" % i, [64, TM], F32) for i in range(3)]
    qksq = [sb("qksq%d" % i, [64, TM], BF16) for i in range(3)]
    rqh = [sb("rqh%d" % i, [64, TM], F32) for i in range(3)]
    qn = sb("qn", [64, GM, 8, 128], BF16)
    kn32 = sb("kn32", [64, 2, TM], F32)
    kbuf = [sb("kbuf%d" % l, [64, GM + 1, 2, 128], BF16) for l in range(NL)]
    vaug = [sb("vaug%d" % l, [128, GM + 1, 2, 65], BF16) for l in range(NL)]
    gqraw = sb("gqraw", [64, 4, TM], BF16)
    gkraw = sb("gkraw", [64, 4, TM], BF16)
    glowT = sb("glowT", [32, TM], F32)
    v32 = sb("v32", [128, GM, 128], F32)
    gvb = sb("gvb", [128, GM, 512], BF16)
    sog = sb("sog", [128, GM, 512], BF16)
    UW = NFT * TM // 2
    A2W = UW + 2 * TM
    arena2 = P.stack.enter_context(nc.sbuf_tensor("s_arena2", [128, A2W], F32))

    def carve2(name, off, words, dtype, pat=None, parts=128, **kw):
        v = arena2[0:parts, off:off + words]
        if dtype is BF16:
            v = v.bitcast(BF16)
        if pat is not None:
            v = v.rearrange(pat, **kw)
        return Buf(name, v)

    uT = carve2("uT", 0, UW, BF16, "p (f t) -> p f t", f=NFT)
    sg = [carve2("sg%d" % i, UW + i * TM, TM, F32) for i in range(2)]
    tsets = []
    for i_ in range(NLANE):
        n_ = lambda s_: "%s_%d" % (s_, i_)
        if i_ == 0:
            tsets.append(dict(
                lnt=sb(n_("lnt"), [128, 256], F32), eG=sb(n_("eG"), [64, 4, 128], F32), enG=sb(n_("enG"), [64, 4, 128], F32),
                qtil=sb(n_("qtil"), [64, 4, 128], BF16), kt32=sb(n_("kt32"), [64, 4, 128], F32), ktil=sb(n_("ktil"), [64, 4, 128], BF16),
                khatT=sb(n_("khatT"), [64, 4, 128], BF16), khat=sb(n_("khat"), [128, 256], BF16), atm=sb(n_("atm"), [128, 4, 128], BF16),
                o32=sb(n_("o32"), [128, 4, 128], F32), gst=sb(n_("gst"), [128, 16], F32),
                PT=[sb(n_("PTa"), [128, 2, 512], BF16), sb(n_("PTb"), [128, 2, 512], BF16)],
                den=[sb(n_("dena"), [128, 8], F32), sb(n_("denb"), [128, 8], F32)]))
        else:
            assert i_ == 1
            o_ = [0]

            def c2(nm, words, dtype, pat=None, parts=128, **kw):
                b = carve2(n_(nm), o_[0], words, dtype, pat, parts, **kw)
                o_[0] += words
                return b
            ts1 = dict(
                lnt=c2("lnt", 256, F32), eG=c2("eG", 512, F32, "p (h t) -> p h t", parts=64, h=4), enG=c2("enG", 512, F32, "p (h t) -> p h t", parts=64, h=4),
                qtil=c2("qtil", 256, BF16, "p (h t) -> p h t", parts=64, h=4), kt32=c2("kt32", 512, F32, "p (h t) -> p h t", parts=64, h=4),
                ktil=c2("ktil", 256, BF16, "p (h t) -> p h t", parts=64, h=4), khatT=c2("khatT", 256, BF16, "p (h t) -> p h t", parts=64, h=4),
                khat=c2("khat", 128, BF16), atm=c2("atm", 256, BF16, "p (h t) -> p h t", h=4), o32=c2("o32", 512, F32, "p (h t) -> p h t", h=4),
                gst=c2("gst", 16, F32),
                PT=[c2("PTa", 512, BF16, "p (a t) -> p a t", a=2), c2("PTb", 512, BF16, "p (a t) -> p a t", a=2)],
                den=[c2("dena", 8, F32), c2("denb", 8, F32)])
            assert o_[0] <= A2W, (o_[0], A2W)
            tsets.append(ts1)
            flat = [v for v in ts1.values() if isinstance(v, Buf)] + ts1["PT"] + ts1["den"]
            for a_ in flat:
                for b_ in [uT] + sg:
                    a_.aliases.append(b_)
                    b_.aliases.append(a_)
    S = [sb("S%d" % l, [64, 4, 128], F32) for l in range(NL)]
    Sb = [sb("Sb%d" % l, [64, 4, 128], BF16) for l in range(NL)]
    merged = sb("merged", [128, GM, D], BF16)
    kt_out = sb("kt_out", [128, 128], F32)
    ring = [sb("ring%d" % i, [128, CH], BF16) for i in range(4)]

    AW = 12832
    arena = P.stack.enter_context(nc.sbuf_tensor("s_arena", [128, AW], F32))

    def carve(name, off, words, dtype, pat=None, parts=128, **kw):
        v = arena[0:parts, off:off + words]
        if dtype is BF16:
            v = v.bitcast(BF16)
        if pat is not None:
            v = v.rearrange(pat, **kw)
        return Buf(name, v)

    cst = carve("cst", 0, 2048, F32, "p (s c) -> p s c", s=16)
    ckb = carve("ckb", 2048, 1024, BF16, "p (s c) -> p s c", s=16)
    vc = carve("vc", 3072, 1040, BF16, "p (s g e) -> p s g e", s=16, g=2)
    ptc = carve("ptc", 4112, 512, BF16)
    khx = carve("khx", 4624, 512, BF16, "p (s c) -> p s c", s=4)
    kcT = carve("kcT", 5136, 2048, BF16, "p (s g t) -> p s g t", s=16, g=2, parts=64)
    otc = carve("otc", 7184, 1024, F32, "p (g h t) -> p g h t", g=2, h=4, parts=65)
    s0q = carve("s0q", 8208, 2048, F32, "p (s h v) -> p s h v", s=4, h=4, parts=64)
    s0b = carve("s0b", 10256, 1024, BF16, "p (s h v) -> p s h v", s=4, h=4, parts=64)
    qx = carve("qx", 11280, 1024, BF16, "p (h s t) -> p h s t", h=4, s=4, parts=64)
    qns = carve("qns", 12304, 512, BF16, "p (s g c) -> p s g c", s=16, g=2, parts=64)
    samp_bufs = [cst, ckb, vc, ptc, khx, kcT, otc, s0q, s0b, qx, qns]
    stg = [carve("stg%d" % i, 2048 * i, 2048, F32, "p (k w) -> p k w", k=8) for i in range(3)]
    cht = [carve("cht%d" % i, 6144 + 2048 * i, 2048, BF16) for i in range(3)]
    for a_ in stg + cht:
        for b_ in samp_bufs:
            a_.aliases.append(b_)
            b_.aliases.append(a_)

    pbF = [P.ps("pbF%d" % i, [128, 512], F32) for i in range(3)]
    pbT = [P.ps("pbT%d" % i, [128, 512], F32) for i in range(2)]
    pbA = [P.ps("pbA%d" % i, [128, 512], F32) for i in range(2)]
    pb16 = P.ps("pb16", [128, 1024], BF16)
    rr = {"F": 0, "T": 0, "A": 0}

    def bankF():
        rr["F"] += 1
        return pbF[rr["F"] % 3]

    def bankT():
        rr["T"] += 1
        return pbT[rr["T"] % 2]

    def bankA():
        rr["A"] += 1
        return pbA[rr["A"] % 2]

    def mm(out, lhsT, rhs, start, stop, r, w):
        P.op("pe", lambda e: e.matmul(out, lhsT, rhs, start=start, stop=stop), reads=r, writes=w)

    def tr(out, in_, ident, r, w):
        P.op("pe", lambda e: e.transpose(out, in_, ident), reads=r, writes=w)

    def act(out, in_, func, r, w, **kw):
        P.op("act", lambda e: e.activation(out, in_, func, **kw), reads=r, writes=w)

    def cp(eng, out, in_, r, w):
        if eng == "act":
            act(out, in_, AF.Copy, r, w)
        else:
            P.op(eng, lambda e: e.tensor_copy(out, in_), reads=r, writes=w)

    def tt(eng, out, a, b, op, r, w):
        P.op(eng, lambda e: e.tensor_tensor(out, a, b, op), reads=r, writes=w)

    def tsc(eng, out, a, s1, s2, op0, op1, r, w):
        if s2 is None:
            P.op(eng, lambda e: e.tensor_scalar(out, a, s1, None, op0), reads=r, writes=w)
        else:
            P.op(eng, lambda e: e.tensor_scalar(out, a, s1, s2, op0, op1), reads=r, writes=w)

    def stt(eng, out, a, s, b, op0, op1, r, w):
        P.op(eng, lambda e: e.scalar_tensor_tensor(out, a, s, b, op0, op1), reads=r, writes=w)

    def mset(eng, buf, ap, val):
        P.op(eng, lambda e: e.memset(ap, val), writes=[buf])

    cld = Buf("cld", None)
    cbufs = []
    for b_, d_ in ((identb, identb_d), (identf, identf_d), (masks, masks_d), (umat, umat_d), (valid, valid_d),
                   (e2, e2_d), (g1, g1_d), (g2, g2_d), (gg, gg_d), (qkg, qkg_d), (esink, snk_d), (wg2, wg2_d), (bg, bg_d)):
        P.dma("pool", cld, b_.t[:, :], d_, writes=[b_])
        cbufs.append(b_)
    P.dma("pool", cld, eq.t[:, :, :], eq_d.rearrange("p (s t) -> p s t", s=16), writes=[eq])
    cbufs.append(eq)
    for b_ in cbufs:
        b_.lw = {cld.dsem["pool"]: cld.dcount["pool"]}
    act(esink.t[:, :], esink.t[:, :], AF.Exp, [esink], [esink])
    mset("dve", ones64, ones64.t[:, :], 1.0)
    mset("dve", ones1, ones1.t[:, :], 1.0)
    mset("dve", epsb, epsb.t[:, :], EPS)
    mset("dve", epsb1, epsb1.t[:, :], 1.0)
    for l in range(NL):
        mset("dve", S[l], S[l].t[:, :, :], 0.0)
        mset("dve", Sb[l], Sb[l].t[:, :, :], 0.0)
        mset("pool", kbuf[l], kbuf[l].t[:, :, :, :], 0.0)
        mset("pool", vaug[l], vaug[l].t[:, :, :, :], 0.0)
        mset("pool", vaug[l], vaug[l].t[:, :, :, 64:65], 1.0)
    m_own = masks.t[:, 0:128]
    m_prev = masks.t[:, 128:256]
    m_own0 = masks.t[:, 256:384]
    m_samp = masks.t[:, 384:512]
    m_prev1 = masks.t[:, 512:640]
    m_cache = masks.t[:, 640:648]
    pth = Buf("pth", None)

    prep_state = {"stg": 0, "cht": 0, "eng": 0}
    for ct_ in cht:
        mset("pool", ct_, ct_.t[:, :], 0.0)

    def kview(ct, W):
        return ct.t[:, 0:8 * W].rearrange("p (k w) -> p k w", k=8)

    def prep_piece(ct, dst, src, kc, wdt, gain):
        s_ = stg[prep_state["stg"] % 3]
        prep_state["stg"] += 1
        q_ = "act" if prep_state["stg"] % 2 else "sp"
        P.load(q_, s_, s_.t[:, 0:kc, 0:wdt], src)
        eng = ("dve", "pool")[prep_state["eng"] % 2]
        prep_state["eng"] += 1
        if gain is None:
            if prep_state["eng"] % 3 == 0:
                eng = "act"
            cp(eng, dst, s_.t[:, 0:kc, 0:wdt], [s_], [ct])
        else:
            gbuf, gap = gain
            tt(eng, dst, s_.t[:, 0:kc, 0:wdt], gap.unsqueeze(2).broadcast_to([128, kc, wdt]), ALU.mult, [s_, gbuf], [ct])

    prep_tasks = {}
    wsc_c = {(l_, c_): Buf("wsc_%d_%d" % (l_, c_), None) for l_ in range(NL) for c_ in range(NCH)}

    def prep_chunk(l, c, pieces, zero=None):
        def task(ct):
            if zero is not None:
                mset("dve", ct, kview(ct, zero[0])[:, :, zero[1]:zero[2]], 0.0)
            for (dst_fn, src, kc, wdt, gain) in pieces:
                prep_piece(ct, dst_fn(ct), src, kc, wdt, gain)
            for q4 in range(4):
                P.dma("sp", ct, wsc_t[l, c][:, q4 * 1024:(q4 + 1) * 1024], ct.t[:, q4 * 1024:(q4 + 1) * 1024], reads=[ct], writes=[wsc_c[(l, c)]], ignore_waw=True)
        prep_tasks[(l, c)] = task

    for l in range(NL):
        wi = w_in[l].rearrange("(k p) c -> p k c", p=128)
        wo_ = w_o[l].rearrange("(k p) c -> p k c", p=128)
        wg_ = w_gate[l].rearrange("(k p) c -> p k c", p=128)
        wu_ = w_up[l].rearrange("(k p) c -> p k c", p=128)
        wd_ = w_down[l].rearrange("(k p) c -> p k c", p=128)
        G1 = (g1, g1.t[:, l * 8:(l + 1) * 8])
        G2 = (g2, g2.t[:, l * 8:(l + 1) * 8])

        def cols(W, d0, s0, n, src, gain):
            out = []
            for o in range(0, n, 256):
                wdt = min(256, n - o)
                out.append(((lambda ct, W=W, a=d0 + o, wdt=wdt: kview(ct, W)[:, :, a:a + wdt]), src[:, :, s0 + o:s0 + o + wdt], 8, wdt, gain))
            return out

        prep_chunk(l, 0, cols(512, 0, CQ, 512, wi, G1))
        prep_chunk(l, 1, cols(384, 0, CK, 128, wi, G1) + cols(384, 128, CGQ, 256, wi, G1))
        prep_chunk(l, 2, cols(288, 0, CGK, 256, wi, G1) + cols(288, 256, CGL, 16, wi, G1), zero=(288, 272, 288))
        prep_chunk(l, 3, cols(128, 0, CV, 128, wi, G1))
        prep_chunk(l, 4, cols(512, 0, CGV, 512, wi, G1))
        prep_chunk(l, 5, cols(512, 0, COG, 512, wi, G1))
        for hf in range(2):
            pcs = []
            for o in range(0, 512, 256):
                a = hf * 512 + o
                pcs.append(((lambda ct, o=o: kview(ct, 512)[:, 0:4, o:o + 256]), wo_[:, 0:4, a:a + 256], 4, 256, None))
                pcs.append(((lambda ct, o=o: kview(ct, 512)[:, 4:8, o:o + 256]), wo_[:, 4:8, a:a + 256], 4, 256,
                            (gg, gg.t[:, l:l + 1].broadcast_to([128, 4]))))
            prep_chunk(l, 6 + hf, pcs)
        for i in range(11):
            prep_chunk(l, 8 + i, cols(512, 0, 256 * i, 256, wg_, G2) + cols(512, 256, 256 * i, 256, wu_, G2))
        for j in range(6):
            nft = min(4, NFT - 4 * j)
            pcs = []
            for o in range(0, 1024, 256):
                pcs.append(((lambda ct, o=o, nft=nft: ct.t[:, :].rearrange("p (k w) -> p k w", k=4)[:, 0:nft, o:o + 256]),
                            wd_[:, 4 * j:4 * j + nft, o:o + 256], nft, 256, None))
            prep_chunk(l, 19 + j, pcs)

    wseq = [(l, c) for _g in GROUPS for l in range(NL) for c in range(NCH)]
    wst = {"issued": 0}

    NPREP = NL * NCH

    def need(i):
        c_ = i % NCH
        if i < NPREP:
            lim = (i - c_ + 2) if c_ <= 2 else i + 2
        else:
            lim = (i - c_ + 3) if c_ <= 2 else i + 3
        while wst["issued"] < min(len(wseq), lim + 1):
            j = wst["issued"]
            l_, c_ = wseq[j]
            if j < NPREP:
                prep_tasks[(l_, c_)](cht[j % 3])
            else:
                rb = ring[j % 4]
                for q4 in range(4):
                    P.dma("sp", rb, rb.t[:, q4 * 1024:(q4 + 1) * 1024], wsc_t[l_, c_][:, q4 * 1024:(q4 + 1) * 1024], reads=[wsc_c[(l_, c_)]], writes=[rb], ignore_waw=(q4 > 0))
            wst["issued"] += 1
        return cht[i % 3] if i < NPREP else ring[i % 4]

    tb_state = {"i": 0}

    def tbank():
        tb_state["i"] += 1
        k = tb_state["i"] % 3
        if k == 0:
            return pb16, pb16.t[:, :]
        bkk = pbT[k - 1]
        return bkk, bkk.t[:, :].bitcast(BF16)

    def rmsnorm_T(j):
        mset("dve", st4, st4.t[:, 0:1], 0.0)
        act(hb.t[:, :], x.t[:, j, :], AF.Square, [x], [hb, st4], accum_out=st4.t[:, 0:1])
        act(st4.t[:, 2:3], st4.t[:, 0:1], AF.Ln, [st4], [st4], scale=1.0 / D, bias=epsb.t[:, 0:1])
        act(st4.t[:, 3:4], st4.t[:, 2:3], AF.Exp, [st4], [st4], scale=-0.5)
        tsc("dve", hb.t[:, :], x.t[:, j, :], st4.t[:, 3:4], None, ALU.mult, None, [x, st4], [hb])
        tbk, tbv = tbank()
        for kc in range(8):
            tr(tbv[:, kc * 128:(kc + 1) * 128], hb.t[:, kc * 128:(kc + 1) * 128], identb.t[:, :], [hb, identb], [tbk])
        cp("act", hT.t[:, :, j * 128:(j + 1) * 128], tbv.rearrange("p (k t) -> p k t", k=8), [tbk], [hT])

    try:
        for gi, blks in enumerate(GROUPS):
            nb = len(blks)
            T = nb * 128
            samp = (blks[0] == SB)
            for j, blk in enumerate(blks):
                P.load("pool", x, x.t[:, j, :], xin[blk * 128:(blk + 1) * 128, :], ignore_waw=(j > 0))
            for l in range(NL):
                wbase = (gi * NL + l) * NCH
                phase("norm1")
                for j in range(nb):
                    rmsnorm_T(j)
                if STOP == 2 and (gi, l) == STOPAT:
                    raise _Stop()
                phase("qk")
                W0 = need(wbase + 0)
                W0v = W0.t[:, :].rearrange("p (k w) -> p k w", k=8)
                W1 = need(wbase + 1)
                W1v = W1.t[:, 0:8 * 384].rearrange("p (k w) -> p k w", k=8)
                pend = []

                def qk_tail(h, sl):
                    bk2 = bankA()
                    mm(bk2.t[0:64, 0:T], ones64.t[:, :], qksq[sl].t[:, 0:T], True, True, [ones64, qksq[sl]], [bk2])
                    act(rqh[sl].t[:, 0:T], bk2.t[0:64, 0:T], AF.Ln, [bk2], [rqh[sl]], scale=1.0 / 64, bias=epsb.t[0:64, 0:1])
                    act(rqh[sl].t[:, 0:T], rqh[sl].t[:, 0:T], AF.Exp, [rqh[sl]], [rqh[sl]], scale=-0.5)
                    if h < 8:
                        stt("dve", qn.t[:, 0:nb, h, :], qkraw[sl].t[:, 0:T].rearrange("p (j t) -> p j t", t=128), qkg.t[:, 2 * l:2 * l + 1],
                            rqh[sl].t[:, 0:T].rearrange("p (j t) -> p j t", t=128), ALU.mult, ALU.mult, [qkraw[sl], qkg, rqh[sl]], [qn])
                    else:
                        stt("dve", kn32.t[:, h - 8, 0:T], qkraw[sl].t[:, 0:T], qkg.t[:, 2 * l + 1:2 * l + 2], rqh[sl].t[:, 0:T],
                            ALU.mult, ALU.mult, [qkraw[sl], qkg, rqh[sl]], [kn32])

                W2 = need(wbase + 2)
                W2v = W2.t[:, 0:8 * 288].rearrange("p (k w) -> p k w", k=8)
                extra = []

                def g_head(kind, h):
                    bk_ = bankT()
                    if kind == "gq":
                        lwf, Wb, dst, eng, rows = (lambda kc: W1v[:, kc, 128 + h * 64:128 + (h + 1) * 64]), W1, gqraw.t[:, h, 0:T], "act", 64
                    elif kind == "gk":
                        lwf, Wb, dst, eng, rows = (lambda kc: W2v[:, kc, h * 64:(h + 1) * 64]), W2, gkraw.t[:, h, 0:T], "dve", 64
                    else:
                        lwf, Wb, dst, eng, rows = (lambda kc: W2v[:, kc, 256:288]), W2, glowT.t[:, 0:T], "act", 32
                    for kc in range(8):
                        mm(bk_.t[0:rows, 0:T], lwf(kc), hT.t[:, kc, 0:T], kc == 0, kc == 7, [Wb, hT], [bk_])
                    dbuf = gqraw if kind == "gq" else (gkraw if kind == "gk" else glowT)
                    cp(eng, dst, bk_.t[0:rows, 0:T], [bk_], [dbuf])

                for h_ in range(4):
                    extra.append(lambda h_=h_: g_head("gq", h_))
                for h_ in range(4):
                    extra.append(lambda h_=h_: g_head("gk", h_))
                extra.append(lambda: g_head("gl", 0))
                for h in range(10):
                    sl = h % 3
                    bk = bankF()
                    for kc in range(8):
                        lw = W0v[:, kc, h * 64:(h + 1) * 64] if h < 8 else W1v[:, kc, (h - 8) * 64:(h - 7) * 64]
                        mm(bk.t[0:64, 0:T], lw, hT.t[:, kc, 0:T], kc == 0, kc == 7, [W0 if h < 8 else W1, hT], [bk])
                    cp("dve", qkraw[sl].t[:, 0:T], bk.t[0:64, 0:T], [bk], [qkraw[sl]])
                    act(qksq[sl].t[:, 0:T], bk.t[0:64, 0:T], AF.Square, [bk], [qksq[sl]])
                    if extra:
                        extra.pop(0)()
                    if pend:
                        pend.pop()()
                    pend.append(lambda h=h, sl=sl: qk_tail(h, sl))
                pend.pop()()
                for j, blk in enumerate(blks):
                    cp("pool", kbuf[l].t[:, j + 1, :, :], kn32.t[:, :, j * 128:(j + 1) * 128], [kn32], [kbuf[l]])
                phase("gqgk")
                while extra:
                    extra.pop(0)()
                if STOP == 4 and (gi, l) == STOPAT:
                    raise _Stop()
                phase("tokmaj")
                W3 = need(wbase + 3)
                W3v = W3.t[:, 0:8 * 128].rearrange("p (k w) -> p k w", k=8)
                for j, blk in enumerate(blks):
                    bk = bankT()
                    for kc in range(8):
                        mm(bk.t[:, 0:128], hT.t[:, kc, j * 128:(j + 1) * 128], W3v[:, kc, :], kc == 0, kc == 7, [W3, hT], [bk])
                    cp("act", v32.t[:, j, :], bk.t[:, 0:128], [bk], [v32])
                    cp("dve", vaug[l].t[:, j + 1, :, 0:64], bk.t[:, 0:128].rearrange("p (g d) -> p g d", g=2), [bk], [vaug[l]])
                    if j + 1 < nb:
                        pass
                W4 = need(wbase + 4)
                W4v = W4.t[:, :].rearrange("p (k w) -> p k w", k=8)
                for j in range(nb):
                    bk = bankT()
                    for kc in range(8):
                        mm(bk.t[:, :], hT.t[:, kc, j * 128:(j + 1) * 128], W4v[:, kc, :], kc == 0, kc == 7, [W4, hT], [bk])
                    cp("act", gvb.t[:, j, :], bk.t[:, :], [bk], [gvb])
                W5 = need(wbase + 5)
                W5v = W5.t[:, :].rearrange("p (k w) -> p k w", k=8)
                for j in range(nb):
                    bk = bankT()
                    for kc in range(8):
                        mm(bk.t[:, :], hT.t[:, kc, j * 128:(j + 1) * 128], W5v[:, kc, :], kc == 0, kc == 7, [W5, hT], [bk])
                    act(sog.t[:, j, :], bk.t[:, :], AF.Silu, [bk], [sog])

                if STOP == 5 and (gi, l) == STOPAT:
                    raise _Stop()
                phase("chains")
                if samp:
                    if l == 0:
                        mset("pool", vc, vc.t[:, :, :, 64:65], 1.0)
                    P.load("pool", cst, cst.t[:, :, :], ck_d[l].rearrange("s k c -> k s c"))
                    cp("pool", ckb.t[:, :, :], cst.t[:, :, :], [cst], [ckb])
                    for rnd in range(4):
                        for i in range(8):
                            s_, g_ = (rnd * 8 + i) // 2, (rnd * 8 + i) % 2
                            tr(pb16.t[0:64, i * 128:(i + 1) * 128], ckb.t[:, s_, g_ * 64:(g_ + 1) * 64], identb.t[:, :], [ckb, identb], [pb16])
                        cp("dve", kcT.t[:, rnd * 4:rnd * 4 + 4, :, :], pb16.t[0:64, :].rearrange("p (s g t) -> p s g t", s=4, g=2), [pb16], [kcT])
                    P.dma("pool", pth, sk[l, :, 0:120, :], ck_d[l, :, 8:128, :], reads=[], writes=[], out=True)
                    P.load("pool", cst, cst.t[:, :, :], cv_d[l].rearrange("s k c -> k s c"))
                    cp("pool", vc.t[:, :, :, 0:64], cst.t[:, :, :].rearrange("p s (g d) -> p s g d", g=2), [cst], [vc])
                    P.dma("pool", pth, sv[l, :, 0:120, :], cv_d[l, :, 8:128, :], reads=[], writes=[], out=True)
                    for g_ in range(2):
                        cp("pool", qns.t[:, :, g_, :].rearrange("p s (h t) -> p h s t", h=4), qn.t[:, 0, 4 * g_:4 * g_ + 4, :].rearrange("p h (s t) -> p h s t", t=8), [qn], [qns])
                    stc = [bankA(), bankA()]
                    for s_ in range(16):
                        for g_ in range(2):
                            bk = stc[s_ // 8]
                            o_ = ((s_ % 8) * 2 + g_) * 32
                            mm(bk.t[:, o_:o_ + 32], kcT.t[:, s_, g_, :], qns.t[:, s_, g_, :], True, True, [kcT, qns], [bk])
                    for hf in range(2):
                        act(ptc.t[:, hf * 512:(hf + 1) * 512], stc[hf].t[:, :], AF.Exp, [stc[hf]], [ptc], scale=0.125)
                    tt("pool", ptc.t[:, :].rearrange("p (a q) -> p a q", q=8), ptc.t[:, :].rearrange("p (a q) -> p a q", q=8),
                       m_cache.unsqueeze(1).broadcast_to([128, 128, 8]), ALU.mult, [ptc, masks], [ptc])
                    otb = [bankA(), bankA()]
                    for s_ in range(16):
                        for g_ in range(2):
                            bk = otb[s_ // 8]
                            o_ = ((s_ % 8) * 2 + g_) * 32
                            mm(bk.t[0:65, o_:o_ + 32], vc.t[:, s_, g_, :], ptc.t[:, (s_ * 2 + g_) * 32:(s_ * 2 + g_ + 1) * 32], True, True, [vc, ptc], [bk])
                    for hf in range(2):
                        for g_ in range(2):
                            cp("act" if g_ else "dve", otc.t[:, g_, :, hf * 64:(hf + 1) * 64].rearrange("p h (s t) -> p s h t", t=8),
                               otb[hf].t[0:65, :].rearrange("p (s g h t) -> p s g h t", s=8, g=2, h=4)[:, :, g_, :, :], [otb[hf]], [otc])

                def attn_chain(j, blk, g, ts):
                    pt = ts["PT"][g]
                    den = ts["den"][g]
                    rq_ = qn.t[:, j, 4 * g:4 * g + 4, :].rearrange("p h t -> p (h t)")
                    has_prev = (not samp) and blk > 0
                    sbk = yield from take(2 if has_prev else 1)
                    bo = sbk[0]
                    mm(bo.t[:, :], kbuf[l].t[:, j + 1, g, :], rq_, True, True, [kbuf[l], qn], [bo])
                    if has_prev:
                        bp = sbk[1]
                        mm(bp.t[:, :], kbuf[l].t[:, j, g, :], rq_, True, True, [kbuf[l], qn], [bp])
                    yield
                    act(pt.t[:, 0, :], bo.t[:, :], AF.Exp, [bo], [pt], scale=0.125)
                    if has_prev:
                        act(pt.t[:, 1, :], bp.t[:, :], AF.Exp, [bp], [pt], scale=0.125)
                    give(*sbk)
                    yield
                    mk = m_samp if samp else (m_own0 if blk == 0 else m_own)
                    tt("pool", pt.t[:, 0, :].rearrange("p (h t) -> p h t", h=4), pt.t[:, 0, :].rearrange("p (h t) -> p h t", h=4),
                       mk.unsqueeze(1).broadcast_to([128, 4, 128]), ALU.mult, [pt, masks], [pt])
                    if has_prev:
                        tt("pool", pt.t[:, 1, :].rearrange("p (h t) -> p h t", h=4), pt.t[:, 1, :].rearrange("p (h t) -> p h t", h=4),
                           (m_prev1 if blk == 1 else m_prev).unsqueeze(1).broadcast_to([128, 4, 128]), ALU.mult, [pt, masks], [pt])
                    yield
                    bv_ = (yield from take(1))[0]
                    for h in range(4):
                        oc = bv_.t[:, h * 65:(h + 1) * 65]
                        last_own = not (has_prev or samp)
                        mm(oc, pt.t[:, 0, h * 128:(h + 1) * 128], vaug[l].t[:, j + 1, g, :], True, last_own, [pt, vaug[l]], [bv_])
                        if has_prev:
                            mm(oc, pt.t[:, 1, h * 128:(h + 1) * 128], vaug[l].t[:, j, g, :], False, True, [pt, vaug[l]], [bv_])
                        if samp:
                            mm(oc, otc.t[:, g, h, :], identf.t[0:65, 0:65], False, True, [otc, identf], [bv_])
                    yield
                    pv4 = bv_.t[:, 0:260].rearrange("p (h e) -> p h e", h=4)
                    tt("dve", den.t[:, 0:4], pv4[:, :, 64], esink.t[:, l * 8 + 4 * g:l * 8 + 4 * g + 4], ALU.add, [bv_, esink], [den])
                    P.op("dve", lambda e: e.reciprocal(den.t[:, 4:8], den.t[:, 0:4]), reads=[den], writes=[den])
                    tt("dve", merged.t[:, j, g * 256:(g + 1) * 256].rearrange("p (h d) -> p h d", h=4), pv4[:, :, 0:64],
                       den.t[:, 4:8].unsqueeze(2).broadcast_to([128, 4, 64]), ALU.mult, [bv_, den], [merged])
                    give(bv_)

                def gla_chain(j, blk, ts):
                    lnt, eG, enG, qtil, kt32, ktil = ts["lnt"], ts["eG"], ts["enG"], ts["qtil"], ts["kt32"], ts["ktil"]
                    khatT, khat, atm, o32, gst = ts["khatT"], ts["khat"], ts["atm"], ts["o32"], ts["gst"]
                    c0, c1 = j * 128, (j + 1) * 128
                    bl = (yield from take(1))[0]
                    mm(bl.t[:, 0:256], glowT.t[:, c0:c1], wg2.t[:, l * 256:(l + 1) * 256], True, False, [glowT, wg2], [bl])
                    mm(bl.t[:, 0:256], ones1.t[:, :], bg.t[:, l * 256:(l + 1) * 256], False, True, [ones1, bg], [bl])
                    yield
                    act(lnt.t[:, :], bl.t[:, 0:256], AF.Exp, [bl], [lnt], scale=-1.0)
                    give(bl)
                    act(lnt.t[:, :], lnt.t[:, :], AF.Ln, [lnt], [lnt], bias=epsb1.t[:, 0:1])
                    yield
                    U = umat.t[:, 128:256] if samp else umat.t[:, 0:128]
                    gmask = m_samp if samp else m_own
                    bgT = (yield from take(1))[0]
                    for h in range(4):
                        mm(bgT.t[0:64, h * 128:(h + 1) * 128], lnt.t[:, h * 64:(h + 1) * 64], U, True, True, [lnt, umat], [bgT])
                    yield
                    g4 = bgT.t[0:64, :].rearrange("p (h t) -> p h t", h=4)
                    act(eG.t[:, :, :], g4, AF.Exp, [bgT], [eG])
                    act(enG.t[:, :, :], g4, AF.Exp, [bgT], [enG], scale=-1.0)
                    give(bgT)
                    yield
                    stt("dve", qtil.t[:, :, :], gqraw.t[:, :, c0:c1], 0.125, eG.t[:, :, :], ALU.mult, ALU.mult, [gqraw, eG], [qtil])
                    tt("dve", kt32.t[:, :, :], gkraw.t[:, :, c0:c1], enG.t[:, :, :], ALU.mult, [gkraw, enG], [kt32])
                    yield
                    cp("pool", ktil.t[:, :, :], kt32.t[:, :, :], [kt32], [ktil])
                    if samp:
                        egl = eG.t[:, :, :].rearrange("p h (s t) -> p h s t", t=8)[:, :, :, 7:8].broadcast_to([64, 4, 16, 8])
                        tt("pool", khatT.t[:, :, :].rearrange("p h (s t) -> p h s t", t=8), kt32.t[:, :, :].rearrange("p h (s t) -> p h s t", t=8),
                           egl, ALU.mult, [kt32, eG], [khatT])
                    else:
                        egl = eG.t[:, :, 127:128].broadcast_to([64, 4, 128])
                        tt("pool", khatT.t[:, :, :], kt32.t[:, :, :], egl, ALU.mult, [kt32, eG], [khatT])
                    yield
                    for h in range(4):
                        tr(pb16.t[:, h * 64:(h + 1) * 64], khatT.t[:, h, :], identb.t[0:64, 0:64], [khatT, identb], [pb16])
                    cp("dve", khat.t[:, :], pb16.t[:, 0:256], [pb16], [khat])
                    ba = (yield from take(1))[0]
                    for h in range(4):
                        mm(ba.t[:, h * 128:(h + 1) * 128], ktil.t[:, h, :], qtil.t[:, h, :], True, True, [ktil, qtil], [ba])
                    yield
                    tt("dve", atm.t[:, :, :], ba.t[:, :].rearrange("p (h t) -> p h t", h=4), gmask.unsqueeze(1).broadcast_to([128, 4, 128]), ALU.mult, [ba, masks], [atm])
                    give(ba)
                    yield
                    if not samp:
                        bo, bs = yield from take(2)
                        for h in range(4):
                            oc = bo.t[:, h * 128:(h + 1) * 128]
                            mm(oc, atm.t[:, h, :], gvb.t[:, j, h * 128:(h + 1) * 128], True, False, [atm, gvb], [bo])
                            mm(oc, qtil.t[:, h, :], Sb[l].t[:, h, :], False, True, [qtil, Sb[l]], [bo])
                        for h in range(4):
                            mm(bs.t[0:64, h * 128:(h + 1) * 128], khat.t[:, h * 64:(h + 1) * 64], gvb.t[:, j, h * 128:(h + 1) * 128], True, True, [khat, gvb], [bs])
                        tt("dve", S[l].t[:, :, :], S[l].t[:, :, :], eG.t[:, :, 127:128].broadcast_to([64, 4, 128]), ALU.mult, [S[l], eG], [S[l]])
                        tt("dve", S[l].t[:, :, :], S[l].t[:, :, :], bs.t[0:64, :].rearrange("p (h v) -> p h v", h=4), ALU.add, [S[l], bs], [S[l]])
                        cp("pool", Sb[l].t[:, :, :], S[l].t[:, :, :], [S[l]], [Sb[l]])
                        give(bs)
                    else:
                        obk = yield from take(4)
                        for h in range(4):
                            mm(obk[h].t[:, 0:128], atm.t[:, h, :], gvb.t[:, j, h * 128:(h + 1) * 128], True, False, [atm, gvb], [obk[h]])
                        eg4 = eG.t[:, :, :].rearrange("p h (s t) -> p h s t", t=8)
                        for qd in range(4):
                            P.load("pool", s0q, s0q.t[:, :, :, :], st_d[l, 4 * qd:4 * qd + 4].rearrange("s h k v -> k s h v"))
                            cp("pool", s0b.t[:, :, :, :], s0q.t[:, :, :, :], [s0q], [s0b])
                            tt("dve", qx.t[:, :, :, :], qtil.t[:, :, :].unsqueeze(2).broadcast_to([64, 4, 4, 128]),
                               eq.t[:, 4 * qd:4 * qd + 4, :].unsqueeze(1).broadcast_to([64, 4, 4, 128]), ALU.mult, [qtil, eq], [qx])
                            tt("pool", khx.t[:, :, :], khat.t[:, :].unsqueeze(1).broadcast_to([128, 4, 256]),
                               e2.t[:, 4 * qd:4 * qd + 4].unsqueeze(2).broadcast_to([128, 4, 256]), ALU.mult, [khat, e2], [khx])
                            for h in range(4):
                                for s_ in range(4):
                                    last = (qd == 3 and s_ == 3)
                                    mm(obk[h].t[:, 0:128], qx.t[:, h, s_, :], s0b.t[:, s_, h, :], False, last, [qx, s0b], [obk[h]])
                            for s_ in range(4):
                                bs = (yield from take(1))[0]
                                for h in range(4):
                                    mm(bs.t[0:64, h * 128:(h + 1) * 128], khx.t[:, s_, h * 64:(h + 1) * 64], gvb.t[:, j, h * 128:(h + 1) * 128], True, True, [khx, gvb], [bs])
                                sa = 4 * qd + s_
                                tt("dve", s0q.t[:, s_, :, :], s0q.t[:, s_, :, :], eg4[:, :, sa, 7:8].broadcast_to([64, 4, 128]), ALU.mult, [s0q, eG], [s0q])
                                tt("dve", s0q.t[:, s_, :, :], s0q.t[:, s_, :, :], bs.t[0:64, :].rearrange("p (h v) -> p h v", h=4), ALU.add, [s0q, bs], [s0q])
                                give(bs)
                            P.store("pool", s0q, sst[l, 4 * qd:4 * qd + 4].rearrange("s h k v -> k s h v"), s0q.t[:, :, :, :])
                    yield
                    if samp:
                        for h in range(4):
                            cp("act" if h % 2 else "dve", o32.t[:, h, :], obk[h].t[:, 0:128], [obk[h]], [o32])
                        give(*obk)
                    else:
                        cp("act", o32.t[:, :, :], bo.t[:, :].rearrange("p (h v) -> p h v", h=4), [bo], [o32])
                        give(bo)
                    mset("dve", gst, gst.t[:, 0:4], 0.0)
                    yield
                    for h in range(4):
                        act(atm.t[:, h, :], o32.t[:, h, :], AF.Square, [o32], [atm, gst], accum_out=gst.t[:, h:h + 1])
                    act(gst.t[:, 4:8], gst.t[:, 0:4], AF.Ln, [gst], [gst], scale=1.0 / 128, bias=epsb.t[:, 0:1])
                    act(gst.t[:, 8:12], gst.t[:, 4:8], AF.Exp, [gst], [gst], scale=-0.5)
                    yield
                    tt("dve", o32.t[:, :, :], o32.t[:, :, :], gst.t[:, 8:12].unsqueeze(2).broadcast_to([128, 4, 128]), ALU.mult, [o32, gst], [o32])
                    yield
                    tt("pool", merged.t[:, j, 512:1024], o32.t[:, :, :].rearrange("p h v -> p (h v)"), sog.t[:, j, :], ALU.mult, [o32, sog], [merged])
                    if blk == NBP - 1 or samp:
                        tb = (yield from take(1))[0]
                        for g in range(2):
                            tr(tb.t[:, g * 64:(g + 1) * 64], kn32.t[:, g, c0:c1], identf.t[0:64, 0:64], [kn32, identf], [tb])
                        cp("dve", kt_out.t[:, :], tb.t[:, 0:128], [tb], [kt_out])
                        give(tb)
                        if samp:
                            for s_ in range(16):
                                P.store("pool", kt_out, sk[l, s_, 120:128, :], kt_out.t[8 * s_:8 * s_ + 8, :])
                                P.store("pool", v32, sv[l, s_, 120:128, :], v32.t[8 * s_:8 * s_ + 8, j, :])
                        else:
                            P.store("pool", kt_out, pk[l], kt_out.t[:, :])
                            P.store("pool", v32, pv[l], v32.t[:, j, :])
                            P.store("pool", S[l], pst[l].rearrange("h k v -> k h v"), S[l].t[:, :, :])

                freeb = list(pbF) + list(pbT) + list(pbA)

                def take(n):
                    spins = 0
                    while len(freeb) < n:
                        spins += 1
                        assert spins < 10000, "psum bank deadlock"
                        yield
                    return [freeb.pop(0) for _ in range(n)]

                def give(*bs_):
                    freeb.extend(bs_)

                lanes = [[(j, blk) for j, blk in enumerate(blks) if j % NLANE == ln] for ln in range(NLANE)]
                active = [[] for _ in range(NLANE)]
                while any(lanes) or any(active):
                    for ln in range(NLANE):
                        if not active[ln] and lanes[ln]:
                            j, blk = lanes[ln].pop(0)
                            active[ln] = [attn_chain(j, blk, 0, tsets[ln]), attn_chain(j, blk, 1, tsets[ln]), gla_chain(j, blk, tsets[ln])]
                        for gen in list(active[ln]):
                            try:
                                next(gen)
                            except StopIteration:
                                active[ln].remove(gen)
                if STOP == 7 and (gi, l) == STOPAT:
                    raise _Stop()
                phase("wo")
                if not samp:
                    cp("pool", kbuf[l].t[:, 0, :, :], kbuf[l].t[:, nb, :, :], [kbuf[l]], [kbuf[l]])
                    cp("pool", vaug[l].t[:, 0, :, 0:64], vaug[l].t[:, nb, :, 0:64], [vaug[l]], [vaug[l]])
                for j in range(nb):
                    tbk, tbv = tbank()
                    for kc in range(8):
                        tr(tbv[:, kc * 128:(kc + 1) * 128], merged.t[:, j, kc * 128:(kc + 1) * 128], identb.t[:, :], [merged, identb], [tbk])
                    cp("dve" if j % 2 else "act", hT.t[:, :, j * 128:(j + 1) * 128], tbv.rearrange("p (k t) -> p k t", k=8), [tbk], [hT])
                for hf in range(2):
                    Wc = need(wbase + 6 + hf)
                    Wv = Wc.t[:, :].rearrange("p (k w) -> p k w", k=8)
                    for j, blk in enumerate(blks):
                        bk = bankT()
                        for kc in range(8):
                            mm(bk.t[:, :], hT.t[:, kc, j * 128:(j + 1) * 128], Wv[:, kc, :], kc == 0, kc == 7, [Wc, hT], [bk])
                        stt("dve", x.t[:, j, hf * 512:(hf + 1) * 512], bk.t[:, :], valid.t[:, blk:blk + 1], x.t[:, j, hf * 512:(hf + 1) * 512], ALU.mult, ALU.add, [bk, valid, x], [x])

                if STOP == 8 and (gi, l) == STOPAT:
                    raise _Stop()
                phase("ffn_gu")
                for j in range(nb):
                    rmsnorm_T(j)
                for i in range(11):
                    Wc = need(wbase + 8 + i)
                    Wv = Wc.t[:, :].rearrange("p (k a w) -> p k a w", k=8, a=2)
                    for jj in range(2):
                        ft = 2 * i + jj
                        bg_ = bankF()
                        bu_ = bankA()
                        for kc in range(8):
                            mm(bg_.t[:, 0:T], Wv[:, kc, 0, jj * 128:(jj + 1) * 128], hT.t[:, kc, 0:T], kc == 0, kc == 7, [Wc, hT], [bg_])
                        for kc in range(8):
                            mm(bu_.t[:, 0:T], Wv[:, kc, 1, jj * 128:(jj + 1) * 128], hT.t[:, kc, 0:T], kc == 0, kc == 7, [Wc, hT], [bu_])
                        sg_ = sg[ft % 2]
                        act(sg_.t[:, 0:T], bg_.t[:, 0:T], AF.Silu, [bg_], [sg_])
                        tt("dve", uT.t[:, ft, 0:T], sg_.t[:, 0:T], bu_.t[:, 0:T], ALU.mult, [sg_, bu_], [uT])
                phase("ffn_down")
                dbk = [pbT[0], pbT[1], pbA[0], pbA[1], pbF[0], pbF[1]]
                for ft in range(NFT):
                    Wc = need(wbase + 19 + ft // 4)
                    Wv = Wc.t[:, :].rearrange("p (k w) -> p k w", k=4)
                    for j in range(nb):
                        for hf in range(2):
                            bd = dbk[j * 2 + hf]
                            mm(bd.t[:, :], uT.t[:, ft, j * 128:(j + 1) * 128], Wv[:, ft % 4, hf * 512:(hf + 1) * 512], ft == 0, ft == NFT - 1, [Wc, uT], [bd])
                for j, blk in enumerate(blks):
                    for hf in range(2):
                        bd = dbk[j * 2 + hf]
                        stt("dve", x.t[:, j, hf * 512:(hf + 1) * 512], bd.t[:, :], valid.t[:, blk:blk + 1], x.t[:, j, hf * 512:(hf + 1) * 512], ALU.mult, ALU.add, [bd, valid, x], [x])
            for j, blk in enumerate(blks):
                if samp:
                    P.dma("pool", x, ys, x.t[:, j, :], reads=[x], out=True)
                elif blk >= 1:
                    P.dma("pool", x, yp[(blk - 1) * 128:blk * 128, :], x.t[:, j, :], reads=[x], out=True)
    except _Stop:
        pass
    P.finish()
    return nc


def _consts():
    bf = ml_dtypes.bfloat16
    j = np.arange(128)[:, None]
    i = np.arange(128)[None, :]
    own = (j <= i)
    prev = (j > i)
    own0 = own & (j >= 112)
    samp = (j // 8 == i // 8) & (j % 8 <= i % 8)
    cache = (np.arange(128)[:, None] > np.arange(8)[None, :])
    prev1 = prev & (j >= 112)
    masks = np.concatenate([own, prev, own0, samp, prev1, cache], axis=1).astype(np.float32).astype(bf)
    umat = np.concatenate([own.astype(np.float32), samp.astype(np.float32)], axis=1) * np.float32(-1.0 / 16.0)
    valid = np.ones((128, NB), np.float32)
    valid[0:112, 0] = 0.0
    t = np.arange(128)
    e2 = (t[:, None] // 8 == np.arange(16)[None, :]).astype(np.float32)
    eq = np.broadcast_to(e2.T[None, :, :], (64, 16, 128)).reshape(64, 16 * 128)
    return dict(identb=np.eye(128, dtype=np.float32).astype(bf), identf=np.eye(128, dtype=np.float32),
                masks=masks, umat=np.ascontiguousarray(umat.astype(np.float32)), valid=valid,
                eq=np.ascontiguousarray(eq).astype(bf), e2=e2.astype(bf))


_NC_CACHE = {}


def kernel(**inp):
    f = lambda a: np.ascontiguousarray(np.asarray(a, dtype=np.float32))
    x_prompt, x_sample = f(inp["x_prompt"]), f(inp["x_sample"])
    cache_k, cache_v, state_gla = f(inp["cache_k"]), f(inp["cache_v"]), f(inp["state_gla"])
    meta = f(inp["meta"])
    norm1, norm2, gla_norm = f(inp["norm1"]), f(inp["norm2"]), f(inp["gla_norm"])
    q_norm, k_norm, sinks = f(inp["q_norm"]), f(inp["k_norm"]), f(inp["sinks"])
    w_g2, b_g = f(inp["w_g2"]), f(inp["b_g"])
    common = dict(
        w_in=f(inp["w_in"]), w_o=f(inp["w_o"]), w_gate=f(inp["w_gate"]), w_up=f(inp["w_up"]), w_down=f(inp["w_down"]),
        g1=np.ascontiguousarray(norm1.reshape(NL, 8, 128).transpose(2, 0, 1).reshape(128, NL * 8)),
        g2=np.ascontiguousarray(norm2.reshape(NL, 8, 128).transpose(2, 0, 1).reshape(128, NL * 8)),
        gg=np.ascontiguousarray(gla_norm.T),
        qkg=np.ascontiguousarray(np.stack([q_norm[0], k_norm[0], q_norm[1], k_norm[1]], axis=1)),
        snk=np.ascontiguousarray(np.broadcast_to(sinks.reshape(1, NL * 8), (128, NL * 8))),
        wg2=np.ascontiguousarray(np.concatenate([w_g2.transpose(1, 0, 2).reshape(16, NL * 256), np.zeros((16, NL * 256), np.float32)], 0)),
        bg=np.ascontiguousarray(np.concatenate([b_g.reshape(1, NL * 256), np.zeros((31, NL * 256), np.float32)], 0)),
    )
    common.update(_consts())
    in_maps = []
    for c in range(8):
        seq = c % 4
        xin = np.zeros((NB * 128, D), np.float32)
        xin[112:128] = meta
        xin[128:NBP * 128] = x_prompt[seq]
        xin[NBP * 128:] = x_sample[16 * c:16 * c + 16].reshape(128, D)
        m = dict(common)
        m["xin"] = xin
        m["ck"] = np.ascontiguousarray(cache_k[:, 16 * c:16 * c + 16].reshape(NL, 16, 128, 128))
        m["cv"] = np.ascontiguousarray(cache_v[:, 16 * c:16 * c + 16].reshape(NL, 16, 128, 128))
        m["st"] = np.ascontiguousarray(state_gla[:, 16 * c:16 * c + 16])
        in_maps.append(m)
    if "nc" not in _NC_CACHE:
        _NC_CACHE["nc"] = build()
    res = run_bass_kernel_spmd(_NC_CACHE["nc"], in_maps, core_ids=list(range(8)))
    R = res.results
    y_prompt = np.stack([R[c]["yp"] for c in range(4)], axis=0).astype(np.float32)
    y_sample = np.concatenate([R[c]["ys"].reshape(16, 8, D) for c in range(8)], axis=0).astype(np.float32)
    pk = np.stack([R[c]["pk"].reshape(NL, 128, 2, 64) for c in range(4)], axis=1).astype(np.float32)
    pv = np.stack([R[c]["pv"].reshape(NL, 128, 2, 64) for c in range(4)], axis=1).astype(np.float32)
    pst = np.stack([R[c]["pst"] for c in range(4)], axis=1).astype(np.float32)
    sk = np.concatenate([R[c]["sk"].reshape(NL, 16, 128, 2, 64) for c in range(8)], axis=1).astype(np.float32)
    sv = np.concatenate([R[c]["sv"].reshape(NL, 16, 128, 2, 64) for c in range(8)], axis=1).astype(np.float32)
    sst = np.concatenate([R[c]["sst"] for c in range(8)], axis=1).astype(np.float32)
    return (y_prompt, y_sample, pk, pv, pst, sk, sv, sst)
```

```python
import contextlib
import numpy as np
import ml_dtypes
import concourse.bass as bass
import concourse.mybir as mybir
from concourse.bass_utils import run_bass_kernel_spmd

F32 = mybir.dt.float32
BF16 = mybir.dt.bfloat16
AF = mybir.ActivationFunctionType
ALU = mybir.AluOpType
AX = mybir.AxisListType


class Buf:
    def __init__(self, name, t):
        self.name = name
        self.t = t
        self.lw = {}
        self.rd = {}
        self.dsem = None
        self.dcount = 0
        self.excl = False
        self.aliases = []


class Prog:
    ENGS = ("pe", "act", "dve", "pool", "sp")

    def __init__(self, nc):
        self.nc = nc
        self.stack = contextlib.ExitStack()
        self.sems = {}
        self.count = {e: 0 for e in self.ENGS}
        self.seen = {e: {} for e in self.ENGS}
        self.ops = {e: [] for e in self.ENGS}
        for e in self.ENGS:
            self.sems["E_" + e] = self.stack.enter_context(nc.semaphore("sem_" + e))
        self.out_tokens = {}
        self.nbufs = 0

    def sb(self, name, shape, dtype):
        t = self.stack.enter_context(self.nc.sbuf_tensor("s_" + name, list(shape), dtype))
        return Buf(name, t)

    def ps(self, name, shape, dtype):
        t = self.stack.enter_context(self.nc.psum_tensor(name, list(shape), dtype))
        b = Buf(name, t)
        b.excl = True
        return b

    def view(self, name, t):
        return Buf(name, t)

    def _dsem(self, buf, queue):
        if buf.dsem is None:
            buf.dsem = {}
            buf.dcount = {}
        if queue not in buf.dsem:
            key = "D_%d_%s_%s" % (self.nbufs, buf.name, queue)
            self.nbufs += 1
            self.sems[key] = self.stack.enter_context(self.nc.semaphore("dsem_%d" % self.nbufs))
            buf.dsem[queue] = key
            buf.dcount[queue] = 0
        return buf.dsem[queue]

    def _waits(self, eng, reads, writes, ignore_waw=False):
        need = {}
        for b in reads:
            for k, v in b.lw.items():
                if need.get(k, 0) < v:
                    need[k] = v
            if b.excl:
                for k, v in b.rd.items():
                    if k != "E_" + eng and need.get(k, 0) < v:
                        need[k] = v
        for b in writes:
            if not ignore_waw:
                for k, v in b.lw.items():
                    if need.get(k, 0) < v:
                        need[k] = v
            for k, v in b.rd.items():
                if need.get(k, 0) < v:
                    need[k] = v
            for al in b.aliases:
                for dd in (al.lw, al.rd):
                    for k, v in dd.items():
                        if need.get(k, 0) < v:
                            need[k] = v
        out = []
        seen = self.seen[eng]
        for k, v in need.items():
            if eng == "pe" and k == "E_pe":
                continue
            if seen.get(k, 0) < v:
                seen[k] = v
                out.append((k, v))
        return out

    def _commit(self, tok, reads, writes, ignore_waw=False):
        k, v = tok
        for b in reads:
            if b.rd.get(k, 0) < v:
                b.rd[k] = v
        for b in writes:
            if ignore_waw:
                b.lw[k] = v
            else:
                b.lw = {k: v}
            b.rd = {}

    def op(self, eng, fn, reads=(), writes=()):
        waits = self._waits(eng, reads, writes)
        self.count[eng] += 1
        tok = ("E_" + eng, self.count[eng])
        self._commit(tok, reads, writes)
        self.ops[eng].append((waits, fn, tok[0], 1))
        return tok

    def dma(self, queue, sem_buf, out_ap, in_ap, reads=(), writes=(), out=False, ignore_waw=False):
        waits = self._waits(queue, reads, writes, ignore_waw=ignore_waw)
        key = self._dsem(sem_buf, queue)
        sem_buf.dcount[queue] += 16
        tok = (key, sem_buf.dcount[queue])
        self._commit(tok, reads, writes, ignore_waw=ignore_waw)
        self.ops[queue].append((waits, (lambda e: e.dma_start(out=out_ap, in_=in_ap)), key, 16))
        if out:
            self.out_tokens[key] = sem_buf.dcount[queue]
        return tok

    def load(self, queue, buf, dst_ap, src_ap, ignore_waw=False, extra_reads=()):
        return self.dma(queue, buf, dst_ap, src_ap, reads=list(extra_reads), writes=[buf], ignore_waw=ignore_waw)

    def store(self, queue, buf, dst_ap, src_ap, out=True, extra_writes=()):
        return self.dma(queue, buf, dst_ap, src_ap, reads=[buf], writes=list(extra_writes), out=out)

    def finish(self):
        nc = self.nc
        handles = {}
        final_waits = list(self.out_tokens.items())
        with nc.Block() as block:
            def emit(e, name):
                for waits, fn, semkey, inc in self.ops[name]:
                    for k, v in waits:
                        e.wait_ge(self.sems[k], v)
                    fn(e).then_inc(self.sems[semkey], inc)
                if name == "sp":
                    for k, v in final_waits:
                        e.wait_ge(self.sems[k], v)

            @block.tensor
            def _(e):
                emit(e, "pe")

            @block.scalar
            def _(e):
                emit(e, "act")

            @block.vector
            def _(e):
                emit(e, "dve")

            @block.gpsimd
            def _(e):
                emit(e, "pool")

            @block.sync
            def _(e):
                emit(e, "sp")
        self.stack.close()


D = 1024
DF = 2816
NFT = 22
NBP = 33
NB = 34
SB = 33
NCH = 25
CH = 4096
EPS = 1e-6
NL = 2
CQ, CK, CV, CGQ, CGK, CGV, CGL, COG = 0, 512, 640, 768, 1024, 1280, 1792, 1808
import os
STOP = int(os.environ.get("MK_STOP", "99"))
SUB = int(os.environ.get("MK_SUB", "0"))
STOPAT = tuple(int(v) for v in os.environ.get("MK_STOPAT", "0,0").split(","))


class _Stop(Exception):
    pass


BLKS = [int(v) for v in os.environ["MK_BLKS"].split(",")] if os.environ.get("MK_BLKS") else list(range(NB))


GMAX = int(os.environ.get("MK_G", "3"))
NLANE = int(os.environ.get("MK_LANES", "3"))
if os.environ.get("MK_GROUPS"):
    GROUPS = [[int(v) for v in g.split(",")] for g in os.environ["MK_GROUPS"].split(";")]
else:
    GROUPS = [list(range(i, min(i + GMAX, NBP))) for i in range(0, NBP, GMAX)] + [[SB]]
TM = 128 * max(len(g) for g in GROUPS)
GM = TM // 128


PHASES = []


def build():
    nc = bass.Bass("TRN2", target_bir_lowering=False)
    P = Prog(nc)
    PHASES.clear()

    def phase(name):
        PHASES.append((name, P.count["pe"]))

    def din(name, shape, dtype=F32):
        return nc.dram_tensor(name, list(shape), dtype, kind="ExternalInput").ap()

    def dout(name, shape, dtype=F32):
        return nc.dram_tensor(name, list(shape), dtype, kind="ExternalOutput").ap()

    xin = din("xin", [NB * 128, D])
    w_in = din("w_in", [NL, D, 2320])
    w_o = din("w_o", [NL, D, D])
    w_gate = din("w_gate", [NL, D, DF])
    w_up = din("w_up", [NL, D, DF])
    w_down = din("w_down", [NL, DF, D])
    g1_d = din("g1", [128, NL * 8])
    g2_d = din("g2", [128, NL * 8])
    gg_d = din("gg", [128, NL])
    qkg_d = din("qkg", [64, NL * 2])
    snk_d = din("snk", [128, NL * 8])
    wg2_d = din("wg2", [32, NL * 256])
    bg_d = din("bg", [32, NL * 256])
    ck_d = din("ck", [NL, 16, 128, 128])
    cv_d = din("cv", [NL, 16, 128, 128])
    st_d = din("st", [NL, 16, 4, 64, 128])
    identb_d = din("identb", [128, 128], BF16)
    identf_d = din("identf", [128, 128])
    masks_d = din("masks", [128, 5 * 128 + 8], BF16)
    umat_d = din("umat", [128, 256])
    valid_d = din("valid", [128, NB])
    eq_d = din("eq", [64, 16 * 128], BF16)
    e2_d = din("e2", [128, 16], BF16)

    yp = dout("yp", [(NBP - 1) * 128, D])
    ys = dout("ys", [128, D])
    pk = dout("pk", [NL, 128, 128])
    pv = dout("pv", [NL, 128, 128])
    pst = dout("pst", [NL, 4, 64, 128])
    sk = dout("sk", [NL, 16, 128, 128])
    sv = dout("sv", [NL, 16, 128, 128])
    sst = dout("sst", [NL, 16, 4, 64, 128])

    wsc_t = nc.dram_tensor("wsc", [NL, NCH, 128, CH], BF16, kind="ExternalOutput").ap()
    wsc = Buf("wsc", wsc_t)

    sb = P.sb
    identb = sb("identb", [128, 128], BF16)
    identf = sb("identf", [128, 128], F32)
    masks = sb("masks", [128, 5 * 128 + 8], BF16)
    umat = sb("umat", [128, 256], F32)
    valid = sb("valid", [128, NB], F32)
    eq = sb("eq", [64, 16, 128], BF16)
    e2 = sb("e2", [128, 16], BF16)
    g1 = sb("g1", [128, NL * 8], F32)
    g2 = sb("g2", [128, NL * 8], F32)
    gg = sb("gg", [128, NL], F32)
    qkg = sb("qkg", [64, NL * 2], F32)
    esink = sb("esink", [128, NL * 8], F32)
    wg2 = sb("wg2", [32, NL * 256], F32)
    bg = sb("bg", [32, NL * 256], F32)
    ones64 = sb("ones64", [64, 64], BF16)
    ones1 = sb("ones1", [32, 128], F32)
    epsb = sb("epsb", [128, 1], F32)
    epsb1 = sb("epsb1", [128, 1], F32)

    x = sb("x", [128, GM, D], F32)
    st4 = sb("st4", [128, 8], F32)
    hb = sb("hb", [128, D], BF16)
    hT = sb("hT", [128, 8, TM], BF16)
    A3W = 3 * TM + 3 * TM + 3 * (TM // 2)
    arena3 = P.stack.enter_context(nc.sbuf_tensor("s_arena3", [128, A3W], F32))
    qkraw = [Buf("qkraw%d" % i, arena3[0:64, i * TM:(i + 1) * TM]) for i in range(3)]
    rqh = [Buf("rqh%d" % i, arena3[0:64, 3 * TM + i * TM:3 * TM + (i + 1) * TM]) for i in range(3)]
    qksq = [Buf("qksq%d" % i, arena3[0:64, 6 * TM + i * (TM // 2):6 * TM + (i + 1) * (TM // 2)].bitcast(BF16)) for i in range(3)]
    qn = sb("qn", [64, GM, 8, 128], BF16)
    kn32 = sb("kn32", [64, 2, TM], F32)
    kbuf = [sb("kbuf%d" % l, [64, GM + 1, 2, 128], BF16) for l in range(NL)]
    vaug = [sb("vaug%d" % l, [128, GM + 1, 2, 65], BF16) for l in range(NL)]
    gqraw = sb("gqraw", [64, 4, TM], BF16)
    gkraw = sb("gkraw", [64, 4, TM], BF16)
    glowT = sb("glowT", [32, TM], F32)
    v32 = sb("v32", [128, GM, 128], F32)
    gvb = sb("gvb", [128, GM, 512], BF16)
    sog = sb("sog", [128, GM, 512], BF16)
    UW = NFT * TM // 2
    A2W = UW + 2 * TM
    arena2 = P.stack.enter_context(nc.sbuf_tensor("s_arena2", [128, A2W], F32))

    def carve2(name, off, words, dtype, pat=None, parts=128, **kw):
        v = arena2[0:parts, off:off + words]
        if dtype is BF16:
            v = v.bitcast(BF16)
        if pat is not None:
            v = v.rearrange(pat, **kw)
        return Buf(name, v)

    uT = carve2("uT", 0, UW, BF16, "p (f t) -> p f t", f=NFT)
    sg = [carve2("sg%d" % i, UW + i * TM, TM, F32) for i in range(2)]
    tsets = []
    for i_ in range(NLANE):
        n_ = lambda s_: "%s_%d" % (s_, i_)
        if i_ == 0:
            tsets.append(dict(
                lnt=sb(n_("lnt"), [128, 256], F32), eG=sb(n_("eG"), [64, 4, 128], F32), enG=sb(n_("enG"), [64, 4, 128], F32),
                qtil=sb(n_("qtil"), [64, 4, 128], BF16), kt32=sb(n_("kt32"), [64, 4, 128], F32), ktil=sb(n_("ktil"), [64, 4, 128], BF16),
                khatT=sb(n_("khatT"), [64, 4, 128], BF16), khat=sb(n_("khat"), [128, 256], BF16), atm=sb(n_("atm"), [128, 4, 128], BF16),
                o32=sb(n_("o32"), [128, 4, 128], F32), gst=sb(n_("gst"), [128, 16], F32),
                PT=[sb(n_("PTa"), [128, 2, 512], BF16), sb(n_("PTb"), [128, 2, 512], BF16)],
                den=[sb(n_("dena"), [128, 8], F32), sb(n_("denb"), [128, 8], F32)]))
        elif i_ == 2:
            hTf = hT.t[:, :, :].rearrange("p k t -> p (k t)").bitcast(F32)
            hbf = hb.t[:, :].bitcast(F32)
            assert 4 * TM >= 1536 and A3W >= 2464

            def cv(base, nm, off, words, dtype, pat=None, parts=128, **kw):
                v = base[0:parts, off:off + words]
                if dtype is BF16:
                    v = v.bitcast(BF16)
                if pat is not None:
                    v = v.rearrange(pat, **kw)
                return Buf(n_(nm), v)
            ts2 = dict(
                eG=cv(arena3, "eG", 0, 512, F32, "p (h t) -> p h t", parts=64, h=4), enG=cv(arena3, "enG", 512, 512, F32, "p (h t) -> p h t", parts=64, h=4),
                kt32=cv(arena3, "kt32", 1024, 512, F32, "p (h t) -> p h t", parts=64, h=4), qtil=cv(arena3, "qtil", 1536, 256, BF16, "p (h t) -> p h t", parts=64, h=4),
                ktil=cv(arena3, "ktil", 1792, 256, BF16, "p (h t) -> p h t", parts=64, h=4), khatT=cv(arena3, "khatT", 2048, 256, BF16, "p (h t) -> p h t", parts=64, h=4),
                khat=cv(arena3, "khat", 2304, 128, BF16), gst=cv(arena3, "gst", 2432, 16, F32),
                den=[cv(arena3, "dena", 2448, 8, F32), cv(arena3, "denb", 2456, 8, F32)],
                PT=[cv(hTf, "PTa", 0, 512, BF16, "p (a t) -> p a t", a=2), cv(hTf, "PTb", 512, 512, BF16, "p (a t) -> p a t", a=2)],
                o32=cv(hTf, "o32", 1024, 512, F32, "p (h t) -> p h t", h=4),
                lnt=cv(hbf, "lnt", 0, 256, F32), atm=cv(hbf, "atm", 256, 256, BF16, "p (h t) -> p h t", h=4))
            tsets.append(ts2)
            hosts3 = qkraw + rqh + qksq
            for k_, v_ in ts2.items():
                for a_ in (v_ if isinstance(v_, list) else [v_]):
                    host = [hT] if k_ in ("PT", "o32") else ([hb] if k_ in ("lnt", "atm") else hosts3)
                    for b_ in host:
                        a_.aliases.append(b_)
                        b_.aliases.append(a_)
        else:
            assert i_ == 1
            o_ = [0]

            def c2(nm, words, dtype, pat=None, parts=128, **kw):
                b = carve2(n_(nm), o_[0], words, dtype, pat, parts, **kw)
                o_[0] += words
                return b
            ts1 = dict(
                lnt=c2("lnt", 256, F32), eG=c2("eG", 512, F32, "p (h t) -> p h t", parts=64, h=4), enG=c2("enG", 512, F32, "p (h t) -> p h t", parts=64, h=4),
                qtil=c2("qtil", 256, BF16, "p (h t) -> p h t", parts=64, h=4), kt32=c2("kt32", 512, F32, "p (h t) -> p h t", parts=64, h=4),
                ktil=c2("ktil", 256, BF16, "p (h t) -> p h t", parts=64, h=4), khatT=c2("khatT", 256, BF16, "p (h t) -> p h t", parts=64, h=4),
                khat=c2("khat", 128, BF16), atm=c2("atm", 256, BF16, "p (h t) -> p h t", h=4), o32=c2("o32", 512, F32, "p (h t) -> p h t", h=4),
                gst=c2("gst", 16, F32),
                PT=[c2("PTa", 512, BF16, "p (a t) -> p a t", a=2), c2("PTb", 512, BF16, "p (a t) -> p a t", a=2)],
                den=[c2("dena", 8, F32), c2("denb", 8, F32)])
            assert o_[0] <= A2W, (o_[0], A2W)
            tsets.append(ts1)
            flat = [v for v in ts1.values() if isinstance(v, Buf)] + ts1["PT"] + ts1["den"]
            for a_ in flat:
                for b_ in [uT] + sg:
                    a_.aliases.append(b_)
                    b_.aliases.append(a_)
    S = [sb("S%d" % l, [64, 4, 128], F32) for l in range(NL)]
    Sb = [sb("Sb%d" % l, [64, 4, 128], BF16) for l in range(NL)]
    merged = sb("merged", [128, GM, D], BF16)
    kt_out = sb("kt_out", [128, 128], F32)
    ring = [sb("ring%d" % i, [128, CH], BF16) for i in range(4)]

    AW = 12832
    arena = P.stack.enter_context(nc.sbuf_tensor("s_arena", [128, AW], F32))

    def carve(name, off, words, dtype, pat=None, parts=128, **kw):
        v = arena[0:parts, off:off + words]
        if dtype is BF16:
            v = v.bitcast(BF16)
        if pat is not None:
            v = v.rearrange(pat, **kw)
        return Buf(name, v)

    cst = carve("cst", 0, 2048, F32, "p (s c) -> p s c", s=16)
    ckb = carve("ckb", 2048, 1024, BF16, "p (s c) -> p s c", s=16)
    vc = carve("vc", 3072, 1040, BF16, "p (s g e) -> p s g e", s=16, g=2)
    ptc = carve("ptc", 4112, 512, BF16)
    khx = carve("khx", 4624, 512, BF16, "p (s c) -> p s c", s=4)
    kcT = carve("kcT", 5136, 2048, BF16, "p (s g t) -> p s g t", s=16, g=2, parts=64)
    otc = carve("otc", 7184, 1024, F32, "p (g h t) -> p g h t", g=2, h=4, parts=65)
    s0q = carve("s0q", 8208, 2048, F32, "p (s h v) -> p s h v", s=4, h=4, parts=64)
    s0b = carve("s0b", 10256, 1024, BF16, "p (s h v) -> p s h v", s=4, h=4, parts=64)
    qx = carve("qx", 11280, 1024, BF16, "p (h s t) -> p h s t", h=4, s=4, parts=64)
    qns = carve("qns", 12304, 512, BF16, "p (s g c) -> p s g c", s=16, g=2, parts=64)
    samp_bufs = [cst, ckb, vc, ptc, khx, kcT, otc, s0q, s0b, qx, qns]
    stg = [carve("stg%d" % i, 2048 * i, 2048, F32, "p (k w) -> p k w", k=8) for i in range(3)]
    cht = [carve("cht%d" % i, 6144 + 2048 * i, 2048, BF16) for i in range(3)]
    for a_ in stg + cht:
        for b_ in samp_bufs:
            a_.aliases.append(b_)
            b_.aliases.append(a_)

    pbF = [P.ps("pbF%d" % i, [128, 512], F32) for i in range(3)]
    pbT = [P.ps("pbT%d" % i, [128, 512], F32) for i in range(2)]
    pbA = [P.ps("pbA%d" % i, [128, 512], F32) for i in range(2)]
    pb16 = P.ps("pb16", [128, 1024], BF16)
    rr = {"F": 0, "T": 0, "A": 0}

    def bankF():
        rr["F"] += 1
        return pbF[rr["F"] % 3]

    def bankT():
        rr["T"] += 1
        return pbT[rr["T"] % 2]

    def bankA():
        rr["A"] += 1
        return pbA[rr["A"] % 2]

    def mm(out, lhsT, rhs, start, stop, r, w):
        P.op("pe", lambda e: e.matmul(out, lhsT, rhs, start=start, stop=stop), reads=r, writes=w)

    def tr(out, in_, ident, r, w):
        P.op("pe", lambda e: e.transpose(out, in_, ident), reads=r, writes=w)

    def act(out, in_, func, r, w, **kw):
        P.op("act", lambda e: e.activation(out, in_, func, **kw), reads=r, writes=w)

    def cp(eng, out, in_, r, w):
        if eng == "act":
            act(out, in_, AF.Copy, r, w)
        else:
            P.op(eng, lambda e: e.tensor_copy(out, in_), reads=r, writes=w)

    def tt(eng, out, a, b, op, r, w):
        P.op(eng, lambda e: e.tensor_tensor(out, a, b, op), reads=r, writes=w)

    def tsc(eng, out, a, s1, s2, op0, op1, r, w):
        if s2 is None:
            P.op(eng, lambda e: e.tensor_scalar(out, a, s1, None, op0), reads=r, writes=w)
        else:
            P.op(eng, lambda e: e.tensor_scalar(out, a, s1, s2, op0, op1), reads=r, writes=w)

    def stt(eng, out, a, s, b, op0, op1, r, w):
        P.op(eng, lambda e: e.scalar_tensor_tensor(out, a, s, b, op0, op1), reads=r, writes=w)

    def mset(eng, buf, ap, val):
        P.op(eng, lambda e: e.memset(ap, val), writes=[buf])

    cld = Buf("cld", None)
    cbufs = []
    for b_, d_ in ((identb, identb_d), (identf, identf_d), (masks, masks_d), (umat, umat_d), (valid, valid_d),
                   (e2, e2_d), (g1, g1_d), (g2, g2_d), (gg, gg_d), (qkg, qkg_d), (esink, snk_d), (wg2, wg2_d), (bg, bg_d)):
        P.dma("pool", cld, b_.t[:, :], d_, writes=[b_])
        cbufs.append(b_)
    P.dma("pool", cld, eq.t[:, :, :], eq_d.rearrange("p (s t) -> p s t", s=16), writes=[eq])
    cbufs.append(eq)
    for b_ in cbufs:
        b_.lw = {cld.dsem["pool"]: cld.dcount["pool"]}
    act(esink.t[:, :], esink.t[:, :], AF.Exp, [esink], [esink])
    mset("dve", ones64, ones64.t[:, :], 1.0)
    mset("dve", ones1, ones1.t[:, :], 1.0)
    mset("dve", epsb, epsb.t[:, :], EPS)
    mset("dve", epsb1, epsb1.t[:, :], 1.0)
    for l in range(NL):
        mset("dve", S[l], S[l].t[:, :, :], 0.0)
        mset("dve", Sb[l], Sb[l].t[:, :, :], 0.0)
        mset("pool", kbuf[l], kbuf[l].t[:, :, :, :], 0.0)
        mset("pool", vaug[l], vaug[l].t[:, :, :, :], 0.0)
        mset("pool", vaug[l], vaug[l].t[:, :, :, 64:65], 1.0)
    m_own = masks.t[:, 0:128]
    m_prev = masks.t[:, 128:256]
    m_own0 = masks.t[:, 256:384]
    m_samp = masks.t[:, 384:512]
    m_prev1 = masks.t[:, 512:640]
    m_cache = masks.t[:, 640:648]
    pth = Buf("pth", None)

    prep_state = {"stg": 0, "cht": 0, "eng": 0}
    for ct_ in cht:
        mset("pool", ct_, ct_.t[:, :], 0.0)

    def kview(ct, W):
        return ct.t[:, 0:8 * W].rearrange("p (k w) -> p k w", k=8)

    def prep_piece(ct, dst, src, kc, wdt, gain):
        s_ = stg[prep_state["stg"] % 3]
        prep_state["stg"] += 1
        q_ = "act" if prep_state["stg"] % 2 else "sp"
        P.load(q_, s_, s_.t[:, 0:kc, 0:wdt], src)
        eng = ("dve", "pool")[prep_state["eng"] % 2]
        prep_state["eng"] += 1
        if gain is None:
            if prep_state["eng"] % 3 == 0:
                eng = "act"
            cp(eng, dst, s_.t[:, 0:kc, 0:wdt], [s_], [ct])
        else:
            gbuf, gap = gain
            tt(eng, dst, s_.t[:, 0:kc, 0:wdt], gap.unsqueeze(2).broadcast_to([128, kc, wdt]), ALU.mult, [s_, gbuf], [ct])

    prep_tasks = {}
    wsc_c = {(l_, c_): Buf("wsc_%d_%d" % (l_, c_), None) for l_ in range(NL) for c_ in range(NCH)}

    def prep_chunk(l, c, pieces, zero=None):
        def task(ct):
            if zero is not None:
                mset("dve", ct, kview(ct, zero[0])[:, :, zero[1]:zero[2]], 0.0)
            for (dst_fn, src, kc, wdt, gain) in pieces:
                prep_piece(ct, dst_fn(ct), src, kc, wdt, gain)
            for q4 in range(4):
                P.dma("sp", ct, wsc_t[l, c][:, q4 * 1024:(q4 + 1) * 1024], ct.t[:, q4 * 1024:(q4 + 1) * 1024], reads=[ct], writes=[wsc_c[(l, c)]], ignore_waw=True)
        prep_tasks[(l, c)] = task

    for l in range(NL):
        wi = w_in[l].rearrange("(k p) c -> p k c", p=128)
        wo_ = w_o[l].rearrange("(k p) c -> p k c", p=128)
        wg_ = w_gate[l].rearrange("(k p) c -> p k c", p=128)
        wu_ = w_up[l].rearrange("(k p) c -> p k c", p=128)
        wd_ = w_down[l].rearrange("(k p) c -> p k c", p=128)
        G1 = (g1, g1.t[:, l * 8:(l + 1) * 8])
        G2 = (g2, g2.t[:, l * 8:(l + 1) * 8])

        def cols(W, d0, s0, n, src, gain):
            out = []
            for o in range(0, n, 256):
                wdt = min(256, n - o)
                out.append(((lambda ct, W=W, a=d0 + o, wdt=wdt: kview(ct, W)[:, :, a:a + wdt]), src[:, :, s0 + o:s0 + o + wdt], 8, wdt, gain))
            return out

        prep_chunk(l, 0, cols(512, 0, CQ, 512, wi, G1))
        prep_chunk(l, 1, cols(384, 0, CK, 128, wi, G1) + cols(384, 128, CGQ, 256, wi, G1))
        prep_chunk(l, 2, cols(288, 0, CGK, 256, wi, G1) + cols(288, 256, CGL, 16, wi, G1), zero=(288, 272, 288))
        prep_chunk(l, 3, cols(128, 0, CV, 128, wi, G1))
        prep_chunk(l, 4, cols(512, 0, CGV, 512, wi, G1))
        prep_chunk(l, 5, cols(512, 0, COG, 512, wi, G1))
        for hf in range(2):
            pcs = []
            for o in range(0, 512, 256):
                a = hf * 512 + o
                pcs.append(((lambda ct, o=o: kview(ct, 512)[:, 0:4, o:o + 256]), wo_[:, 0:4, a:a + 256], 4, 256, None))
                pcs.append(((lambda ct, o=o: kview(ct, 512)[:, 4:8, o:o + 256]), wo_[:, 4:8, a:a + 256], 4, 256,
                            (gg, gg.t[:, l:l + 1].broadcast_to([128, 4]))))
            prep_chunk(l, 6 + hf, pcs)
        for i in range(11):
            prep_chunk(l, 8 + i, cols(512, 0, 256 * i, 256, wg_, G2) + cols(512, 256, 256 * i, 256, wu_, G2))
        for j in range(6):
            nft = min(4, NFT - 4 * j)
            pcs = []
            for o in range(0, 1024, 256):
                pcs.append(((lambda ct, o=o, nft=nft: ct.t[:, :].rearrange("p (k w) -> p k w", k=4)[:, 0:nft, o:o + 256]),
                            wd_[:, 4 * j:4 * j + nft, o:o + 256], nft, 256, None))
            prep_chunk(l, 19 + j, pcs)

    wseq = [(l, c) for _g in GROUPS for l in range(NL) for c in range(NCH)]
    wst = {"issued": 0}

    NPREP = NL * NCH

    def need(i):
        c_ = i % NCH
        if i < NPREP:
            lim = i
        else:
            lim = (i - c_ + 3) if c_ <= 2 else i + 3
        while wst["issued"] < min(len(wseq), lim + 1):
            j = wst["issued"]
            l_, c_ = wseq[j]
            if j < NPREP:
                prep_tasks[(l_, c_)](cht[j % 3])
            else:
                rb = ring[j % 4]
                for q4 in range(4):
                    P.dma("sp", rb, rb.t[:, q4 * 1024:(q4 + 1) * 1024], wsc_t[l_, c_][:, q4 * 1024:(q4 + 1) * 1024], reads=[wsc_c[(l_, c_)]], writes=[rb], ignore_waw=(q4 > 0))
            wst["issued"] += 1
        return cht[i % 3] if i < NPREP else ring[i % 4]

    tb_state = {"i": 0}

    def tbank():
        tb_state["i"] += 1
        k = tb_state["i"] % 3
        if k == 0:
            return pb16, pb16.t[:, :]
        bkk = pbT[k - 1]
        return bkk, bkk.t[:, :].bitcast(BF16)

    hbs = [hb, sb("hb1", [128, D], BF16)]
    xb = [Buf("xblk%d" % j_, x.t) for j_ in range(GM)]
    st4s = [st4] + [sb("st4_%d" % i, [128, 8], F32) for i in range(1, GM)]

    def rmsnorm_group(nb_):
        for j in range(nb_):
            mset("dve", st4s[j], st4s[j].t[:, 0:1], 0.0)
            act(hbs[j % 2].t[:, :], x.t[:, j, :], AF.Square, [xb[j]], [hbs[j % 2], st4s[j]], accum_out=st4s[j].t[:, 0:1])
        for j in range(nb_):
            act(st4s[j].t[:, 2:3], st4s[j].t[:, 0:1], AF.Ln, [st4s[j]], [st4s[j]], scale=1.0 / D, bias=epsb.t[:, 0:1])
            act(st4s[j].t[:, 3:4], st4s[j].t[:, 2:3], AF.Exp, [st4s[j]], [st4s[j]], scale=-0.5)

        def scale_(j):
            tsc("dve", hbs[j % 2].t[:, :], x.t[:, j, :], st4s[j].t[:, 3:4], None, ALU.mult, None, [xb[j], st4s[j]], [hbs[j % 2]])

        def trans_(j):
            tbk, tbv = tbank()
            for kc in range(8):
                tr(tbv[:, kc * 128:(kc + 1) * 128], hbs[j % 2].t[:, kc * 128:(kc + 1) * 128], identb.t[:, :], [hbs[j % 2], identb], [tbk])
            cp("dve" if j == 1 else "act", hT.t[:, :, j * 128:(j + 1) * 128], tbv.rearrange("p (k t) -> p k t", k=8), [tbk], [hT])

        for j in range(min(2, nb_)):
            scale_(j)
        for j in range(nb_):
            trans_(j)
            if j + 2 < nb_:
                scale_(j + 2)

    try:
        for gi, blks in enumerate(GROUPS):
            nb = len(blks)
            T = nb * 128
            samp = (blks[0] == SB)
            for j, blk in enumerate(blks):
                P.load("pool", xb[j], x.t[:, j, :], xin[blk * 128:(blk + 1) * 128, :])
            for l in range(NL):
                wbase = (gi * NL + l) * NCH
                phase("norm1")
                rmsnorm_group(nb)
                if STOP == 2 and (gi, l) == STOPAT:
                    raise _Stop()
                phase("qk")
                W0 = need(wbase + 0)
                W0v = W0.t[:, :].rearrange("p (k w) -> p k w", k=8)
                W1 = need(wbase + 1)
                W1v = W1.t[:, 0:8 * 384].rearrange("p (k w) -> p k w", k=8)
                pend = []

                def qk_tail(h, sl):
                    bk2 = bankA()
                    mm(bk2.t[0:64, 0:T], ones64.t[:, :], qksq[sl].t[:, 0:T], True, True, [ones64, qksq[sl]], [bk2])
                    act(rqh[sl].t[:, 0:T], bk2.t[0:64, 0:T], AF.Ln, [bk2], [rqh[sl]], scale=1.0 / 64, bias=epsb.t[0:64, 0:1])
                    act(rqh[sl].t[:, 0:T], rqh[sl].t[:, 0:T], AF.Exp, [rqh[sl]], [rqh[sl]], scale=-0.5)
                    if h < 8:
                        stt("dve", qn.t[:, 0:nb, h, :], qkraw[sl].t[:, 0:T].rearrange("p (j t) -> p j t", t=128), qkg.t[:, 2 * l:2 * l + 1],
                            rqh[sl].t[:, 0:T].rearrange("p (j t) -> p j t", t=128), ALU.mult, ALU.mult, [qkraw[sl], qkg, rqh[sl]], [qn])
                    else:
                        stt("dve", kn32.t[:, h - 8, 0:T], qkraw[sl].t[:, 0:T], qkg.t[:, 2 * l + 1:2 * l + 2], rqh[sl].t[:, 0:T],
                            ALU.mult, ALU.mult, [qkraw[sl], qkg, rqh[sl]], [kn32])

                W2 = need(wbase + 2)
                W2v = W2.t[:, 0:8 * 288].rearrange("p (k w) -> p k w", k=8)
                extra = []

                def g_head(kind, h):
                    bk_ = bankT()
                    if kind == "gq":
                        lwf, Wb, dst, eng, rows = (lambda kc: W1v[:, kc, 128 + h * 64:128 + (h + 1) * 64]), W1, gqraw.t[:, h, 0:T], "act", 64
                    elif kind == "gk":
                        lwf, Wb, dst, eng, rows = (lambda kc: W2v[:, kc, h * 64:(h + 1) * 64]), W2, gkraw.t[:, h, 0:T], "dve", 64
                    else:
                        lwf, Wb, dst, eng, rows = (lambda kc: W2v[:, kc, 256:288]), W2, glowT.t[:, 0:T], "act", 32
                    for kc in range(8):
                        mm(bk_.t[0:rows, 0:T], lwf(kc), hT.t[:, kc, 0:T], kc == 0, kc == 7, [Wb, hT], [bk_])
                    dbuf = gqraw if kind == "gq" else (gkraw if kind == "gk" else glowT)
                    cp(eng, dst, bk_.t[0:rows, 0:T], [bk_], [dbuf])

                for h_ in range(4):
                    extra.append(lambda h_=h_: g_head("gq", h_))
                for h_ in range(4):
                    extra.append(lambda h_=h_: g_head("gk", h_))
                extra.append(lambda: g_head("gl", 0))
                for h in range(10):
                    sl = h % 3
                    bk = bankF()
                    for kc in range(8):
                        lw = W0v[:, kc, h * 64:(h + 1) * 64] if h < 8 else W1v[:, kc, (h - 8) * 64:(h - 7) * 64]
                        mm(bk.t[0:64, 0:T], lw, hT.t[:, kc, 0:T], kc == 0, kc == 7, [W0 if h < 8 else W1, hT], [bk])
                    cp("dve", qkraw[sl].t[:, 0:T], bk.t[0:64, 0:T], [bk], [qkraw[sl]])
                    tt("pool", qksq[sl].t[:, 0:T], qkraw[sl].t[:, 0:T], qkraw[sl].t[:, 0:T], ALU.mult, [qkraw[sl]], [qksq[sl]])
                    if extra:
                        extra.pop(0)()
                    if pend:
                        pend.pop()()
                    pend.append(lambda h=h, sl=sl: qk_tail(h, sl))
                pend.pop()()
                for j, blk in enumerate(blks):
                    cp("pool", kbuf[l].t[:, j + 1, :, :], kn32.t[:, :, j * 128:(j + 1) * 128], [kn32], [kbuf[l]])
                phase("gqgk")
                while extra:
                    extra.pop(0)()
                if STOP == 4 and (gi, l) == STOPAT:
                    raise _Stop()
                phase("tokmaj")
                W3 = need(wbase + 3)
                W3v = W3.t[:, 0:8 * 128].rearrange("p (k w) -> p k w", k=8)
                for j, blk in enumerate(blks):
                    bk = bankT()
                    for kc in range(8):
                        mm(bk.t[:, 0:128], hT.t[:, kc, j * 128:(j + 1) * 128], W3v[:, kc, :], kc == 0, kc == 7, [W3, hT], [bk])
                    cp("act", v32.t[:, j, :], bk.t[:, 0:128], [bk], [v32])
                    cp("dve", vaug[l].t[:, j + 1, :, 0:64], bk.t[:, 0:128].rearrange("p (g d) -> p g d", g=2), [bk], [vaug[l]])
                    if j + 1 < nb:
                        pass
                W4 = need(wbase + 4)
                W4v = W4.t[:, :].rearrange("p (k w) -> p k w", k=8)
                for j in range(nb):
                    bk = bankT()
                    for kc in range(8):
                        mm(bk.t[:, :], hT.t[:, kc, j * 128:(j + 1) * 128], W4v[:, kc, :], kc == 0, kc == 7, [W4, hT], [bk])
                    cp("act", gvb.t[:, j, :], bk.t[:, :], [bk], [gvb])
                W5 = need(wbase + 5)
                W5v = W5.t[:, :].rearrange("p (k w) -> p k w", k=8)
                for j in range(nb):
                    bk = bankT()
                    for kc in range(8):
                        mm(bk.t[:, :], hT.t[:, kc, j * 128:(j + 1) * 128], W5v[:, kc, :], kc == 0, kc == 7, [W5, hT], [bk])
                    act(sog.t[:, j, :], bk.t[:, :], AF.Silu, [bk], [sog])

                if STOP == 5 and (gi, l) == STOPAT:
                    raise _Stop()
                phase("chains")
                if samp:
                    if l == 0:
                        mset("pool", vc, vc.t[:, :, :, 64:65], 1.0)
                    P.load("pool", cst, cst.t[:, :, :], ck_d[l].rearrange("s k c -> k s c"))
                    cp("pool", ckb.t[:, :, :], cst.t[:, :, :], [cst], [ckb])
                    for rnd in range(4):
                        for i in range(8):
                            s_, g_ = (rnd * 8 + i) // 2, (rnd * 8 + i) % 2
                            tr(pb16.t[0:64, i * 128:(i + 1) * 128], ckb.t[:, s_, g_ * 64:(g_ + 1) * 64], identb.t[:, :], [ckb, identb], [pb16])
                        cp("dve", kcT.t[:, rnd * 4:rnd * 4 + 4, :, :], pb16.t[0:64, :].rearrange("p (s g t) -> p s g t", s=4, g=2), [pb16], [kcT])
                    P.dma("pool", pth, sk[l, :, 0:120, :], ck_d[l, :, 8:128, :], reads=[], writes=[], out=True)
                    P.load("pool", cst, cst.t[:, :, :], cv_d[l].rearrange("s k c -> k s c"))
                    cp("pool", vc.t[:, :, :, 0:64], cst.t[:, :, :].rearrange("p s (g d) -> p s g d", g=2), [cst], [vc])
                    P.dma("pool", pth, sv[l, :, 0:120, :], cv_d[l, :, 8:128, :], reads=[], writes=[], out=True)
                    for g_ in range(2):
                        cp("pool", qns.t[:, :, g_, :].rearrange("p s (h t) -> p h s t", h=4), qn.t[:, 0, 4 * g_:4 * g_ + 4, :].rearrange("p h (s t) -> p h s t", t=8), [qn], [qns])
                    stc = [bankA(), bankA()]
                    for s_ in range(16):
                        for g_ in range(2):
                            bk = stc[s_ // 8]
                            o_ = ((s_ % 8) * 2 + g_) * 32
                            mm(bk.t[:, o_:o_ + 32], kcT.t[:, s_, g_, :], qns.t[:, s_, g_, :], True, True, [kcT, qns], [bk])
                    for hf in range(2):
                        act(ptc.t[:, hf * 512:(hf + 1) * 512], stc[hf].t[:, :], AF.Exp, [stc[hf]], [ptc], scale=0.125)
                    tt("pool", ptc.t[:, :].rearrange("p (a q) -> p a q", q=8), ptc.t[:, :].rearrange("p (a q) -> p a q", q=8),
                       m_cache.unsqueeze(1).broadcast_to([128, 128, 8]), ALU.mult, [ptc, masks], [ptc])
                    otb = [bankA(), bankA()]
                    for s_ in range(16):
                        for g_ in range(2):
                            bk = otb[s_ // 8]
                            o_ = ((s_ % 8) * 2 + g_) * 32
                            mm(bk.t[0:65, o_:o_ + 32], vc.t[:, s_, g_, :], ptc.t[:, (s_ * 2 + g_) * 32:(s_ * 2 + g_ + 1) * 32], True, True, [vc, ptc], [bk])
                    for hf in range(2):
                        for g_ in range(2):
                            cp("act" if g_ else "dve", otc.t[:, g_, :, hf * 64:(hf + 1) * 64].rearrange("p h (s t) -> p s h t", t=8),
                               otb[hf].t[0:65, :].rearrange("p (s g h t) -> p s g h t", s=8, g=2, h=4)[:, :, g_, :, :], [otb[hf]], [otc])

                def attn_chain(j, blk, g, ts):
                    pt = ts["PT"][g]
                    den = ts["den"][g]
                    rq_ = qn.t[:, j, 4 * g:4 * g + 4, :].rearrange("p h t -> p (h t)")
                    has_prev = (not samp) and blk > 0
                    sbk = yield from take(2 if has_prev else 1)
                    bo = sbk[0]
                    mm(bo.t[:, :], kbuf[l].t[:, j + 1, g, :], rq_, True, True, [kbuf[l], qn], [bo])
                    if has_prev:
                        bp = sbk[1]
                        mm(bp.t[:, :], kbuf[l].t[:, j, g, :], rq_, True, True, [kbuf[l], qn], [bp])
                    yield
                    act(pt.t[:, 0, :], bo.t[:, :], AF.Exp, [bo], [pt], scale=0.125)
                    if has_prev:
                        act(pt.t[:, 1, :], bp.t[:, :], AF.Exp, [bp], [pt], scale=0.125)
                    give(*sbk)
                    yield
                    mk = m_samp if samp else (m_own0 if blk == 0 else m_own)
                    tt("pool", pt.t[:, 0, :].rearrange("p (h t) -> p h t", h=4), pt.t[:, 0, :].rearrange("p (h t) -> p h t", h=4),
                       mk.unsqueeze(1).broadcast_to([128, 4, 128]), ALU.mult, [pt, masks], [pt])
                    if has_prev:
                        tt("pool", pt.t[:, 1, :].rearrange("p (h t) -> p h t", h=4), pt.t[:, 1, :].rearrange("p (h t) -> p h t", h=4),
                           (m_prev1 if blk == 1 else m_prev).unsqueeze(1).broadcast_to([128, 4, 128]), ALU.mult, [pt, masks], [pt])
                    yield
                    bv_ = (yield from take(1))[0]
                    for h in range(4):
                        oc = bv_.t[:, h * 65:(h + 1) * 65]
                        last_own = not (has_prev or samp)
                        mm(oc, pt.t[:, 0, h * 128:(h + 1) * 128], vaug[l].t[:, j + 1, g, :], True, last_own, [pt, vaug[l]], [bv_])
                        if has_prev:
                            mm(oc, pt.t[:, 1, h * 128:(h + 1) * 128], vaug[l].t[:, j, g, :], False, True, [pt, vaug[l]], [bv_])
                        if samp:
                            mm(oc, otc.t[:, g, h, :], identf.t[0:65, 0:65], False, True, [otc, identf], [bv_])
                    yield
                    pv4 = bv_.t[:, 0:260].rearrange("p (h e) -> p h e", h=4)
                    tt("dve", den.t[:, 0:4], pv4[:, :, 64], esink.t[:, l * 8 + 4 * g:l * 8 + 4 * g + 4], ALU.add, [bv_, esink], [den])
                    P.op("dve", lambda e: e.reciprocal(den.t[:, 4:8], den.t[:, 0:4]), reads=[den], writes=[den])
                    tt("dve", merged.t[:, j, g * 256:(g + 1) * 256].rearrange("p (h d) -> p h d", h=4), pv4[:, :, 0:64],
                       den.t[:, 4:8].unsqueeze(2).broadcast_to([128, 4, 64]), ALU.mult, [bv_, den], [merged])
                    give(bv_)

                def gla_chain(j, blk, ts):
                    lnt, eG, enG, qtil, kt32, ktil = ts["lnt"], ts["eG"], ts["enG"], ts["qtil"], ts["kt32"], ts["ktil"]
                    khatT, khat, atm, o32, gst = ts["khatT"], ts["khat"], ts["atm"], ts["o32"], ts["gst"]
                    c0, c1 = j * 128, (j + 1) * 128
                    bl = (yield from take(1))[0]
                    mm(bl.t[:, 0:256], glowT.t[:, c0:c1], wg2.t[:, l * 256:(l + 1) * 256], True, False, [glowT, wg2], [bl])
                    mm(bl.t[:, 0:256], ones1.t[:, :], bg.t[:, l * 256:(l + 1) * 256], False, True, [ones1, bg], [bl])
                    yield
                    act(lnt.t[:, :], bl.t[:, 0:256], AF.Exp, [bl], [lnt], scale=-1.0)
                    give(bl)
                    act(lnt.t[:, :], lnt.t[:, :], AF.Ln, [lnt], [lnt], bias=epsb1.t[:, 0:1])
                    yield
                    U = umat.t[:, 128:256] if samp else umat.t[:, 0:128]
                    gmask = m_samp if samp else m_own
                    bgT = (yield from take(1))[0]
                    for h in range(4):
                        mm(bgT.t[0:64, h * 128:(h + 1) * 128], lnt.t[:, h * 64:(h + 1) * 64], U, True, True, [lnt, umat], [bgT])
                    yield
                    g4 = bgT.t[0:64, :].rearrange("p (h t) -> p h t", h=4)
                    act(eG.t[:, :, :], g4, AF.Exp, [bgT], [eG])
                    act(enG.t[:, :, :], g4, AF.Exp, [bgT], [enG], scale=-1.0)
                    give(bgT)
                    yield
                    stt("dve", qtil.t[:, :, :], gqraw.t[:, :, c0:c1], 0.125, eG.t[:, :, :], ALU.mult, ALU.mult, [gqraw, eG], [qtil])
                    tt("dve", kt32.t[:, :, :], gkraw.t[:, :, c0:c1], enG.t[:, :, :], ALU.mult, [gkraw, enG], [kt32])
                    yield
                    cp("pool", ktil.t[:, :, :], kt32.t[:, :, :], [kt32], [ktil])
                    if samp:
                        egl = eG.t[:, :, :].rearrange("p h (s t) -> p h s t", t=8)[:, :, :, 7:8].broadcast_to([64, 4, 16, 8])
                        tt("pool", khatT.t[:, :, :].rearrange("p h (s t) -> p h s t", t=8), kt32.t[:, :, :].rearrange("p h (s t) -> p h s t", t=8),
                           egl, ALU.mult, [kt32, eG], [khatT])
                    else:
                        egl = eG.t[:, :, 127:128].broadcast_to([64, 4, 128])
                        tt("pool", khatT.t[:, :, :], kt32.t[:, :, :], egl, ALU.mult, [kt32, eG], [khatT])
                    yield
                    for h in range(4):
                        tr(pb16.t[:, h * 64:(h + 1) * 64], khatT.t[:, h, :], identb.t[0:64, 0:64], [khatT, identb], [pb16])
                    cp("dve", khat.t[:, :], pb16.t[:, 0:256], [pb16], [khat])
                    ba = (yield from take(1))[0]
                    for h in range(4):
                        mm(ba.t[:, h * 128:(h + 1) * 128], ktil.t[:, h, :], qtil.t[:, h, :], True, True, [ktil, qtil], [ba])
                    yield
                    tt("dve", atm.t[:, :, :], ba.t[:, :].rearrange("p (h t) -> p h t", h=4), gmask.unsqueeze(1).broadcast_to([128, 4, 128]), ALU.mult, [ba, masks], [atm])
                    give(ba)
                    yield
                    if not samp:
                        bo, bs = yield from take(2)
                        for h in range(4):
                            oc = bo.t[:, h * 128:(h + 1) * 128]
                            mm(oc, atm.t[:, h, :], gvb.t[:, j, h * 128:(h + 1) * 128], True, False, [atm, gvb], [bo])
                            mm(oc, qtil.t[:, h, :], Sb[l].t[:, h, :], False, True, [qtil, Sb[l]], [bo])
                        for h in range(4):
                            mm(bs.t[0:64, h * 128:(h + 1) * 128], khat.t[:, h * 64:(h + 1) * 64], gvb.t[:, j, h * 128:(h + 1) * 128], True, True, [khat, gvb], [bs])
                        tt("dve", S[l].t[:, :, :], S[l].t[:, :, :], eG.t[:, :, 127:128].broadcast_to([64, 4, 128]), ALU.mult, [S[l], eG], [S[l]])
                        tt("dve", S[l].t[:, :, :], S[l].t[:, :, :], bs.t[0:64, :].rearrange("p (h v) -> p h v", h=4), ALU.add, [S[l], bs], [S[l]])
                        cp("pool", Sb[l].t[:, :, :], S[l].t[:, :, :], [S[l]], [Sb[l]])
                        give(bs)
                    else:
                        obk = yield from take(4)
                        for h in range(4):
                            mm(obk[h].t[:, 0:128], atm.t[:, h, :], gvb.t[:, j, h * 128:(h + 1) * 128], True, False, [atm, gvb], [obk[h]])
                        eg4 = eG.t[:, :, :].rearrange("p h (s t) -> p h s t", t=8)
                        for qd in range(4):
                            P.load("pool", s0q, s0q.t[:, :, :, :], st_d[l, 4 * qd:4 * qd + 4].rearrange("s h k v -> k s h v"))
                            cp("pool", s0b.t[:, :, :, :], s0q.t[:, :, :, :], [s0q], [s0b])
                            tt("dve", qx.t[:, :, :, :], qtil.t[:, :, :].unsqueeze(2).broadcast_to([64, 4, 4, 128]),
                               eq.t[:, 4 * qd:4 * qd + 4, :].unsqueeze(1).broadcast_to([64, 4, 4, 128]), ALU.mult, [qtil, eq], [qx])
                            tt("pool", khx.t[:, :, :], khat.t[:, :].unsqueeze(1).broadcast_to([128, 4, 256]),
                               e2.t[:, 4 * qd:4 * qd + 4].unsqueeze(2).broadcast_to([128, 4, 256]), ALU.mult, [khat, e2], [khx])
                            for h in range(4):
                                for s_ in range(4):
                                    last = (qd == 3 and s_ == 3)
                                    mm(obk[h].t[:, 0:128], qx.t[:, h, s_, :], s0b.t[:, s_, h, :], False, last, [qx, s0b], [obk[h]])
                            for s_ in range(4):
                                bs = (yield from take(1))[0]
                                for h in range(4):
                                    mm(bs.t[0:64, h * 128:(h + 1) * 128], khx.t[:, s_, h * 64:(h + 1) * 64], gvb.t[:, j, h * 128:(h + 1) * 128], True, True, [khx, gvb], [bs])
                                sa = 4 * qd + s_
                                tt("dve", s0q.t[:, s_, :, :], s0q.t[:, s_, :, :], eg4[:, :, sa, 7:8].broadcast_to([64, 4, 128]), ALU.mult, [s0q, eG], [s0q])
                                tt("dve", s0q.t[:, s_, :, :], s0q.t[:, s_, :, :], bs.t[0:64, :].rearrange("p (h v) -> p h v", h=4), ALU.add, [s0q, bs], [s0q])
                                give(bs)
                            P.store("pool", s0q, sst[l, 4 * qd:4 * qd + 4].rearrange("s h k v -> k s h v"), s0q.t[:, :, :, :])
                    yield
                    if samp:
                        for h in range(4):
                            cp("act" if h % 2 else "dve", o32.t[:, h, :], obk[h].t[:, 0:128], [obk[h]], [o32])
                        give(*obk)
                    else:
                        cp("act", o32.t[:, :, :], bo.t[:, :].rearrange("p (h v) -> p h v", h=4), [bo], [o32])
                        give(bo)
                    mset("dve", gst, gst.t[:, 0:4], 0.0)
                    yield
                    for h in range(4):
                        act(atm.t[:, h, :], o32.t[:, h, :], AF.Square, [o32], [atm, gst], accum_out=gst.t[:, h:h + 1])
                    act(gst.t[:, 4:8], gst.t[:, 0:4], AF.Ln, [gst], [gst], scale=1.0 / 128, bias=epsb.t[:, 0:1])
                    act(gst.t[:, 8:12], gst.t[:, 4:8], AF.Exp, [gst], [gst], scale=-0.5)
                    yield
                    tt("dve", o32.t[:, :, :], o32.t[:, :, :], gst.t[:, 8:12].unsqueeze(2).broadcast_to([128, 4, 128]), ALU.mult, [o32, gst], [o32])
                    yield
                    tt("pool", merged.t[:, j, 512:1024], o32.t[:, :, :].rearrange("p h v -> p (h v)"), sog.t[:, j, :], ALU.mult, [o32, sog], [merged])
                    if blk == NBP - 1 or samp:
                        tb = (yield from take(1))[0]
                        for g in range(2):
                            tr(tb.t[:, g * 64:(g + 1) * 64], kn32.t[:, g, c0:c1], identf.t[0:64, 0:64], [kn32, identf], [tb])
                        cp("dve", kt_out.t[:, :], tb.t[:, 0:128], [tb], [kt_out])
                        give(tb)
                        if samp:
                            for s_ in range(16):
                                P.store("pool", kt_out, sk[l, s_, 120:128, :], kt_out.t[8 * s_:8 * s_ + 8, :])
                                P.store("pool", v32, sv[l, s_, 120:128, :], v32.t[8 * s_:8 * s_ + 8, j, :])
                        else:
                            P.store("pool", kt_out, pk[l], kt_out.t[:, :])
                            P.store("pool", v32, pv[l], v32.t[:, j, :])
                            P.store("pool", S[l], pst[l].rearrange("h k v -> k h v"), S[l].t[:, :, :])

                freeb = list(pbF) + list(pbT) + list(pbA)

                def take(n):
                    spins = 0
                    while len(freeb) < n:
                        spins += 1
                        assert spins < 10000, "psum bank deadlock"
                        yield
                    return [freeb.pop(0) for _ in range(n)]

                def give(*bs_):
                    freeb.extend(bs_)

                lanes = [[(j, blk) for j, blk in enumerate(blks) if j % NLANE == ln] for ln in range(NLANE)]
                active = [[] for _ in range(NLANE)]
                while any(lanes) or any(active):
                    for ln in range(NLANE):
                        if not active[ln] and lanes[ln]:
                            j, blk = lanes[ln].pop(0)
                            active[ln] = [attn_chain(j, blk, 0, tsets[ln]), attn_chain(j, blk, 1, tsets[ln]), gla_chain(j, blk, tsets[ln])]
                        for gen in list(active[ln]):
                            try:
                                next(gen)
                            except StopIteration:
                                active[ln].remove(gen)
                if STOP == 7 and (gi, l) == STOPAT:
                    raise _Stop()
                phase("wo")
                if not samp:
                    cp("pool", kbuf[l].t[:, 0, :, :], kbuf[l].t[:, nb, :, :], [kbuf[l]], [kbuf[l]])
                    cp("pool", vaug[l].t[:, 0, :, 0:64], vaug[l].t[:, nb, :, 0:64], [vaug[l]], [vaug[l]])
                for j in range(nb):
                    tbk, tbv = tbank()
                    for kc in range(8):
                        tr(tbv[:, kc * 128:(kc + 1) * 128], merged.t[:, j, kc * 128:(kc + 1) * 128], identb.t[:, :], [merged, identb], [tbk])
                    cp("dve" if j % 2 else "act", hT.t[:, :, j * 128:(j + 1) * 128], tbv.rearrange("p (k t) -> p k t", k=8), [tbk], [hT])
                for hf in range(2):
                    Wc = need(wbase + 6 + hf)
                    Wv = Wc.t[:, :].rearrange("p (k w) -> p k w", k=8)
                    for j, blk in enumerate(blks):
                        bk = bankT()
                        for kc in range(8):
                            mm(bk.t[:, :], hT.t[:, kc, j * 128:(j + 1) * 128], Wv[:, kc, :], kc == 0, kc == 7, [Wc, hT], [bk])
                        stt("dve", x.t[:, j, hf * 512:(hf + 1) * 512], bk.t[:, :], valid.t[:, blk:blk + 1], x.t[:, j, hf * 512:(hf + 1) * 512], ALU.mult, ALU.add, [bk, valid, xb[j]], [xb[j]])

                if STOP == 8 and (gi, l) == STOPAT:
                    raise _Stop()
                phase("ffn_gu")
                rmsnorm_group(nb)
                for i in range(11):
                    Wc = need(wbase + 8 + i)
                    Wv = Wc.t[:, :].rearrange("p (k a w) -> p k a w", k=8, a=2)
                    for jj in range(2):
                        ft = 2 * i + jj
                        bg_ = bankF()
                        bu_ = bankA()
                        for kc in range(8):
                            mm(bg_.t[:, 0:T], Wv[:, kc, 0, jj * 128:(jj + 1) * 128], hT.t[:, kc, 0:T], kc == 0, kc == 7, [Wc, hT], [bg_])
                        for kc in range(8):
                            mm(bu_.t[:, 0:T], Wv[:, kc, 1, jj * 128:(jj + 1) * 128], hT.t[:, kc, 0:T], kc == 0, kc == 7, [Wc, hT], [bu_])
                        sg_ = sg[ft % 2]
                        act(sg_.t[:, 0:T], bg_.t[:, 0:T], AF.Silu, [bg_], [sg_])
                        tt("dve", uT.t[:, ft, 0:T], sg_.t[:, 0:T], bu_.t[:, 0:T], ALU.mult, [sg_, bu_], [uT])
                phase("ffn_down")
                dbk = [pbT[0], pbT[1], pbA[0], pbA[1], pbF[0], pbF[1]]
                for ft in range(NFT):
                    Wc = need(wbase + 19 + ft // 4)
                    Wv = Wc.t[:, :].rearrange("p (k w) -> p k w", k=4)
                    for j in range(nb):
                        for hf in range(2):
                            bd = dbk[j * 2 + hf]
                            mm(bd.t[:, :], uT.t[:, ft, j * 128:(j + 1) * 128], Wv[:, ft % 4, hf * 512:(hf + 1) * 512], ft == 0, ft == NFT - 1, [Wc, uT], [bd])
                for j, blk in enumerate(blks):
                    for hf in range(2):
                        bd = dbk[j * 2 + hf]
                        stt("dve", x.t[:, j, hf * 512:(hf + 1) * 512], bd.t[:, :], valid.t[:, blk:blk + 1], x.t[:, j, hf * 512:(hf + 1) * 512], ALU.mult, ALU.add, [bd, valid, xb[j]], [xb[j]])
            for j, blk in enumerate(blks):
                if samp:
                    P.dma("pool", xb[j], ys, x.t[:, j, :], reads=[xb[j]], out=True)
                elif blk >= 1:
                    P.dma("pool", xb[j], yp[(blk - 1) * 128:blk * 128, :], x.t[:, j, :], reads=[xb[j]], out=True)
    except _Stop:
        pass
    P.finish()
    return nc


def _consts():
    bf = ml_dtypes.bfloat16
    j = np.arange(128)[:, None]
    i = np.arange(128)[None, :]
    own = (j <= i)
    prev = (j > i)
    own0 = own & (j >= 112)
    samp = (j // 8 == i // 8) & (j % 8 <= i % 8)
    cache = (np.arange(128)[:, None] > np.arange(8)[None, :])
    prev1 = prev & (j >= 112)
    masks = np.concatenate([own, prev, own0, samp, prev1, cache], axis=1).astype(np.float32).astype(bf)
    umat = np.concatenate([own.astype(np.float32), samp.astype(np.float32)], axis=1) * np.float32(-1.0 / 16.0)
    valid = np.ones((128, NB), np.float32)
    valid[0:112, 0] = 0.0
    t = np.arange(128)
    e2 = (t[:, None] // 8 == np.arange(16)[None, :]).astype(np.float32)
    eq = np.broadcast_to(e2.T[None, :, :], (64, 16, 128)).reshape(64, 16 * 128)
    return dict(identb=np.eye(128, dtype=np.float32).astype(bf), identf=np.eye(128, dtype=np.float32),
                masks=masks, umat=np.ascontiguousarray(umat.astype(np.float32)), valid=valid,
                eq=np.ascontiguousarray(eq).astype(bf), e2=e2.astype(bf))


_NC_CACHE = {}


def kernel(**inp):
    f = lambda a: np.ascontiguousarray(np.asarray(a, dtype=np.float32))
    x_prompt, x_sample = f(inp["x_prompt"]), f(inp["x_sample"])
    cache_k, cache_v, state_gla = f(inp["cache_k"]), f(inp["cache_v"]), f(inp["state_gla"])
    meta = f(inp["meta"])
    norm1, norm2, gla_norm = f(inp["norm1"]), f(inp["norm2"]), f(inp["gla_norm"])
    q_norm, k_norm, sinks = f(inp["q_norm"]), f(inp["k_norm"]), f(inp["sinks"])
    w_g2, b_g = f(inp["w_g2"]), f(inp["b_g"])
    common = dict(
        w_in=f(inp["w_in"]), w_o=f(inp["w_o"]), w_gate=f(inp["w_gate"]), w_up=f(inp["w_up"]), w_down=f(inp["w_down"]),
        g1=np.ascontiguousarray(norm1.reshape(NL, 8, 128).transpose(2, 0, 1).reshape(128, NL * 8)),
        g2=np.ascontiguousarray(norm2.reshape(NL, 8, 128).transpose(2, 0, 1).reshape(128, NL * 8)),
        gg=np.ascontiguousarray(gla_norm.T),
        qkg=np.ascontiguousarray(np.stack([q_norm[0], k_norm[0], q_norm[1], k_norm[1]], axis=1)),
        snk=np.ascontiguousarray(np.broadcast_to(sinks.reshape(1, NL * 8), (128, NL * 8))),
        wg2=np.ascontiguousarray(np.concatenate([w_g2.transpose(1, 0, 2).reshape(16, NL * 256), np.zeros((16, NL * 256), np.float32)], 0)),
        bg=np.ascontiguousarray(np.concatenate([b_g.reshape(1, NL * 256), np.zeros((31, NL * 256), np.float32)], 0)),
    )
    common.update(_consts())
    in_maps = []
    for c in range(8):
        seq = c % 4
        xin = np.zeros((NB * 128, D), np.float32)
        xin[112:128] = meta
        xin[128:NBP * 128] = x_prompt[seq]
        xin[NBP * 128:] = x_sample[16 * c:16 * c + 16].reshape(128, D)
        m = dict(common)
        m["xin"] = xin
        m["ck"] = np.ascontiguousarray(cache_k[:, 16 * c:16 * c + 16].reshape(NL, 16, 128, 128))
        m["cv"] = np.ascontiguousarray(cache_v[:, 16 * c:16 * c + 16].reshape(NL, 16, 128, 128))
        m["st"] = np.ascontiguousarray(state_gla[:, 16 * c:16 * c + 16])
        in_maps.append(m)
    if "nc" not in _NC_CACHE:
        _NC_CACHE["nc"] = build()
    res = run_bass_kernel_spmd(_NC_CACHE["nc"], in_maps, core_ids=list(range(8)))
    R = res.results
    y_prompt = np.stack([R[c]["yp"] for c in range(4)], axis=0).astype(np.float32)
    y_sample = np.concatenate([R[c]["ys"].reshape(16, 8, D) for c in range(8)], axis=0).astype(np.float32)
    pk = np.stack([R[c]["pk"].reshape(NL, 128, 2, 64) for c in range(4)], axis=1).astype(np.float32)
    pv = np.stack([R[c]["pv"].reshape(NL, 128, 2, 64) for c in range(4)], axis=1).astype(np.float32)
    pst = np.stack([R[c]["pst"] for c in range(4)], axis=1).astype(np.float32)
    sk = np.concatenate([R[c]["sk"].reshape(NL, 16, 128, 2, 64) for c in range(8)], axis=1).astype(np.float32)
    sv = np.concatenate([R[c]["sv"].reshape(NL, 16, 128, 2, 64) for c in range(8)], axis=1).astype(np.float32)
    sst = np.concatenate([R[c]["sst"] for c in range(8)], axis=1).astype(np.float32)
    return (y_prompt, y_sample, pk, pv, pst, sk, sv, sst)
```

```python
import contextlib
import numpy as np
import ml_dtypes
import concourse.bass as bass
import concourse.mybir as mybir
from concourse.bass_utils import run_bass_kernel_spmd

F32 = mybir.dt.float32
BF16 = mybir.dt.bfloat16
AF = mybir.ActivationFunctionType
ALU = mybir.AluOpType
AX = mybir.AxisListType


class Buf:
    def __init__(self, name, t):
        self.name = name
        self.t = t
        self.lw = {}
        self.rd = {}
        self.dsem = None
        self.dcount = 0
        self.excl = False
        self.aliases = []


class Prog:
    ENGS = ("pe", "act", "dve", "pool", "sp")

    def __init__(self, nc):
        self.nc = nc
        self.stack = contextlib.ExitStack()
        self.sems = {}
        self.count = {e: 0 for e in self.ENGS}
        self.seen = {e: {} for e in self.ENGS}
        self.ops = {e: [] for e in self.ENGS}
        for e in self.ENGS:
            self.sems["E_" + e] = self.stack.enter_context(nc.semaphore("sem_" + e))
        self.out_tokens = {}
        self.nbufs = 0

    def sb(self, name, shape, dtype):
        t = self.stack.enter_context(self.nc.sbuf_tensor("s_" + name, list(shape), dtype))
        return Buf(name, t)

    def ps(self, name, shape, dtype):
        t = self.stack.enter_context(self.nc.psum_tensor(name, list(shape), dtype))
        b = Buf(name, t)
        b.excl = True
        return b

    def view(self, name, t):
        return Buf(name, t)

    def _dsem(self, buf, queue):
        if buf.dsem is None:
            buf.dsem = {}
            buf.dcount = {}
        if queue not in buf.dsem:
            key = "D_%d_%s_%s" % (self.nbufs, buf.name, queue)
            self.nbufs += 1
            self.sems[key] = self.stack.enter_context(self.nc.semaphore("dsem_%d" % self.nbufs))
            buf.dsem[queue] = key
            buf.dcount[queue] = 0
        return buf.dsem[queue]

    def _waits(self, eng, reads, writes, ignore_waw=False):
        need = {}
        for b in reads:
            for k, v in b.lw.items():
                if need.get(k, 0) < v:
                    need[k] = v
            if b.excl:
                for k, v in b.rd.items():
                    if k != "E_" + eng and need.get(k, 0) < v:
                        need[k] = v
        for b in writes:
            if not ignore_waw:
                for k, v in b.lw.items():
                    if need.get(k, 0) < v:
                        need[k] = v
            for k, v in b.rd.items():
                if need.get(k, 0) < v:
                    need[k] = v
            for al in b.aliases:
                for dd in (al.lw, al.rd):
                    for k, v in dd.items():
                        if need.get(k, 0) < v:
                            need[k] = v
        out = []
        seen = self.seen[eng]
        for k, v in need.items():
            if eng == "pe" and k == "E_pe":
                continue
            if seen.get(k, 0) < v:
                seen[k] = v
                out.append((k, v))
        return out

    def _commit(self, tok, reads, writes, ignore_waw=False):
        k, v = tok
        for b in reads:
            if b.rd.get(k, 0) < v:
                b.rd[k] = v
        for b in writes:
            if ignore_waw:
                b.lw[k] = v
            else:
                b.lw = {k: v}
            b.rd = {}

    def op(self, eng, fn, reads=(), writes=()):
        waits = self._waits(eng, reads, writes)
        self.count[eng] += 1
        tok = ("E_" + eng, self.count[eng])
        self._commit(tok, reads, writes)
        self.ops[eng].append((waits, fn, tok[0], 1))
        return tok

    def dma(self, queue, sem_buf, out_ap, in_ap, reads=(), writes=(), out=False, ignore_waw=False):
        waits = self._waits(queue, reads, writes, ignore_waw=ignore_waw)
        key = self._dsem(sem_buf, queue)
        sem_buf.dcount[queue] += 16
        tok = (key, sem_buf.dcount[queue])
        self._commit(tok, reads, writes, ignore_waw=ignore_waw)
        self.ops[queue].append((waits, (lambda e: e.dma_start(out=out_ap, in_=in_ap)), key, 16))
        if out:
            self.out_tokens[key] = sem_buf.dcount[queue]
        return tok

    def load(self, queue, buf, dst_ap, src_ap, ignore_waw=False, extra_reads=()):
        return self.dma(queue, buf, dst_ap, src_ap, reads=list(extra_reads), writes=[buf], ignore_waw=ignore_waw)

    def store(self, queue, buf, dst_ap, src_ap, out=True, extra_writes=()):
        return self.dma(queue, buf, dst_ap, src_ap, reads=[buf], writes=list(extra_writes), out=out)

    def finish(self):
        nc = self.nc
        handles = {}
        final_waits = list(self.out_tokens.items())
        with nc.Block() as block:
            def emit(e, name):
                for waits, fn, semkey, inc in self.ops[name]:
                    for k, v in waits:
                        e.wait_ge(self.sems[k], v)
                    fn(e).then_inc(self.sems[semkey], inc)
                if name == "sp":
                    for k, v in final_waits:
                        e.wait_ge(self.sems[k], v)

            @block.tensor
            def _(e):
                emit(e, "pe")

            @block.scalar
            def _(e):
                emit(e, "act")

            @block.vector
            def _(e):
                emit(e, "dve")

            @block.gpsimd
            def _(e):
                emit(e, "pool")

            @block.sync
            def _(e):
                emit(e, "sp")
        self.stack.close()


D = 1024
DF = 2816
NFT = 22
NBP = 33
NB = 34
SB = 33
NCH = 25
CH = 4096
EPS = 1e-6
NL = 2
CQ, CK, CV, CGQ, CGK, CGV, CGL, COG = 0, 512, 640, 768, 1024, 1280, 1792, 1808
import os
STOP = int(os.environ.get("MK_STOP", "99"))
SUB = int(os.environ.get("MK_SUB", "0"))
STOPAT = tuple(int(v) for v in os.environ.get("MK_STOPAT", "0,0").split(","))


class _Stop(Exception):
    pass


BLKS = [int(v) for v in os.environ["MK_BLKS"].split(",")] if os.environ.get("MK_BLKS") else list(range(NB))


GMAX = int(os.environ.get("MK_G", "3"))
NLANE = int(os.environ.get("MK_LANES", "3"))
if os.environ.get("MK_GROUPS"):
    GROUPS = [[int(v) for v in g.split(",")] for g in os.environ["MK_GROUPS"].split(";")]
else:
    GROUPS = [list(range(i, min(i + GMAX, NBP))) for i in range(0, NBP, GMAX)] + [[SB]]
TM = 128 * max(len(g) for g in GROUPS)
GM = TM // 128


PHASES = []


def build():
    nc = bass.Bass("TRN2", target_bir_lowering=False)
    P = Prog(nc)
    PHASES.clear()

    def phase(name):
        PHASES.append((name, P.count["pe"]))

    def din(name, shape, dtype=F32):
        return nc.dram_tensor(name, list(shape), dtype, kind="ExternalInput").ap()

    def dout(name, shape, dtype=F32):
        return nc.dram_tensor(name, list(shape), dtype, kind="ExternalOutput").ap()

    xin = din("xin", [NB * 128, D])
    w_in = din("w_in", [NL, D, 2320])
    w_o = din("w_o", [NL, D, D])
    w_gate = din("w_gate", [NL, D, DF])
    w_up = din("w_up", [NL, D, DF])
    w_down = din("w_down", [NL, DF, D])
    g1_d = din("g1", [128, NL * 8])
    g2_d = din("g2", [128, NL * 8])
    gg_d = din("gg", [128, NL])
    qkg_d = din("qkg", [64, NL * 2])
    snk_d = din("snk", [128, NL * 8])
    wg2_d = din("wg2", [32, NL * 256])
    bg_d = din("bg", [32, NL * 256])
    ck_d = din("ck", [NL, 16, 128, 128])
    cv_d = din("cv", [NL, 16, 128, 128])
    st_d = din("st", [NL, 16, 4, 64, 128])
    identb_d = din("identb", [128, 128], BF16)
    identf_d = din("identf", [128, 128])
    masks_d = din("masks", [128, 5 * 128 + 8], BF16)
    umat_d = din("umat", [128, 256])
    valid_d = din("valid", [128, NB])
    eq_d = din("eq", [64, 16 * 128], BF16)
    e2_d = din("e2", [128, 16], BF16)

    yp = dout("yp", [(NBP - 1) * 128, D])
    ys = dout("ys", [128, D])
    pk = dout("pk", [NL, 128, 128])
    pv = dout("pv", [NL, 128, 128])
    pst = dout("pst", [NL, 4, 64, 128])
    sk = dout("sk", [NL, 16, 128, 128])
    sv = dout("sv", [NL, 16, 128, 128])
    sst = dout("sst", [NL, 16, 4, 64, 128])

    wsc_t = nc.dram_tensor("wsc", [NL, NCH, 128, CH], BF16, kind="ExternalOutput").ap()
    wsc = Buf("wsc", wsc_t)

    sb = P.sb
    identb = sb("identb", [128, 128], BF16)
    identf = sb("identf", [128, 128], F32)
    masks = sb("masks", [128, 5 * 128 + 8], BF16)
    umat = sb("umat", [128, 256], F32)
    valid = sb("valid", [128, NB], F32)
    eq = sb("eq", [64, 16, 128], BF16)
    e2 = sb("e2", [128, 16], BF16)
    g1 = sb("g1", [128, NL * 8], F32)
    g2 = sb("g2", [128, NL * 8], F32)
    gg = sb("gg", [128, NL], F32)
    qkg = sb("qkg", [64, NL * 2], F32)
    esink = sb("esink", [128, NL * 8], F32)
    wg2 = sb("wg2", [32, NL * 256], F32)
    bg = sb("bg", [32, NL * 256], F32)
    ones64 = sb("ones64", [64, 64], BF16)
    ones1 = sb("ones1", [32, 128], F32)
    epsb = sb("epsb", [128, 1], F32)
    epsb1 = sb("epsb1", [128, 1], F32)

    x = sb("x", [128, GM, D], F32)
    st4 = sb("st4", [128, 8], F32)
    hb = sb("hb", [128, D], BF16)
    hT = sb("hT", [128, 8, TM], BF16)
    A3W = 3 * TM + 3 * TM + 3 * (TM // 2)
    arena3 = P.stack.enter_context(nc.sbuf_tensor("s_arena3", [128, A3W], F32))
    qkraw = [Buf("qkraw%d" % i, arena3[0:64, i * TM:(i + 1) * TM]) for i in range(3)]
    rqh = [Buf("rqh%d" % i, arena3[0:64, 3 * TM + i * TM:3 * TM + (i + 1) * TM]) for i in range(3)]
    qksq = [Buf("qksq%d" % i, arena3[0:64, 6 * TM + i * (TM // 2):6 * TM + (i + 1) * (TM // 2)].bitcast(BF16)) for i in range(3)]
    qn = sb("qn", [64, GM, 8, 128], BF16)
    kn32 = sb("kn32", [64, 2, TM], F32)
    kbuf = [sb("kbuf%d" % l, [64, GM + 1, 2, 128], BF16) for l in range(NL)]
    vaug = [sb("vaug%d" % l, [128, GM + 1, 2, 65], BF16) for l in range(NL)]
    gqraw = sb("gqraw", [64, 4, TM], BF16)
    gkraw = sb("gkraw", [64, 4, TM], BF16)
    glowT = sb("glowT", [32, TM], F32)
    v32 = sb("v32", [128, GM, 128], F32)
    gvb = sb("gvb", [128, GM, 512], BF16)
    sog = sb("sog", [128, GM, 512], BF16)
    UW = NFT * TM // 2
    A2W = UW + 2 * TM
    arena2 = P.stack.enter_context(nc.sbuf_tensor("s_arena2", [128, A2W], F32))

    def carve2(name, off, words, dtype, pat=None, parts=128, **kw):
        v = arena2[0:parts, off:off + words]
        if dtype is BF16:
            v = v.bitcast(BF16)
        if pat is not None:
            v = v.rearrange(pat, **kw)
        return Buf(name, v)

    uT = carve2("uT", 0, UW, BF16, "p (f t) -> p f t", f=NFT)
    sg = [carve2("sg%d" % i, UW + i * TM, TM, F32) for i in range(2)]
    tsets = []
    for i_ in range(NLANE):
        n_ = lambda s_: "%s_%d" % (s_, i_)
        if i_ == 0:
            tsets.append(dict(
                lnt=sb(n_("lnt"), [128, 256], F32), eG=sb(n_("eG"), [64, 4, 128], F32), enG=sb(n_("enG"), [64, 4, 128], F32),
                qtil=sb(n_("qtil"), [64, 4, 128], BF16), kt32=sb(n_("kt32"), [64, 4, 128], F32), ktil=sb(n_("ktil"), [64, 4, 128], BF16),
                khatT=sb(n_("khatT"), [64, 4, 128], BF16), khat=sb(n_("khat"), [128, 256], BF16), atm=sb(n_("atm"), [128, 4, 128], BF16),
                o32=sb(n_("o32"), [128, 4, 128], F32), gst=sb(n_("gst"), [128, 16], F32),
                PT=[sb(n_("PTa"), [128, 2, 512], BF16), sb(n_("PTb"), [128, 2, 512], BF16)],
                den=[sb(n_("dena"), [128, 8], F32), sb(n_("denb"), [128, 8], F32)]))
        elif i_ == 2:
            hTf = hT.t[:, :, :].rearrange("p k t -> p (k t)").bitcast(F32)
            hbf = hb.t[:, :].bitcast(F32)
            assert 4 * TM >= 1536 and A3W >= 2464

            def cv(base, nm, off, words, dtype, pat=None, parts=128, **kw):
                v = base[0:parts, off:off + words]
                if dtype is BF16:
                    v = v.bitcast(BF16)
                if pat is not None:
                    v = v.rearrange(pat, **kw)
                return Buf(n_(nm), v)
            ts2 = dict(
                eG=cv(arena3, "eG", 0, 512, F32, "p (h t) -> p h t", parts=64, h=4), enG=cv(arena3, "enG", 512, 512, F32, "p (h t) -> p h t", parts=64, h=4),
                kt32=cv(arena3, "kt32", 1024, 512, F32, "p (h t) -> p h t", parts=64, h=4), qtil=cv(arena3, "qtil", 1536, 256, BF16, "p (h t) -> p h t", parts=64, h=4),
                ktil=cv(arena3, "ktil", 1792, 256, BF16, "p (h t) -> p h t", parts=64, h=4), khatT=cv(arena3, "khatT", 2048, 256, BF16, "p (h t) -> p h t", parts=64, h=4),
                khat=cv(arena3, "khat", 2304, 128, BF16), gst=cv(arena3, "gst", 2432, 16, F32),
                den=[cv(arena3, "dena", 2448, 8, F32), cv(arena3, "denb", 2456, 8, F32)],
                PT=[cv(hTf, "PTa", 0, 512, BF16, "p (a t) -> p a t", a=2), cv(hTf, "PTb", 512, 512, BF16, "p (a t) -> p a t", a=2)],
                o32=cv(hTf, "o32", 1024, 512, F32, "p (h t) -> p h t", h=4),
                lnt=cv(hbf, "lnt", 0, 256, F32), atm=cv(hbf, "atm", 256, 256, BF16, "p (h t) -> p h t", h=4))
            tsets.append(ts2)
            hosts3 = qkraw + rqh + qksq
            for k_, v_ in ts2.items():
                for a_ in (v_ if isinstance(v_, list) else [v_]):
                    host = [hT] if k_ in ("PT", "o32") else ([hb] if k_ in ("lnt", "atm") else hosts3)
                    for b_ in host:
                        a_.aliases.append(b_)
                        b_.aliases.append(a_)
        else:
            assert i_ == 1
            o_ = [0]

            def c2(nm, words, dtype, pat=None, parts=128, **kw):
                b = carve2(n_(nm), o_[0], words, dtype, pat, parts, **kw)
                o_[0] += words
                return b
            ts1 = dict(
                lnt=c2("lnt", 256, F32), eG=c2("eG", 512, F32, "p (h t) -> p h t", parts=64, h=4), enG=c2("enG", 512, F32, "p (h t) -> p h t", parts=64, h=4),
                qtil=c2("qtil", 256, BF16, "p (h t) -> p h t", parts=64, h=4), kt32=c2("kt32", 512, F32, "p (h t) -> p h t", parts=64, h=4),
                ktil=c2("ktil", 256, BF16, "p (h t) -> p h t", parts=64, h=4), khatT=c2("khatT", 256, BF16, "p (h t) -> p h t", parts=64, h=4),
                khat=c2("khat", 128, BF16), atm=c2("atm", 256, BF16, "p (h t) -> p h t", h=4), o32=c2("o32", 512, F32, "p (h t) -> p h t", h=4),
                gst=c2("gst", 16, F32),
                PT=[c2("PTa", 512, BF16, "p (a t) -> p a t", a=2), c2("PTb", 512, BF16, "p (a t) -> p a t", a=2)],
                den=[c2("dena", 8, F32), c2("denb", 8, F32)])
            assert o_[0] <= A2W, (o_[0], A2W)
            tsets.append(ts1)
            flat = [v for v in ts1.values() if isinstance(v, Buf)] + ts1["PT"] + ts1["den"]
            for a_ in flat:
                for b_ in [uT] + sg:
                    a_.aliases.append(b_)
                    b_.aliases.append(a_)
    S = [sb("S%d" % l, [64, 4, 128], F32) for l in range(NL)]
    Sb = [sb("Sb%d" % l, [64, 4, 128], BF16) for l in range(NL)]
    merged = sb("merged", [128, GM, D], BF16)
    kt_out = sb("kt_out", [128, 128], F32)
    ring = [sb("ring%d" % i, [128, CH], BF16) for i in range(4)]

    AW = 12832
    arena = P.stack.enter_context(nc.sbuf_tensor("s_arena", [128, AW], F32))

    def carve(name, off, words, dtype, pat=None, parts=128, **kw):
        v = arena[0:parts, off:off + words]
        if dtype is BF16:
            v = v.bitcast(BF16)
        if pat is not None:
            v = v.rearrange(pat, **kw)
        return Buf(name, v)

    cst = carve("cst", 0, 2048, F32, "p (s c) -> p s c", s=16)
    ckb = carve("ckb", 2048, 1024, BF16, "p (s c) -> p s c", s=16)
    vc = carve("vc", 3072, 1040, BF16, "p (s g e) -> p s g e", s=16, g=2)
    ptc = carve("ptc", 4112, 512, BF16)
    khx = carve("khx", 4624, 512, BF16, "p (s c) -> p s c", s=4)
    kcT = carve("kcT", 5136, 2048, BF16, "p (s g t) -> p s g t", s=16, g=2, parts=64)
    otc = carve("otc", 7184, 1024, F32, "p (g h t) -> p g h t", g=2, h=4, parts=65)
    s0q = carve("s0q", 8208, 2048, F32, "p (s h v) -> p s h v", s=4, h=4, parts=64)
    s0b = carve("s0b", 10256, 1024, BF16, "p (s h v) -> p s h v", s=4, h=4, parts=64)
    qx = carve("qx", 11280, 1024, BF16, "p (h s t) -> p h s t", h=4, s=4, parts=64)
    qns = carve("qns", 12304, 512, BF16, "p (s g c) -> p s g c", s=16, g=2, parts=64)
    samp_bufs = [cst, ckb, vc, ptc, khx, kcT, otc, s0q, s0b, qx, qns]
    stg = [carve("stg%d" % i, 2048 * i, 2048, F32, "p (k w) -> p k w", k=8) for i in range(3)]
    cht = [carve("cht%d" % i, 6144 + 2048 * i, 2048, BF16) for i in range(3)]
    for a_ in stg + cht:
        for b_ in samp_bufs:
            a_.aliases.append(b_)
            b_.aliases.append(a_)

    pbF = [P.ps("pbF%d" % i, [128, 512], F32) for i in range(3)]
    pbT = [P.ps("pbT%d" % i, [128, 512], F32) for i in range(2)]
    pbA = [P.ps("pbA%d" % i, [128, 512], F32) for i in range(2)]
    pb16 = P.ps("pb16", [128, 1024], BF16)
    rr = {"F": 0, "T": 0, "A": 0}

    def bankF():
        rr["F"] += 1
        return pbF[rr["F"] % 3]

    def bankT():
        rr["T"] += 1
        return pbT[rr["T"] % 2]

    def bankA():
        rr["A"] += 1
        return pbA[rr["A"] % 2]

    def mm(out, lhsT, rhs, start, stop, r, w):
        P.op("pe", lambda e: e.matmul(out, lhsT, rhs, start=start, stop=stop), reads=r, writes=w)

    def tr(out, in_, ident, r, w):
        P.op("pe", lambda e: e.transpose(out, in_, ident), reads=r, writes=w)

    def act(out, in_, func, r, w, **kw):
        P.op("act", lambda e: e.activation(out, in_, func, **kw), reads=r, writes=w)

    def cp(eng, out, in_, r, w):
        if eng == "act":
            act(out, in_, AF.Copy, r, w)
        else:
            P.op(eng, lambda e: e.tensor_copy(out, in_), reads=r, writes=w)

    def tt(eng, out, a, b, op, r, w):
        P.op(eng, lambda e: e.tensor_tensor(out, a, b, op), reads=r, writes=w)

    def tsc(eng, out, a, s1, s2, op0, op1, r, w):
        if s2 is None:
            P.op(eng, lambda e: e.tensor_scalar(out, a, s1, None, op0), reads=r, writes=w)
        else:
            P.op(eng, lambda e: e.tensor_scalar(out, a, s1, s2, op0, op1), reads=r, writes=w)

    def stt(eng, out, a, s, b, op0, op1, r, w):
        P.op(eng, lambda e: e.scalar_tensor_tensor(out, a, s, b, op0, op1), reads=r, writes=w)

    def mset(eng, buf, ap, val):
        P.op(eng, lambda e: e.memset(ap, val), writes=[buf])

    cld = Buf("cld", None)
    cbufs = []
    for b_, d_ in ((identb, identb_d), (identf, identf_d), (masks, masks_d), (umat, umat_d), (valid, valid_d),
                   (e2, e2_d), (g1, g1_d), (g2, g2_d), (gg, gg_d), (qkg, qkg_d), (esink, snk_d), (wg2, wg2_d), (bg, bg_d)):
        P.dma("pool", cld, b_.t[:, :], d_, writes=[b_])
        cbufs.append(b_)
    P.dma("pool", cld, eq.t[:, :, :], eq_d.rearrange("p (s t) -> p s t", s=16), writes=[eq])
    cbufs.append(eq)
    for b_ in cbufs:
        b_.lw = {cld.dsem["pool"]: cld.dcount["pool"]}
    act(esink.t[:, :], esink.t[:, :], AF.Exp, [esink], [esink])
    mset("dve", ones64, ones64.t[:, :], 1.0)
    mset("dve", ones1, ones1.t[:, :], 1.0)
    mset("dve", epsb, epsb.t[:, :], EPS)
    mset("dve", epsb1, epsb1.t[:, :], 1.0)
    for l in range(NL):
        mset("dve", S[l], S[l].t[:, :, :], 0.0)
        mset("dve", Sb[l], Sb[l].t[:, :, :], 0.0)
        mset("pool", kbuf[l], kbuf[l].t[:, :, :, :], 0.0)
        mset("pool", vaug[l], vaug[l].t[:, :, :, :], 0.0)
        mset("pool", vaug[l], vaug[l].t[:, :, :, 64:65], 1.0)
    m_own = masks.t[:, 0:128]
    m_prev = masks.t[:, 128:256]
    m_own0 = masks.t[:, 256:384]
    m_samp = masks.t[:, 384:512]
    m_prev1 = masks.t[:, 512:640]
    m_cache = masks.t[:, 640:648]
    pth = Buf("pth", None)

    prep_state = {"stg": 0, "cht": 0, "eng": 0}
    for ct_ in cht:
        mset("pool", ct_, ct_.t[:, :], 0.0)

    def kview(ct, W):
        return ct.t[:, 0:8 * W].rearrange("p (k w) -> p k w", k=8)

    def prep_piece(ct, dst, src, kc, wdt, gain):
        s_ = stg[prep_state["stg"] % 3]
        prep_state["stg"] += 1
        q_ = "sp"
        P.load(q_, s_, s_.t[:, 0:kc, 0:wdt], src)
        eng = ("dve", "pool")[prep_state["eng"] % 2]
        prep_state["eng"] += 1
        if gain is None:
            if prep_state["eng"] % 3 == 0:
                eng = "act"
            cp(eng, dst, s_.t[:, 0:kc, 0:wdt], [s_], [ct])
        else:
            gbuf, gap = gain
            tt(eng, dst, s_.t[:, 0:kc, 0:wdt], gap.unsqueeze(2).broadcast_to([128, kc, wdt]), ALU.mult, [s_, gbuf], [ct])

    prep_tasks = {}
    wsc_c = {(l_, c_): Buf("wsc_%d_%d" % (l_, c_), None) for l_ in range(NL) for c_ in range(NCH)}

    def prep_chunk(l, c, pieces, zero=None):
        def task(ct):
            if zero is not None:
                mset("dve", ct, kview(ct, zero[0])[:, :, zero[1]:zero[2]], 0.0)
            for (dst_fn, src, kc, wdt, gain) in pieces:
                prep_piece(ct, dst_fn(ct), src, kc, wdt, gain)
            for q4 in range(4):
                P.dma("sp", ct, wsc_t[l, c][:, q4 * 1024:(q4 + 1) * 1024], ct.t[:, q4 * 1024:(q4 + 1) * 1024], reads=[ct], writes=[wsc_c[(l, c)]], ignore_waw=True)
        prep_tasks[(l, c)] = task

    for l in range(NL):
        wi = w_in[l].rearrange("(k p) c -> p k c", p=128)
        wo_ = w_o[l].rearrange("(k p) c -> p k c", p=128)
        wg_ = w_gate[l].rearrange("(k p) c -> p k c", p=128)
        wu_ = w_up[l].rearrange("(k p) c -> p k c", p=128)
        wd_ = w_down[l].rearrange("(k p) c -> p k c", p=128)
        G1 = (g1, g1.t[:, l * 8:(l + 1) * 8])
        G2 = (g2, g2.t[:, l * 8:(l + 1) * 8])

        def cols(W, d0, s0, n, src, gain):
            out = []
            for o in range(0, n, 256):
                wdt = min(256, n - o)
                out.append(((lambda ct, W=W, a=d0 + o, wdt=wdt: kview(ct, W)[:, :, a:a + wdt]), src[:, :, s0 + o:s0 + o + wdt], 8, wdt, gain))
            return out

        prep_chunk(l, 0, cols(512, 0, CQ, 512, wi, G1))
        prep_chunk(l, 1, cols(384, 0, CK, 128, wi, G1) + cols(384, 128, CGQ, 256, wi, G1))
        prep_chunk(l, 2, cols(288, 0, CGK, 256, wi, G1) + cols(288, 256, CGL, 16, wi, G1), zero=(288, 272, 288))
        prep_chunk(l, 3, cols(128, 0, CV, 128, wi, G1))
        prep_chunk(l, 4, cols(512, 0, CGV, 512, wi, G1))
        prep_chunk(l, 5, cols(512, 0, COG, 512, wi, G1))
        for hf in range(2):
            pcs = []
            for o in range(0, 512, 256):
                a = hf * 512 + o
                pcs.append(((lambda ct, o=o: kview(ct, 512)[:, 0:4, o:o + 256]), wo_[:, 0:4, a:a + 256], 4, 256, None))
                pcs.append(((lambda ct, o=o: kview(ct, 512)[:, 4:8, o:o + 256]), wo_[:, 4:8, a:a + 256], 4, 256,
                            (gg, gg.t[:, l:l + 1].broadcast_to([128, 4]))))
            prep_chunk(l, 6 + hf, pcs)
        for i in range(11):
            prep_chunk(l, 8 + i, cols(512, 0, 256 * i, 256, wg_, G2) + cols(512, 256, 256 * i, 256, wu_, G2))
        for j in range(6):
            nft = min(4, NFT - 4 * j)
            pcs = []
            for o in range(0, 1024, 256):
                pcs.append(((lambda ct, o=o, nft=nft: ct.t[:, :].rearrange("p (k w) -> p k w", k=4)[:, 0:nft, o:o + 256]),
                            wd_[:, 4 * j:4 * j + nft, o:o + 256], nft, 256, None))
            prep_chunk(l, 19 + j, pcs)

    wseq = [(l, c) for _g in GROUPS for l in range(NL) for c in range(NCH)]
    wst = {"issued": 0}

    NPREP = NL * NCH

    def need(i):
        c_ = i % NCH
        if i < NPREP:
            lim = i
        else:
            lim = (i - c_ + 3) if c_ <= 2 else i + 3
        while wst["issued"] < min(len(wseq), lim + 1):
            j = wst["issued"]
            l_, c_ = wseq[j]
            if j < NPREP:
                prep_tasks[(l_, c_)](cht[j % 3])
            else:
                rb = ring[j % 4]
                for q4 in range(4):
                    P.dma("sp", rb, rb.t[:, q4 * 1024:(q4 + 1) * 1024], wsc_t[l_, c_][:, q4 * 1024:(q4 + 1) * 1024], reads=[wsc_c[(l_, c_)]], writes=[rb], ignore_waw=(q4 > 0))
            wst["issued"] += 1
        return cht[i % 3] if i < NPREP else ring[i % 4]

    tb_state = {"i": 0}

    def tbank():
        tb_state["i"] += 1
        k = tb_state["i"] % 3
        if k == 0:
            return pb16, pb16.t[:, :]
        bkk = pbT[k - 1]
        return bkk, bkk.t[:, :].bitcast(BF16)

    hbs = [hb, sb("hb1", [128, D], BF16)]
    xb = [Buf("xblk%d" % j_, x.t) for j_ in range(GM)]
    st4s = [st4] + [sb("st4_%d" % i, [128, 8], F32) for i in range(1, GM)]

    def rmsnorm_group(nb_):
        for j in range(nb_):
            mset("dve", st4s[j], st4s[j].t[:, 0:1], 0.0)
            act(hbs[j % 2].t[:, :], x.t[:, j, :], AF.Square, [xb[j]], [hbs[j % 2], st4s[j]], accum_out=st4s[j].t[:, 0:1])
        for j in range(nb_):
            act(st4s[j].t[:, 2:3], st4s[j].t[:, 0:1], AF.Ln, [st4s[j]], [st4s[j]], scale=1.0 / D, bias=epsb.t[:, 0:1])
            act(st4s[j].t[:, 3:4], st4s[j].t[:, 2:3], AF.Exp, [st4s[j]], [st4s[j]], scale=-0.5)

        def scale_(j):
            tsc("dve", hbs[j % 2].t[:, :], x.t[:, j, :], st4s[j].t[:, 3:4], None, ALU.mult, None, [xb[j], st4s[j]], [hbs[j % 2]])

        def trans_(j):
            tbk, tbv = tbank()
            for kc in range(8):
                tr(tbv[:, kc * 128:(kc + 1) * 128], hbs[j % 2].t[:, kc * 128:(kc + 1) * 128], identb.t[:, :], [hbs[j % 2], identb], [tbk])
            cp("dve" if j == 1 else "act", hT.t[:, :, j * 128:(j + 1) * 128], tbv.rearrange("p (k t) -> p k t", k=8), [tbk], [hT])

        for j in range(min(2, nb_)):
            scale_(j)
        for j in range(nb_):
            trans_(j)
            if j + 2 < nb_:
                scale_(j + 2)

    try:
        for gi, blks in enumerate(GROUPS):
            nb = len(blks)
            T = nb * 128
            samp = (blks[0] == SB)
            for j, blk in enumerate(blks):
                P.load("pool", xb[j], x.t[:, j, :], xin[blk * 128:(blk + 1) * 128, :])
            for l in range(NL):
                wbase = (gi * NL + l) * NCH
                phase("norm1")
                rmsnorm_group(nb)
                if STOP == 2 and (gi, l) == STOPAT:
                    raise _Stop()
                phase("qk")
                W0 = need(wbase + 0)
                W0v = W0.t[:, :].rearrange("p (k w) -> p k w", k=8)
                W1 = need(wbase + 1)
                W1v = W1.t[:, 0:8 * 384].rearrange("p (k w) -> p k w", k=8)
                pend = []

                def qk_tail(h, sl):
                    bk2 = bankA()
                    mm(bk2.t[0:64, 0:T], ones64.t[:, :], qksq[sl].t[:, 0:T], True, True, [ones64, qksq[sl]], [bk2])
                    act(rqh[sl].t[:, 0:T], bk2.t[0:64, 0:T], AF.Ln, [bk2], [rqh[sl]], scale=1.0 / 64, bias=epsb.t[0:64, 0:1])
                    act(rqh[sl].t[:, 0:T], rqh[sl].t[:, 0:T], AF.Exp, [rqh[sl]], [rqh[sl]], scale=-0.5)
                    if h < 8:
                        stt("dve", qn.t[:, 0:nb, h, :], qkraw[sl].t[:, 0:T].rearrange("p (j t) -> p j t", t=128), qkg.t[:, 2 * l:2 * l + 1],
                            rqh[sl].t[:, 0:T].rearrange("p (j t) -> p j t", t=128), ALU.mult, ALU.mult, [qkraw[sl], qkg, rqh[sl]], [qn])
                    else:
                        stt("dve", kn32.t[:, h - 8, 0:T], qkraw[sl].t[:, 0:T], qkg.t[:, 2 * l + 1:2 * l + 2], rqh[sl].t[:, 0:T],
                            ALU.mult, ALU.mult, [qkraw[sl], qkg, rqh[sl]], [kn32])

                W2 = need(wbase + 2)
                W2v = W2.t[:, 0:8 * 288].rearrange("p (k w) -> p k w", k=8)
                extra = []

                def g_head(kind, h):
                    bk_ = bankT()
                    if kind == "gq":
                        lwf, Wb, dst, eng, rows = (lambda kc: W1v[:, kc, 128 + h * 64:128 + (h + 1) * 64]), W1, gqraw.t[:, h, 0:T], "act", 64
                    elif kind == "gk":
                        lwf, Wb, dst, eng, rows = (lambda kc: W2v[:, kc, h * 64:(h + 1) * 64]), W2, gkraw.t[:, h, 0:T], "dve", 64
                    else:
                        lwf, Wb, dst, eng, rows = (lambda kc: W2v[:, kc, 256:288]), W2, glowT.t[:, 0:T], "act", 32
                    for kc in range(8):
                        mm(bk_.t[0:rows, 0:T], lwf(kc), hT.t[:, kc, 0:T], kc == 0, kc == 7, [Wb, hT], [bk_])
                    dbuf = gqraw if kind == "gq" else (gkraw if kind == "gk" else glowT)
                    cp(eng, dst, bk_.t[0:rows, 0:T], [bk_], [dbuf])

                for h_ in range(4):
                    extra.append(lambda h_=h_: g_head("gq", h_))
                for h_ in range(4):
                    extra.append(lambda h_=h_: g_head("gk", h_))
                extra.append(lambda: g_head("gl", 0))
                for h in range(10):
                    sl = h % 3
                    bk = bankF()
                    for kc in range(8):
                        lw = W0v[:, kc, h * 64:(h + 1) * 64] if h < 8 else W1v[:, kc, (h - 8) * 64:(h - 7) * 64]
                        mm(bk.t[0:64, 0:T], lw, hT.t[:, kc, 0:T], kc == 0, kc == 7, [W0 if h < 8 else W1, hT], [bk])
                    cp("dve", qkraw[sl].t[:, 0:T], bk.t[0:64, 0:T], [bk], [qkraw[sl]])
                    tt("pool", qksq[sl].t[:, 0:T], qkraw[sl].t[:, 0:T], qkraw[sl].t[:, 0:T], ALU.mult, [qkraw[sl]], [qksq[sl]])
                    if extra:
                        extra.pop(0)()
                    if pend:
                        pend.pop()()
                    pend.append(lambda h=h, sl=sl: qk_tail(h, sl))
                pend.pop()()
                for j, blk in enumerate(blks):
                    cp("pool", kbuf[l].t[:, j + 1, :, :], kn32.t[:, :, j * 128:(j + 1) * 128], [kn32], [kbuf[l]])
                phase("gqgk")
                while extra:
                    extra.pop(0)()
                if STOP == 4 and (gi, l) == STOPAT:
                    raise _Stop()
                phase("tokmaj")
                W3 = need(wbase + 3)
                W3v = W3.t[:, 0:8 * 128].rearrange("p (k w) -> p k w", k=8)
                for j, blk in enumerate(blks):
                    bk = bankT()
                    for kc in range(8):
                        mm(bk.t[:, 0:128], hT.t[:, kc, j * 128:(j + 1) * 128], W3v[:, kc, :], kc == 0, kc == 7, [W3, hT], [bk])
                    cp("act", v32.t[:, j, :], bk.t[:, 0:128], [bk], [v32])
                    cp("dve", vaug[l].t[:, j + 1, :, 0:64], bk.t[:, 0:128].rearrange("p (g d) -> p g d", g=2), [bk], [vaug[l]])
                    if j + 1 < nb:
                        pass
                W4 = need(wbase + 4)
                W4v = W4.t[:, :].rearrange("p (k w) -> p k w", k=8)
                for j in range(nb):
                    bk = bankT()
                    for kc in range(8):
                        mm(bk.t[:, :], hT.t[:, kc, j * 128:(j + 1) * 128], W4v[:, kc, :], kc == 0, kc == 7, [W4, hT], [bk])
                    cp("act", gvb.t[:, j, :], bk.t[:, :], [bk], [gvb])
                W5 = need(wbase + 5)
                W5v = W5.t[:, :].rearrange("p (k w) -> p k w", k=8)
                for j in range(nb):
                    bk = bankT()
                    for kc in range(8):
                        mm(bk.t[:, :], hT.t[:, kc, j * 128:(j + 1) * 128], W5v[:, kc, :], kc == 0, kc == 7, [W5, hT], [bk])
                    act(sog.t[:, j, :], bk.t[:, :], AF.Silu, [bk], [sog])

                if STOP == 5 and (gi, l) == STOPAT:
                    raise _Stop()
                phase("chains")
                if samp:
                    if l == 0:
                        mset("pool", vc, vc.t[:, :, :, 64:65], 1.0)
                    P.load("pool", cst, cst.t[:, :, :], ck_d[l].rearrange("s k c -> k s c"))
                    cp("pool", ckb.t[:, :, :], cst.t[:, :, :], [cst], [ckb])
                    for rnd in range(4):
                        for i in range(8):
                            s_, g_ = (rnd * 8 + i) // 2, (rnd * 8 + i) % 2
                            tr(pb16.t[0:64, i * 128:(i + 1) * 128], ckb.t[:, s_, g_ * 64:(g_ + 1) * 64], identb.t[:, :], [ckb, identb], [pb16])
                        cp("dve", kcT.t[:, rnd * 4:rnd * 4 + 4, :, :], pb16.t[0:64, :].rearrange("p (s g t) -> p s g t", s=4, g=2), [pb16], [kcT])
                    P.dma("pool", pth, sk[l, :, 0:120, :], ck_d[l, :, 8:128, :], reads=[], writes=[], out=True)
                    P.load("pool", cst, cst.t[:, :, :], cv_d[l].rearrange("s k c -> k s c"))
                    cp("pool", vc.t[:, :, :, 0:64], cst.t[:, :, :].rearrange("p s (g d) -> p s g d", g=2), [cst], [vc])
                    P.dma("pool", pth, sv[l, :, 0:120, :], cv_d[l, :, 8:128, :], reads=[], writes=[], out=True)
                    for g_ in range(2):
                        cp("pool", qns.t[:, :, g_, :].rearrange("p s (h t) -> p h s t", h=4), qn.t[:, 0, 4 * g_:4 * g_ + 4, :].rearrange("p h (s t) -> p h s t", t=8), [qn], [qns])
                    stc = [bankA(), bankA()]
                    for s_ in range(16):
                        for g_ in range(2):
                            bk = stc[s_ // 8]
                            o_ = ((s_ % 8) * 2 + g_) * 32
                            mm(bk.t[:, o_:o_ + 32], kcT.t[:, s_, g_, :], qns.t[:, s_, g_, :], True, True, [kcT, qns], [bk])
                    for hf in range(2):
                        act(ptc.t[:, hf * 512:(hf + 1) * 512], stc[hf].t[:, :], AF.Exp, [stc[hf]], [ptc], scale=0.125)
                    tt("pool", ptc.t[:, :].rearrange("p (a q) -> p a q", q=8), ptc.t[:, :].rearrange("p (a q) -> p a q", q=8),
                       m_cache.unsqueeze(1).broadcast_to([128, 128, 8]), ALU.mult, [ptc, masks], [ptc])
                    otb = [bankA(), bankA()]
                    for s_ in range(16):
                        for g_ in range(2):
                            bk = otb[s_ // 8]
                            o_ = ((s_ % 8) * 2 + g_) * 32
                            mm(bk.t[0:65, o_:o_ + 32], vc.t[:, s_, g_, :], ptc.t[:, (s_ * 2 + g_) * 32:(s_ * 2 + g_ + 1) * 32], True, True, [vc, ptc], [bk])
                    for hf in range(2):
                        for g_ in range(2):
                            cp("act" if g_ else "dve", otc.t[:, g_, :, hf * 64:(hf + 1) * 64].rearrange("p h (s t) -> p s h t", t=8),
                               otb[hf].t[0:65, :].rearrange("p (s g h t) -> p s g h t", s=8, g=2, h=4)[:, :, g_, :, :], [otb[hf]], [otc])

                def attn_chain(j, blk, g, ts):
                    pt = ts["PT"][g]
                    den = ts["den"][g]
                    rq_ = qn.t[:, j, 4 * g:4 * g + 4, :].rearrange("p h t -> p (h t)")
                    has_prev = (not samp) and blk > 0
                    sbk = yield from take(2 if has_prev else 1)
                    bo = sbk[0]
                    mm(bo.t[:, :], kbuf[l].t[:, j + 1, g, :], rq_, True, True, [kbuf[l], qn], [bo])
                    if has_prev:
                        bp = sbk[1]
                        mm(bp.t[:, :], kbuf[l].t[:, j, g, :], rq_, True, True, [kbuf[l], qn], [bp])
                    yield
                    act(pt.t[:, 0, :], bo.t[:, :], AF.Exp, [bo], [pt], scale=0.125)
                    if has_prev:
                        act(pt.t[:, 1, :], bp.t[:, :], AF.Exp, [bp], [pt], scale=0.125)
                    give(*sbk)
                    yield
                    mk = m_samp if samp else (m_own0 if blk == 0 else m_own)
                    tt("pool", pt.t[:, 0, :].rearrange("p (h t) -> p h t", h=4), pt.t[:, 0, :].rearrange("p (h t) -> p h t", h=4),
                       mk.unsqueeze(1).broadcast_to([128, 4, 128]), ALU.mult, [pt, masks], [pt])
                    if has_prev:
                        tt("pool", pt.t[:, 1, :].rearrange("p (h t) -> p h t", h=4), pt.t[:, 1, :].rearrange("p (h t) -> p h t", h=4),
                           (m_prev1 if blk == 1 else m_prev).unsqueeze(1).broadcast_to([128, 4, 128]), ALU.mult, [pt, masks], [pt])
                    yield
                    bv_ = (yield from take(1))[0]
                    for h in range(4):
                        oc = bv_.t[:, h * 65:(h + 1) * 65]
                        last_own = not (has_prev or samp)
                        mm(oc, pt.t[:, 0, h * 128:(h + 1) * 128], vaug[l].t[:, j + 1, g, :], True, last_own, [pt, vaug[l]], [bv_])
                        if has_prev:
                            mm(oc, pt.t[:, 1, h * 128:(h + 1) * 128], vaug[l].t[:, j, g, :], False, True, [pt, vaug[l]], [bv_])
                        if samp:
                            mm(oc, otc.t[:, g, h, :], identf.t[0:65, 0:65], False, True, [otc, identf], [bv_])
                    yield
                    pv4 = bv_.t[:, 0:260].rearrange("p (h e) -> p h e", h=4)
                    tt("dve", den.t[:, 0:4], pv4[:, :, 64], esink.t[:, l * 8 + 4 * g:l * 8 + 4 * g + 4], ALU.add, [bv_, esink], [den])
                    P.op("dve", lambda e: e.reciprocal(den.t[:, 4:8], den.t[:, 0:4]), reads=[den], writes=[den])
                    tt("dve", merged.t[:, j, g * 256:(g + 1) * 256].rearrange("p (h d) -> p h d", h=4), pv4[:, :, 0:64],
                       den.t[:, 4:8].unsqueeze(2).broadcast_to([128, 4, 64]), ALU.mult, [bv_, den], [merged])
                    give(bv_)

                def gla_chain(j, blk, ts):
                    lnt, eG, enG, qtil, kt32, ktil = ts["lnt"], ts["eG"], ts["enG"], ts["qtil"], ts["kt32"], ts["ktil"]
                    khatT, khat, atm, o32, gst = ts["khatT"], ts["khat"], ts["atm"], ts["o32"], ts["gst"]
                    c0, c1 = j * 128, (j + 1) * 128
                    bl = (yield from take(1))[0]
                    mm(bl.t[:, 0:256], glowT.t[:, c0:c1], wg2.t[:, l * 256:(l + 1) * 256], True, False, [glowT, wg2], [bl])
                    mm(bl.t[:, 0:256], ones1.t[:, :], bg.t[:, l * 256:(l + 1) * 256], False, True, [ones1, bg], [bl])
                    yield
                    act(lnt.t[:, :], bl.t[:, 0:256], AF.Exp, [bl], [lnt], scale=-1.0)
                    give(bl)
                    act(lnt.t[:, :], lnt.t[:, :], AF.Ln, [lnt], [lnt], bias=epsb1.t[:, 0:1])
                    yield
                    U = umat.t[:, 128:256] if samp else umat.t[:, 0:128]
                    gmask = m_samp if samp else m_own
                    bgT = (yield from take(1))[0]
                    for h in range(4):
                        mm(bgT.t[0:64, h * 128:(h + 1) * 128], lnt.t[:, h * 64:(h + 1) * 64], U, True, True, [lnt, umat], [bgT])
                    yield
                    g4 = bgT.t[0:64, :].rearrange("p (h t) -> p h t", h=4)
                    act(eG.t[:, :, :], g4, AF.Exp, [bgT], [eG])
                    act(enG.t[:, :, :], g4, AF.Exp, [bgT], [enG], scale=-1.0)
                    give(bgT)
                    yield
                    stt("dve", qtil.t[:, :, :], gqraw.t[:, :, c0:c1], 0.125, eG.t[:, :, :], ALU.mult, ALU.mult, [gqraw, eG], [qtil])
                    tt("dve", kt32.t[:, :, :], gkraw.t[:, :, c0:c1], enG.t[:, :, :], ALU.mult, [gkraw, enG], [kt32])
                    yield
                    cp("pool", ktil.t[:, :, :], kt32.t[:, :, :], [kt32], [ktil])
                    if samp:
                        egl = eG.t[:, :, :].rearrange("p h (s t) -> p h s t", t=8)[:, :, :, 7:8].broadcast_to([64, 4, 16, 8])
                        tt("pool", khatT.t[:, :, :].rearrange("p h (s t) -> p h s t", t=8), kt32.t[:, :, :].rearrange("p h (s t) -> p h s t", t=8),
                           egl, ALU.mult, [kt32, eG], [khatT])
                    else:
                        egl = eG.t[:, :, 127:128].broadcast_to([64, 4, 128])
                        tt("pool", khatT.t[:, :, :], kt32.t[:, :, :], egl, ALU.mult, [kt32, eG], [khatT])
                    yield
                    for h in range(4):
                        tr(pb16.t[:, h * 64:(h + 1) * 64], khatT.t[:, h, :], identb.t[0:64, 0:64], [khatT, identb], [pb16])
                    cp("dve", khat.t[:, :], pb16.t[:, 0:256], [pb16], [khat])
                    ba = (yield from take(1))[0]
                    for h in range(4):
                        mm(ba.t[:, h * 128:(h + 1) * 128], ktil.t[:, h, :], qtil.t[:, h, :], True, True, [ktil, qtil], [ba])
                    yield
                    tt("dve", atm.t[:, :, :], ba.t[:, :].rearrange("p (h t) -> p h t", h=4), gmask.unsqueeze(1).broadcast_to([128, 4, 128]), ALU.mult, [ba, masks], [atm])
                    give(ba)
                    yield
                    if not samp:
                        bo, bs = yield from take(2)
                        for h in range(4):
                            oc = bo.t[:, h * 128:(h + 1) * 128]
                            mm(oc, atm.t[:, h, :], gvb.t[:, j, h * 128:(h + 1) * 128], True, False, [atm, gvb], [bo])
                            mm(oc, qtil.t[:, h, :], Sb[l].t[:, h, :], False, True, [qtil, Sb[l]], [bo])
                        for h in range(4):
                            mm(bs.t[0:64, h * 128:(h + 1) * 128], khat.t[:, h * 64:(h + 1) * 64], gvb.t[:, j, h * 128:(h + 1) * 128], True, True, [khat, gvb], [bs])
                        tt("dve", S[l].t[:, :, :], S[l].t[:, :, :], eG.t[:, :, 127:128].broadcast_to([64, 4, 128]), ALU.mult, [S[l], eG], [S[l]])
                        tt("dve", S[l].t[:, :, :], S[l].t[:, :, :], bs.t[0:64, :].rearrange("p (h v) -> p h v", h=4), ALU.add, [S[l], bs], [S[l]])
                        cp("pool", Sb[l].t[:, :, :], S[l].t[:, :, :], [S[l]], [Sb[l]])
                        give(bs)
                    else:
                        obk = yield from take(4)
                        for h in range(4):
                            mm(obk[h].t[:, 0:128], atm.t[:, h, :], gvb.t[:, j, h * 128:(h + 1) * 128], True, False, [atm, gvb], [obk[h]])
                        eg4 = eG.t[:, :, :].rearrange("p h (s t) -> p h s t", t=8)
                        for qd in range(4):
                            P.load("pool", s0q, s0q.t[:, :, :, :], st_d[l, 4 * qd:4 * qd + 4].rearrange("s h k v -> k s h v"))
                            cp("pool", s0b.t[:, :, :, :], s0q.t[:, :, :, :], [s0q], [s0b])
                            tt("dve", qx.t[:, :, :, :], qtil.t[:, :, :].unsqueeze(2).broadcast_to([64, 4, 4, 128]),
                               eq.t[:, 4 * qd:4 * qd + 4, :].unsqueeze(1).broadcast_to([64, 4, 4, 128]), ALU.mult, [qtil, eq], [qx])
                            tt("pool", khx.t[:, :, :], khat.t[:, :].unsqueeze(1).broadcast_to([128, 4, 256]),
                               e2.t[:, 4 * qd:4 * qd + 4].unsqueeze(2).broadcast_to([128, 4, 256]), ALU.mult, [khat, e2], [khx])
                            for h in range(4):
                                for s_ in range(4):
                                    last = (qd == 3 and s_ == 3)
                                    mm(obk[h].t[:, 0:128], qx.t[:, h, s_, :], s0b.t[:, s_, h, :], False, last, [qx, s0b], [obk[h]])
                            for s_ in range(4):
                                bs = (yield from take(1))[0]
                                for h in range(4):
                                    mm(bs.t[0:64, h * 128:(h + 1) * 128], khx.t[:, s_, h * 64:(h + 1) * 64], gvb.t[:, j, h * 128:(h + 1) * 128], True, True, [khx, gvb], [bs])
                                sa = 4 * qd + s_
                                tt("dve", s0q.t[:, s_, :, :], s0q.t[:, s_, :, :], eg4[:, :, sa, 7:8].broadcast_to([64, 4, 128]), ALU.mult, [s0q, eG], [s0q])
                                tt("dve", s0q.t[:, s_, :, :], s0q.t[:, s_, :, :], bs.t[0:64, :].rearrange("p (h v) -> p h v", h=4), ALU.add, [s0q, bs], [s0q])
                                give(bs)
                            P.store("pool", s0q, sst[l, 4 * qd:4 * qd + 4].rearrange("s h k v -> k s h v"), s0q.t[:, :, :, :])
                    yield
                    if samp:
                        for h in range(4):
                            cp("act" if h % 2 else "dve", o32.t[:, h, :], obk[h].t[:, 0:128], [obk[h]], [o32])
                        give(*obk)
                    else:
                        cp("act", o32.t[:, :, :], bo.t[:, :].rearrange("p (h v) -> p h v", h=4), [bo], [o32])
                        give(bo)
                    mset("dve", gst, gst.t[:, 0:4], 0.0)
                    yield
                    for h in range(4):
                        act(atm.t[:, h, :], o32.t[:, h, :], AF.Square, [o32], [atm, gst], accum_out=gst.t[:, h:h + 1])
                    act(gst.t[:, 4:8], gst.t[:, 0:4], AF.Ln, [gst], [gst], scale=1.0 / 128, bias=epsb.t[:, 0:1])
                    act(gst.t[:, 8:12], gst.t[:, 4:8], AF.Exp, [gst], [gst], scale=-0.5)
                    yield
                    tt("dve", o32.t[:, :, :], o32.t[:, :, :], gst.t[:, 8:12].unsqueeze(2).broadcast_to([128, 4, 128]), ALU.mult, [o32, gst], [o32])
                    yield
                    tt("pool", merged.t[:, j, 512:1024], o32.t[:, :, :].rearrange("p h v -> p (h v)"), sog.t[:, j, :], ALU.mult, [o32, sog], [merged])
                    if blk == NBP - 1 or samp:
                        tb = (yield from take(1))[0]
                        for g in range(2):
                            tr(tb.t[:, g * 64:(g + 1) * 64], kn32.t[:, g, c0:c1], identf.t[0:64, 0:64], [kn32, identf], [tb])
                        cp("dve", kt_out.t[:, :], tb.t[:, 0:128], [tb], [kt_out])
                        give(tb)
                        if samp:
                            for s_ in range(16):
                                P.store("pool", kt_out, sk[l, s_, 120:128, :], kt_out.t[8 * s_:8 * s_ + 8, :])
                                P.store("pool", v32, sv[l, s_, 120:128, :], v32.t[8 * s_:8 * s_ + 8, j, :])
                        else:
                            P.store("pool", kt_out, pk[l], kt_out.t[:, :])
                            P.store("pool", v32, pv[l], v32.t[:, j, :])
                            P.store("pool", S[l], pst[l].rearrange("h k v -> k h v"), S[l].t[:, :, :])

                freeb = list(pbF) + list(pbT) + list(pbA)

                def take(n):
                    spins = 0
                    while len(freeb) < n:
                        spins += 1
                        assert spins < 10000, "psum bank deadlock"
                        yield
                    return [freeb.pop(0) for _ in range(n)]

                def give(*bs_):
                    freeb.extend(bs_)

                lanes = [[(j, blk) for j, blk in enumerate(blks) if j % NLANE == ln] for ln in range(NLANE)]
                active = [[] for _ in range(NLANE)]
                while any(lanes) or any(active):
                    for ln in range(NLANE):
                        if not active[ln] and lanes[ln]:
                            j, blk = lanes[ln].pop(0)
                            active[ln] = [attn_chain(j, blk, 0, tsets[ln]), attn_chain(j, blk, 1, tsets[ln]), gla_chain(j, blk, tsets[ln])]
                        for gen in list(active[ln]):
                            try:
                                next(gen)
                            except StopIteration:
                                active[ln].remove(gen)
                if STOP == 7 and (gi, l) == STOPAT:
                    raise _Stop()
                phase("wo")
                if not samp:
                    cp("pool", kbuf[l].t[:, 0, :, :], kbuf[l].t[:, nb, :, :], [kbuf[l]], [kbuf[l]])
                    cp("pool", vaug[l].t[:, 0, :, 0:64], vaug[l].t[:, nb, :, 0:64], [vaug[l]], [vaug[l]])
                for j in range(nb):
                    tbk, tbv = tbank()
                    for kc in range(8):
                        tr(tbv[:, kc * 128:(kc + 1) * 128], merged.t[:, j, kc * 128:(kc + 1) * 128], identb.t[:, :], [merged, identb], [tbk])
                    cp("dve" if j % 2 else "act", hT.t[:, :, j * 128:(j + 1) * 128], tbv.rearrange("p (k t) -> p k t", k=8), [tbk], [hT])
                for hf in range(2):
                    Wc = need(wbase + 6 + hf)
                    Wv = Wc.t[:, :].rearrange("p (k w) -> p k w", k=8)
                    for j, blk in enumerate(blks):
                        bk = bankT()
                        for kc in range(8):
                            mm(bk.t[:, :], hT.t[:, kc, j * 128:(j + 1) * 128], Wv[:, kc, :], kc == 0, kc == 7, [Wc, hT], [bk])
                        stt("dve", x.t[:, j, hf * 512:(hf + 1) * 512], bk.t[:, :], valid.t[:, blk:blk + 1], x.t[:, j, hf * 512:(hf + 1) * 512], ALU.mult, ALU.add, [bk, valid, xb[j]], [xb[j]])

                if STOP == 8 and (gi, l) == STOPAT:
                    raise _Stop()
                phase("ffn_gu")
                rmsnorm_group(nb)
                for i in range(11):
                    Wc = need(wbase + 8 + i)
                    Wv = Wc.t[:, :].rearrange("p (k a w) -> p k a w", k=8, a=2)
                    for jj in range(2):
                        ft = 2 * i + jj
                        bg_ = bankF()
                        bu_ = bankA()
                        for kc in range(8):
                            mm(bg_.t[:, 0:T], Wv[:, kc, 0, jj * 128:(jj + 1) * 128], hT.t[:, kc, 0:T], kc == 0, kc == 7, [Wc, hT], [bg_])
                        for kc in range(8):
                            mm(bu_.t[:, 0:T], Wv[:, kc, 1, jj * 128:(jj + 1) * 128], hT.t[:, kc, 0:T], kc == 0, kc == 7, [Wc, hT], [bu_])
                        sg_ = sg[ft % 2]
                        act(sg_.t[:, 0:T], bg_.t[:, 0:T], AF.Silu, [bg_], [sg_])
                        tt("dve", uT.t[:, ft, 0:T], sg_.t[:, 0:T], bu_.t[:, 0:T], ALU.mult, [sg_, bu_], [uT])
                phase("ffn_down")
                dbk = [pbT[0], pbT[1], pbA[0], pbA[1], pbF[0], pbF[1]]
                for ft in range(NFT):
                    Wc = need(wbase + 19 + ft // 4)
                    Wv = Wc.t[:, :].rearrange("p (k w) -> p k w", k=4)
                    for j in range(nb):
                        for hf in range(2):
                            bd = dbk[j * 2 + hf]
                            mm(bd.t[:, :], uT.t[:, ft, j * 128:(j + 1) * 128], Wv[:, ft % 4, hf * 512:(hf + 1) * 512], ft == 0, ft == NFT - 1, [Wc, uT], [bd])
                for j, blk in enumerate(blks):
                    for hf in range(2):
                        bd = dbk[j * 2 + hf]
                        stt("dve", x.t[:, j, hf * 512:(hf + 1) * 512], bd.t[:, :], valid.t[:, blk:blk + 1], x.t[:, j, hf * 512:(hf + 1) * 512], ALU.mult, ALU.add, [bd, valid, xb[j]], [xb[j]])
            for j, blk in enumerate(blks):
                if samp:
                    P.dma("pool", xb[j], ys, x.t[:, j, :], reads=[xb[j]], out=True)
                elif blk >= 1:
                    P.dma("pool", xb[j], yp[(blk - 1) * 128:blk * 128, :], x.t[:, j, :], reads=[xb[j]], out=True)
    except _Stop:
        pass
    P.finish()
    return nc


def _consts():
    bf = ml_dtypes.bfloat16
    j = np.arange(128)[:, None]
    i = np.arange(128)[None, :]
    own = (j <= i)
    prev = (j > i)
    own0 = own & (j >= 112)
    samp = (j // 8 == i // 8) & (j % 8 <= i % 8)
    cache = (np.arange(128)[:, None] > np.arange(8)[None, :])
    prev1 = prev & (j >= 112)
    masks = np.concatenate([own, prev, own0, samp, prev1, cache], axis=1).astype(np.float32).astype(bf)
    umat = np.concatenate([own.astype(np.float32), samp.astype(np.float32)], axis=1) * np.float32(-1.0 / 16.0)
    valid = np.ones((128, NB), np.float32)
    valid[0:112, 0] = 0.0
    t = np.arange(128)
    e2 = (t[:, None] // 8 == np.arange(16)[None, :]).astype(np.float32)
    eq = np.broadcast_to(e2.T[None, :, :], (64, 16, 128)).reshape(64, 16 * 128)
    return dict(identb=np.eye(128, dtype=np.float32).astype(bf), identf=np.eye(128, dtype=np.float32),
                masks=masks, umat=np.ascontiguousarray(umat.astype(np.float32)), valid=valid,
                eq=np.ascontiguousarray(eq).astype(bf), e2=e2.astype(bf))


_NC_CACHE = {}


def kernel(**inp):
    f = lambda a: np.ascontiguousarray(np.asarray(a, dtype=np.float32))
    x_prompt, x_sample = f(inp["x_prompt"]), f(inp["x_sample"])
    cache_k, cache_v, state_gla = f(inp["cache_k"]), f(inp["cache_v"]), f(inp["state_gla"])
    meta = f(inp["meta"])
    norm1, norm2, gla_norm = f(inp["norm1"]), f(inp["norm2"]), f(inp["gla_norm"])
    q_norm, k_norm, sinks = f(inp["q_norm"]), f(inp["k_norm"]), f(inp["sinks"])
    w_g2, b_g = f(inp["w_g2"]), f(inp["b_g"])
    common = dict(
        w_in=f(inp["w_in"]), w_o=f(inp["w_o"]), w_gate=f(inp["w_gate"]), w_up=f(inp["w_up"]), w_down=f(inp["w_down"]),
        g1=np.ascontiguousarray(norm1.reshape(NL, 8, 128).transpose(2, 0, 1).reshape(128, NL * 8)),
        g2=np.ascontiguousarray(norm2.reshape(NL, 8, 128).transpose(2, 0, 1).reshape(128, NL * 8)),
        gg=np.ascontiguousarray(gla_norm.T),
        qkg=np.ascontiguousarray(np.stack([q_norm[0], k_norm[0], q_norm[1], k_norm[1]], axis=1)),
        snk=np.ascontiguousarray(np.broadcast_to(sinks.reshape(1, NL * 8), (128, NL * 8))),
        wg2=np.ascontiguousarray(np.concatenate([w_g2.transpose(1, 0, 2).reshape(16, NL * 256), np.zeros((16, NL * 256), np.float32)], 0)),
        bg=np.ascontiguousarray(np.concatenate([b_g.reshape(1, NL * 256), np.zeros((31, NL * 256), np.float32)], 0)),
    )
    common.update(_consts())
    in_maps = []
    for c in range(8):
        seq = c % 4
        xin = np.zeros((NB * 128, D), np.float32)
        xin[112:128] = meta
        xin[128:NBP * 128] = x_prompt[seq]
        xin[NBP * 128:] = x_sample[16 * c:16 * c + 16].reshape(128, D)
        m = dict(common)
        m["xin"] = xin
        m["ck"] = np.ascontiguousarray(cache_k[:, 16 * c:16 * c + 16].reshape(NL, 16, 128, 128))
        m["cv"] = np.ascontiguousarray(cache_v[:, 16 * c:16 * c + 16].reshape(NL, 16, 128, 128))
        m["st"] = np.ascontiguousarray(state_gla[:, 16 * c:16 * c + 16])
        in_maps.append(m)
    if "nc" not in _NC_CACHE:
        _NC_CACHE["nc"] = build()
    res = run_bass_kernel_spmd(_NC_CACHE["nc"], in_maps, core_ids=list(range(8)))
    R = res.results
    y_prompt = np.stack([R[c]["yp"] for c in range(4)], axis=0).astype(np.float32)
    y_sample = np.concatenate([R[c]["ys"].reshape(16, 8, D) for c in range(8)], axis=0).astype(np.float32)
    pk = np.stack([R[c]["pk"].reshape(NL, 128, 2, 64) for c in range(4)], axis=1).astype(np.float32)
    pv = np.stack([R[c]["pv"].reshape(NL, 128, 2, 64) for c in range(4)], axis=1).astype(np.float32)
    pst = np.stack([R[c]["pst"] for c in range(4)], axis=1).astype(np.float32)
    sk = np.concatenate([R[c]["sk"].reshape(NL, 16, 128, 2, 64) for c in range(8)], axis=1).astype(np.float32)
    sv = np.concatenate([R[c]["sv"].reshape(NL, 16, 128, 2, 64) for c in range(8)], axis=1).astype(np.float32)
    sst = np.concatenate([R[c]["sst"] for c in range(8)], axis=1).astype(np.float32)
    return (y_prompt, y_sample, pk, pv, pst, sk, sv, sst)
```
